# Optimizing a Trainium2 kernel written in Bass

```python
import math
import jax, jax.numpy as jnp
from jax import lax
import numpy as np

D_MODEL = 1024
BATCH = 16
SEQ = 4096
DEPTH = 1

HEAD_DIM = 64
D_MIX = D_MODEL
A_HEADS = D_MIX // 2 // HEAD_DIM
A_KV = A_HEADS // 4
B_HEADS = D_MIX // 2 // HEAD_DIM
B_KV = B_HEADS // 4
D_A = A_HEADS * HEAD_DIM
D_B = B_HEADS * HEAD_DIM
KV_A = A_KV * HEAD_DIM
KV_B = B_KV * HEAD_DIM
N_HEADS_TOTAL = A_HEADS + B_HEADS
SWA_WINDOW = 128
ATTN_BLOCK = 128
CMP_LEN = 32
CMP_STRIDE = 16
CMP_HIDDEN = 256
SEL_LEN = 64
SEL_TOPK = 16
NSA_WINDOW = 512
NSA_Q_CHUNK = 32
N_NSA_BRANCHES = 3
N_BUCKETS = 32
MAX_DISTANCE = 128
FORCE_BONUS = 1e4
EPS = 1e-6
PROJ_SIZES = (D_A, KV_A, KV_A, D_A,
              D_B, KV_B, KV_B, KV_B, KV_B, KV_B, KV_B, D_B, B_HEADS * N_NSA_BRANCHES)
D_PROJ = sum(PROJ_SIZES)

kernel_name = "hybrid_swa_sink_nsa_adaln_block"


def _split_points():
    pts, acc = [], 0
    for s in PROJ_SIZES[:-1]:
        acc += s
        pts.append(acc)
    return pts


def rms_norm(x, gain):
    xf = x.astype(jnp.float32)
    y = xf * lax.rsqrt(jnp.mean(xf * xf, -1, keepdims=True) + EPS)
    return (y * gain.astype(jnp.float32)).astype(x.dtype)


def qk_norm(t, gain):
    tf = t.astype(jnp.float32)
    return tf * lax.rsqrt(jnp.mean(tf * tf, -1, keepdims=True) + EPS) * gain.astype(jnp.float32)


def t5_bucket(dist):
    n = jnp.maximum(dist, 0)
    max_exact = N_BUCKETS // 2
    nf = jnp.maximum(n, 1).astype(jnp.float32)
    large = max_exact + (jnp.log(nf / max_exact) / math.log(MAX_DISTANCE / max_exact)
                         * (N_BUCKETS - max_exact)).astype(jnp.int32)
    large = jnp.minimum(large, N_BUCKETS - 1)
    return jnp.where(n < max_exact, n, large)


def masked_softmax(logits, mask, sink=None):
    z = jnp.where(mask, logits, -jnp.inf)
    m = jnp.max(z, -1, keepdims=True)
    if sink is not None:
        m = jnp.maximum(m, sink)
    m = jnp.where(jnp.isfinite(m), m, 0.0)
    e = jnp.where(mask, jnp.exp(z - m), 0.0)
    denom = jnp.sum(e, -1, keepdims=True)
    if sink is not None:
        denom = denom + jnp.exp(sink - m)
    return e / jnp.maximum(denom, 1e-30)


def banded_gqa(q, k, v, window, head_bias, sinks):
    Bn, S, H, D = q.shape
    G = k.shape[2]
    R = H // G
    nblk = S // ATTN_BLOCK
    span = window + ATTN_BLOCK
    kp = jnp.pad(k.astype(jnp.float32), ((0, 0), (window, 0), (0, 0), (0, 0)))
    vp = jnp.pad(v.astype(jnp.float32), ((0, 0), (window, 0), (0, 0), (0, 0)))
    qb = q.reshape(Bn, nblk, ATTN_BLOCK, G, R, D).transpose(1, 0, 2, 3, 4, 5)
    hb = head_bias.astype(jnp.float32)
    sink = None if sinks is None else sinks.astype(jnp.float32).reshape(G, R, 1, 1)
    q_off = jnp.arange(ATTN_BLOCK)
    k_off = jnp.arange(span)

    def block(args):
        qi, i = args
        start = i * ATTN_BLOCK
        kb = lax.dynamic_slice_in_dim(kp, start, span, axis=1)
        vb = lax.dynamic_slice_in_dim(vp, start, span, axis=1)
        t = start + q_off
        s = start - window + k_off
        dist = t[:, None] - s[None, :]
        mask = (dist >= 0) & (dist < window) & (s[None, :] >= 0)
        bias = hb[t5_bucket(dist)].reshape(ATTN_BLOCK, span, G, R).transpose(2, 3, 0, 1)
        logits = jnp.einsum('btgrd,bsgd->bgrts', qi, kb) + bias
        p = masked_softmax(logits, mask, sink)
        return jnp.einsum('bgrts,bsgd->btgrd', p, vb)

    out = lax.map(block, (qb, jnp.arange(nblk)))
    return out.transpose(1, 0, 2, 3, 4, 5).reshape(Bn, S, H, D)


def compress_blocks(t, pos, w1, w2):
    Bn, S, G, D = t.shape
    nc = (S - CMP_LEN) // CMP_STRIDE + 1
    idx = jnp.arange(nc)[:, None] * CMP_STRIDE + jnp.arange(CMP_LEN)[None, :]
    blocks = t[:, idx] + pos[None, None, :, None, :]
    flat = blocks.transpose(0, 1, 3, 2, 4).reshape(Bn, nc, G, CMP_LEN * D)
    return jax.nn.silu(flat @ w1) @ w2


def nsa_cmp_sel(q, k_cmp, v_cmp, k_sel, v_sel, head_bias):
    Bn, S, H, D = q.shape
    G = k_sel.shape[2]
    R = H // G
    nc = k_cmp.shape[1]
    ns = S // SEL_LEN
    topk = min(SEL_TOPK, ns)
    qc_len = NSA_Q_CHUNK
    nch = S // qc_len
    kc = k_cmp.astype(jnp.float32)
    vc = v_cmp.astype(jnp.float32)
    ks_b = k_sel.astype(jnp.float32).reshape(Bn, ns, SEL_LEN, G, D).transpose(0, 3, 1, 2, 4)
    vs_b = v_sel.astype(jnp.float32).reshape(Bn, ns, SEL_LEN, G, D).transpose(0, 3, 1, 2, 4)
    c_lo = jnp.arange(nc) * CMP_STRIDE
    c_end = c_lo + CMP_LEN - 1
    s_lo = jnp.arange(ns) * SEL_LEN
    overlap = jnp.clip(jnp.minimum(c_lo[:, None] + CMP_LEN, s_lo[None, :] + SEL_LEN)
                       - jnp.maximum(c_lo[:, None], s_lo[None, :]), 0, None).astype(jnp.float32) / CMP_LEN
    tbl = head_bias.astype(jnp.float32).reshape(N_BUCKETS, G, R).transpose(1, 0, 2)
    g_idx = jnp.arange(G)[None, :, None, None]
    blk = jnp.arange(ns)
    gather = jax.vmap(jax.vmap(lambda kb, ix: kb[ix]))
    qch = q.reshape(Bn, nch, qc_len, G, R, D).transpose(1, 0, 2, 3, 4, 5)

    def chunk(args):
        qi, i = args
        t = i * qc_len + jnp.arange(qc_len)
        logits_c = jnp.einsum('btgrd,bngd->bgrtn', qi, kc)
        p_c = masked_softmax(logits_c, c_end[None, :] <= t[:, None])
        o_cmp = jnp.einsum('bgrtn,bngd->btgrd', p_c, vc)
        imp = jnp.einsum('bgrtn,nj->bgtj', p_c, overlap)
        cur = t // SEL_LEN
        valid = blk[None, :] <= cur[:, None]
        forced = (blk[None, :] == 0) | (blk[None, :] == cur[:, None]) | (blk[None, :] == cur[:, None] - 1)
        score = jnp.where(valid, imp + jnp.where(forced, FORCE_BONUS, 0.0), -jnp.inf)
        _, sel = lax.top_k(score, topk)
        kg = gather(ks_b, sel).reshape(Bn, G, qc_len, topk * SEL_LEN, D)
        vg = gather(vs_b, sel).reshape(Bn, G, qc_len, topk * SEL_LEN, D)
        spos = (sel[..., None] * SEL_LEN + jnp.arange(SEL_LEN)).reshape(Bn, G, qc_len, topk * SEL_LEN)
        dist = t[None, None, :, None] - spos
        bias = tbl[g_idx, t5_bucket(dist)].transpose(0, 1, 4, 2, 3)
        logits_s = jnp.einsum('btgrd,bgtld->bgrtl', qi, kg) + bias
        p_s = masked_softmax(logits_s, (dist >= 0)[:, :, None])
        o_sel = jnp.einsum('bgrtl,bgtld->btgrd', p_s, vg)
        return o_cmp, o_sel

    o_cmp, o_sel = lax.map(chunk, (qch, jnp.arange(nch)))
    o_cmp = o_cmp.transpose(1, 0, 2, 3, 4, 5).reshape(Bn, S, H, D)
    o_sel = o_sel.transpose(1, 0, 2, 3, 4, 5).reshape(Bn, S, H, D)
    return o_cmp, o_sel


def hybrid_layer(x, c, w_ada, b_ada, norm_gain, w_in, b_nsa_gate, q_gain_a, k_gain_a, sinks,
                 q_gain_b, k_gain_cmp, k_gain_sel, k_gain_win, cmp_pos_k, cmp_pos_v,
                 w_cmp_k1, w_cmp_k2, w_cmp_v1, w_cmp_v2, w_out, rel_bias):
    Bn, S, _ = x.shape
    qscale = HEAD_DIM ** -0.5
    mod = jax.nn.silu(c) @ w_ada + b_ada
    shift, scale, gate = jnp.split(mod, 3, axis=-1)
    h = rms_norm(x, norm_gain) * (1 + scale[:, None, :]) + shift[:, None, :]
    proj = h @ w_in
    (q_a, k_a, v_a, z_a, q_b, kc, vc, ks, vs, kw, vw, z_b, g_b) = jnp.split(proj, _split_points(), axis=-1)
    heads = lambda t, n: t.reshape(Bn, S, n, HEAD_DIM)

    qa = qk_norm(heads(q_a, A_HEADS), q_gain_a) * qscale
    ka = qk_norm(heads(k_a, A_KV), k_gain_a)
    o_a = banded_gqa(qa, ka, heads(v_a, A_KV), SWA_WINDOW, rel_bias[:, :A_HEADS], sinks)

    bias_b = rel_bias[:, A_HEADS:]
    qb = qk_norm(heads(q_b, B_HEADS), q_gain_b) * qscale
    k_cmp = qk_norm(compress_blocks(heads(kc, B_KV), cmp_pos_k, w_cmp_k1, w_cmp_k2), k_gain_cmp)
    v_cmp = compress_blocks(heads(vc, B_KV), cmp_pos_v, w_cmp_v1, w_cmp_v2)
    k_sel = qk_norm(heads(ks, B_KV), k_gain_sel)
    k_win = qk_norm(heads(kw, B_KV), k_gain_win)
    o_cmp, o_sel = nsa_cmp_sel(qb, k_cmp, v_cmp, k_sel, heads(vs, B_KV), bias_b)
    o_win = banded_gqa(qb, k_win, heads(vw, B_KV), NSA_WINDOW, bias_b, None)
    gb = jax.nn.sigmoid((g_b + b_nsa_gate).astype(jnp.float32)).reshape(Bn, S, B_HEADS, N_NSA_BRANCHES, 1)
    o_b = gb[..., 0, :] * o_cmp + gb[..., 1, :] * o_sel + gb[..., 2, :] * o_win

    y = jnp.concatenate([o_a.reshape(Bn, S, D_A) * jax.nn.silu(z_a.astype(jnp.float32)),
                         o_b.reshape(Bn, S, D_B) * jax.nn.silu(z_b.astype(jnp.float32))], axis=-1)
    out = y.astype(x.dtype) @ w_out
    return x + gate[:, None, :] * out


def setup_inputs(seed: int = 0) -> dict:
    key = jax.random.key(seed)
    ks = jax.random.split(key, 24)
    nrm = lambda k, shape, s: jax.random.normal(k, shape, jnp.float32) * s
    gain = lambda k, shape: 1.0 + 0.1 * jax.random.normal(k, shape, jnp.float32)
    L = DEPTH
    return {
        "x": nrm(ks[0], (BATCH, SEQ, D_MODEL), 1.0),
        "c": nrm(ks[1], (BATCH, D_MODEL), 1.0),
        "w_ada": nrm(ks[2], (L, D_MODEL, 3 * D_MODEL), 0.5 * D_MODEL ** -0.5),
        "b_ada": nrm(ks[3], (L, 3 * D_MODEL), 0.01),
        "norm_gain": gain(ks[4], (L, D_MODEL)),
        "w_in": nrm(ks[5], (L, D_MODEL, D_PROJ), D_MODEL ** -0.5),
        "b_nsa_gate": nrm(ks[6], (L, B_HEADS * N_NSA_BRANCHES), 0.1),
        "q_gain_a": gain(ks[7], (L, HEAD_DIM)),
        "k_gain_a": gain(ks[8], (L, HEAD_DIM)),
        "sinks": nrm(ks[9], (L, A_HEADS), 1.0),
        "q_gain_b": gain(ks[10], (L, HEAD_DIM)),
        "k_gain_cmp": gain(ks[11], (L, HEAD_DIM)),
        "k_gain_sel": gain(ks[12], (L, HEAD_DIM)),
        "k_gain_win": gain(ks[13], (L, HEAD_DIM)),
        "cmp_pos_k": nrm(ks[14], (L, CMP_LEN, HEAD_DIM), 0.5),
        "cmp_pos_v": nrm(ks[15], (L, CMP_LEN, HEAD_DIM), 0.5),
        "w_cmp_k1": nrm(ks[16], (L, CMP_LEN * HEAD_DIM, CMP_HIDDEN), (CMP_LEN * HEAD_DIM) ** -0.5),
        "w_cmp_k2": nrm(ks[17], (L, CMP_HIDDEN, HEAD_DIM), CMP_HIDDEN ** -0.5),
        "w_cmp_v1": nrm(ks[18], (L, CMP_LEN * HEAD_DIM, CMP_HIDDEN), (CMP_LEN * HEAD_DIM) ** -0.5),
        "w_cmp_v2": nrm(ks[19], (L, CMP_HIDDEN, HEAD_DIM), CMP_HIDDEN ** -0.5),
        "w_out": nrm(ks[20], (L, D_MIX, D_MODEL), D_MIX ** -0.5),
        "rel_bias": nrm(ks[21], (N_BUCKETS, N_HEADS_TOTAL), 0.5),
    }


def reference(x, c, w_ada, b_ada, norm_gain, w_in, b_nsa_gate, q_gain_a, k_gain_a, sinks,
              q_gain_b, k_gain_cmp, k_gain_sel, k_gain_win, cmp_pos_k, cmp_pos_v,
              w_cmp_k1, w_cmp_k2, w_cmp_v1, w_cmp_v2, w_out, rel_bias):
    for l in range(DEPTH):
        x = hybrid_layer(x, c, w_ada[l], b_ada[l], norm_gain[l], w_in[l], b_nsa_gate[l],
                         q_gain_a[l], k_gain_a[l], sinks[l], q_gain_b[l], k_gain_cmp[l],
                         k_gain_sel[l], k_gain_win[l], cmp_pos_k[l], cmp_pos_v[l],
                         w_cmp_k1[l], w_cmp_k2[l], w_cmp_v1[l], w_cmp_v2[l], w_out[l], rel_bias)
    return x
```

```python
import numpy as np
import ml_dtypes
from contextlib import ExitStack
import concourse.bass as bass
import concourse.mybir as mybir
from concourse.bass_utils import run_bass_kernel_spmd

F32 = mybir.dt.float32
BF16 = mybir.dt.bfloat16
F32R = mybir.dt.float32r
AF = mybir.ActivationFunctionType
ALU = mybir.AluOpType
AX = mybir.AxisListType

NCORES = 8
SEQ = 4096
DM = 1024
NT = SEQ // 128
NB = 2
WCOLS = 3352
NEG = -30000.0
BONUS = 1.0e4
EPS = 1e-6
G_QA, G_QB, G_K, G_C, G_V, G_ZA, G_ZB = 0, 512, 1024, 1408, 1920, 2328, 2840
RA, RW = 3, 6


class Buf:
    __slots__ = ("w", "r", "dsem", "dcount", "name")

    def __init__(self, name=""):
        self.w = None
        self.r = {}
        self.dsem = None
        self.dcount = 0
        self.name = name


class Sched:
    ENG = ("pe", "act", "dve", "pool", "sp")

    def __init__(self, nc, same_engine_sync=True):
        self.nc = nc
        self.ops = {e: [] for e in self.ENG}
        self.count = {e: 0 for e in self.ENG}
        self.waited = {e: {} for e in self.ENG}
        self.sems = {}
        self.dma_sems = []
        self.same_engine_sync = same_engine_sync
        self.out_waits = []

    def _need(self, eng, dep, needs):
        if dep is None:
            return
        k, v = dep
        if k == eng:
            if eng in ("pe", "sp"):
                return
            if not self.same_engine_sync:
                return
        if self.waited[eng].get(k, 0) >= v:
            return
        if needs.get(k, 0) < v:
            needs[k] = v

    def _emit_waits(self, eng, needs):
        for k, v in needs.items():
            self.ops[eng].append(("w", k, v))
            self.waited[eng][k] = v

    def op(self, eng, fn, reads=(), writes=(), inc=True):
        needs = {}
        for b in reads:
            self._need(eng, b.w, needs)
        for b in writes:
            self._need(eng, b.w, needs)
            for k, v in b.r.items():
                self._need(eng, (k, v), needs)
        self._emit_waits(eng, needs)
        if inc:
            self.count[eng] += 1
            val = self.count[eng]
        else:
            val = self.count[eng] + 1
        self.ops[eng].append(("op", fn, [(eng, 1)] if inc else []))
        for b in writes:
            b.w = (eng, val)
            b.r = {}
        for b in reads:
            if b.r.get(eng, 0) < val:
                b.r[eng] = val
        return val

    def dma(self, fn, reads=(), writes=(), q="sp", is_output=False):
        needs = {}
        for b in reads:
            self._need(q, b.w, needs)
        for b in writes:
            self._need(q, b.w, needs)
            for k, v in b.r.items():
                self._need(q, (k, v), needs)
        self._emit_waits(q, needs)
        owner = writes[0] if writes else reads[0]
        if owner.dsem is None:
            owner.dsem = "dma%d" % len(self.dma_sems)
            self.dma_sems.append(owner.dsem)
        owner.dcount += 16
        k, v = owner.dsem, owner.dcount
        self.ops[q].append(("op", fn, [(k, 16)]))
        for b in writes:
            b.w = (k, v)
            b.r = {}
        for b in reads:
            b.r[k] = v
        if is_output:
            self.out_waits.append((k, v))

    def emit(self):
        nc = self.nc
        with ExitStack() as es:
            for e in self.ENG:
                self.sems[e] = es.enter_context(nc.semaphore("prog_" + e))
            for k in self.dma_sems:
                self.sems[k] = es.enter_context(nc.semaphore(k))
            fin = {}
            for k, v in self.out_waits:
                fin[k] = max(fin.get(k, 0), v)
            for k, v in fin.items():
                self.ops["sp"].append(("w", k, v))
            block = es.enter_context(nc.Block())
            sems = self.sems
            ops = self.ops

            def run(engine, lst):
                for item in lst:
                    if item[0] == "w":
                        engine.wait_ge(sems[item[1]], item[2])
                    else:
                        name, args, kw = item[1]
                        ins = getattr(engine, name)(*args, **kw)
                        for (k, n) in item[2]:
                            ins.then_inc(sems[k], n)

            @block.sync
            def _(e):
                run(e, ops["sp"])

            @block.tensor
            def _(e):
                run(e, ops["pe"])

            @block.scalar
            def _(e):
                run(e, ops["act"])

            @block.vector
            def _(e):
                run(e, ops["dve"])

            @block.gpsimd
            def _(e):
                run(e, ops["pool"])


def C(name, *args, **kw):
    return (name, args, kw)


def V(ap, dims):
    return bass.AP(tensor=ap.tensor, offset=ap.offset, ap=[list(ap.ap[0])] + [list(d) for d in dims])


def build_program(nb=NB, n_sb=NT // 4, stages="PCA", n_a=999, dumps=None, pro=9, nch=99):
    nc = bass.Bass("TRN2", target_bir_lowering=False)
    S = Sched(nc)

    def din(name, shape, dt=F32):
        return nc.dram_tensor(name, list(shape), dt, kind="ExternalInput").ap()

    x_d = din("x", [NB, SEQ, DM])
    cT_d = din("cT", [128, 8, NB])
    wada_d = din("w_ada", [DM, 3 * DM])
    badac_d = din("bada_col", [128, 16])
    badag_d = din("bada_gate", [1, DM])
    ngc_d = din("ng_col", [128, 8])
    win_d = din("w_in_p", [DM, WCOLS])
    bgate_d = din("bgate", [1, 24])
    gcols_d = din("gcols", [128, 6])
    sinks_d = din("sinks", [1, 8])
    posk_d = din("posk", [128, 16])
    posv_d = din("posv", [128, 16])
    w1k_d = din("w1k", [2048, 256])
    w1v_d = din("w1v", [2048, 256])
    w2k_d = din("w2k", [256, 64])
    w2v_d = din("w2v", [256, 64])
    wout_d = din("w_out", [DM, DM])
    biasA_d = din("biasA", [128, 2, 8, 128])
    biasB_d = din("biasB", [128, 2, 8, 128])
    c31_d = din("c31", [1, 8])
    maskAB_d = din("maskAB", [128, 4, 128])
    E_d = din("Emat", [128, SEQ], BF16)
    cmask_d = din("cmask", [128, 16, 128], BF16)
    tri_d = din("tri", [128, 128], BF16)
    pat_d = din("pat", [128, 192], BF16)
    ov_d = din("ov", [128, 2, 62], BF16)
    out_d = nc.dram_tensor("out", [NB, SEQ, DM], F32, kind="ExternalOutput").ap()

    with ExitStack() as es:
        def sb(name, shape, dt):
            return es.enter_context(nc.sbuf_tensor("s_" + name, list(shape), dt))

        def ps(name, shape, dt):
            return es.enter_context(nc.psum_tensor("p_" + name, list(shape), dt))

        PJ = ps("PJ", [128, 2, 512], F32)
        PT = ps("PT", [128, 2, 512], F32)
        ST = ps("ST", [128, 2, 512], F32)
        OA = ps("OA", [128, 2, 512], F32)
        b_pj = [Buf("pj0"), Buf("pj1")]
        b_pt = [Buf("pt0"), Buf("pt1")]
        b_st = [Buf("st0"), Buf("st1")]
        b_oa = [Buf("oa0"), Buf("oa1")]
        PTb = [PT[:, 0, :].bitcast(BF16), PT[:, 1, :].bitcast(BF16)]

        Win = sb("Win", [128, 8, WCOLS], BF16); b_win = Buf("win")
        Wout = sb("Wout", [128, 8, DM], BF16); b_wout = Buf("wout")
        w1k = sb("w1k", [128, 16, 256], BF16); b_w1k = Buf()
        w1v = sb("w1v", [128, 16, 256], BF16); b_w1v = Buf()
        w2k = sb("w2k", [128, 2, 64], BF16); b_w2k = Buf()
        w2v = sb("w2v", [128, 2, 64], BF16); b_w2v = Buf()
        b1k = sb("b1k", [128, 2], F32); b_b1k = Buf()
        b1v = sb("b1v", [128, 2], F32); b_b1v = Buf()
        STG = [sb("stg0", [128, 2048], F32), sb("stg1", [128, 2048], F32)]
        b_stg = [[Buf("s00"), Buf("s01")], [Buf("s10"), Buf("s11")]]
        kTa = [sb("kTa0", [128, RA * 128], BF16), sb("kTa1", [128, RA * 128], BF16)]; b_kTa = [Buf() for _ in range(RA)]
        kTw = [sb("kTw0", [128, RW * 128], BF16), sb("kTw1", [128, RW * 128], BF16)]; b_kTw = [Buf() for _ in range(RW)]
        KE = [sb("KE0", [128, SEQ], BF16), sb("KE1", [128, SEQ], BF16)]
        b_kTs = [Buf() for _ in range(NT)]
        Va = sb("Va", [128, RA, 2, 66], BF16); b_Va = [Buf() for _ in range(RA)]
        Vw = sb("Vw", [128, RW, 2, 66], BF16); b_Vw = [Buf() for _ in range(RW)]
        Vs = sb("Vs", [128, NT, 2, 66], BF16); b_Vs = [Buf() for _ in range(NT)]
        XW = 529
        Xk0 = sb("Xk0", [128, 2, XW], BF16)
        Xv0 = sb("Xv0", [128, 2, XW], BF16)
        Xk = [Xk0, Xk0]
        Xv = [Xv0, Xv0]
        bxk, bxv = Buf(), Buf()
        b_Xk = [bxk, bxk]
        b_Xv = [bxv, bxv]
        kcT = [sb("kcT0", [128, 256], BF16), sb("kcT1", [128, 256], BF16)]; b_kcT = Buf("kcT")
        VC = sb("VC", [128, 2, 2, 128], BF16); b_VC = Buf("VC")
        BA = sb("BA", [128, 2, 8, 128], F32R); b_BA = Buf()
        BB = sb("BB", [128, 2, 8, 128], F32R); b_BB = Buf()
        b_E = Buf()
        CM = sb("CM", [128, 16, 128], BF16); b_CM = Buf()
        TRI = sb("TRI", [128, 128], BF16); b_TRI = Buf()
        PAT = sb("PAT", [128, 192], BF16); b_PAT = Buf()
        GATE = sb("GATE", [128, NB, DM], F32); b_GATE = Buf()
        Gcol = sb("Gcol", [128, 8, NB], F32); b_Gcol = Buf()
        SHcol = sb("SHcol", [128, 8, NB], F32); b_SHcol = Buf()
        idf = sb("idf", [128, 128], F32); b_idf = Buf()
        idb = sb("idb", [128, 128], BF16); b_idb = Buf()
        idr = sb("idr", [128, 128], F32R); b_idr = Buf()
        gcols = sb("gcols", [128, 6], F32); b_gcols = Buf()
        esink = sb("esink", [128, 8], F32); b_esink = Buf()
        bgate = sb("bgate", [128, 24], F32); b_bgate = Buf()
        c31 = sb("c31", [128, 8], F32); b_c31 = Buf()


        stat = sb("stat", [128, 4], F32); b_stat = Buf()
        hT0 = sb("hT0", [128, 8, 128], BF16)
        hT = [hT0, hT0]
        bht = Buf()
        b_hT = [bht, bht]

        ssq = sb("ssq", [128, 16], F32); b_ssq = Buf()
        qn = [sb("qn0", [128, 512], BF16), sb("qn1", [128, 512], BF16)]
        b_qn = [Buf(), Buf()]
        qTa = sb("qTa", [128, 2, 4, 128], BF16); b_qTa = [Buf() for _ in range(2)]
        sz = sb("sz", [128, 2, DM], BF16); b_sz = [Buf() for _ in range(2)]
        scb = KE[0][:, :].bitcast(F32).rearrange("p (k b m) -> p k b m", k=8, b=NB)
        gts = sb("gts", [128, 2, 24], F32); b_gts = [Buf() for _ in range(2)]
        gtmp = sb("gtmp", [128, 24], F32); b_gtmp = Buf()
        zt2 = sb("zt2", [128, 512], F32); b_zt2 = Buf()
        hsig = sb("hsig", [128, 64], F32); b_hsig = Buf()
        nb1k = sb("nb1k", [128, 2], F32)
        nb1v = sb("nb1v", [128, 2], F32); b_nb1 = Buf()
        hidk = sb("hidk", [128, 2, 16], BF16); b_hidk = Buf()
        hidv = sb("hidv", [128, 2, 16], BF16); b_hidv = Buf()
        kc32 = sb("kc32", [16, 4], F32); b_kc32 = Buf()
        kndup = sb("kndup", [16, 128], BF16); b_kndup = Buf()
        kjunk = sb("kjunk", [16, 64], F32); b_kjunk = Buf()
        vstag = sb("vstag", [16, 64], BF16); b_vstag = Buf()
        PTs = [sb("PTs%d" % i, [128, 512], BF16) for i in range(2)]
        b_PTs = [Buf() for _ in range(2)]
        den = sb("den", [128, 16], F32); b_den = Buf()
        coef = sb("coef", [128, 16], F32); b_coef = Buf()
        impA = sb("impA", [128, 64], F32); b_impA = Buf()
        impB = sb("impB", [128, 64], F32); b_impB = Buf()
        score = sb("score", [128, 64], F32); b_score = Buf()
        wk = sb("wk", [128, 64], F32); b_wk = Buf()
        m8 = sb("m8", [128, 16], F32); b_m8 = Buf()
        negmg = [sb("negm0", [128, 128], F32), sb("negm1", [128, 128], F32)]
        b_negm = [Buf(), Buf()]
        QNt = sb("QNw", [128, 2, 2, 512], BF16)
        b_QNw = [[Buf(), Buf()], [Buf(), Buf()]]
        yB = sb("yB", [128, 2, 256], F32); b_yB = [Buf(), Buf()]
        maskAB = yB[:, :, :].rearrange("p a (b c) -> p (a b) c", b=2); b_maskAB = b_yB[0]
        ytmp = [sb("ytmp0", [128, 256], F32), sb("ytmp1", [128, 256], F32)]
        b_ytmp = [Buf(), Buf()]
        y = sb("y", [128, DM], BF16); b_y = Buf()
        sq = sb("sq", [128, 512], F32); b_sq = Buf()
        junk = sq[:, :].bitcast(BF16); b_junk = b_sq
        yT = sb("yT", [128, 8, 128], BF16); b_yT = Buf()
        rtmp = sb("rtmp", [128, 512], F32); b_rtmp = Buf()
        xn = sb("xn", [128, DM], F32); b_xn = Buf()
        scol = sb("scol", [128, 8, NB], F32); b_scol = Buf()

        modc = sb("modc", [128, 16, NB], F32); b_modc = Buf()
        posk = sb("posk", [128, 16], F32); b_posk = Buf()
        posv = sb("posv", [128, 16], F32); b_posv = Buf()
        poskb = sb("poskb", [128, 16], BF16); b_poskb = Buf()
        posvb = sb("posvb", [128, 16], BF16); b_posvb = Buf()
        badac = sb("badac", [128, 16], F32); b_badac = Buf()
        ngc = sb("ngc", [128, 8], F32); b_ngc = Buf()
        badag = sb("badag", [128, DM], F32) if False else None

        xs = [STG[0][:, 0:1024], STG[0][:, 1024:2048]]
        b_xs = b_stg[0]
        xr = [STG[1][:, 0:1024], STG[1][:, 1024:2048]]
        b_xr = b_stg[1]

        def pbc(d_ap, n):
            return bass.AP(tensor=d_ap.tensor, offset=d_ap.offset, ap=[[0, 128], [1, n]])

        if dumps:
            for (t_, bl_) in ((KE[0], b_kTs), (Vs, b_Vs), (Va, b_Va), (Vw, b_Vw), (qTa, b_qTa),
                              (sz, b_sz), (gts, b_gts), (hT0, [bht]), (QNt, b_QNw[0] + b_QNw[1]), (y, [b_y]),
                              (hidk, [b_hidk]), (score, [b_score]), (yB, b_yB), (den, [b_den]), (coef, [b_coef])):
                shp_ = list(t_.shape)
                S.op("pool", C("memset", t_[tuple(slice(None) for _ in shp_)], 0.0), writes=list(bl_))
        def ld(dst_ap, src_ap, buf):
            S.dma(C("dma_start", out=dst_ap, in_=src_ap), writes=[buf])

        ld(CM[:], cmask_d, b_CM)
        ld(TRI[:], tri_d, b_TRI)
        ld(PAT[:], pat_d, b_PAT)
        ld(gcols[:], gcols_d, b_gcols)
        ld(esink[:], pbc(sinks_d, 8), b_esink)
        ld(bgate[:], pbc(bgate_d, 24), b_bgate)
        ld(c31[:], pbc(c31_d, 8), b_c31)

        ld(scol[:], cT_d, b_scol)
        ld(posk[:], posk_d, b_posk)
        ld(posv[:], posv_d, b_posv)
        ld(badac[:], badac_d, b_badac)
        ld(ngc[:], ngc_d, b_ngc)
        S.op("pool", C("memset", idf[:], 0.0), writes=[b_idf])
        S.op("pool", C("affine_select", out=idf[:], in_=idf[:], pattern=[[-1, 128]], compare_op=ALU.not_equal,
                                                fill=1.0, base=0, channel_multiplier=1), reads=[b_idf], writes=[b_idf])
        S.op("dve", C("tensor_copy", out=idb[:], in_=idf[:]), reads=[b_idf], writes=[b_idb])
        S.op("dve", C("tensor_copy", out=idr[:], in_=idf[:]), reads=[b_idf], writes=[b_idr])
        S.op("act", C("activation", out=esink[:], in_=esink[:], func=AF.Exp), reads=[b_esink], writes=[b_esink])
        S.op("dve", C("tensor_scalar", out=gcols[:, 0:1], in0=gcols[:, 0:1], scalar1=0.125, scalar2=None, op0=ALU.mult),
             reads=[b_gcols], writes=[b_gcols])
        S.op("dve", C("tensor_scalar", out=gcols[:, 2:3], in0=gcols[:, 2:3], scalar1=0.125, scalar2=None, op0=ALU.mult),
             reads=[b_gcols], writes=[b_gcols])
        S.op("dve", C("tensor_copy", out=poskb[:], in_=posk[:]), reads=[b_posk], writes=[b_poskb])
        S.op("dve", C("tensor_copy", out=posvb[:], in_=posv[:]), reads=[b_posv], writes=[b_posvb])
        S.op("pool", C("memset", score[:], 0.0), writes=[b_score])
        S.op("pool", C("memset", score[:, 0:1], BONUS), reads=[b_score], writes=[b_score])
        for g_ in range(2):
            S.op("pool", C("memset", kcT[g_][:], 0.0), writes=[b_kcT])
            S.op("pool", C("memset", kTa[g_][:], 0.0), writes=b_kTa)
            S.op("pool", C("memset", kTw[g_][:], 0.0), writes=b_kTw)
        S.op("pool", C("memset", QNt[:, :, :, :], 0.0), writes=b_QNw[0] + b_QNw[1])
        S.op("pool", C("memset", VC[:], 0.0), writes=[b_VC])
        S.op("pool", C("memset", VC[:, :, :, 64:65], 1.0), reads=[b_VC], writes=[b_VC])
        S.op("pool", C("memset", VC[0:1, 0, :, 64:65], 0.0), reads=[b_VC], writes=[b_VC])
        for g in range(2):
            S.dma(C("dma_start", out=VC[:, :, g, 66:128], in_=ov_d), writes=[b_VC])
        for (Vt_, bV_) in ((Va, b_Va), (Vw, b_Vw), (Vs, b_Vs)):
            S.op("pool", C("memset", Vt_[:, :, :, 64:65], 1.0), writes=bV_)
            S.op("pool", C("memset", Vt_[:, :, :, 65:66], 0.0), writes=bV_)
        for s_ in range(1):
            S.op("pool", C("memset", Xk[s_][:], 0.0), writes=[b_Xk[s_]])
            S.op("pool", C("memset", Xv[s_][:], 0.0), writes=[b_Xv[s_]])

        stg_i = [0]

        def stage_load(src_ap, ncols, view=None):
            s_ = stg_i[0] % 2
            stg_i[0] += 1
            dst = STG[s_][:, 0:ncols] if view is None else view(STG[s_])
            S.dma(C("dma_start", out=dst, in_=src_ap), writes=b_stg[s_])
            return s_

        def cast_from_stage(eng, s_, dst_ap, ncols, dst_buf, src_view=None):
            src = STG[s_][:, 0:ncols] if src_view is None else src_view(STG[s_])
            S.op(eng, C("copy" if eng == "act" else "tensor_copy", out=dst_ap, in_=src), reads=b_stg[s_], writes=[dst_buf])

        ADA = pro >= 2
        S.op("act", C("activation", out=scol[:], in_=scol[:], func=AF.Silu), reads=[b_scol], writes=[b_scol])
        S.op("dve", C("tensor_copy", out=scb, in_=V(scol[:, 0, 0:1], [[NB, 8], [1, NB], [0, 128]])),
             reads=[b_scol], writes=[b_E])
        pj_i = [0]

        def next_pj():
            i_ = pj_i[0] % 2
            pj_i[0] += 1
            return i_

        win_chunks = [(kc, c0, c1) for kc in range(8) for (c0, c1) in ((0, 2048), (2048, WCOLS))] if pro >= 3 else []

        def emit_win_chunk(k):
            kc, c0, c1 = win_chunks[k]
            s2_ = stage_load(win_d[kc * 128:(kc + 1) * 128, c0:c1], c1 - c0)
            cast_from_stage("act" if c0 == 0 else "dve", s2_, Win[:, kc, c0:c1], c1 - c0, b_win)
        n_win_done = [0]
        for cg in range(12 if ADA else 0):
            src = wada_d[:, cg * 256:(cg + 1) * 256].rearrange("(kc p) c -> p kc c", p=128)
            s_ = stage_load(src, 2048, view=lambda t: t[:, :].rearrange("p (kc c) -> p kc c", kc=8))
            stv = STG[s_][:, :].rearrange("p (kc c) -> p kc c", kc=8)
            if cg < 8:
                for cc in range(2):
                    ch = cg * 2 + cc
                    pj = next_pj()
                    for kc in range(8):
                        S.op("pe", C("matmul",
                            PJ[:, pj, 0:NB], lhsT=stv[:, kc, cc * 128:(cc + 1) * 128], rhs=scol[:, kc, :],
                            start=(kc == 0), stop=(kc == 7)),
                            reads=b_stg[s_] + [b_scol], writes=[b_pj[pj]], inc=(kc == 7))
                    S.op("dve", C("tensor_scalar",
                        out=modc[:, ch, :], in0=PJ[:, pj, 0:NB], scalar1=badac[:, ch:ch + 1], scalar2=None, op0=ALU.add),
                        reads=[b_pj[pj], b_badac], writes=[b_modc])
            else:
                for b in range(NB):
                    pj = next_pj()
                    for kc in range(8):
                        S.op("pe", C("matmul",
                            PJ[:, pj, 0:256], lhsT=scb[:, kc, b, :], rhs=stv[:, kc, :],
                            start=(kc == 0), stop=(kc == 7)),
                            reads=b_stg[s_] + [b_E], writes=[b_pj[pj]], inc=(kc == 7))
                    c0 = (cg - 8) * 256
                    S.op("dve", C("tensor_copy", out=GATE[:, b, c0:c0 + 256], in_=PJ[:, pj, 0:256]),
                         reads=[b_pj[pj]], writes=[b_GATE])
            if n_win_done[0] < len(win_chunks):
                emit_win_chunk(n_win_done[0])
                n_win_done[0] += 1
        s_ = stage_load(pbc(badag_d, DM), DM)
        for b in range(NB if ADA else 0):
            S.op("dve", C("tensor_tensor", out=GATE[:, b, :], in0=GATE[:, b, :], in1=STG[s_][:, 0:DM], op=ALU.add),
                 reads=b_stg[s_] + [b_GATE], writes=[b_GATE])
        S.op("dve", C("tensor_copy", out=SHcol[:], in_=modc[:, 0:8, :]), reads=[b_modc], writes=[b_SHcol])
        S.op("dve", C("tensor_scalar", out=Gcol[:], in0=modc[:, 8:16, :], scalar1=1.0, scalar2=None, op0=ALU.add),
             reads=[b_modc], writes=[b_Gcol])
        S.op("dve", C("tensor_tensor", out=Gcol[:], in0=Gcol[:], in1=V(ngc[:, 0:1], [[1, 8], [0, NB]]), op=ALU.mult),
             reads=[b_Gcol, b_ngc], writes=[b_Gcol])

        S.dma(C("dma_start", out=KE[0][64:128, :], in_=E_d[64:128, :]), writes=[b_E])
        S.dma(C("dma_start", out=KE[1][0:64, :], in_=E_d[0:64, :]), writes=[b_E])
        while n_win_done[0] < len(win_chunks):
            emit_win_chunk(n_win_done[0])
            n_win_done[0] += 1
        for kc in range(0, 8 if pro >= 3 else 0, 2):
            src = wout_d[kc * 128:(kc + 2) * 128, :].rearrange("(a p) c -> p a c", p=128)
            s_ = stage_load(src, 2048, view=lambda t: t[:, :].rearrange("p (a c) -> p a c", a=2))
            cast_from_stage("act", s_, Wout[:, kc:kc + 2, :], 2048, b_wout,
                            src_view=lambda t: t[:, :].rearrange("p (a c) -> p a c", a=2))
        for (wd, wsb, wb) in ((w1k_d, w1k, b_w1k), (w1v_d, w1v, b_w1v)):
            for hlf in range(2 if pro >= 3 else 0):
                src = wd[hlf * 1024:(hlf + 1) * 1024, :].rearrange("(lp p) j -> p lp j", p=128)
                s_ = stage_load(src, 2048, view=lambda t: t[:, :].rearrange("p (lp j) -> p lp j", lp=8))
                cast_from_stage("dve", s_, wsb[:, hlf * 8:(hlf + 1) * 8, :], 2048, wb,
                                src_view=lambda t: t[:, :].rearrange("p (lp j) -> p lp j", lp=8))
        for (wd, wsb, wb) in ((w2k_d, w2k, b_w2k), (w2v_d, w2v, b_w2v)) if pro >= 3 else ():
            src = wd.rearrange("(jc p) d -> p jc d", p=128)
            s_ = stage_load(src, 128, view=lambda t: t[:, 0:128].rearrange("p (jc d) -> p jc d", jc=2))
            cast_from_stage("dve", s_, wsb[:], 128, wb, src_view=lambda t: t[:, 0:128].rearrange("p (jc d) -> p jc d", jc=2))
        S.dma(C("dma_start", out=maskAB, in_=maskAB_d), writes=b_yB)
        for (bd, Bt, bB, m0, isB) in ((biasA_d, BA, b_BA, 0, False), (biasB_d, BB, b_BB, 2, True)) if pro >= 4 else ():
            s_ = stage_load(bd, 2048, view=lambda t: t[:, :].rearrange("p (a h c) -> p a h c", a=2, h=8))
            stv = STG[s_][:, :].rearrange("p (a h c) -> p a h c", a=2, h=8)
            for ty in range(2):
                S.op("dve", C("tensor_tensor",
                    out=stv[:, ty], in0=stv[:, ty], in1=V(maskAB[:, m0 + ty, 0:1], [[0, 8], [1, 128]]), op=ALU.add),
                    reads=b_stg[s_] + b_yB, writes=b_stg[s_])
                if isB:
                    S.op("dve", C("tensor_tensor",
                        out=stv[:, ty], in0=stv[:, ty], in1=V(c31[:, 0:1], [[1, 8], [0, 128]]), op=ALU.subtract),
                        reads=b_stg[s_] + [b_c31], writes=b_stg[s_])
            S.op("dve", C("tensor_copy", out=Bt[:, :, :, :], in_=stv), reads=b_stg[s_], writes=[bB])
        for (wsb, wb, pb, bpb, b1, bb1) in ((w1k, b_w1k, poskb, b_poskb, b1k, b_b1k), (w1v, b_w1v, posvb, b_posvb, b1v, b_b1v)) if pro >= 5 else ():
            for jc in range(2):
                pj = next_pj()
                for lp in range(16):
                    S.op("pe", C("matmul",
                        PJ[:, pj, 0:2], lhsT=wsb[:, lp, jc * 128:(jc + 1) * 128], rhs=V(pb[:, lp:lp + 1], [[0, 2]]),
                        start=(lp == 0), stop=(lp == 15)), reads=[wb, bpb], writes=[b_pj[pj]], inc=(lp == 15))
                S.op("dve", C("tensor_copy", out=b1[:, jc:jc + 1], in_=PJ[:, pj, 0:1]),
                     reads=[b_pj[pj]], writes=[bb1])

        if pro >= 5:
            S.op("dve", C("tensor_scalar", out=nb1k[:], in0=b1k[:], scalar1=-1.0, scalar2=None, op0=ALU.mult), reads=[b_b1k], writes=[b_nb1])
            S.op("dve", C("tensor_scalar", out=nb1v[:], in0=b1v[:], scalar1=-1.0, scalar2=None, op0=ALU.mult), reads=[b_b1v], writes=[b_nb1])
        st_i = [0]
        pts_i = [0]

        def build_PC(b, i):
            u = i % 2
            xsl = i % 2
            ra, rw = i % RA, i % RW
            CPb = PT[:, 1, :]
            CPbb = PTb[1]
            b_cp = b_pt[1]
            chains = []

            def proj(c0, n):
                pj = next_pj()
                for kc in range(8):
                    S.op("pe", C("matmul", PJ[:, pj, 0:n], lhsT=hT0[:, kc, :], rhs=Win[:, kc, c0:c0 + n],
                                 start=(kc == 0), stop=(kc == 7)),
                         reads=[bht, b_win], writes=[b_pj[pj]], inc=(kc == 7))
                return pj

            def L0():
                S.dma(C("dma_start", out=xs[xsl], in_=x_d[b, i * 128:(i + 1) * 128, :]), writes=[b_xs[xsl]])

            def L1():
                S.op("act", C("activation", out=junk, in_=xs[xsl], func=AF.Square, accum_out=stat[:, 0:1]),
                     reads=[b_xs[xsl]], writes=[b_junk, b_stat])

            def L2():
                S.op("act", C("activation", out=stat[:, 1:2], in_=stat[:, 0:1], func=AF.Ln, scale=1.0 / DM, bias=EPS),
                     reads=[b_stat], writes=[b_stat])
                S.op("act", C("activation", out=stat[:, 2:3], in_=stat[:, 1:2], func=AF.Exp, scale=-0.5),
                     reads=[b_stat], writes=[b_stat])

            def L3():
                S.op("pool", C("tensor_scalar", out=xn[:], in0=xs[xsl], scalar1=stat[:, 2:3], scalar2=1.0,
                               op0=ALU.mult, op1=ALU.mult), reads=[b_xs[xsl], b_stat], writes=[b_xn])

            def LT(h):
                def f():
                    for kc in range(4 * h, 4 * h + 4):
                        S.op("pe", C("transpose", out=PT[:, 0, (kc % 4) * 128:(kc % 4 + 1) * 128],
                                     in_=xn[:, kc * 128:(kc + 1) * 128], identity=idf[:]),
                             reads=[b_xn, b_idf], writes=[b_pt[0]])
                return f

            def LE(h):
                def f():
                    for kc in range(4 * h, 4 * h + 4):
                        S.op("dve", C("tensor_scalar", out=hT0[:, kc, :], in0=PT[:, 0, (kc % 4) * 128:(kc % 4 + 1) * 128],
                                      scalar1=Gcol[:, kc, b:b + 1], scalar2=SHcol[:, kc, b:b + 1], op0=ALU.mult, op1=ALU.add),
                             reads=[b_pt[0], b_Gcol, b_SHcol], writes=[bht])
                return f
            chains.append((0, [L1, L2, L3, LT(0), LE(0), LT(1), LE(1)]))

            def norm_chain(c0, nh, qs, nchunks, evac):
                n = nh * 64
                st = {}

                def s0():
                    st["pj"] = proj(c0, n)

                def s1():
                    S.op("act", C("activation", out=sq[:, 0:n], in_=PJ[:, st["pj"], 0:n], func=AF.Square),
                         reads=[b_pj[st["pj"]]], writes=[b_sq])

                def s2():
                    S.op("dve", C("tensor_reduce", out=ssq[:, 0:nh], in_=sq[:, 0:n].rearrange("p (h d) -> p h d", d=64),
                                  axis=AX.X, op=ALU.add), reads=[b_sq], writes=[b_ssq])

                def s3():
                    S.op("act", C("activation", out=ssq[:, 0:nh], in_=ssq[:, 0:nh], func=AF.Ln, scale=1.0 / 64, bias=EPS),
                         reads=[b_ssq], writes=[b_ssq])
                    S.op("act", C("activation", out=ssq[:, 0:nh], in_=ssq[:, 0:nh], func=AF.Exp, scale=-0.5),
                         reads=[b_ssq], writes=[b_ssq])

                def s4():
                    pj = st["pj"]
                    S.op("dve", C("tensor_tensor", out=qn[qs][:, 0:n].rearrange("p (h d) -> p h d", d=64),
                                  in0=PJ[:, pj, 0:n].rearrange("p (h d) -> p h d", d=64),
                                  in1=V(ssq[:, 0:1], [[1, nh], [0, 64]]), op=ALU.mult),
                         reads=[b_pj[pj], b_ssq], writes=[b_qn[qs]])

                def s5():
                    for c in range(nchunks):
                        S.op("pe", C("transpose", out=PTb[0][:, c * 128:(c + 1) * 128], in_=qn[qs][:, c * 128:(c + 1) * 128], identity=idb[:]),
                             reads=[b_qn[qs], b_idb], writes=[b_pt[0]])
                return [s0, s1, s2, s3, s4, s5, evac]

            def evac_qa():
                S.op("dve", C("tensor_scalar", out=qTa[:, u].rearrange("p r t -> p (r t)"), in0=PTb[0][:, 0:512],
                              scalar1=gcols[:, 0:1], scalar2=None, op0=ALU.mult),
                     reads=[b_pt[0], b_gcols], writes=[b_qTa[u]])

            def evac_qb():
                for g_ in range(2):
                    hp_ = slice(g_ * 64, (g_ + 1) * 64)
                    S.op("dve", C("tensor_scalar", out=QNt[hp_, u, g_, :], in0=PTb[0][hp_, 0:512],
                                  scalar1=gcols[hp_, 2:3], scalar2=None, op0=ALU.mult),
                         reads=[b_pt[0], b_gcols], writes=[b_QNw[u][g_]])

            def evac_k():
                for g_ in range(2):
                    hp_ = slice(g_ * 64, (g_ + 1) * 64)
                    S.op("dve", C("tensor_scalar", out=kTa[g_][hp_, ra * 128:(ra + 1) * 128], in0=PTb[0][hp_, 0:128],
                                  scalar1=gcols[hp_, 1:2], scalar2=None, op0=ALU.mult),
                         reads=[b_pt[0], b_gcols], writes=[b_kTa[ra]])
                    S.op("dve", C("tensor_scalar", out=kTw[g_][hp_, rw * 128:(rw + 1) * 128], in0=PTb[0][hp_, 256:384],
                                  scalar1=gcols[hp_, 5:6], scalar2=None, op0=ALU.mult),
                         reads=[b_pt[0], b_gcols], writes=[b_kTw[rw]])
                    S.op("dve", C("tensor_scalar", out=KE[g_][hp_, i * 128:(i + 1) * 128], in0=PTb[0][hp_, 128:256],
                                  scalar1=gcols[hp_, 4:5], scalar2=None, op0=ALU.mult),
                         reads=[b_pt[0], b_gcols], writes=[b_kTs[i]])

            stc = {}
            tau0 = (i % 4) * 128

            def c0_():
                stc["pj"] = proj(G_C, 512)

            def c1_():
                S.op("act", C("copy", out=qn[1][:], in_=PJ[:, stc["pj"], :]), reads=[b_pj[stc["pj"]]], writes=[b_qn[1]])

            def c2_():
                for c in range(4):
                    S.op("pe", C("transpose", out=PTb[0][:, c * 128:(c + 1) * 128], in_=qn[1][:, c * 128:(c + 1) * 128], identity=idb[:]),
                         reads=[b_qn[1], b_idb], writes=[b_pt[0]])

            def c3_():
                for (Xt, bX, cb) in ((Xk, b_Xk, 0), (Xv, b_Xv, 256)):
                    S.op("dve", C("tensor_copy", out=Xt[0][0:64, :, 17 + tau0:17 + tau0 + 128],
                                  in_=PTb[0][0:64, cb:cb + 256].rearrange("p (g t) -> p g t", g=2)),
                         reads=[b_pt[0]], writes=[bX[0]])
                    S.op("dve", C("tensor_copy", out=Xt[0][64:128, :, 16 + tau0:16 + tau0 + 128],
                                  in_=PTb[0][64:128, cb:cb + 256].rearrange("p (g t) -> p g t", g=2)),
                         reads=[b_pt[0]], writes=[bX[0]])
            chains.append((8, [c0_, c1_, c2_, c3_]))
            chains.append((10, norm_chain(G_K, 6, 0, 3, evac_k)))

            stv_ = {}

            def v0_():
                stv_["pj"] = proj(G_V, 408)

            def v1_():
                pj = stv_["pj"]
                S.op("dve", C("tensor_copy", out=Va[:, ra, :, 0:64], in_=PJ[:, pj, 0:128].rearrange("p (g d) -> p g d", g=2)),
                     reads=[b_pj[pj]], writes=[b_Va[ra]])
                S.op("dve", C("tensor_copy", out=Vs[:, i, :, 0:64], in_=PJ[:, pj, 128:256].rearrange("p (g d) -> p g d", g=2)),
                     reads=[b_pj[pj]], writes=[b_Vs[i]])
                S.op("dve", C("tensor_copy", out=Vw[:, rw, :, 0:64], in_=PJ[:, pj, 256:384].rearrange("p (g d) -> p g d", g=2)),
                     reads=[b_pj[pj]], writes=[b_Vw[rw]])
                S.op("dve", C("tensor_tensor", out=gtmp[:], in0=PJ[:, pj, 384:408], in1=bgate[:], op=ALU.add),
                     reads=[b_pj[pj], b_bgate], writes=[b_gtmp])

            def v2_():
                S.op("act", C("activation", out=gtmp[:], in_=gtmp[:], func=AF.Exp, scale=-1.0), reads=[b_gtmp], writes=[b_gtmp])

            def v3_():
                S.op("dve", C("tensor_scalar", out=gtmp[:], in0=gtmp[:], scalar1=1.0, scalar2=None, op0=ALU.add),
                     reads=[b_gtmp], writes=[b_gtmp])
                S.op("dve", C("reciprocal", out=gts[:, u, :], in_=gtmp[:]), reads=[b_gtmp], writes=[b_gts[u]])
            chains.append((12, [v0_, v1_, v2_, v3_]))

            c0x = 128 * (i % 4)

            def cm(part):
                (Xt, bX, w1, bw1) = ((Xk, b_Xk, w1k, b_w1k), (Xv, b_Xv, w1v, b_w1v))[part // 2]
                jc = part % 2

                def f():
                    for lp in range(16):
                        S.op("pe", C("matmul", CPb[:, part * 16:(part + 1) * 16].rearrange("p (g m) -> p g m", g=2),
                                     lhsT=w1[:, lp, jc * 128:(jc + 1) * 128],
                                     rhs=Xt[0][:, :, c0x + 1 + 2 * lp:c0x + 1 + 2 * lp + 16 * 7 + 1:16],
                                     start=(lp == 0), stop=(lp == 15), skip_group_check=True),
                             reads=[bX[0], bw1], writes=[b_cp], inc=(lp == 15))
                return f

            def cm_act():
                for part in range(4):
                    nb1 = (nb1k, nb1v)[part // 2]
                    S.op("act", C("activation", out=hsig[:, part * 16:(part + 1) * 16], in_=CPb[:, part * 16:(part + 1) * 16],
                                  func=AF.Exp, scale=-1.0, bias=nb1[:, part % 2:part % 2 + 1]),
                         reads=[b_cp, b_nb1], writes=[b_hsig])

            def cm_dve1():
                S.op("dve", C("tensor_scalar", out=hsig[:], in0=hsig[:], scalar1=1.0, scalar2=None, op0=ALU.add),
                     reads=[b_hsig], writes=[b_hsig])
                S.op("dve", C("reciprocal", out=hsig[:], in_=hsig[:]), reads=[b_hsig], writes=[b_hsig])

            def cm_dve2():
                for part in range(4):
                    hid, bh = ((hidk, b_hidk), (hidv, b_hidv))[part // 2]
                    b1 = (b1k, b1v)[part // 2]
                    S.op("dve", C("scalar_tensor_tensor", out=hid[:, part % 2, :], in0=CPb[:, part * 16:(part + 1) * 16],
                                  scalar=b1[:, part % 2:part % 2 + 1], in1=hsig[:, part * 16:(part + 1) * 16],
                                  op0=ALU.add, op1=ALU.mult),
                         reads=[b_cp, b_b1k, b_b1v, b_hsig], writes=[bh])

            def cm_mm2():
                for jc in range(2):
                    S.op("pe", C("matmul", CPb[0:16, 64:128], lhsT=hidk[:, jc, :], rhs=w2k[:, jc, :],
                                 start=(jc == 0), stop=(jc == 1), skip_group_check=True),
                         reads=[b_hidk, b_w2k], writes=[b_cp], inc=False)
                for jc in range(2):
                    S.op("pe", C("matmul", CPb[0:16, 128:192], lhsT=hidv[:, jc, :], rhs=w2v[:, jc, :],
                                 start=(jc == 0), stop=(jc == 1), skip_group_check=True),
                         reads=[b_hidv, b_w2v], writes=[b_cp], inc=(jc == 1))

            def cm_a2():
                S.op("act", C("activation", out=kjunk[:], in_=CPb[0:16, 64:128], func=AF.Square, accum_out=kc32[:, 0:1]),
                     reads=[b_cp], writes=[b_kjunk, b_kc32])
                S.op("dve", C("tensor_copy", out=vstag[:], in_=CPb[0:16, 128:192]), reads=[b_cp, b_kc32], writes=[b_vstag])

            def cm_a3():
                S.op("act", C("activation", out=kc32[:, 1:2], in_=kc32[:, 0:1], func=AF.Ln, scale=1.0 / 64, bias=EPS),
                     reads=[b_kc32], writes=[b_kc32])
                S.op("act", C("activation", out=kc32[:, 2:3], in_=kc32[:, 1:2], func=AF.Exp, scale=-0.5),
                     reads=[b_kc32], writes=[b_kc32])
                n0 = 8 * i
                T, p0 = n0 // 128, n0 % 128
                m0 = 1 if i == 0 else 0
                for g in range(2):
                    S.dma(C("dma_start", out=VC[p0 + m0:p0 + 8, T, g, 0:64], in_=vstag[g * 8 + m0:(g + 1) * 8, :]),
                          reads=[b_vstag], writes=[b_VC])

            def cm_d3():
                for h2 in range(2):
                    S.op("dve", C("tensor_scalar", out=kndup[:, h2 * 64:(h2 + 1) * 64], in0=CPb[0:16, 64:128],
                                  scalar1=kc32[:, 2:3], scalar2=None, op0=ALU.mult),
                         reads=[b_cp, b_kc32], writes=[b_kndup])

            def cm_t():
                S.op("pe", C("transpose", out=PTb[0][:, 0:16], in_=kndup[:], identity=idb[0:16, 0:16]),
                     reads=[b_kndup, b_idb], writes=[b_pt[0]])

            def cm_e():
                n0 = 8 * i
                for g in range(2):
                    S.op("dve", C("tensor_scalar", out=kcT[g][g * 64:(g + 1) * 64, n0:n0 + 8],
                                  in0=PTb[0][g * 64:(g + 1) * 64, g * 8:(g + 1) * 8],
                                  scalar1=gcols[g * 64:(g + 1) * 64, 3:4], scalar2=None, op0=ALU.mult),
                         reads=[b_pt[0], b_gcols], writes=[b_kcT])
            nop = lambda: None
            import os as _os
            chains.append((12, [cm(0), cm(1), cm(2), cm(3), cm_act, cm_dve1, cm_dve2, cm_mm2, cm_a2, cm_a3, cm_d3, nop, nop, nop, cm_t, cm_e][:int(_os.environ.get('NCM', '99'))]))
            chains.append((15, norm_chain(G_QA, 8, 1, 4, evac_qa)))
            chains.append((18, norm_chain(G_QB, 8, 0, 4, evac_qb)))

            def z_chain(c0, zbuf, bz, col0):
                st = {}

                def s0():
                    st["pj"] = proj(c0, 512)

                def s1():
                    S.op("act", C("activation", out=zbuf, in_=PJ[:, st["pj"], :], func=AF.Exp, scale=-1.0),
                         reads=[b_pj[st["pj"]]], writes=[bz])

                def s2():
                    S.op("act", C("activation", out=zbuf, in_=zbuf, func=AF.Ln, bias=1.0), reads=[bz], writes=[bz])

                def s3():
                    S.op("act", C("activation", out=zbuf, in_=zbuf, func=AF.Exp, scale=-1.0), reads=[bz], writes=[bz])

                def s4():
                    pj = st["pj"]
                    S.op("dve", C("tensor_tensor", out=sz[:, u, col0:col0 + 512], in0=PJ[:, pj, :], in1=zbuf, op=ALU.mult),
                         reads=[b_pj[pj], bz], writes=[b_sz[u]])
                return [s0, s1, s2, s3, s4]
            chains.append((21, z_chain(G_ZA, sq[:, :], b_sq, 0)))
            chains.append((24, z_chain(G_ZB, zt2[:, :], b_zt2, 512)))
            return chains

        NSTEP = 32
        ntile = 4 * n_sb

        def x_load(b, i):
            S.dma(C("dma_start", out=xs[i % 2], in_=x_d[b, i * 128:(i + 1) * 128, :]), writes=[b_xs[i % 2]])

        def gen_steps(b, pc, tl):
            chains = []
            if pc is not None:
                if pc % 4 == 0 and pc > 0:
                    for (Xt, bX) in ((Xk, b_Xk), (Xv, b_Xv)):
                        S.op("pool", C("tensor_copy", out=Xt[0][:, :, 0:17], in_=Xt[0][:, :, 512:529]), reads=[bX[0]], writes=[bX[0]])
                chains += build_PC(b, pc)[:nch]
                if pc + 1 < ntile:
                    chains.append((10, [lambda: x_load(b, pc + 1)]))
            if tl is not None:
                chains.append(tail_chain(b, tl))
            T_ = max(o + len(ch) for o, ch in chains)
            assert T_ <= NSTEP, T_
            for tau in range(NSTEP):
                for o, ch in chains:
                    k = tau - o
                    if 0 <= k < len(ch):
                        ch[k]()
                yield

        def score_tile(lhsT_ap, rhs_ap, extra, reads):
            st = st_i[0] % 2
            st_i[0] += 1
            n = len(extra)
            S.op("pe", C("matmul", ST[:, st, :], lhsT=lhsT_ap, rhs=rhs_ap, start=True, stop=(n == 0)),
                 reads=reads, writes=[b_st[st]], inc=(n == 0))
            for j, (l_ap, r_ap, rd) in enumerate(extra):
                S.op("pe", C("matmul", ST[:, st, :], lhsT=l_ap, rhs=r_ap, start=False, stop=(j == n - 1)),
                     reads=rd, writes=[b_st[st]], inc=(j == n - 1))
            return st

        def exp_tile(st):
            p = pts_i[0] % 2
            pts_i[0] += 1
            S.op("act", C("activation", out=PTs[p][:], in_=ST[:, st, :], func=AF.Exp), reads=[b_st[st]], writes=[b_PTs[p]])
            return p

        def bc4(ap2d):
            return V(ap2d, [[0, 4], [1, 128]])

        def phaseA(b, i, u, nxt):
            gsl = gts[:, u, :]
            jobs = []

            def Bx(Bt, ty, g):
                return [(idr[:], Bt[:, ty, 4 * g:4 * g + 4, :].rearrange("p h t -> p (h t)"), [b_idr, b_BA if Bt is BA else b_BB])]

            def add_branch(g, kts, lhs_fn, rhs_fn, extras_fn, V_t, V_bufs, vidx, oab, post):
                nk = len(kts)
                for j, kt in enumerate(kts):
                    def fscore(kt=kt):
                        l_ap, l_rd = lhs_fn(kt)
                        r_ap, r_rd = rhs_fn(kt)
                        return score_tile(l_ap, r_ap, extras_fn(kt), l_rd + r_rd)

                    def pv(p, kt=kt, j=j):
                        for r in range(4):
                            S.op("pe", C("matmul", OA[:, oab, r * 66:(r + 1) * 66], lhsT=PTs[p][:, r * 128:(r + 1) * 128],
                                         rhs=V_t[:, vidx(kt), g, :], start=(j == 0 and r == 0), stop=(j == nk - 1), skip_group_check=True),
                                 reads=[b_PTs[p], V_bufs(kt)], writes=[b_oa[oab]], inc=(j == nk - 1 and r == 3))
                    jobs.append((fscore, pv, post if j == nk - 1 else None))

            def accumulate(g, oab, dcol, gate_br):
                S.op("dve", C("reciprocal", out=den[:, dcol:dcol + 4], in_=V(OA[:, oab, 64:65], [[66, 4]])),
                     reads=[b_oa[oab]], writes=[b_den])
                S.op("dve", C("tensor_tensor", out=coef[:, dcol:dcol + 4], in0=den[:, dcol:dcol + 4],
                              in1=gsl[:, g * 12 + gate_br:(g + 1) * 12:3], op=ALU.mult),
                     reads=[b_den, b_gts[u]], writes=[b_coef])
                S.op("dve", C("tensor_tensor", out=ytmp[oab][:, :].rearrange("p (r d) -> p r d", r=4),
                              in0=V(OA[:, oab, 0:1], [[66, 4], [1, 64]]),
                              in1=V(coef[:, dcol:dcol + 1], [[1, 4], [0, 64]]), op=ALU.mult),
                     reads=[b_oa[oab], b_coef], writes=[b_ytmp[oab]])
                S.op("dve", C("tensor_tensor", out=yB[:, g, :], in0=yB[:, g, :], in1=ytmp[oab][:, :], op=ALU.add),
                     reads=[b_yB[g], b_ytmp[oab]], writes=[b_yB[g]])

            Tmax = i // 16
            post_b_fns = {}
            for g in range(2):
                hp = slice(g * 64, (g + 1) * 64)
                qb_ap = QNt[hp, u, g, :]
                for T in range(Tmax + 1):
                    def fscore(T=T, hp=hp, qb_ap=qb_ap, g=g):
                        extra = []
                        if T == Tmax:
                            extra.append((idb[:], bc4(CM[:, i % 16, :]), [b_idb, b_CM]))
                        return score_tile(kcT[g][:, T * 128:(T + 1) * 128], QNt[:, u, g, :], extra, [b_kcT, b_QNw[u][g]])

                    def pv(p, T=T, g=g):
                        for r in range(4):
                            S.op("pe", C("matmul", OA[:, g, r * 128:(r + 1) * 128], lhsT=PTs[p][:, r * 128:(r + 1) * 128],
                                         rhs=VC[:, T, g, :], start=(T == 0 and r == 0), stop=(T == Tmax), skip_group_check=True),
                                 reads=[b_PTs[p], b_VC], writes=[b_oa[g]], inc=(T == Tmax and r == 3))

                    def post(st, g=g, hp=hp):
                        S.op("dve", C("tensor_scalar", out=den[:, 0:4], in0=V(OA[:, g, 64:65], [[128, 4]]),
                                      scalar1=1e-30, scalar2=None, op0=ALU.max),
                             reads=[b_oa[g]], writes=[b_den])
                        S.op("dve", C("reciprocal", out=den[:, 0:4], in_=den[:, 0:4]), reads=[b_den], writes=[b_den])
                        S.op("dve", C("tensor_tensor", out=coef[:, 0:4], in0=den[:, 0:4],
                                      in1=gsl[:, g * 12:(g + 1) * 12:3], op=ALU.mult),
                             reads=[b_den, b_gts[u]], writes=[b_coef])
                        pat_ap = PAT[:, 64 - 2 * i + 64:128 - 2 * i + 64]
                        for r in range(4):
                            dst, bdst = (impA, b_impA) if r % 2 == 0 else (impB, b_impB)
                            in0 = OA[:, g, r * 128 + 66:r * 128 + 128]
                            if r == 1:
                                S.op("dve", C("tensor_scalar", out=dst[:, 1:63], in0=in0, scalar1=den[:, r:r + 1], scalar2=None, op0=ALU.mult),
                                     reads=[b_oa[g], b_den], writes=[bdst])
                            else:
                                in1, rd1 = (pat_ap[:, 1:63], b_PAT) if r == 0 else (dst[:, 1:63], bdst)
                                S.op("dve", C("scalar_tensor_tensor", out=dst[:, 1:63], in0=in0, scalar=den[:, r:r + 1],
                                              in1=in1, op0=ALU.mult, op1=ALU.add),
                                     reads=[b_oa[g], b_den, rd1], writes=[bdst])
                        S.op("dve", C("tensor_tensor", out=yB[:, g, :].rearrange("p (r d) -> p r d", r=4),
                                      in0=V(OA[:, g, 0:1], [[128, 4], [1, 64]]), in1=V(coef[:, 0:1], [[1, 4], [0, 64]]), op=ALU.mult),
                             reads=[b_oa[g], b_coef], writes=[b_yB[g]])
                        S.op("dve", C("tensor_tensor", out=score[:, 1:63], in0=impA[:, 1:63], in1=impB[:, 1:63], op=ALU.add),
                             reads=[b_impA, b_impB], writes=[b_score])
                        if i == NT - 1:
                            S.op("dve", C("tensor_copy", out=score[:, 63:64], in_=pat_ap[:, 63:64]), reads=[b_PAT, b_score], writes=[b_score])
                        S.op("dve", C("max", out=m8[:, 0:8], in_=score[:]), reads=[b_score], writes=[b_m8])
                        S.op("dve", C("match_replace", out=wk[:], in_to_replace=m8[:, 0:8], in_values=score[:], imm_value=-1e30),
                             reads=[b_score, b_m8], writes=[b_wk])
                        S.op("dve", C("max", out=m8[:, 8:16], in_=wk[:]), reads=[b_wk], writes=[b_m8])
                        S.op("dve", C("tensor_scalar", out=negmg[g][:, :].rearrange("p (a j) -> p a j", a=2), in0=V(score[:, 0:1], [[0, 2], [1, 64]]),
                                      scalar1=m8[:, 15:16], scalar2=NEG,
                                      op0=ALU.is_lt, op1=ALU.mult), reads=[b_score, b_m8], writes=[b_negm[g]])

                    def post_b(g=g):
                        S.op("pe", C("transpose", out=OA[:, g, 384:512], in_=negmg[g][:], identity=idf[:]),
                             reads=[b_negm[g], b_idf], writes=[b_oa[g]])
                        op_ = slice((1 - g) * 64, (2 - g) * 64)
                        S.op("dve", C("tensor_copy", out=QNt[op_, u, g, :].rearrange("p (r t) -> p r t", r=4), in_=bc4(OA[op_, g, 384:512])),
                             reads=[b_oa[g]], writes=[b_QNw[u][g]])
                    if T == Tmax:
                        post_b_fns[g] = post_b
                    jobs.append((fscore, pv, post if T == Tmax else None))

            for g in range(2):
                hp = slice(g * 64, (g + 1) * 64)
                qa_ap = qTa[hp, u].rearrange("p r t -> p (r t)")

                def a_post(st, g=g):
                    oab = g
                    post_b_fns[g]()
                    S.op("dve", C("tensor_tensor", out=den[:, 12:16], in0=V(OA[:, oab, 64:65], [[66, 4]]),
                                  in1=esink[:, 4 * g:4 * g + 4], op=ALU.add),
                         reads=[b_oa[oab], b_esink], writes=[b_den])
                    S.op("dve", C("reciprocal", out=den[:, 12:16], in_=den[:, 12:16]), reads=[b_den], writes=[b_den])
                    S.op("dve", C("tensor_tensor", out=ytmp[oab][:, :].rearrange("p (r d) -> p r d", r=4),
                                  in0=V(OA[:, oab, 0:1], [[66, 4], [1, 64]]),
                                  in1=V(den[:, 12:13], [[1, 4], [0, 64]]), op=ALU.mult),
                         reads=[b_oa[oab], b_den], writes=[b_ytmp[oab]])
                    S.op("pool", C("tensor_tensor", out=y[:, g * 256:(g + 1) * 256], in0=ytmp[oab][:, :],
                                   in1=sz[:, u, g * 256:(g + 1) * 256], op=ALU.mult),
                         reads=[b_ytmp[oab], b_sz[u]], writes=[b_y])
                add_branch(g, list(range(max(0, i - 1), i + 1)),
                           lambda kt, g=g: (kTa[g][:, (kt % RA) * 128:(kt % RA) * 128 + 128], [b_kTa[kt % RA]]),
                           lambda kt: (qTa[:, u].rearrange("p r t -> p (r t)"), [b_qTa[u]]),
                           lambda kt, g=g: Bx(BA, 0, g) if kt == i else Bx(BA, 1, g),
                           Va, lambda kt: b_Va[kt % RA], lambda kt: kt % RA, g, a_post)
            for g in range(2):
                hp = slice(g * 64, (g + 1) * 64)
                qb_ap = QNt[hp, u, g, :]

                def win_extras(kt, g=g):
                    dk = i - kt
                    if dk == 0:
                        return Bx(BB, 0, g)
                    if dk == 1:
                        return Bx(BB, 1, g)
                    if dk == 4:
                        return [(idb[:], bc4(TRI[:, :]), [b_idb, b_TRI])]
                    return []
                add_branch(g, list(range(max(0, i - 4), i + 1)),
                           lambda kt, g=g: (kTw[g][:, (kt % RW) * 128:(kt % RW) * 128 + 128], [b_kTw[kt % RW]]),
                           lambda kt, g=g: (QNt[:, u, g, :], [b_QNw[u][g]]),
                           win_extras, Vw, lambda kt: b_Vw[kt % RW], lambda kt: kt % RW, g,
                           lambda st, g=g: accumulate(g, g, 8, 2))
            for g in range(2):
                hp = slice(g * 64, (g + 1) * 64)
                qb_ap = QNt[hp, u, g, :]

                def sel_lhs(kt, g=g, hp=hp):
                    return (KE[g][:, kt * 128:(kt + 1) * 128], [b_kTs[kt], b_E])

                def sel_rhs(kt, g=g, qb_ap=qb_ap):
                    return (QNt[:, u, g, :], [b_QNw[u][g]])

                def sel_extras(kt, g=g):
                    if kt == i:
                        return Bx(BB, 0, g)
                    if kt == i - 1:
                        return Bx(BB, 1, g)
                    return []

                def sel_post(st, g=g):
                    accumulate(g, g, 4, 1)
                    S.op("dve", C("tensor_tensor", out=y[:, 512 + g * 256:512 + (g + 1) * 256], in0=yB[:, g, :],
                                  in1=sz[:, u, 512 + g * 256:512 + (g + 1) * 256], op=ALU.mult),
                         reads=[b_yB[g], b_sz[u]], writes=[b_y])
                add_branch(g, list(range(0, i + 1)), sel_lhs, sel_rhs, sel_extras, Vs, lambda kt: b_Vs[kt], lambda kt: kt, g, sel_post)

            pend = None
            nj = len(jobs)
            sdone = 0
            ncmp = 2 * (Tmax + 1)
            for jx, (fscore, pv, post) in enumerate(jobs):
                st = fscore()
                want = ((jx + 1) * NSTEP) // nj
                if jx + 1 >= ncmp + 1:
                    want = max(2, want)
                while sdone < want:
                    next(nxt, None)
                    sdone += 1
                if pend is not None:
                    pst, ppv, ppost = pend
                    p = exp_tile(pst)
                    ppv(p)
                    if ppost is not None:
                        ppost(pst)
                pend = (st, pv, post)
            pst, ppv, ppost = pend
            p = exp_tile(pst)
            ppv(p)
            if ppost is not None:
                ppost(pst)

        def tail_chain(b, i):
            xrs = i % 2

            def t0():
                S.dma(C("dma_start", out=xr[xrs], in_=x_d[b, i * 128:(i + 1) * 128, :]), writes=[b_xr[xrs]])
                for kc in range(8):
                    S.op("pe", C("transpose", out=PTb[1][:, kc * 128:(kc + 1) * 128], in_=y[:, kc * 128:(kc + 1) * 128],
                                 identity=idb[:]), reads=[b_y, b_idb], writes=[b_pt[1]])

            def t1():
                S.op("act", C("copy", out=yT[:, :, :].rearrange("p k t -> p (k t)"), in_=PTb[1][:, :]), reads=[b_pt[1]], writes=[b_yT])
            stt = {}

            def mm(hf):
                def f():
                    pj = next_pj()
                    stt[hf] = pj
                    for kc in range(8):
                        S.op("pe", C("matmul", PJ[:, pj, :], lhsT=yT[:, kc, :], rhs=Wout[:, kc, hf * 512:(hf + 1) * 512],
                                     start=(kc == 0), stop=(kc == 7)),
                             reads=[b_yT, b_wout], writes=[b_pj[pj]], inc=(kc == 7))
                return f

            def ml(hf):
                def f():
                    pj = stt[hf]
                    S.op("dve", C("tensor_tensor", out=rtmp[:, :], in0=PJ[:, pj, :],
                                  in1=GATE[:, b, hf * 512:(hf + 1) * 512], op=ALU.mult),
                         reads=[b_pj[pj], b_GATE], writes=[b_rtmp])
                return f

            def t6(hf):
                def f():
                    S.op("dve", C("tensor_tensor", out=xr[xrs][:, hf * 512:(hf + 1) * 512], in0=xr[xrs][:, hf * 512:(hf + 1) * 512],
                                  in1=rtmp[:, :], op=ALU.add),
                         reads=[b_xr[xrs], b_rtmp], writes=[b_xr[xrs]])
                return f

            def t7():
                S.dma(C("dma_start", out=out_d[b, i * 128:(i + 1) * 128, :], in_=xr[xrs]), reads=[b_xr[xrs]], is_output=True)
            return (0, [t0, t1, mm(0), ml(0), t6(0), mm(1), ml(1), t6(1), t7])


        def drain(gen):
            for _ in gen:
                pass

        for b in range(nb):
            if b > 0:
                for g_ in range(2):
                    S.op("pool", C("memset", kcT[g_][:], 0.0), writes=[b_kcT])
                S.op("pool", C("memset", VC[:, :, :, 0:64], 0.0), writes=[b_VC])
                S.op("pool", C("memset", Xk[0][:, :, 0:17], 0.0), writes=[b_Xk[0]])
                S.op("pool", C("memset", Xv[0][:, :, 0:17], 0.0), writes=[b_Xv[0]])
                S.op("pool", C("memset", score[:, 63:64], 0.0), reads=[b_score], writes=[b_score])
            x_load(b, 0)
            drain(gen_steps(b, 0, None))
            for i in range(ntile):
                pc_ = i + 1 if i + 1 < ntile else None
                tl_ = i - 1 if i >= 1 else None
                nxt = gen_steps(b, pc_, tl_) if (pc_ is not None or tl_ is not None) else iter(())
                if "A" in stages and i < n_a:
                    phaseA(b, i, i % 2, nxt)
                drain(nxt)
            drain(gen_steps(b, None, ntile - 1))
        if dumps:
            L = dict(GATE=(GATE, [b_GATE]), Gcol=(Gcol, [b_Gcol]), SHcol=(SHcol, [b_SHcol]), BA=(BA, [b_BA]), BB=(BB, [b_BB]),
                     b1k=(b1k, [b_b1k]), b1v=(b1v, [b_b1v]), Win=(Win, [b_win]), Wout=(Wout, [b_wout]), w1k=(w1k, [b_w1k]),
                     qTa=(qTa, b_qTa), kTs=(KE[0], b_kTs),
                     Vs=(Vs, b_Vs), Va=(Va, b_Va), Vw=(Vw, b_Vw), VC=(VC, [b_VC]), sz=(sz, b_sz), gts=(gts, b_gts),
                      y=(y, [b_y]), hT=(hT0, [bht]),
                     Xk=(Xk0, [bxk]), hidk=(hidk, [b_hidk]), score=(score, [b_score]), yB=(yB, b_yB), esink=(esink, [b_esink]),
                     den=(den, [b_den]), coef=(coef, [b_coef]))
            for nm in dumps:
                t, bufs = L[nm]
                shp = list(t.shape)
                dd = nc.dram_tensor("d_" + nm, shp, t.dtype, kind="ExternalOutput").ap()
                full = t[tuple(slice(None) for _ in shp)]
                S.dma(C("dma_start", out=dd, in_=full), reads=list(bufs), is_output=True)
        S.emit()
    return nc


def _t5_bucket(dist):
    n = np.maximum(dist, 0)
    nf = np.maximum(n, 1).astype(np.float32)
    large = 16 + (np.log(nf / np.float32(16)) / np.float32(np.log(128 / 16)) * np.float32(16)).astype(np.int32)
    large = np.minimum(large, 31)
    return np.where(n < 16, n, large)


def _constants():
    bf = ml_dtypes.bfloat16
    sl = np.arange(128)[:, None]
    tl = np.arange(128)[None, :]
    d_diag = tl - sl
    d_prev = 128 + tl - sl
    idx = np.stack([_t5_bucket(d_diag), _t5_bucket(d_prev)], 0)
    maskAB = np.zeros((128, 4, 128), np.float32)
    maskAB[:, 0, :] = np.where(d_diag >= 0, 0.0, NEG)
    maskAB[:, 1, :] = np.where(d_prev < 128, 0.0, NEG)
    maskAB[:, 2, :] = np.where(d_diag >= 0, 0.0, NEG)
    maskAB[:, 3, :] = 0.0
    E = (np.arange(SEQ)[None, :] // 64 == (np.arange(128) % 64)[:, None]).astype(np.float32).astype(bf)
    nl = np.arange(128)[:, None, None]
    o = np.arange(16)[None, :, None]
    t3 = np.arange(128)[None, None, :]
    cmask = np.where(16 * nl + 15 <= 128 * o + t3, 0.0, NEG).astype(np.float32).astype(bf)
    tri = np.where(sl > tl, 0.0, NEG).astype(np.float32).astype(bf)
    pat = np.zeros((128, 192), np.float32)
    pat[:, 0] = BONUS
    hi = (np.arange(128) >= 64).astype(np.int64)
    pat[np.arange(128), 64 + 63 + hi] = BONUS
    pat[np.arange(128), 64 + 64 + hi] = BONUS
    pat = pat.astype(bf)
    npr = np.arange(256)
    n = npr - 1
    c_lo = 16 * n
    s_lo = 64 * np.arange(64)
    ov = np.clip(np.minimum(c_lo[:, None] + 32, s_lo[None, :] + 64) - np.maximum(c_lo[:, None], s_lo[None, :]), 0, None) / 32.0
    ov[0, :] = 0.0
    ov = np.ascontiguousarray(ov.reshape(2, 128, 64).transpose(1, 0, 2)[:, :, 1:63]).astype(np.float32).astype(bf)
    return idx, maskAB, E, cmask, tri, pat, ov


def _perm_cols():
    o_qa, o_ka, o_va, o_za, o_qb, o_kc, o_vc, o_ks, o_vs, o_kw, o_vw, o_zb, o_gb = (
        0, 512, 640, 768, 1280, 1792, 1920, 2048, 2176, 2304, 2432, 2560, 3072)
    r64 = np.arange(64)
    cols = []
    for base in (o_qa, o_qb):
        for r in range(4):
            cols += [base + r * 64 + r64, base + (4 + r) * 64 + r64]
    cols += [o_ka + np.arange(128), o_ks + np.arange(128), o_kw + np.arange(128)]
    for base in (o_kc, o_vc):
        for g in range(2):
            cols += [base + g * 64 + r64, base + g * 64 + r64]
    cols += [o_va + np.arange(128), o_vs + np.arange(128), o_vw + np.arange(128), o_gb + np.arange(24)]
    cols += [o_za + np.arange(512), o_zb + np.arange(512)]
    cols = np.concatenate(cols)
    assert cols.shape[0] == WCOLS
    return cols


_NC_CACHE = {}


def kernel(x, c, w_ada, b_ada, norm_gain, w_in, b_nsa_gate, q_gain_a, k_gain_a, sinks,
           q_gain_b, k_gain_cmp, k_gain_sel, k_gain_win, cmp_pos_k, cmp_pos_v,
           w_cmp_k1, w_cmp_k2, w_cmp_v1, w_cmp_v2, w_out, rel_bias):
    f = lambda a: np.ascontiguousarray(np.asarray(a, dtype=np.float32))
    x = f(x); c = f(c); w_ada = f(w_ada)[0]; b_ada = f(b_ada)[0]; norm_gain = f(norm_gain)[0]
    w_in = f(w_in)[0]; b_nsa_gate = f(b_nsa_gate)[0]; sinks = f(sinks)[0]
    rel_bias = f(rel_bias); w_out = f(w_out)[0]
    idx, maskAB, E, cmask, tri, pat, ov = _constants()
    biasAB = rel_bias[idx]
    biasA = np.ascontiguousarray(biasAB[..., 0:8].transpose(1, 0, 3, 2))
    biasB = np.ascontiguousarray(biasAB[..., 8:16].transpose(1, 0, 3, 2))
    c31 = np.ascontiguousarray(rel_bias[31:32, 8:16])
    gcols = np.stack([np.tile(f(g)[0], 2) for g in (q_gain_a, k_gain_a, q_gain_b, k_gain_cmp, k_gain_sel, k_gain_win)], 1)
    shared = {
        "w_ada": w_ada,
        "bada_col": np.ascontiguousarray(b_ada[0:2048].reshape(16, 128).T),
        "bada_gate": np.ascontiguousarray(b_ada[2048:3072].reshape(1, DM)),
        "ng_col": np.ascontiguousarray(norm_gain.reshape(8, 128).T),
        "w_in_p": np.ascontiguousarray(w_in[:, _perm_cols()]),
        "bgate": b_nsa_gate.reshape(1, 24),
        "gcols": np.ascontiguousarray(gcols),
        "sinks": sinks.reshape(1, 8),
        "posk": np.ascontiguousarray(f(cmp_pos_k)[0].reshape(16, 128).T),
        "posv": np.ascontiguousarray(f(cmp_pos_v)[0].reshape(16, 128).T),
        "w1k": f(w_cmp_k1)[0], "w1v": f(w_cmp_v1)[0], "w2k": f(w_cmp_k2)[0], "w2v": f(w_cmp_v2)[0],
        "w_out": w_out, "biasA": biasA, "biasB": biasB, "c31": c31, "maskAB": maskAB,
        "Emat": E, "cmask": cmask, "tri": tri, "pat": pat, "ov": ov,
    }
    in_maps = []
    for core in range(NCORES):
        m = dict(shared)
        m["x"] = x[NB * core:NB * (core + 1)]
        cc = c[NB * core:NB * (core + 1)]
        m["cT"] = np.ascontiguousarray(cc.reshape(NB, 8, 128).transpose(2, 1, 0))
        in_maps.append(m)
    if "nc" not in _NC_CACHE:
        _NC_CACHE["nc"] = build_program()
    res = run_bass_kernel_spmd(_NC_CACHE["nc"], in_maps, core_ids=list(range(NCORES)))
    return np.concatenate([np.asarray(r["out"], dtype=np.float32) for r in res.results], axis=0)
```

```python
import numpy as np
import ml_dtypes
from contextlib import ExitStack
import concourse.bass as bass
import concourse.mybir as mybir
from concourse.bass_utils import run_bass_kernel_spmd

F32 = mybir.dt.float32
BF16 = mybir.dt.bfloat16
F32R = mybir.dt.float32r
AF = mybir.ActivationFunctionType
ALU = mybir.AluOpType
AX = mybir.AxisListType

NCORES = 8
SEQ = 4096
DM = 1024
NT = SEQ // 128
NB = 2
WCOLS = 3352
NEG = -30000.0
BONUS = 1.0e4
EPS = 1e-6
G_QA, G_QB, G_K, G_C, G_V, G_ZA, G_ZB = 0, 512, 1024, 1408, 1920, 2328, 2840
RA, RW = 3, 6


class Buf:
    __slots__ = ("w", "r", "dsem", "dcount", "name")

    def __init__(self, name=""):
        self.w = None
        self.r = {}
        self.dsem = None
        self.dcount = 0
        self.name = name


class Sched:
    ENG = ("pe", "act", "dve", "pool", "sp")

    def __init__(self, nc, same_engine_sync=True):
        self.nc = nc
        self.ops = {e: [] for e in self.ENG}
        self.count = {e: 0 for e in self.ENG}
        self.waited = {e: {} for e in self.ENG}
        self.sems = {}
        self.dma_sems = []
        self.same_engine_sync = same_engine_sync
        self.out_waits = []

    def _need(self, eng, dep, needs):
        if dep is None:
            return
        k, v = dep
        if k == eng:
            if eng in ("pe", "sp"):
                return
            if not self.same_engine_sync:
                return
        if self.waited[eng].get(k, 0) >= v:
            return
        if needs.get(k, 0) < v:
            needs[k] = v

    def _emit_waits(self, eng, needs):
        for k, v in needs.items():
            self.ops[eng].append(("w", k, v))
            self.waited[eng][k] = v

    def op(self, eng, fn, reads=(), writes=(), inc=True):
        needs = {}
        for b in reads:
            self._need(eng, b.w, needs)
        for b in writes:
            self._need(eng, b.w, needs)
            for k, v in b.r.items():
                self._need(eng, (k, v), needs)
        self._emit_waits(eng, needs)
        if inc:
            self.count[eng] += 1
            val = self.count[eng]
        else:
            val = self.count[eng] + 1
        self.ops[eng].append(("op", fn, [(eng, 1)] if inc else []))
        for b in writes:
            b.w = (eng, val)
            b.r = {}
        for b in reads:
            if b.r.get(eng, 0) < val:
                b.r[eng] = val
        return val

    def dma(self, fn, reads=(), writes=(), q="sp", is_output=False):
        needs = {}
        for b in reads:
            self._need(q, b.w, needs)
        for b in writes:
            self._need(q, b.w, needs)
            for k, v in b.r.items():
                self._need(q, (k, v), needs)
        self._emit_waits(q, needs)
        owner = writes[0] if writes else reads[0]
        if owner.dsem is None:
            owner.dsem = "dma%d" % len(self.dma_sems)
            self.dma_sems.append(owner.dsem)
        owner.dcount += 16
        k, v = owner.dsem, owner.dcount
        self.ops[q].append(("op", fn, [(k, 16)]))
        for b in writes:
            b.w = (k, v)
            b.r = {}
        for b in reads:
            b.r[k] = v
        if is_output:
            self.out_waits.append((k, v))

    def emit(self):
        nc = self.nc
        with ExitStack() as es:
            for e in self.ENG:
                self.sems[e] = es.enter_context(nc.semaphore("prog_" + e))
            for k in self.dma_sems:
                self.sems[k] = es.enter_context(nc.semaphore(k))
            fin = {}
            for k, v in self.out_waits:
                fin[k] = max(fin.get(k, 0), v)
            for k, v in fin.items():
                self.ops["sp"].append(("w", k, v))
            block = es.enter_context(nc.Block())
            sems = self.sems
            ops = self.ops

            def run(engine, lst):
                for item in lst:
                    if item[0] == "w":
                        engine.wait_ge(sems[item[1]], item[2])
                    else:
                        name, args, kw = item[1]
                        ins = getattr(engine, name)(*args, **kw)
                        for (k, n) in item[2]:
                            ins.then_inc(sems[k], n)

            @block.sync
            def _(e):
                run(e, ops["sp"])

            @block.tensor
            def _(e):
                run(e, ops["pe"])

            @block.scalar
            def _(e):
                run(e, ops["act"])

            @block.vector
            def _(e):
                run(e, ops["dve"])

            @block.gpsimd
            def _(e):
                run(e, ops["pool"])


def C(name, *args, **kw):
    return (name, args, kw)


def V(ap, dims):
    return bass.AP(tensor=ap.tensor, offset=ap.offset, ap=[list(ap.ap[0])] + [list(d) for d in dims])


def build_program(nb=NB, n_sb=NT // 4, stages="PCA", n_a=999, dumps=None, pro=9, nch=99):
    nc = bass.Bass("TRN2", target_bir_lowering=False)
    S = Sched(nc)

    def din(name, shape, dt=F32):
        return nc.dram_tensor(name, list(shape), dt, kind="ExternalInput").ap()

    x_d = din("x", [NB, SEQ, DM])
    cT_d = din("cT", [128, 8, NB])
    wada_d = din("w_ada", [DM, 3 * DM])
    badac_d = din("bada_col", [128, 16])
    badag_d = din("bada_gate", [1, DM])
    ngc_d = din("ng_col", [128, 8])
    win_d = din("w_in_p", [DM, WCOLS])
    bgate_d = din("bgate", [1, 24])
    gcols_d = din("gcols", [128, 6])
    sinks_d = din("sinks", [1, 8])
    posk_d = din("posk", [128, 16])
    posv_d = din("posv", [128, 16])
    w1k_d = din("w1k", [2048, 256])
    w1v_d = din("w1v", [2048, 256])
    w2k_d = din("w2k", [256, 64])
    w2v_d = din("w2v", [256, 64])
    wout_d = din("w_out", [DM, DM])
    biasA_d = din("biasA", [128, 2, 8, 128])
    biasB_d = din("biasB", [128, 2, 8, 128])
    c31_d = din("c31", [1, 8])
    maskAB_d = din("maskAB", [128, 4, 128])
    E_d = din("Emat", [128, SEQ], BF16)
    cmask_d = din("cmask", [128, 16, 128], BF16)
    tri_d = din("tri", [128, 128], BF16)
    pat_d = din("pat", [128, 192], BF16)
    ov_d = din("ov", [128, 2, 62], BF16)
    out_d = nc.dram_tensor("out", [NB, SEQ, DM], F32, kind="ExternalOutput").ap()

    with ExitStack() as es:
        def sb(name, shape, dt):
            return es.enter_context(nc.sbuf_tensor("s_" + name, list(shape), dt))

        def ps(name, shape, dt):
            return es.enter_context(nc.psum_tensor("p_" + name, list(shape), dt))

        PJ = ps("PJ", [128, 2, 512], F32)
        PT = ps("PT", [128, 2, 512], F32)
        ST = ps("ST", [128, 2, 512], F32)
        OA = ps("OA", [128, 2, 512], F32)
        b_pj = [Buf("pj0"), Buf("pj1")]
        b_pt = [Buf("pt0"), Buf("pt1")]
        b_st = [Buf("st0"), Buf("st1")]
        b_oa = [Buf("oa0"), Buf("oa1")]
        PTb = [PT[:, 0, :].bitcast(BF16), PT[:, 1, :].bitcast(BF16)]

        Win = sb("Win", [128, 8, WCOLS], BF16); b_win = Buf("win")
        b_win_l = [Buf("win%d" % i_) for i_ in range(16)]
        Wout = sb("Wout", [128, 8, DM], BF16); b_wout = Buf("wout")
        w1k = sb("w1k", [128, 16, 256], BF16); b_w1k = Buf()
        w1v = sb("w1v", [128, 16, 256], BF16); b_w1v = Buf()
        w2k = sb("w2k", [128, 2, 64], BF16); b_w2k = Buf()
        w2v = sb("w2v", [128, 2, 64], BF16); b_w2v = Buf()
        b1k = sb("b1k", [128, 2], F32); b_b1k = Buf()
        b1v = sb("b1v", [128, 2], F32); b_b1v = Buf()
        STG = [sb("stg0", [128, 2048], F32), sb("stg1", [128, 2048], F32)]
        b_stg = [[Buf("s00"), Buf("s01")], [Buf("s10"), Buf("s11")]]
        kTa = [sb("kTa0", [128, RA * 128], BF16), sb("kTa1", [128, RA * 128], BF16)]; b_kTa = [Buf() for _ in range(RA)]
        kTw = [sb("kTw0", [128, RW * 128], BF16), sb("kTw1", [128, RW * 128], BF16)]; b_kTw = [Buf() for _ in range(RW)]
        KE = [sb("KE0", [128, SEQ], BF16), sb("KE1", [128, SEQ], BF16)]
        b_kTs = [Buf() for _ in range(NT)]
        Va = sb("Va", [128, RA, 2, 66], BF16); b_Va = [Buf() for _ in range(RA)]
        Vw = sb("Vw", [128, RW, 2, 66], BF16); b_Vw = [Buf() for _ in range(RW)]
        Vs = sb("Vs", [128, NT, 2, 66], BF16); b_Vs = [Buf() for _ in range(NT)]
        XW = 529
        Xk0 = sb("Xk0", [128, 2, XW], BF16)
        Xv0 = sb("Xv0", [128, 2, XW], BF16)
        Xk = [Xk0, Xk0]
        Xv = [Xv0, Xv0]
        bxk, bxv = Buf(), Buf()
        b_Xk = [bxk, bxk]
        b_Xv = [bxv, bxv]
        kcT = [sb("kcT0", [128, 256], BF16), sb("kcT1", [128, 256], BF16)]; b_kcT = Buf("kcT")
        VC = sb("VC", [128, 2, 2, 128], BF16); b_VC = Buf("VC")
        BA = sb("BA", [128, 2, 8, 128], F32R); b_BA = Buf()
        BB = sb("BB", [128, 2, 8, 128], F32R); b_BB = Buf()
        b_E = Buf()
        CM = sb("CM", [128, 16, 128], BF16); b_CM = Buf()
        TRI = sb("TRI", [128, 128], BF16); b_TRI = Buf()
        PAT = sb("PAT", [128, 192], BF16); b_PAT = Buf()
        GATE = sb("GATE", [128, NB, DM], F32); b_GATE = Buf()
        Gcol = sb("Gcol", [128, 8, NB], F32); b_Gcol = Buf()
        SHcol = sb("SHcol", [128, 8, NB], F32); b_SHcol = Buf()
        idf = sb("idf", [128, 128], F32); b_idf = Buf()
        idb = sb("idb", [128, 128], BF16); b_idb = Buf()
        idr = sb("idr", [128, 128], F32R); b_idr = Buf()
        gcols = sb("gcols", [128, 6], F32); b_gcols = Buf()
        esink = sb("esink", [128, 8], F32); b_esink = Buf()
        bgate = sb("bgate", [128, 24], F32); b_bgate = Buf()
        c31 = sb("c31", [128, 8], F32); b_c31 = Buf()


        stat = sb("stat", [128, 4], F32); b_stat = Buf()
        hT0 = sb("hT0", [128, 8, 128], BF16)
        hT = [hT0, hT0]
        bht = Buf()
        b_hT = [bht, bht]

        ssq = sb("ssq", [128, 16], F32); b_ssq = Buf()
        qn = [sb("qn0", [128, 512], BF16), sb("qn1", [128, 512], BF16)]
        b_qn = [Buf(), Buf()]
        qTa = sb("qTa", [128, 2, 4, 128], BF16); b_qTa = [Buf() for _ in range(2)]
        sz = sb("sz", [128, 2, DM], BF16); b_sz = [Buf() for _ in range(2)]
        scb = KE[0][:, :].bitcast(F32).rearrange("p (k b m) -> p k b m", k=8, b=NB)
        gts = sb("gts", [128, 2, 24], F32); b_gts = [Buf() for _ in range(2)]
        gtmp = sb("gtmp", [128, 24], F32); b_gtmp = Buf()
        zt2 = sb("zt2", [128, 512], F32); b_zt2 = Buf()
        hsig = sb("hsig", [128, 64], F32); b_hsig = Buf()
        nb1k = sb("nb1k", [128, 2], F32)
        nb1v = sb("nb1v", [128, 2], F32); b_nb1 = Buf()
        hidk = sb("hidk", [128, 2, 16], BF16); b_hidk = Buf()
        hidv = sb("hidv", [128, 2, 16], BF16); b_hidv = Buf()
        kc32 = sb("kc32", [16, 4], F32); b_kc32 = Buf()
        kndup = sb("kndup", [16, 128], BF16); b_kndup = Buf()
        kjunk = sb("kjunk", [16, 64], F32); b_kjunk = Buf()
        vstag = sb("vstag", [16, 64], BF16); b_vstag = Buf()
        PTs = [sb("PTs%d" % i, [128, 512], BF16) for i in range(2)]
        b_PTs = [Buf() for _ in range(2)]
        den = sb("den", [128, 16], F32); b_den = Buf()
        coef = sb("coef", [128, 16], F32); b_coef = Buf()
        impA = sb("impA", [128, 64], F32); b_impA = Buf()
        impB = sb("impB", [128, 64], F32); b_impB = Buf()
        score = sb("score", [128, 64], F32); b_score = Buf()
        wk = sb("wk", [128, 64], F32); b_wk = Buf()
        m8 = sb("m8", [128, 16], F32); b_m8 = Buf()
        negmg = [sb("negm0", [128, 128], F32), sb("negm1", [128, 128], F32)]
        b_negm = [Buf(), Buf()]
        QNt = sb("QNw", [128, 2, 2, 512], BF16)
        b_QNw = [[Buf(), Buf()], [Buf(), Buf()]]
        yB = sb("yB", [128, 2, 256], F32); b_yB = [Buf(), Buf()]
        maskAB = yB[:, :, :].rearrange("p a (b c) -> p (a b) c", b=2); b_maskAB = b_yB[0]
        ytmp = [sb("ytmp0", [128, 256], F32), sb("ytmp1", [128, 256], F32)]
        b_ytmp = [Buf(), Buf()]
        y = sb("y", [128, DM], BF16); b_y = Buf()
        sq = sb("sq", [128, 512], F32); b_sq = Buf()
        junk = sq[:, :].bitcast(BF16); b_junk = b_sq
        yT = sb("yT", [128, 8, 128], BF16); b_yT = Buf()
        rtmp = sb("rtmp", [128, 512], F32); b_rtmp = Buf()
        xn = sb("xn", [128, DM], F32); b_xn = Buf()
        scol = sb("scol", [128, 8, NB], F32); b_scol = Buf()

        modc = sb("modc", [128, 16, NB], F32); b_modc = Buf()
        posk = sb("posk", [128, 16], F32); b_posk = Buf()
        posv = sb("posv", [128, 16], F32); b_posv = Buf()
        poskb = sb("poskb", [128, 16], BF16); b_poskb = Buf()
        posvb = sb("posvb", [128, 16], BF16); b_posvb = Buf()
        badac = sb("badac", [128, 16], F32); b_badac = Buf()
        ngc = sb("ngc", [128, 8], F32); b_ngc = Buf()
        badag = sb("badag", [128, DM], F32) if False else None

        xs = [STG[0][:, 0:1024], STG[0][:, 1024:2048]]
        b_xs = b_stg[0]
        xr = [STG[1][:, 0:1024], STG[1][:, 1024:2048]]
        b_xr = b_stg[1]

        def pbc(d_ap, n):
            return bass.AP(tensor=d_ap.tensor, offset=d_ap.offset, ap=[[0, 128], [1, n]])

        if dumps:
            for (t_, bl_) in ((KE[0], b_kTs), (Vs, b_Vs), (Va, b_Va), (Vw, b_Vw), (qTa, b_qTa),
                              (sz, b_sz), (gts, b_gts), (hT0, [bht]), (QNt, b_QNw[0] + b_QNw[1]), (y, [b_y]),
                              (hidk, [b_hidk]), (score, [b_score]), (yB, b_yB), (den, [b_den]), (coef, [b_coef])):
                shp_ = list(t_.shape)
                S.op("pool", C("memset", t_[tuple(slice(None) for _ in shp_)], 0.0), writes=list(bl_))
        def ld(dst_ap, src_ap, buf):
            S.dma(C("dma_start", out=dst_ap, in_=src_ap), writes=[buf])

        ld(CM[:], cmask_d, b_CM)
        ld(TRI[:], tri_d, b_TRI)
        ld(PAT[:], pat_d, b_PAT)
        ld(gcols[:], gcols_d, b_gcols)
        ld(esink[:], pbc(sinks_d, 8), b_esink)
        ld(bgate[:], pbc(bgate_d, 24), b_bgate)
        ld(c31[:], pbc(c31_d, 8), b_c31)

        ld(scol[:], cT_d, b_scol)
        ld(posk[:], posk_d, b_posk)
        ld(posv[:], posv_d, b_posv)
        ld(badac[:], badac_d, b_badac)
        ld(ngc[:], ngc_d, b_ngc)
        S.op("pool", C("memset", idf[:], 0.0), writes=[b_idf])
        S.op("pool", C("affine_select", out=idf[:], in_=idf[:], pattern=[[-1, 128]], compare_op=ALU.not_equal,
                                                fill=1.0, base=0, channel_multiplier=1), reads=[b_idf], writes=[b_idf])
        S.op("dve", C("tensor_copy", out=idb[:], in_=idf[:]), reads=[b_idf], writes=[b_idb])
        S.op("dve", C("tensor_copy", out=idr[:], in_=idf[:]), reads=[b_idf], writes=[b_idr])
        S.op("act", C("activation", out=esink[:], in_=esink[:], func=AF.Exp), reads=[b_esink], writes=[b_esink])
        S.op("dve", C("tensor_scalar", out=gcols[:, 0:1], in0=gcols[:, 0:1], scalar1=0.125, scalar2=None, op0=ALU.mult),
             reads=[b_gcols], writes=[b_gcols])
        S.op("dve", C("tensor_scalar", out=gcols[:, 2:3], in0=gcols[:, 2:3], scalar1=0.125, scalar2=None, op0=ALU.mult),
             reads=[b_gcols], writes=[b_gcols])
        S.op("dve", C("tensor_copy", out=poskb[:], in_=posk[:]), reads=[b_posk], writes=[b_poskb])
        S.op("dve", C("tensor_copy", out=posvb[:], in_=posv[:]), reads=[b_posv], writes=[b_posvb])
        S.op("pool", C("memset", score[:], 0.0), writes=[b_score])
        S.op("pool", C("memset", score[:, 0:1], BONUS), reads=[b_score], writes=[b_score])
        for g_ in range(2):
            S.op("pool", C("memset", kcT[g_][:], 0.0), writes=[b_kcT])
            S.op("pool", C("memset", kTa[g_][:], 0.0), writes=b_kTa)
            S.op("pool", C("memset", kTw[g_][:], 0.0), writes=b_kTw)
        S.op("pool", C("memset", QNt[:, :, :, :], 0.0), writes=b_QNw[0] + b_QNw[1])
        S.op("pool", C("memset", VC[:], 0.0), writes=[b_VC])
        S.op("pool", C("memset", VC[:, :, :, 64:65], 1.0), reads=[b_VC], writes=[b_VC])
        S.op("pool", C("memset", VC[0:1, 0, :, 64:65], 0.0), reads=[b_VC], writes=[b_VC])
        for g in range(2):
            S.dma(C("dma_start", out=VC[:, :, g, 66:128], in_=ov_d), writes=[b_VC])
        for (Vt_, bV_) in ((Va, b_Va), (Vw, b_Vw), (Vs, b_Vs)):
            S.op("pool", C("memset", Vt_[:, :, :, 64:65], 1.0), writes=bV_)
            S.op("pool", C("memset", Vt_[:, :, :, 65:66], 0.0), writes=bV_)
        for s_ in range(1):
            S.op("pool", C("memset", Xk[s_][:], 0.0), writes=[b_Xk[s_]])
            S.op("pool", C("memset", Xv[s_][:], 0.0), writes=[b_Xv[s_]])

        stg_i = [0]

        def stage_load(src_ap, ncols, view=None):
            s_ = stg_i[0] % 2
            stg_i[0] += 1
            dst = STG[s_][:, 0:ncols] if view is None else view(STG[s_])
            S.dma(C("dma_start", out=dst, in_=src_ap), writes=b_stg[s_])
            return s_

        def cast_from_stage(eng, s_, dst_ap, ncols, dst_buf, src_view=None):
            src = STG[s_][:, 0:ncols] if src_view is None else src_view(STG[s_])
            S.op(eng, C("copy" if eng == "act" else "tensor_copy", out=dst_ap, in_=src), reads=b_stg[s_], writes=[dst_buf])

        ADA = pro >= 2
        S.op("act", C("activation", out=scol[:], in_=scol[:], func=AF.Silu), reads=[b_scol], writes=[b_scol])
        S.op("dve", C("tensor_copy", out=scb, in_=V(scol[:, 0, 0:1], [[NB, 8], [1, NB], [0, 128]])),
             reads=[b_scol], writes=[b_E])
        pj_i = [0]

        def next_pj():
            i_ = pj_i[0] % 2
            pj_i[0] += 1
            return i_

        for cg in range(12 if ADA else 0):
            src = wada_d[:, cg * 256:(cg + 1) * 256].rearrange("(kc p) c -> p kc c", p=128)
            s_ = stage_load(src, 2048, view=lambda t: t[:, :].rearrange("p (kc c) -> p kc c", kc=8))
            stv = STG[s_][:, :].rearrange("p (kc c) -> p kc c", kc=8)
            if cg < 8:
                for cc in range(2):
                    ch = cg * 2 + cc
                    pj = next_pj()
                    for kc in range(8):
                        S.op("pe", C("matmul",
                            PJ[:, pj, 0:NB], lhsT=stv[:, kc, cc * 128:(cc + 1) * 128], rhs=scol[:, kc, :],
                            start=(kc == 0), stop=(kc == 7)),
                            reads=b_stg[s_] + [b_scol], writes=[b_pj[pj]], inc=(kc == 7))
                    S.op("dve", C("tensor_scalar",
                        out=modc[:, ch, :], in0=PJ[:, pj, 0:NB], scalar1=badac[:, ch:ch + 1], scalar2=None, op0=ALU.add),
                        reads=[b_pj[pj], b_badac], writes=[b_modc])
            else:
                for b in range(NB):
                    pj = next_pj()
                    for kc in range(8):
                        S.op("pe", C("matmul",
                            PJ[:, pj, 0:256], lhsT=scb[:, kc, b, :], rhs=stv[:, kc, :],
                            start=(kc == 0), stop=(kc == 7)),
                            reads=b_stg[s_] + [b_E], writes=[b_pj[pj]], inc=(kc == 7))
                    c0 = (cg - 8) * 256
                    S.op("dve", C("tensor_copy", out=GATE[:, b, c0:c0 + 256], in_=PJ[:, pj, 0:256]),
                         reads=[b_pj[pj]], writes=[b_GATE])
        s_ = stage_load(pbc(badag_d, DM), DM)
        for b in range(NB if ADA else 0):
            S.op("dve", C("tensor_tensor", out=GATE[:, b, :], in0=GATE[:, b, :], in1=STG[s_][:, 0:DM], op=ALU.add),
                 reads=b_stg[s_] + [b_GATE], writes=[b_GATE])
        S.op("dve", C("tensor_copy", out=SHcol[:], in_=modc[:, 0:8, :]), reads=[b_modc], writes=[b_SHcol])
        S.op("dve", C("tensor_scalar", out=Gcol[:], in0=modc[:, 8:16, :], scalar1=1.0, scalar2=None, op0=ALU.add),
             reads=[b_modc], writes=[b_Gcol])
        S.op("dve", C("tensor_tensor", out=Gcol[:], in0=Gcol[:], in1=V(ngc[:, 0:1], [[1, 8], [0, NB]]), op=ALU.mult),
             reads=[b_Gcol, b_ngc], writes=[b_Gcol])

        S.dma(C("dma_start", out=KE[0][64:128, :], in_=E_d[64:128, :]), writes=[b_E])
        S.dma(C("dma_start", out=KE[1][0:64, :], in_=E_d[0:64, :]), writes=[b_E])
        for kc in range(8 if pro >= 3 else 0):
            for hx, (c0, c1) in enumerate(((0, 1676), (1676, WCOLS))):
                S.dma(C("dma_start", out=Win[:, kc, c0:c1], in_=win_d[kc * 128:(kc + 1) * 128, c0:c1]),
                      writes=[b_win_l[kc * 2 + hx]], q="pool")
        for kc in range(0, 8 if pro >= 3 else 0, 2):
            src = wout_d[kc * 128:(kc + 2) * 128, :].rearrange("(a p) c -> p a c", p=128)
            s_ = stage_load(src, 2048, view=lambda t: t[:, :].rearrange("p (a c) -> p a c", a=2))
            cast_from_stage("act", s_, Wout[:, kc:kc + 2, :], 2048, b_wout,
                            src_view=lambda t: t[:, :].rearrange("p (a c) -> p a c", a=2))
        for (wd, wsb, wb) in ((w1k_d, w1k, b_w1k), (w1v_d, w1v, b_w1v)):
            for hlf in range(2 if pro >= 3 else 0):
                src = wd[hlf * 1024:(hlf + 1) * 1024, :].rearrange("(lp p) j -> p lp j", p=128)
                s_ = stage_load(src, 2048, view=lambda t: t[:, :].rearrange("p (lp j) -> p lp j", lp=8))
                cast_from_stage("dve", s_, wsb[:, hlf * 8:(hlf + 1) * 8, :], 2048, wb,
                                src_view=lambda t: t[:, :].rearrange("p (lp j) -> p lp j", lp=8))
        for (wd, wsb, wb) in ((w2k_d, w2k, b_w2k), (w2v_d, w2v, b_w2v)) if pro >= 3 else ():
            src = wd.rearrange("(jc p) d -> p jc d", p=128)
            s_ = stage_load(src, 128, view=lambda t: t[:, 0:128].rearrange("p (jc d) -> p jc d", jc=2))
            cast_from_stage("dve", s_, wsb[:], 128, wb, src_view=lambda t: t[:, 0:128].rearrange("p (jc d) -> p jc d", jc=2))
        S.dma(C("dma_start", out=maskAB, in_=maskAB_d), writes=b_yB)
        for (bd, Bt, bB, m0, isB) in ((biasA_d, BA, b_BA, 0, False), (biasB_d, BB, b_BB, 2, True)) if pro >= 4 else ():
            s_ = stage_load(bd, 2048, view=lambda t: t[:, :].rearrange("p (a h c) -> p a h c", a=2, h=8))
            stv = STG[s_][:, :].rearrange("p (a h c) -> p a h c", a=2, h=8)
            for ty in range(2):
                S.op("dve", C("tensor_tensor",
                    out=stv[:, ty], in0=stv[:, ty], in1=V(maskAB[:, m0 + ty, 0:1], [[0, 8], [1, 128]]), op=ALU.add),
                    reads=b_stg[s_] + b_yB, writes=b_stg[s_])
                if isB:
                    S.op("dve", C("tensor_tensor",
                        out=stv[:, ty], in0=stv[:, ty], in1=V(c31[:, 0:1], [[1, 8], [0, 128]]), op=ALU.subtract),
                        reads=b_stg[s_] + [b_c31], writes=b_stg[s_])
            S.op("dve", C("tensor_copy", out=Bt[:, :, :, :], in_=stv), reads=b_stg[s_], writes=[bB])
        for (wsb, wb, pb, bpb, b1, bb1) in ((w1k, b_w1k, poskb, b_poskb, b1k, b_b1k), (w1v, b_w1v, posvb, b_posvb, b1v, b_b1v)) if pro >= 5 else ():
            for jc in range(2):
                pj = next_pj()
                for lp in range(16):
                    S.op("pe", C("matmul",
                        PJ[:, pj, 0:2], lhsT=wsb[:, lp, jc * 128:(jc + 1) * 128], rhs=V(pb[:, lp:lp + 1], [[0, 2]]),
                        start=(lp == 0), stop=(lp == 15)), reads=[wb, bpb], writes=[b_pj[pj]], inc=(lp == 15))
                S.op("dve", C("tensor_copy", out=b1[:, jc:jc + 1], in_=PJ[:, pj, 0:1]),
                     reads=[b_pj[pj]], writes=[bb1])

        if pro >= 5:
            S.op("dve", C("tensor_scalar", out=nb1k[:], in0=b1k[:], scalar1=-1.0, scalar2=None, op0=ALU.mult), reads=[b_b1k], writes=[b_nb1])
            S.op("dve", C("tensor_scalar", out=nb1v[:], in0=b1v[:], scalar1=-1.0, scalar2=None, op0=ALU.mult), reads=[b_b1v], writes=[b_nb1])
        st_i = [0]
        pts_i = [0]

        def build_PC(b, i):
            u = i % 2
            xsl = i % 2
            ra, rw = i % RA, i % RW
            CPb = PT[:, 1, :]
            CPbb = PTb[1]
            b_cp = b_pt[1]
            chains = []

            def proj(c0, n):
                pj = next_pj()
                for kc in range(8):
                    S.op("pe", C("matmul", PJ[:, pj, 0:n], lhsT=hT0[:, kc, :], rhs=Win[:, kc, c0:c0 + n],
                                 start=(kc == 0), stop=(kc == 7)),
                         reads=[bht, b_win] + b_win_l, writes=[b_pj[pj]], inc=(kc == 7))
                return pj

            def L0():
                S.dma(C("dma_start", out=xs[xsl], in_=x_d[b, i * 128:(i + 1) * 128, :]), writes=[b_xs[xsl]])

            def L1():
                S.op("act", C("activation", out=junk, in_=xs[xsl], func=AF.Square, accum_out=stat[:, 0:1]),
                     reads=[b_xs[xsl]], writes=[b_junk, b_stat])

            def L2():
                S.op("act", C("activation", out=stat[:, 1:2], in_=stat[:, 0:1], func=AF.Ln, scale=1.0 / DM, bias=EPS),
                     reads=[b_stat], writes=[b_stat])
                S.op("act", C("activation", out=stat[:, 2:3], in_=stat[:, 1:2], func=AF.Exp, scale=-0.5),
                     reads=[b_stat], writes=[b_stat])

            def L3():
                S.op("pool", C("tensor_scalar", out=xn[:], in0=xs[xsl], scalar1=stat[:, 2:3], scalar2=1.0,
                               op0=ALU.mult, op1=ALU.mult), reads=[b_xs[xsl], b_stat], writes=[b_xn])

            def LT(h):
                def f():
                    for kc in range(4 * h, 4 * h + 4):
                        S.op("pe", C("transpose", out=PT[:, 0, (kc % 4) * 128:(kc % 4 + 1) * 128],
                                     in_=xn[:, kc * 128:(kc + 1) * 128], identity=idf[:]),
                             reads=[b_xn, b_idf], writes=[b_pt[0]])
                return f

            def LE(h):
                def f():
                    for kc in range(4 * h, 4 * h + 4):
                        S.op("dve", C("tensor_scalar", out=hT0[:, kc, :], in0=PT[:, 0, (kc % 4) * 128:(kc % 4 + 1) * 128],
                                      scalar1=Gcol[:, kc, b:b + 1], scalar2=SHcol[:, kc, b:b + 1], op0=ALU.mult, op1=ALU.add),
                             reads=[b_pt[0], b_Gcol, b_SHcol], writes=[bht])
                return f
            chains.append((0, [L1, L2, L3, LT(0), LE(0), LT(1), LE(1)]))

            def norm_chain(c0, nh, qs, nchunks, evac):
                n = nh * 64
                st = {}

                def s0():
                    st["pj"] = proj(c0, n)

                def s1():
                    S.op("act", C("activation", out=sq[:, 0:n], in_=PJ[:, st["pj"], 0:n], func=AF.Square),
                         reads=[b_pj[st["pj"]]], writes=[b_sq])

                def s2():
                    S.op("dve", C("tensor_reduce", out=ssq[:, 0:nh], in_=sq[:, 0:n].rearrange("p (h d) -> p h d", d=64),
                                  axis=AX.X, op=ALU.add), reads=[b_sq], writes=[b_ssq])

                def s3():
                    S.op("act", C("activation", out=ssq[:, 0:nh], in_=ssq[:, 0:nh], func=AF.Ln, scale=1.0 / 64, bias=EPS),
                         reads=[b_ssq], writes=[b_ssq])
                    S.op("act", C("activation", out=ssq[:, 0:nh], in_=ssq[:, 0:nh], func=AF.Exp, scale=-0.5),
                         reads=[b_ssq], writes=[b_ssq])

                def s4():
                    pj = st["pj"]
                    S.op("dve", C("tensor_tensor", out=qn[qs][:, 0:n].rearrange("p (h d) -> p h d", d=64),
                                  in0=PJ[:, pj, 0:n].rearrange("p (h d) -> p h d", d=64),
                                  in1=V(ssq[:, 0:1], [[1, nh], [0, 64]]), op=ALU.mult),
                         reads=[b_pj[pj], b_ssq], writes=[b_qn[qs]])

                def s5():
                    for c in range(nchunks):
                        S.op("pe", C("transpose", out=PTb[0][:, c * 128:(c + 1) * 128], in_=qn[qs][:, c * 128:(c + 1) * 128], identity=idb[:]),
                             reads=[b_qn[qs], b_idb], writes=[b_pt[0]])
                return [s0, s1, s2, s3, s4, s5, evac]

            def evac_qa():
                S.op("dve", C("tensor_scalar", out=qTa[:, u].rearrange("p r t -> p (r t)"), in0=PTb[0][:, 0:512],
                              scalar1=gcols[:, 0:1], scalar2=None, op0=ALU.mult),
                     reads=[b_pt[0], b_gcols], writes=[b_qTa[u]])

            def evac_qb():
                for g_ in range(2):
                    hp_ = slice(g_ * 64, (g_ + 1) * 64)
                    S.op("dve", C("tensor_scalar", out=QNt[hp_, u, g_, :], in0=PTb[0][hp_, 0:512],
                                  scalar1=gcols[hp_, 2:3], scalar2=None, op0=ALU.mult),
                         reads=[b_pt[0], b_gcols], writes=[b_QNw[u][g_]])

            def evac_k():
                for g_ in range(2):
                    hp_ = slice(g_ * 64, (g_ + 1) * 64)
                    S.op("dve", C("tensor_scalar", out=kTa[g_][hp_, ra * 128:(ra + 1) * 128], in0=PTb[0][hp_, 0:128],
                                  scalar1=gcols[hp_, 1:2], scalar2=None, op0=ALU.mult),
                         reads=[b_pt[0], b_gcols], writes=[b_kTa[ra]])
                    S.op("dve", C("tensor_scalar", out=kTw[g_][hp_, rw * 128:(rw + 1) * 128], in0=PTb[0][hp_, 256:384],
                                  scalar1=gcols[hp_, 5:6], scalar2=None, op0=ALU.mult),
                         reads=[b_pt[0], b_gcols], writes=[b_kTw[rw]])
                    S.op("dve", C("tensor_scalar", out=KE[g_][hp_, i * 128:(i + 1) * 128], in0=PTb[0][hp_, 128:256],
                                  scalar1=gcols[hp_, 4:5], scalar2=None, op0=ALU.mult),
                         reads=[b_pt[0], b_gcols], writes=[b_kTs[i]])

            stc = {}
            tau0 = (i % 4) * 128

            def c0_():
                stc["pj"] = proj(G_C, 512)

            def c1_():
                S.op("act", C("copy", out=qn[1][:], in_=PJ[:, stc["pj"], :]), reads=[b_pj[stc["pj"]]], writes=[b_qn[1]])

            def c2_():
                for c in range(4):
                    S.op("pe", C("transpose", out=PTb[0][:, c * 128:(c + 1) * 128], in_=qn[1][:, c * 128:(c + 1) * 128], identity=idb[:]),
                         reads=[b_qn[1], b_idb], writes=[b_pt[0]])

            def c3_():
                for (Xt, bX, cb) in ((Xk, b_Xk, 0), (Xv, b_Xv, 256)):
                    S.op("dve", C("tensor_copy", out=Xt[0][0:64, :, 17 + tau0:17 + tau0 + 128],
                                  in_=PTb[0][0:64, cb:cb + 256].rearrange("p (g t) -> p g t", g=2)),
                         reads=[b_pt[0]], writes=[bX[0]])
                    S.op("dve", C("tensor_copy", out=Xt[0][64:128, :, 16 + tau0:16 + tau0 + 128],
                                  in_=PTb[0][64:128, cb:cb + 256].rearrange("p (g t) -> p g t", g=2)),
                         reads=[b_pt[0]], writes=[bX[0]])
            chains.append((8, [c0_, c1_, c2_, c3_]))
            chains.append((10, norm_chain(G_K, 6, 0, 3, evac_k)))

            stv_ = {}

            def v0_():
                stv_["pj"] = proj(G_V, 408)

            def v1_():
                pj = stv_["pj"]
                S.op("dve", C("tensor_copy", out=Va[:, ra, :, 0:64], in_=PJ[:, pj, 0:128].rearrange("p (g d) -> p g d", g=2)),
                     reads=[b_pj[pj]], writes=[b_Va[ra]])
                S.op("dve", C("tensor_copy", out=Vs[:, i, :, 0:64], in_=PJ[:, pj, 128:256].rearrange("p (g d) -> p g d", g=2)),
                     reads=[b_pj[pj]], writes=[b_Vs[i]])
                S.op("dve", C("tensor_copy", out=Vw[:, rw, :, 0:64], in_=PJ[:, pj, 256:384].rearrange("p (g d) -> p g d", g=2)),
                     reads=[b_pj[pj]], writes=[b_Vw[rw]])
                S.op("dve", C("tensor_tensor", out=gtmp[:], in0=PJ[:, pj, 384:408], in1=bgate[:], op=ALU.add),
                     reads=[b_pj[pj], b_bgate], writes=[b_gtmp])

            def v2_():
                S.op("act", C("activation", out=gtmp[:], in_=gtmp[:], func=AF.Exp, scale=-1.0), reads=[b_gtmp], writes=[b_gtmp])

            def v3_():
                S.op("dve", C("tensor_scalar", out=gtmp[:], in0=gtmp[:], scalar1=1.0, scalar2=None, op0=ALU.add),
                     reads=[b_gtmp], writes=[b_gtmp])
                S.op("dve", C("reciprocal", out=gts[:, u, :], in_=gtmp[:]), reads=[b_gtmp], writes=[b_gts[u]])
            chains.append((12, [v0_, v1_, v2_, v3_]))

            c0x = 128 * (i % 4)

            def cm(part):
                (Xt, bX, w1, bw1) = ((Xk, b_Xk, w1k, b_w1k), (Xv, b_Xv, w1v, b_w1v))[part // 2]
                jc = part % 2

                def f():
                    for lp in range(16):
                        S.op("pe", C("matmul", CPb[:, part * 16:(part + 1) * 16].rearrange("p (g m) -> p g m", g=2),
                                     lhsT=w1[:, lp, jc * 128:(jc + 1) * 128],
                                     rhs=Xt[0][:, :, c0x + 1 + 2 * lp:c0x + 1 + 2 * lp + 16 * 7 + 1:16],
                                     start=(lp == 0), stop=(lp == 15), skip_group_check=True),
                             reads=[bX[0], bw1], writes=[b_cp], inc=(lp == 15))
                return f

            def cm_act():
                for part in range(4):
                    nb1 = (nb1k, nb1v)[part // 2]
                    S.op("act", C("activation", out=hsig[:, part * 16:(part + 1) * 16], in_=CPb[:, part * 16:(part + 1) * 16],
                                  func=AF.Exp, scale=-1.0, bias=nb1[:, part % 2:part % 2 + 1]),
                         reads=[b_cp, b_nb1], writes=[b_hsig])

            def cm_dve1():
                S.op("dve", C("tensor_scalar", out=hsig[:], in0=hsig[:], scalar1=1.0, scalar2=None, op0=ALU.add),
                     reads=[b_hsig], writes=[b_hsig])
                S.op("dve", C("reciprocal", out=hsig[:], in_=hsig[:]), reads=[b_hsig], writes=[b_hsig])

            def cm_dve2():
                for part in range(4):
                    hid, bh = ((hidk, b_hidk), (hidv, b_hidv))[part // 2]
                    b1 = (b1k, b1v)[part // 2]
                    S.op("dve", C("scalar_tensor_tensor", out=hid[:, part % 2, :], in0=CPb[:, part * 16:(part + 1) * 16],
                                  scalar=b1[:, part % 2:part % 2 + 1], in1=hsig[:, part * 16:(part + 1) * 16],
                                  op0=ALU.add, op1=ALU.mult),
                         reads=[b_cp, b_b1k, b_b1v, b_hsig], writes=[bh])

            def cm_mm2():
                for jc in range(2):
                    S.op("pe", C("matmul", CPb[0:16, 64:128], lhsT=hidk[:, jc, :], rhs=w2k[:, jc, :],
                                 start=(jc == 0), stop=(jc == 1), skip_group_check=True),
                         reads=[b_hidk, b_w2k], writes=[b_cp], inc=False)
                for jc in range(2):
                    S.op("pe", C("matmul", CPb[0:16, 128:192], lhsT=hidv[:, jc, :], rhs=w2v[:, jc, :],
                                 start=(jc == 0), stop=(jc == 1), skip_group_check=True),
                         reads=[b_hidv, b_w2v], writes=[b_cp], inc=(jc == 1))

            def cm_a2():
                S.op("act", C("activation", out=kjunk[:], in_=CPb[0:16, 64:128], func=AF.Square, accum_out=kc32[:, 0:1]),
                     reads=[b_cp], writes=[b_kjunk, b_kc32])
                S.op("dve", C("tensor_copy", out=vstag[:], in_=CPb[0:16, 128:192]), reads=[b_cp, b_kc32], writes=[b_vstag])

            def cm_a3():
                S.op("act", C("activation", out=kc32[:, 1:2], in_=kc32[:, 0:1], func=AF.Ln, scale=1.0 / 64, bias=EPS),
                     reads=[b_kc32], writes=[b_kc32])
                S.op("act", C("activation", out=kc32[:, 2:3], in_=kc32[:, 1:2], func=AF.Exp, scale=-0.5),
                     reads=[b_kc32], writes=[b_kc32])
                n0 = 8 * i
                T, p0 = n0 // 128, n0 % 128
                m0 = 1 if i == 0 else 0
                for g in range(2):
                    S.dma(C("dma_start", out=VC[p0 + m0:p0 + 8, T, g, 0:64], in_=vstag[g * 8 + m0:(g + 1) * 8, :]),
                          reads=[b_vstag], writes=[b_VC])

            def cm_d3():
                for h2 in range(2):
                    S.op("dve", C("tensor_scalar", out=kndup[:, h2 * 64:(h2 + 1) * 64], in0=CPb[0:16, 64:128],
                                  scalar1=kc32[:, 2:3], scalar2=None, op0=ALU.mult),
                         reads=[b_cp, b_kc32], writes=[b_kndup])

            def cm_t():
                S.op("pe", C("transpose", out=PTb[0][:, 0:16], in_=kndup[:], identity=idb[0:16, 0:16]),
                     reads=[b_kndup, b_idb], writes=[b_pt[0]])

            def cm_e():
                n0 = 8 * i
                for g in range(2):
                    S.op("dve", C("tensor_scalar", out=kcT[g][g * 64:(g + 1) * 64, n0:n0 + 8],
                                  in0=PTb[0][g * 64:(g + 1) * 64, g * 8:(g + 1) * 8],
                                  scalar1=gcols[g * 64:(g + 1) * 64, 3:4], scalar2=None, op0=ALU.mult),
                         reads=[b_pt[0], b_gcols], writes=[b_kcT])
            nop = lambda: None
            import os as _os
            chains.append((12, [cm(0), cm(1), cm(2), cm(3), cm_act, cm_dve1, cm_dve2, cm_mm2, cm_a2, cm_a3, cm_d3, nop, nop, nop, cm_t, cm_e][:int(_os.environ.get('NCM', '99'))]))
            chains.append((15, norm_chain(G_QA, 8, 1, 4, evac_qa)))
            chains.append((18, norm_chain(G_QB, 8, 0, 4, evac_qb)))

            def z_chain(c0, zbuf, bz, col0):
                st = {}

                def s0():
                    st["pj"] = proj(c0, 512)

                def s1():
                    S.op("act", C("activation", out=zbuf, in_=PJ[:, st["pj"], :], func=AF.Exp, scale=-1.0),
                         reads=[b_pj[st["pj"]]], writes=[bz])

                def s2():
                    S.op("act", C("activation", out=zbuf, in_=zbuf, func=AF.Ln, bias=1.0), reads=[bz], writes=[bz])

                def s3():
                    S.op("act", C("activation", out=zbuf, in_=zbuf, func=AF.Exp, scale=-1.0), reads=[bz], writes=[bz])

                def s4():
                    pj = st["pj"]
                    S.op("dve", C("tensor_tensor", out=sz[:, u, col0:col0 + 512], in0=PJ[:, pj, :], in1=zbuf, op=ALU.mult),
                         reads=[b_pj[pj], bz], writes=[b_sz[u]])
                return [s0, s1, s2, s3, s4]
            chains.append((21, z_chain(G_ZA, sq[:, :], b_sq, 0)))
            chains.append((24, z_chain(G_ZB, zt2[:, :], b_zt2, 512)))
            return chains

        NSTEP = 32
        ntile = 4 * n_sb

        def x_load(b, i):
            S.dma(C("dma_start", out=xs[i % 2], in_=x_d[b, i * 128:(i + 1) * 128, :]), writes=[b_xs[i % 2]])

        def gen_steps(b, pc, tl):
            chains = []
            if pc is not None:
                if pc % 4 == 0 and pc > 0:
                    for (Xt, bX) in ((Xk, b_Xk), (Xv, b_Xv)):
                        S.op("pool", C("tensor_copy", out=Xt[0][:, :, 0:17], in_=Xt[0][:, :, 512:529]), reads=[bX[0]], writes=[bX[0]])
                chains += build_PC(b, pc)[:nch]
                if pc + 1 < ntile:
                    chains.append((10, [lambda: x_load(b, pc + 1)]))
            if tl is not None:
                chains.append(tail_chain(b, tl))
            T_ = max(o + len(ch) for o, ch in chains)
            assert T_ <= NSTEP, T_
            for tau in range(NSTEP):
                for o, ch in chains:
                    k = tau - o
                    if 0 <= k < len(ch):
                        ch[k]()
                yield

        def score_tile(lhsT_ap, rhs_ap, extra, reads):
            st = st_i[0] % 2
            st_i[0] += 1
            n = len(extra)
            S.op("pe", C("matmul", ST[:, st, :], lhsT=lhsT_ap, rhs=rhs_ap, start=True, stop=(n == 0)),
                 reads=reads, writes=[b_st[st]], inc=(n == 0))
            for j, (l_ap, r_ap, rd) in enumerate(extra):
                S.op("pe", C("matmul", ST[:, st, :], lhsT=l_ap, rhs=r_ap, start=False, stop=(j == n - 1)),
                     reads=rd, writes=[b_st[st]], inc=(j == n - 1))
            return st

        def exp_tile(st):
            p = pts_i[0] % 2
            pts_i[0] += 1
            S.op("act", C("activation", out=PTs[p][:], in_=ST[:, st, :], func=AF.Exp), reads=[b_st[st]], writes=[b_PTs[p]])
            return p

        def bc4(ap2d):
            return V(ap2d, [[0, 4], [1, 128]])

        def phaseA(b, i, u, nxt):
            gsl = gts[:, u, :]
            jobs = []

            def Bx(Bt, ty, g):
                return [(idr[:], Bt[:, ty, 4 * g:4 * g + 4, :].rearrange("p h t -> p (h t)"), [b_idr, b_BA if Bt is BA else b_BB])]

            def add_branch(g, kts, lhs_fn, rhs_fn, extras_fn, V_t, V_bufs, vidx, oab, post):
                nk = len(kts)
                for j, kt in enumerate(kts):
                    def fscore(kt=kt):
                        l_ap, l_rd = lhs_fn(kt)
                        r_ap, r_rd = rhs_fn(kt)
                        return score_tile(l_ap, r_ap, extras_fn(kt), l_rd + r_rd)

                    def pv(p, kt=kt, j=j):
                        for r in range(4):
                            S.op("pe", C("matmul", OA[:, oab, r * 66:(r + 1) * 66], lhsT=PTs[p][:, r * 128:(r + 1) * 128],
                                         rhs=V_t[:, vidx(kt), g, :], start=(j == 0 and r == 0), stop=(j == nk - 1), skip_group_check=True),
                                 reads=[b_PTs[p], V_bufs(kt)], writes=[b_oa[oab]], inc=(j == nk - 1 and r == 3))
                    jobs.append((fscore, pv, post if j == nk - 1 else None))

            def accumulate(g, oab, dcol, gate_br):
                S.op("dve", C("reciprocal", out=den[:, dcol:dcol + 4], in_=V(OA[:, oab, 64:65], [[66, 4]])),
                     reads=[b_oa[oab]], writes=[b_den])
                S.op("dve", C("tensor_tensor", out=coef[:, dcol:dcol + 4], in0=den[:, dcol:dcol + 4],
                              in1=gsl[:, g * 12 + gate_br:(g + 1) * 12:3], op=ALU.mult),
                     reads=[b_den, b_gts[u]], writes=[b_coef])
                S.op("dve", C("tensor_tensor", out=ytmp[oab][:, :].rearrange("p (r d) -> p r d", r=4),
                              in0=V(OA[:, oab, 0:1], [[66, 4], [1, 64]]),
                              in1=V(coef[:, dcol:dcol + 1], [[1, 4], [0, 64]]), op=ALU.mult),
                     reads=[b_oa[oab], b_coef], writes=[b_ytmp[oab]])
                S.op("dve", C("tensor_tensor", out=yB[:, g, :], in0=yB[:, g, :], in1=ytmp[oab][:, :], op=ALU.add),
                     reads=[b_yB[g], b_ytmp[oab]], writes=[b_yB[g]])

            Tmax = i // 16
            post_b_fns = {}
            for g in range(2):
                hp = slice(g * 64, (g + 1) * 64)
                qb_ap = QNt[hp, u, g, :]
                for T in range(Tmax + 1):
                    def fscore(T=T, hp=hp, qb_ap=qb_ap, g=g):
                        extra = []
                        if T == Tmax:
                            extra.append((idb[:], bc4(CM[:, i % 16, :]), [b_idb, b_CM]))
                        return score_tile(kcT[g][:, T * 128:(T + 1) * 128], QNt[:, u, g, :], extra, [b_kcT, b_QNw[u][g]])

                    def pv(p, T=T, g=g):
                        for r in range(4):
                            S.op("pe", C("matmul", OA[:, g, r * 128:(r + 1) * 128], lhsT=PTs[p][:, r * 128:(r + 1) * 128],
                                         rhs=VC[:, T, g, :], start=(T == 0 and r == 0), stop=(T == Tmax), skip_group_check=True),
                                 reads=[b_PTs[p], b_VC], writes=[b_oa[g]], inc=(T == Tmax and r == 3))

                    def post(st, g=g, hp=hp):
                        S.op("dve", C("tensor_scalar", out=den[:, 0:4], in0=V(OA[:, g, 64:65], [[128, 4]]),
                                      scalar1=1e-30, scalar2=None, op0=ALU.max),
                             reads=[b_oa[g]], writes=[b_den])
                        S.op("dve", C("reciprocal", out=den[:, 0:4], in_=den[:, 0:4]), reads=[b_den], writes=[b_den])
                        S.op("dve", C("tensor_tensor", out=coef[:, 0:4], in0=den[:, 0:4],
                                      in1=gsl[:, g * 12:(g + 1) * 12:3], op=ALU.mult),
                             reads=[b_den, b_gts[u]], writes=[b_coef])
                        pat_ap = PAT[:, 64 - 2 * i + 64:128 - 2 * i + 64]
                        for r in range(4):
                            dst, bdst = (impA, b_impA) if r % 2 == 0 else (impB, b_impB)
                            in0 = OA[:, g, r * 128 + 66:r * 128 + 128]
                            if r == 1:
                                S.op("dve", C("tensor_scalar", out=dst[:, 1:63], in0=in0, scalar1=den[:, r:r + 1], scalar2=None, op0=ALU.mult),
                                     reads=[b_oa[g], b_den], writes=[bdst])
                            else:
                                in1, rd1 = (pat_ap[:, 1:63], b_PAT) if r == 0 else (dst[:, 1:63], bdst)
                                S.op("dve", C("scalar_tensor_tensor", out=dst[:, 1:63], in0=in0, scalar=den[:, r:r + 1],
                                              in1=in1, op0=ALU.mult, op1=ALU.add),
                                     reads=[b_oa[g], b_den, rd1], writes=[bdst])
                        S.op("dve", C("tensor_tensor", out=yB[:, g, :].rearrange("p (r d) -> p r d", r=4),
                                      in0=V(OA[:, g, 0:1], [[128, 4], [1, 64]]), in1=V(coef[:, 0:1], [[1, 4], [0, 64]]), op=ALU.mult),
                             reads=[b_oa[g], b_coef], writes=[b_yB[g]])
                        S.op("dve", C("tensor_tensor", out=score[:, 1:63], in0=impA[:, 1:63], in1=impB[:, 1:63], op=ALU.add),
                             reads=[b_impA, b_impB], writes=[b_score])
                        if i == NT - 1:
                            S.op("dve", C("tensor_copy", out=score[:, 63:64], in_=pat_ap[:, 63:64]), reads=[b_PAT, b_score], writes=[b_score])
                        S.op("dve", C("max", out=m8[:, 0:8], in_=score[:]), reads=[b_score], writes=[b_m8])
                        S.op("dve", C("match_replace", out=wk[:], in_to_replace=m8[:, 0:8], in_values=score[:], imm_value=-1e30),
                             reads=[b_score, b_m8], writes=[b_wk])
                        S.op("dve", C("max", out=m8[:, 8:16], in_=wk[:]), reads=[b_wk], writes=[b_m8])
                        S.op("dve", C("tensor_scalar", out=negmg[g][:, :].rearrange("p (a j) -> p a j", a=2), in0=V(score[:, 0:1], [[0, 2], [1, 64]]),
                                      scalar1=m8[:, 15:16], scalar2=NEG,
                                      op0=ALU.is_lt, op1=ALU.mult), reads=[b_score, b_m8], writes=[b_negm[g]])

                    def post_b(g=g):
                        S.op("pe", C("transpose", out=OA[:, g, 384:512], in_=negmg[g][:], identity=idf[:]),
                             reads=[b_negm[g], b_idf], writes=[b_oa[g]])
                        op_ = slice((1 - g) * 64, (2 - g) * 64)
                        S.op("dve", C("tensor_copy", out=QNt[op_, u, g, :].rearrange("p (r t) -> p r t", r=4), in_=bc4(OA[op_, g, 384:512])),
                             reads=[b_oa[g]], writes=[b_QNw[u][g]])
                    if T == Tmax:
                        post_b_fns[g] = post_b
                    jobs.append((fscore, pv, post if T == Tmax else None))

            for g in range(2):
                hp = slice(g * 64, (g + 1) * 64)
                qa_ap = qTa[hp, u].rearrange("p r t -> p (r t)")

                def a_post(st, g=g):
                    oab = g
                    post_b_fns[g]()
                    S.op("dve", C("tensor_tensor", out=den[:, 12:16], in0=V(OA[:, oab, 64:65], [[66, 4]]),
                                  in1=esink[:, 4 * g:4 * g + 4], op=ALU.add),
                         reads=[b_oa[oab], b_esink], writes=[b_den])
                    S.op("dve", C("reciprocal", out=den[:, 12:16], in_=den[:, 12:16]), reads=[b_den], writes=[b_den])
                    S.op("dve", C("tensor_tensor", out=ytmp[oab][:, :].rearrange("p (r d) -> p r d", r=4),
                                  in0=V(OA[:, oab, 0:1], [[66, 4], [1, 64]]),
                                  in1=V(den[:, 12:13], [[1, 4], [0, 64]]), op=ALU.mult),
                         reads=[b_oa[oab], b_den], writes=[b_ytmp[oab]])
                    S.op("pool", C("tensor_tensor", out=y[:, g * 256:(g + 1) * 256], in0=ytmp[oab][:, :],
                                   in1=sz[:, u, g * 256:(g + 1) * 256], op=ALU.mult),
                         reads=[b_ytmp[oab], b_sz[u]], writes=[b_y])
                add_branch(g, list(range(max(0, i - 1), i + 1)),
                           lambda kt, g=g: (kTa[g][:, (kt % RA) * 128:(kt % RA) * 128 + 128], [b_kTa[kt % RA]]),
                           lambda kt: (qTa[:, u].rearrange("p r t -> p (r t)"), [b_qTa[u]]),
                           lambda kt, g=g: Bx(BA, 0, g) if kt == i else Bx(BA, 1, g),
                           Va, lambda kt: b_Va[kt % RA], lambda kt: kt % RA, g, a_post)
            for g in range(2):
                hp = slice(g * 64, (g + 1) * 64)
                qb_ap = QNt[hp, u, g, :]

                def win_extras(kt, g=g):
                    dk = i - kt
                    if dk == 0:
                        return Bx(BB, 0, g)
                    if dk == 1:
                        return Bx(BB, 1, g)
                    if dk == 4:
                        return [(idb[:], bc4(TRI[:, :]), [b_idb, b_TRI])]
                    return []
                add_branch(g, list(range(max(0, i - 4), i + 1)),
                           lambda kt, g=g: (kTw[g][:, (kt % RW) * 128:(kt % RW) * 128 + 128], [b_kTw[kt % RW]]),
                           lambda kt, g=g: (QNt[:, u, g, :], [b_QNw[u][g]]),
                           win_extras, Vw, lambda kt: b_Vw[kt % RW], lambda kt: kt % RW, g,
                           lambda st, g=g: accumulate(g, g, 8, 2))
            for g in range(2):
                hp = slice(g * 64, (g + 1) * 64)
                qb_ap = QNt[hp, u, g, :]

                def sel_lhs(kt, g=g, hp=hp):
                    return (KE[g][:, kt * 128:(kt + 1) * 128], [b_kTs[kt], b_E])

                def sel_rhs(kt, g=g, qb_ap=qb_ap):
                    return (QNt[:, u, g, :], [b_QNw[u][g]])

                def sel_extras(kt, g=g):
                    if kt == i:
                        return Bx(BB, 0, g)
                    if kt == i - 1:
                        return Bx(BB, 1, g)
                    return []

                def sel_post(st, g=g):
                    accumulate(g, g, 4, 1)
                    S.op("dve", C("tensor_tensor", out=y[:, 512 + g * 256:512 + (g + 1) * 256], in0=yB[:, g, :],
                                  in1=sz[:, u, 512 + g * 256:512 + (g + 1) * 256], op=ALU.mult),
                         reads=[b_yB[g], b_sz[u]], writes=[b_y])
                add_branch(g, list(range(0, i + 1)), sel_lhs, sel_rhs, sel_extras, Vs, lambda kt: b_Vs[kt], lambda kt: kt, g, sel_post)

            pend = None
            nj = len(jobs)
            sdone = 0
            ncmp = 2 * (Tmax + 1)
            for jx, (fscore, pv, post) in enumerate(jobs):
                st = fscore()
                want = ((jx + 1) * NSTEP) // nj
                if jx + 1 >= ncmp + 1:
                    want = max(2, want)
                while sdone < want:
                    next(nxt, None)
                    sdone += 1
                if pend is not None:
                    pst, ppv, ppost = pend
                    p = exp_tile(pst)
                    ppv(p)
                    if ppost is not None:
                        ppost(pst)
                pend = (st, pv, post)
            pst, ppv, ppost = pend
            p = exp_tile(pst)
            ppv(p)
            if ppost is not None:
                ppost(pst)

        def tail_chain(b, i):
            xrs = i % 2

            def t0():
                S.dma(C("dma_start", out=xr[xrs], in_=x_d[b, i * 128:(i + 1) * 128, :]), writes=[b_xr[xrs]])
                for kc in range(8):
                    S.op("pe", C("transpose", out=PTb[1][:, kc * 128:(kc + 1) * 128], in_=y[:, kc * 128:(kc + 1) * 128],
                                 identity=idb[:]), reads=[b_y, b_idb], writes=[b_pt[1]])

            def t1():
                S.op("act", C("copy", out=yT[:, :, :].rearrange("p k t -> p (k t)"), in_=PTb[1][:, :]), reads=[b_pt[1]], writes=[b_yT])
            stt = {}

            def mm(hf):
                def f():
                    pj = next_pj()
                    stt[hf] = pj
                    for kc in range(8):
                        S.op("pe", C("matmul", PJ[:, pj, :], lhsT=yT[:, kc, :], rhs=Wout[:, kc, hf * 512:(hf + 1) * 512],
                                     start=(kc == 0), stop=(kc == 7)),
                             reads=[b_yT, b_wout], writes=[b_pj[pj]], inc=(kc == 7))
                return f

            def ml(hf):
                def f():
                    pj = stt[hf]
                    S.op("dve", C("tensor_tensor", out=rtmp[:, :], in0=PJ[:, pj, :],
                                  in1=GATE[:, b, hf * 512:(hf + 1) * 512], op=ALU.mult),
                         reads=[b_pj[pj], b_GATE], writes=[b_rtmp])
                return f

            def t6(hf):
                def f():
                    S.op("dve", C("tensor_tensor", out=xr[xrs][:, hf * 512:(hf + 1) * 512], in0=xr[xrs][:, hf * 512:(hf + 1) * 512],
                                  in1=rtmp[:, :], op=ALU.add),
                         reads=[b_xr[xrs], b_rtmp], writes=[b_xr[xrs]])
                return f

            def t7():
                S.dma(C("dma_start", out=out_d[b, i * 128:(i + 1) * 128, :], in_=xr[xrs]), reads=[b_xr[xrs]], is_output=True)
            return (0, [t0, t1, mm(0), ml(0), t6(0), mm(1), ml(1), t6(1), t7])


        def drain(gen):
            for _ in gen:
                pass

        for b in range(nb):
            if b > 0:
                for g_ in range(2):
                    S.op("pool", C("memset", kcT[g_][:], 0.0), writes=[b_kcT])
                S.op("pool", C("memset", VC[:, :, :, 0:64], 0.0), writes=[b_VC])
                S.op("pool", C("memset", Xk[0][:, :, 0:17], 0.0), writes=[b_Xk[0]])
                S.op("pool", C("memset", Xv[0][:, :, 0:17], 0.0), writes=[b_Xv[0]])
                S.op("pool", C("memset", score[:, 63:64], 0.0), reads=[b_score], writes=[b_score])
            x_load(b, 0)
            drain(gen_steps(b, 0, None))
            for i in range(ntile):
                pc_ = i + 1 if i + 1 < ntile else None
                tl_ = i - 1 if i >= 1 else None
                nxt = gen_steps(b, pc_, tl_) if (pc_ is not None or tl_ is not None) else iter(())
                if "A" in stages and i < n_a:
                    phaseA(b, i, i % 2, nxt)
                drain(nxt)
            drain(gen_steps(b, None, ntile - 1))
        if dumps:
            L = dict(GATE=(GATE, [b_GATE]), Gcol=(Gcol, [b_Gcol]), SHcol=(SHcol, [b_SHcol]), BA=(BA, [b_BA]), BB=(BB, [b_BB]),
                     b1k=(b1k, [b_b1k]), b1v=(b1v, [b_b1v]), Win=(Win, [b_win]), Wout=(Wout, [b_wout]), w1k=(w1k, [b_w1k]),
                     qTa=(qTa, b_qTa), kTs=(KE[0], b_kTs),
                     Vs=(Vs, b_Vs), Va=(Va, b_Va), Vw=(Vw, b_Vw), VC=(VC, [b_VC]), sz=(sz, b_sz), gts=(gts, b_gts),
                      y=(y, [b_y]), hT=(hT0, [bht]),
                     Xk=(Xk0, [bxk]), hidk=(hidk, [b_hidk]), score=(score, [b_score]), yB=(yB, b_yB), esink=(esink, [b_esink]),
                     den=(den, [b_den]), coef=(coef, [b_coef]))
            for nm in dumps:
                t, bufs = L[nm]
                shp = list(t.shape)
                dd = nc.dram_tensor("d_" + nm, shp, t.dtype, kind="ExternalOutput").ap()
                full = t[tuple(slice(None) for _ in shp)]
                S.dma(C("dma_start", out=dd, in_=full), reads=list(bufs), is_output=True)
        S.emit()
    return nc


def _t5_bucket(dist):
    n = np.maximum(dist, 0)
    nf = np.maximum(n, 1).astype(np.float32)
    large = 16 + (np.log(nf / np.float32(16)) / np.float32(np.log(128 / 16)) * np.float32(16)).astype(np.int32)
    large = np.minimum(large, 31)
    return np.where(n < 16, n, large)


def _constants():
    bf = ml_dtypes.bfloat16
    sl = np.arange(128)[:, None]
    tl = np.arange(128)[None, :]
    d_diag = tl - sl
    d_prev = 128 + tl - sl
    idx = np.stack([_t5_bucket(d_diag), _t5_bucket(d_prev)], 0)
    maskAB = np.zeros((128, 4, 128), np.float32)
    maskAB[:, 0, :] = np.where(d_diag >= 0, 0.0, NEG)
    maskAB[:, 1, :] = np.where(d_prev < 128, 0.0, NEG)
    maskAB[:, 2, :] = np.where(d_diag >= 0, 0.0, NEG)
    maskAB[:, 3, :] = 0.0
    E = (np.arange(SEQ)[None, :] // 64 == (np.arange(128) % 64)[:, None]).astype(np.float32).astype(bf)
    nl = np.arange(128)[:, None, None]
    o = np.arange(16)[None, :, None]
    t3 = np.arange(128)[None, None, :]
    cmask = np.where(16 * nl + 15 <= 128 * o + t3, 0.0, NEG).astype(np.float32).astype(bf)
    tri = np.where(sl > tl, 0.0, NEG).astype(np.float32).astype(bf)
    pat = np.zeros((128, 192), np.float32)
    pat[:, 0] = BONUS
    hi = (np.arange(128) >= 64).astype(np.int64)
    pat[np.arange(128), 64 + 63 + hi] = BONUS
    pat[np.arange(128), 64 + 64 + hi] = BONUS
    pat = pat.astype(bf)
    npr = np.arange(256)
    n = npr - 1
    c_lo = 16 * n
    s_lo = 64 * np.arange(64)
    ov = np.clip(np.minimum(c_lo[:, None] + 32, s_lo[None, :] + 64) - np.maximum(c_lo[:, None], s_lo[None, :]), 0, None) / 32.0
    ov[0, :] = 0.0
    ov = np.ascontiguousarray(ov.reshape(2, 128, 64).transpose(1, 0, 2)[:, :, 1:63]).astype(np.float32).astype(bf)
    return idx, maskAB, E, cmask, tri, pat, ov


def _perm_cols():
    o_qa, o_ka, o_va, o_za, o_qb, o_kc, o_vc, o_ks, o_vs, o_kw, o_vw, o_zb, o_gb = (
        0, 512, 640, 768, 1280, 1792, 1920, 2048, 2176, 2304, 2432, 2560, 3072)
    r64 = np.arange(64)
    cols = []
    for base in (o_qa, o_qb):
        for r in range(4):
            cols += [base + r * 64 + r64, base + (4 + r) * 64 + r64]
    cols += [o_ka + np.arange(128), o_ks + np.arange(128), o_kw + np.arange(128)]
    for base in (o_kc, o_vc):
        for g in range(2):
            cols += [base + g * 64 + r64, base + g * 64 + r64]
    cols += [o_va + np.arange(128), o_vs + np.arange(128), o_vw + np.arange(128), o_gb + np.arange(24)]
    cols += [o_za + np.arange(512), o_zb + np.arange(512)]
    cols = np.concatenate(cols)
    assert cols.shape[0] == WCOLS
    return cols


_NC_CACHE = {}


def kernel(x, c, w_ada, b_ada, norm_gain, w_in, b_nsa_gate, q_gain_a, k_gain_a, sinks,
           q_gain_b, k_gain_cmp, k_gain_sel, k_gain_win, cmp_pos_k, cmp_pos_v,
           w_cmp_k1, w_cmp_k2, w_cmp_v1, w_cmp_v2, w_out, rel_bias):
    f = lambda a: np.ascontiguousarray(np.asarray(a, dtype=np.float32))
    x = f(x); c = f(c); w_ada = f(w_ada)[0]; b_ada = f(b_ada)[0]; norm_gain = f(norm_gain)[0]
    w_in = f(w_in)[0]; b_nsa_gate = f(b_nsa_gate)[0]; sinks = f(sinks)[0]
    rel_bias = f(rel_bias); w_out = f(w_out)[0]
    idx, maskAB, E, cmask, tri, pat, ov = _constants()
    biasAB = rel_bias[idx]
    biasA = np.ascontiguousarray(biasAB[..., 0:8].transpose(1, 0, 3, 2))
    biasB = np.ascontiguousarray(biasAB[..., 8:16].transpose(1, 0, 3, 2))
    c31 = np.ascontiguousarray(rel_bias[31:32, 8:16])
    gcols = np.stack([np.tile(f(g)[0], 2) for g in (q_gain_a, k_gain_a, q_gain_b, k_gain_cmp, k_gain_sel, k_gain_win)], 1)
    shared = {
        "w_ada": w_ada,
        "bada_col": np.ascontiguousarray(b_ada[0:2048].reshape(16, 128).T),
        "bada_gate": np.ascontiguousarray(b_ada[2048:3072].reshape(1, DM)),
        "ng_col": np.ascontiguousarray(norm_gain.reshape(8, 128).T),
        "w_in_p": np.ascontiguousarray(w_in[:, _perm_cols()]),
        "bgate": b_nsa_gate.reshape(1, 24),
        "gcols": np.ascontiguousarray(gcols),
        "sinks": sinks.reshape(1, 8),
        "posk": np.ascontiguousarray(f(cmp_pos_k)[0].reshape(16, 128).T),
        "posv": np.ascontiguousarray(f(cmp_pos_v)[0].reshape(16, 128).T),
        "w1k": f(w_cmp_k1)[0], "w1v": f(w_cmp_v1)[0], "w2k": f(w_cmp_k2)[0], "w2v": f(w_cmp_v2)[0],
        "w_out": w_out, "biasA": biasA, "biasB": biasB, "c31": c31, "maskAB": maskAB,
        "Emat": E, "cmask": cmask, "tri": tri, "pat": pat, "ov": ov,
    }
    in_maps = []
    for core in range(NCORES):
        m = dict(shared)
        m["x"] = x[NB * core:NB * (core + 1)]
        cc = c[NB * core:NB * (core + 1)]
        m["cT"] = np.ascontiguousarray(cc.reshape(NB, 8, 128).transpose(2, 1, 0))
        in_maps.append(m)
    if "nc" not in _NC_CACHE:
        _NC_CACHE["nc"] = build_program()
    res = run_bass_kernel_spmd(_NC_CACHE["nc"], in_maps, core_ids=list(range(NCORES)))
    return np.concatenate([np.asarray(r["out"], dtype=np.float32) for r in res.results], axis=0)
```

```python
import numpy as np
import ml_dtypes
from contextlib import ExitStack
import concourse.bass as bass
import concourse.mybir as mybir
from concourse.bass_utils import run_bass_kernel_spmd

F32 = mybir.dt.float32
BF16 = mybir.dt.bfloat16
F32R = mybir.dt.float32r
AF = mybir.ActivationFunctionType
ALU = mybir.AluOpType
AX = mybir.AxisListType

NCORES = 8
SEQ = 4096
DM = 1024
NT = SEQ // 128
NB = 2
WCOLS = 3352
NEG = -30000.0
BONUS = 1.0e4
EPS = 1e-6
G_QA, G_QB, G_K, G_C, G_V, G_ZA, G_ZB = 0, 512, 1024, 1408, 1920, 2328, 2840
RA, RW = 3, 6


class Buf:
    __slots__ = ("w", "r", "dsem", "dcount", "name")

    def __init__(self, name=""):
        self.w = None
        self.r = {}
        self.dsem = None
        self.dcount = 0
        self.name = name


class Sched:
    ENG = ("pe", "act", "dve", "pool", "sp")

    def __init__(self, nc, same_engine_sync=True):
        self.nc = nc
        self.ops = {e: [] for e in self.ENG}
        self.count = {e: 0 for e in self.ENG}
        self.waited = {e: {} for e in self.ENG}
        self.sems = {}
        self.dma_sems = []
        self.same_engine_sync = same_engine_sync
        self.out_waits = []

    def _need(self, eng, dep, needs):
        if dep is None:
            return
        k, v = dep
        if k == eng:
            if eng in ("pe", "sp"):
                return
            if not self.same_engine_sync:
                return
        if self.waited[eng].get(k, 0) >= v:
            return
        if needs.get(k, 0) < v:
            needs[k] = v

    def _emit_waits(self, eng, needs):
        for k, v in needs.items():
            self.ops[eng].append(("w", k, v))
            self.waited[eng][k] = v

    def op(self, eng, fn, reads=(), writes=(), inc=True):
        needs = {}
        for b in reads:
            self._need(eng, b.w, needs)
        for b in writes:
            self._need(eng, b.w, needs)
            for k, v in b.r.items():
                self._need(eng, (k, v), needs)
        self._emit_waits(eng, needs)
        if inc:
            self.count[eng] += 1
            val = self.count[eng]
        else:
            val = self.count[eng] + 1
        self.ops[eng].append(("op", fn, [(eng, 1)] if inc else []))
        for b in writes:
            b.w = (eng, val)
            b.r = {}
        for b in reads:
            if b.r.get(eng, 0) < val:
                b.r[eng] = val
        return val

    def dma(self, fn, reads=(), writes=(), q="sp", is_output=False):
        needs = {}
        for b in reads:
            self._need(q, b.w, needs)
        for b in writes:
            self._need(q, b.w, needs)
            for k, v in b.r.items():
                self._need(q, (k, v), needs)
        self._emit_waits(q, needs)
        owner = writes[0] if writes else reads[0]
        if owner.dsem is None:
            owner.dsem = "dma%d" % len(self.dma_sems)
            self.dma_sems.append(owner.dsem)
        owner.dcount += 16
        k, v = owner.dsem, owner.dcount
        self.ops[q].append(("op", fn, [(k, 16)]))
        for b in writes:
            b.w = (k, v)
            b.r = {}
        for b in reads:
            b.r[k] = v
        if is_output:
            self.out_waits.append((k, v))

    def emit(self):
        nc = self.nc
        with ExitStack() as es:
            for e in self.ENG:
                self.sems[e] = es.enter_context(nc.semaphore("prog_" + e))
            for k in self.dma_sems:
                self.sems[k] = es.enter_context(nc.semaphore(k))
            fin = {}
            for k, v in self.out_waits:
                fin[k] = max(fin.get(k, 0), v)
            for k, v in fin.items():
                self.ops["sp"].append(("w", k, v))
            block = es.enter_context(nc.Block())
            sems = self.sems
            ops = self.ops

            def run(engine, lst):
                for item in lst:
                    if item[0] == "w":
                        engine.wait_ge(sems[item[1]], item[2])
                    else:
                        name, args, kw = item[1]
                        ins = getattr(engine, name)(*args, **kw)
                        for (k, n) in item[2]:
                            ins.then_inc(sems[k], n)

            @block.sync
            def _(e):
                run(e, ops["sp"])

            @block.tensor
            def _(e):
                run(e, ops["pe"])

            @block.scalar
            def _(e):
                run(e, ops["act"])

            @block.vector
            def _(e):
                run(e, ops["dve"])

            @block.gpsimd
            def _(e):
                run(e, ops["pool"])


def C(name, *args, **kw):
    return (name, args, kw)


def V(ap, dims):
    return bass.AP(tensor=ap.tensor, offset=ap.offset, ap=[list(ap.ap[0])] + [list(d) for d in dims])


def build_program(nb=NB, n_sb=NT // 4, stages="PCA", n_a=999, dumps=None, pro=9, nch=99):
    nc = bass.Bass("TRN2", target_bir_lowering=False)
    S = Sched(nc)

    def din(name, shape, dt=F32):
        return nc.dram_tensor(name, list(shape), dt, kind="ExternalInput").ap()

    x_d = din("x", [NB, SEQ, DM])
    cT_d = din("cT", [128, 8, NB])
    wada_d = din("w_ada", [DM, 3 * DM])
    badac_d = din("bada_col", [128, 16])
    badag_d = din("bada_gate", [1, DM])
    ngc_d = din("ng_col", [128, 8])
    win_d = din("w_in_p", [DM, WCOLS])
    bgate_d = din("bgate", [1, 24])
    gcols_d = din("gcols", [128, 6])
    sinks_d = din("sinks", [1, 8])
    posk_d = din("posk", [128, 16])
    posv_d = din("posv", [128, 16])
    w1k_d = din("w1k", [2048, 256])
    w1v_d = din("w1v", [2048, 256])
    w2k_d = din("w2k", [256, 64])
    w2v_d = din("w2v", [256, 64])
    wout_d = din("w_out", [DM, DM])
    biasA_d = din("biasA", [128, 2, 8, 128])
    biasB_d = din("biasB", [128, 2, 8, 128])
    c31_d = din("c31", [1, 8])
    maskAB_d = din("maskAB", [128, 4, 128])
    E_d = din("Emat", [128, SEQ], BF16)
    cmask_d = din("cmask", [128, 16, 128], BF16)
    tri_d = din("tri", [128, 128], BF16)
    pat_d = din("pat", [128, 192], BF16)
    ov_d = din("ov", [128, 2, 62], BF16)
    out_d = nc.dram_tensor("out", [NB, SEQ, DM], F32, kind="ExternalOutput").ap()

    with ExitStack() as es:
        def sb(name, shape, dt):
            return es.enter_context(nc.sbuf_tensor("s_" + name, list(shape), dt))

        def ps(name, shape, dt):
            return es.enter_context(nc.psum_tensor("p_" + name, list(shape), dt))

        PJ = ps("PJ", [128, 2, 512], F32)
        PT = ps("PT", [128, 2, 512], F32)
        ST = ps("ST", [128, 2, 512], F32)
        OA = ps("OA", [128, 2, 512], F32)
        b_pj = [Buf("pj0"), Buf("pj1")]
        b_pt = [Buf("pt0"), Buf("pt1")]
        b_st = [Buf("st0"), Buf("st1")]
        b_oa = [Buf("oa0"), Buf("oa1")]
        PTb = [PT[:, 0, :].bitcast(BF16), PT[:, 1, :].bitcast(BF16)]

        Win = sb("Win", [128, 8, WCOLS], BF16); b_win = Buf("win")
        b_win_l = [Buf("win%d" % i_) for i_ in range(16)]
        Wout = sb("Wout", [128, 8, DM], BF16); b_wout = Buf("wout")
        b_wout_l = [Buf("wout%d" % i_) for i_ in range(8)]
        w1k = sb("w1k", [128, 16, 256], BF16); b_w1k = Buf()
        w1v = sb("w1v", [128, 16, 256], BF16); b_w1v = Buf()
        w2k = sb("w2k", [128, 2, 64], BF16); b_w2k = Buf()
        w2v = sb("w2v", [128, 2, 64], BF16); b_w2v = Buf()
        b1k = sb("b1k", [128, 2], F32); b_b1k = Buf()
        b1v = sb("b1v", [128, 2], F32); b_b1v = Buf()
        STG = [sb("stg0", [128, 2048], F32), sb("stg1", [128, 2048], F32)]
        b_stg = [[Buf("s00"), Buf("s01")], [Buf("s10"), Buf("s11")]]
        kTa = [sb("kTa0", [128, RA * 128], BF16), sb("kTa1", [128, RA * 128], BF16)]; b_kTa = [Buf() for _ in range(RA)]
        kTw = [sb("kTw0", [128, RW * 128], BF16), sb("kTw1", [128, RW * 128], BF16)]; b_kTw = [Buf() for _ in range(RW)]
        KE = [sb("KE0", [128, SEQ], BF16), sb("KE1", [128, SEQ], BF16)]
        b_kTs = [Buf() for _ in range(NT)]
        Va = sb("Va", [128, RA, 2, 66], BF16); b_Va = [Buf() for _ in range(RA)]
        Vw = sb("Vw", [128, RW, 2, 66], BF16); b_Vw = [Buf() for _ in range(RW)]
        Vs = sb("Vs", [128, NT, 2, 66], BF16); b_Vs = [Buf() for _ in range(NT)]
        XW = 529
        Xk0 = sb("Xk0", [128, 2, XW], BF16)
        Xv0 = sb("Xv0", [128, 2, XW], BF16)
        Xk = [Xk0, Xk0]
        Xv = [Xv0, Xv0]
        bxk, bxv = Buf(), Buf()
        b_Xk = [bxk, bxk]
        b_Xv = [bxv, bxv]
        kcT = [sb("kcT0", [128, 256], BF16), sb("kcT1", [128, 256], BF16)]; b_kcT = Buf("kcT")
        VC = sb("VC", [128, 2, 2, 128], BF16); b_VC = Buf("VC")
        BA = sb("BA", [128, 2, 8, 128], F32R); b_BA = Buf()
        BB = sb("BB", [128, 2, 8, 128], F32R); b_BB = Buf()
        b_E = Buf()
        CM = sb("CM", [128, 16, 128], BF16); b_CM = Buf()
        TRI = sb("TRI", [128, 128], BF16); b_TRI = Buf()
        PAT = sb("PAT", [128, 192], BF16); b_PAT = Buf()
        GATE = sb("GATE", [128, NB, DM], F32); b_GATE = Buf()
        Gcol = sb("Gcol", [128, 8, NB], F32); b_Gcol = Buf()
        SHcol = sb("SHcol", [128, 8, NB], F32); b_SHcol = Buf()
        idf = sb("idf", [128, 128], F32); b_idf = Buf()
        idb = sb("idb", [128, 128], BF16); b_idb = Buf()
        idr = sb("idr", [128, 128], F32R); b_idr = Buf()
        gcols = sb("gcols", [128, 6], F32); b_gcols = Buf()
        esink = sb("esink", [128, 8], F32); b_esink = Buf()
        bgate = sb("bgate", [128, 24], F32); b_bgate = Buf()
        c31 = sb("c31", [128, 8], F32); b_c31 = Buf()


        stat = sb("stat", [128, 4], F32); b_stat = Buf()
        hT0 = sb("hT0", [128, 8, 128], BF16)
        hT = [hT0, hT0]
        bht = Buf()
        b_hT = [bht, bht]

        ssq = sb("ssq", [128, 16], F32); b_ssq = Buf()
        qn = [sb("qn0", [128, 512], BF16), sb("qn1", [128, 512], BF16)]
        b_qn = [Buf(), Buf()]
        qTa = sb("qTa", [128, 2, 4, 128], BF16); b_qTa = [Buf() for _ in range(2)]
        sz = sb("sz", [128, 2, DM], BF16); b_sz = [Buf() for _ in range(2)]
        scb = KE[0][:, :].bitcast(F32).rearrange("p (k b m) -> p k b m", k=8, b=NB)
        gts = sb("gts", [128, 2, 24], F32); b_gts = [Buf() for _ in range(2)]
        gtmp = sb("gtmp", [128, 24], F32); b_gtmp = Buf()
        zt2 = sb("zt2", [128, 512], F32); b_zt2 = Buf()
        hsig = sb("hsig", [128, 64], F32); b_hsig = Buf()
        nb1k = sb("nb1k", [128, 2], F32)
        nb1v = sb("nb1v", [128, 2], F32); b_nb1 = Buf()
        hidk = sb("hidk", [128, 2, 16], BF16); b_hidk = Buf()
        hidv = sb("hidv", [128, 2, 16], BF16); b_hidv = Buf()
        kc32 = sb("kc32", [16, 4], F32); b_kc32 = Buf()
        kndup = sb("kndup", [16, 128], BF16); b_kndup = Buf()
        kjunk = sb("kjunk", [16, 64], F32); b_kjunk = Buf()
        vstag = sb("vstag", [16, 64], BF16); b_vstag = Buf()
        PTs = [sb("PTs%d" % i, [128, 512], BF16) for i in range(2)]
        b_PTs = [Buf() for _ in range(2)]
        den = sb("den", [128, 16], F32); b_den = Buf()
        coef = sb("coef", [128, 16], F32); b_coef = Buf()
        impA = sb("impA", [128, 64], F32); b_impA = Buf()
        impB = sb("impB", [128, 64], F32); b_impB = Buf()
        score = sb("score", [128, 64], F32); b_score = Buf()
        wk = sb("wk", [128, 64], F32); b_wk = Buf()
        m8 = sb("m8", [128, 16], F32); b_m8 = Buf()
        negmg = [sb("negm0", [128, 128], F32), sb("negm1", [128, 128], F32)]
        b_negm = [Buf(), Buf()]
        QNt = sb("QNw", [128, 2, 2, 512], BF16)
        b_QNw = [[Buf(), Buf()], [Buf(), Buf()]]
        yB = sb("yB", [128, 2, 256], F32); b_yB = [Buf(), Buf()]
        maskAB = yB[:, :, :].rearrange("p a (b c) -> p (a b) c", b=2); b_maskAB = b_yB[0]
        ytmp = [sb("ytmp0", [128, 256], F32), sb("ytmp1", [128, 256], F32)]
        b_ytmp = [Buf(), Buf()]
        y = sb("y", [128, DM], BF16); b_y = Buf()
        sq = sb("sq", [128, 512], F32); b_sq = Buf()
        junk = sq[:, :].bitcast(BF16); b_junk = b_sq
        yT = sb("yT", [128, 8, 128], BF16); b_yT = Buf()
        rtmp = sb("rtmp", [128, 512], F32); b_rtmp = Buf()
        xn = sb("xn", [128, DM], F32); b_xn = Buf()
        scol = sb("scol", [128, 8, NB], F32); b_scol = Buf()

        modc = sb("modc", [128, 16, NB], F32); b_modc = Buf()
        posk = sb("posk", [128, 16], F32); b_posk = Buf()
        posv = sb("posv", [128, 16], F32); b_posv = Buf()
        poskb = sb("poskb", [128, 16], BF16); b_poskb = Buf()
        posvb = sb("posvb", [128, 16], BF16); b_posvb = Buf()
        badac = sb("badac", [128, 16], F32); b_badac = Buf()
        ngc = sb("ngc", [128, 8], F32); b_ngc = Buf()
        badag = sb("badag", [128, DM], F32) if False else None

        xs = [STG[0][:, 0:1024], STG[0][:, 1024:2048]]
        b_xs = b_stg[0]
        xr = [STG[1][:, 0:1024], STG[1][:, 1024:2048]]
        b_xr = b_stg[1]

        def pbc(d_ap, n):
            return bass.AP(tensor=d_ap.tensor, offset=d_ap.offset, ap=[[0, 128], [1, n]])

        if dumps:
            for (t_, bl_) in ((KE[0], b_kTs), (Vs, b_Vs), (Va, b_Va), (Vw, b_Vw), (qTa, b_qTa),
                              (sz, b_sz), (gts, b_gts), (hT0, [bht]), (QNt, b_QNw[0] + b_QNw[1]), (y, [b_y]),
                              (hidk, [b_hidk]), (score, [b_score]), (yB, b_yB), (den, [b_den]), (coef, [b_coef])):
                shp_ = list(t_.shape)
                S.op("pool", C("memset", t_[tuple(slice(None) for _ in shp_)], 0.0), writes=list(bl_))
        def ld(dst_ap, src_ap, buf):
            S.dma(C("dma_start", out=dst_ap, in_=src_ap), writes=[buf])

        ld(CM[:], cmask_d, b_CM)
        ld(TRI[:], tri_d, b_TRI)
        ld(PAT[:], pat_d, b_PAT)
        ld(gcols[:], gcols_d, b_gcols)
        ld(esink[:], pbc(sinks_d, 8), b_esink)
        ld(bgate[:], pbc(bgate_d, 24), b_bgate)
        ld(c31[:], pbc(c31_d, 8), b_c31)

        ld(scol[:], cT_d, b_scol)
        ld(posk[:], posk_d, b_posk)
        ld(posv[:], posv_d, b_posv)
        ld(badac[:], badac_d, b_badac)
        ld(ngc[:], ngc_d, b_ngc)
        S.op("pool", C("memset", idf[:], 0.0), writes=[b_idf])
        S.op("pool", C("affine_select", out=idf[:], in_=idf[:], pattern=[[-1, 128]], compare_op=ALU.not_equal,
                                                fill=1.0, base=0, channel_multiplier=1), reads=[b_idf], writes=[b_idf])
        S.op("dve", C("tensor_copy", out=idb[:], in_=idf[:]), reads=[b_idf], writes=[b_idb])
        S.op("dve", C("tensor_copy", out=idr[:], in_=idf[:]), reads=[b_idf], writes=[b_idr])
        S.op("act", C("activation", out=esink[:], in_=esink[:], func=AF.Exp), reads=[b_esink], writes=[b_esink])
        S.op("dve", C("tensor_scalar", out=gcols[:, 0:1], in0=gcols[:, 0:1], scalar1=0.125, scalar2=None, op0=ALU.mult),
             reads=[b_gcols], writes=[b_gcols])
        S.op("dve", C("tensor_scalar", out=gcols[:, 2:3], in0=gcols[:, 2:3], scalar1=0.125, scalar2=None, op0=ALU.mult),
             reads=[b_gcols], writes=[b_gcols])
        S.op("dve", C("tensor_copy", out=poskb[:], in_=posk[:]), reads=[b_posk], writes=[b_poskb])
        S.op("dve", C("tensor_copy", out=posvb[:], in_=posv[:]), reads=[b_posv], writes=[b_posvb])
        S.op("pool", C("memset", score[:], 0.0), writes=[b_score])
        S.op("pool", C("memset", score[:, 0:1], BONUS), reads=[b_score], writes=[b_score])
        for g_ in range(2):
            S.op("pool", C("memset", kcT[g_][:], 0.0), writes=[b_kcT])
            S.op("pool", C("memset", kTa[g_][:], 0.0), writes=b_kTa)
            S.op("pool", C("memset", kTw[g_][:], 0.0), writes=b_kTw)
        S.op("pool", C("memset", QNt[:, :, :, :], 0.0), writes=b_QNw[0] + b_QNw[1])
        S.op("pool", C("memset", VC[:], 0.0), writes=[b_VC])
        S.op("pool", C("memset", VC[:, :, :, 64:65], 1.0), reads=[b_VC], writes=[b_VC])
        S.op("pool", C("memset", VC[0:1, 0, :, 64:65], 0.0), reads=[b_VC], writes=[b_VC])
        for g in range(2):
            S.dma(C("dma_start", out=VC[:, :, g, 66:128], in_=ov_d), writes=[b_VC])
        for (Vt_, bV_) in ((Va, b_Va), (Vw, b_Vw), (Vs, b_Vs)):
            S.op("pool", C("memset", Vt_[:, :, :, 64:65], 1.0), writes=bV_)
            S.op("pool", C("memset", Vt_[:, :, :, 65:66], 0.0), writes=bV_)
        for s_ in range(1):
            S.op("pool", C("memset", Xk[s_][:], 0.0), writes=[b_Xk[s_]])
            S.op("pool", C("memset", Xv[s_][:], 0.0), writes=[b_Xv[s_]])

        stg_i = [0]

        def stage_load(src_ap, ncols, view=None):
            s_ = stg_i[0] % 2
            stg_i[0] += 1
            dst = STG[s_][:, 0:ncols] if view is None else view(STG[s_])
            S.dma(C("dma_start", out=dst, in_=src_ap), writes=b_stg[s_])
            return s_

        def cast_from_stage(eng, s_, dst_ap, ncols, dst_buf, src_view=None):
            src = STG[s_][:, 0:ncols] if src_view is None else src_view(STG[s_])
            S.op(eng, C("copy" if eng == "act" else "tensor_copy", out=dst_ap, in_=src), reads=b_stg[s_], writes=[dst_buf])

        ADA = pro >= 2
        S.op("act", C("activation", out=scol[:], in_=scol[:], func=AF.Silu), reads=[b_scol], writes=[b_scol])
        S.op("dve", C("tensor_copy", out=scb, in_=V(scol[:, 0, 0:1], [[NB, 8], [1, NB], [0, 128]])),
             reads=[b_scol], writes=[b_E])
        pj_i = [0]

        def next_pj():
            i_ = pj_i[0] % 2
            pj_i[0] += 1
            return i_

        for cg in range(12 if ADA else 0):
            src = wada_d[:, cg * 256:(cg + 1) * 256].rearrange("(kc p) c -> p kc c", p=128)
            s_ = stage_load(src, 2048, view=lambda t: t[:, :].rearrange("p (kc c) -> p kc c", kc=8))
            stv = STG[s_][:, :].rearrange("p (kc c) -> p kc c", kc=8)
            if cg < 8:
                for cc in range(2):
                    ch = cg * 2 + cc
                    pj = next_pj()
                    for kc in range(8):
                        S.op("pe", C("matmul",
                            PJ[:, pj, 0:NB], lhsT=stv[:, kc, cc * 128:(cc + 1) * 128], rhs=scol[:, kc, :],
                            start=(kc == 0), stop=(kc == 7)),
                            reads=b_stg[s_] + [b_scol], writes=[b_pj[pj]], inc=(kc == 7))
                    S.op("dve", C("tensor_scalar",
                        out=modc[:, ch, :], in0=PJ[:, pj, 0:NB], scalar1=badac[:, ch:ch + 1], scalar2=None, op0=ALU.add),
                        reads=[b_pj[pj], b_badac], writes=[b_modc])
            else:
                for b in range(NB):
                    pj = next_pj()
                    for kc in range(8):
                        S.op("pe", C("matmul",
                            PJ[:, pj, 0:256], lhsT=scb[:, kc, b, :], rhs=stv[:, kc, :],
                            start=(kc == 0), stop=(kc == 7)),
                            reads=b_stg[s_] + [b_E], writes=[b_pj[pj]], inc=(kc == 7))
                    c0 = (cg - 8) * 256
                    S.op("dve", C("tensor_copy", out=GATE[:, b, c0:c0 + 256], in_=PJ[:, pj, 0:256]),
                         reads=[b_pj[pj]], writes=[b_GATE])
        s_ = stage_load(pbc(badag_d, DM), DM)
        for b in range(NB if ADA else 0):
            S.op("dve", C("tensor_tensor", out=GATE[:, b, :], in0=GATE[:, b, :], in1=STG[s_][:, 0:DM], op=ALU.add),
                 reads=b_stg[s_] + [b_GATE], writes=[b_GATE])
        S.op("dve", C("tensor_copy", out=SHcol[:], in_=modc[:, 0:8, :]), reads=[b_modc], writes=[b_SHcol])
        S.op("dve", C("tensor_scalar", out=Gcol[:], in0=modc[:, 8:16, :], scalar1=1.0, scalar2=None, op0=ALU.add),
             reads=[b_modc], writes=[b_Gcol])
        S.op("dve", C("tensor_tensor", out=Gcol[:], in0=Gcol[:], in1=V(ngc[:, 0:1], [[1, 8], [0, NB]]), op=ALU.mult),
             reads=[b_Gcol, b_ngc], writes=[b_Gcol])

        S.dma(C("dma_start", out=KE[0][64:128, :], in_=E_d[64:128, :]), writes=[b_E])
        S.dma(C("dma_start", out=KE[1][0:64, :], in_=E_d[0:64, :]), writes=[b_E])
        for kc in range(8 if pro >= 3 else 0):
            for hx, (c0, c1) in enumerate(((0, 1676), (1676, WCOLS))):
                S.dma(C("dma_start", out=Win[:, kc, c0:c1], in_=win_d[kc * 128:(kc + 1) * 128, c0:c1]),
                      writes=[b_win_l[kc * 2 + hx]], q="pool")
        if pro >= 3:
            for (wd, wsb, wb) in ((w1k_d, w1k, b_w1k), (w1v_d, w1v, b_w1v)):
                for hlf in range(2):
                    S.dma(C("dma_start", out=wsb[:, hlf * 8:(hlf + 1) * 8, :],
                            in_=wd[hlf * 1024:(hlf + 1) * 1024, :].rearrange("(lp p) j -> p lp j", p=128)), writes=[wb], q="pool")
            for (wd, wsb, wb) in ((w2k_d, w2k, b_w2k), (w2v_d, w2v, b_w2v)):
                S.dma(C("dma_start", out=wsb[:], in_=wd.rearrange("(jc p) d -> p jc d", p=128)), writes=[wb], q="pool")
            for kc in range(8):
                S.dma(C("dma_start", out=Wout[:, kc, :], in_=wout_d[kc * 128:(kc + 1) * 128, :]), writes=[b_wout_l[kc]], q="pool")
        S.dma(C("dma_start", out=maskAB, in_=maskAB_d), writes=b_yB)
        for (bd, Bt, bB, m0, isB) in ((biasA_d, BA, b_BA, 0, False), (biasB_d, BB, b_BB, 2, True)) if pro >= 4 else ():
            s_ = stage_load(bd, 2048, view=lambda t: t[:, :].rearrange("p (a h c) -> p a h c", a=2, h=8))
            stv = STG[s_][:, :].rearrange("p (a h c) -> p a h c", a=2, h=8)
            for ty in range(2):
                S.op("dve", C("tensor_tensor",
                    out=stv[:, ty], in0=stv[:, ty], in1=V(maskAB[:, m0 + ty, 0:1], [[0, 8], [1, 128]]), op=ALU.add),
                    reads=b_stg[s_] + b_yB, writes=b_stg[s_])
                if isB:
                    S.op("dve", C("tensor_tensor",
                        out=stv[:, ty], in0=stv[:, ty], in1=V(c31[:, 0:1], [[1, 8], [0, 128]]), op=ALU.subtract),
                        reads=b_stg[s_] + [b_c31], writes=b_stg[s_])
            S.op("dve", C("tensor_copy", out=Bt[:, :, :, :], in_=stv), reads=b_stg[s_], writes=[bB])
        for (wsb, wb, pb, bpb, b1, bb1) in ((w1k, b_w1k, poskb, b_poskb, b1k, b_b1k), (w1v, b_w1v, posvb, b_posvb, b1v, b_b1v)) if pro >= 5 else ():
            for jc in range(2):
                pj = next_pj()
                for lp in range(16):
                    S.op("pe", C("matmul",
                        PJ[:, pj, 0:2], lhsT=wsb[:, lp, jc * 128:(jc + 1) * 128], rhs=V(pb[:, lp:lp + 1], [[0, 2]]),
                        start=(lp == 0), stop=(lp == 15)), reads=[wb, bpb], writes=[b_pj[pj]], inc=(lp == 15))
                S.op("dve", C("tensor_copy", out=b1[:, jc:jc + 1], in_=PJ[:, pj, 0:1]),
                     reads=[b_pj[pj]], writes=[bb1])

        if pro >= 5:
            S.op("dve", C("tensor_scalar", out=nb1k[:], in0=b1k[:], scalar1=-1.0, scalar2=None, op0=ALU.mult), reads=[b_b1k], writes=[b_nb1])
            S.op("dve", C("tensor_scalar", out=nb1v[:], in0=b1v[:], scalar1=-1.0, scalar2=None, op0=ALU.mult), reads=[b_b1v], writes=[b_nb1])
        st_i = [0]
        pts_i = [0]

        def build_PC(b, i):
            u = i % 2
            xsl = i % 2
            ra, rw = i % RA, i % RW
            CPb = PT[:, 1, :]
            CPbb = PTb[1]
            b_cp = b_pt[1]
            chains = []

            def proj(c0, n):
                pj = next_pj()
                for kc in range(8):
                    S.op("pe", C("matmul", PJ[:, pj, 0:n], lhsT=hT0[:, kc, :], rhs=Win[:, kc, c0:c0 + n],
                                 start=(kc == 0), stop=(kc == 7)),
                         reads=[bht, b_win] + b_win_l, writes=[b_pj[pj]], inc=(kc == 7))
                return pj

            def L0():
                S.dma(C("dma_start", out=xs[xsl], in_=x_d[b, i * 128:(i + 1) * 128, :]), writes=[b_xs[xsl]])

            def L1():
                S.op("act", C("activation", out=junk, in_=xs[xsl], func=AF.Square, accum_out=stat[:, 0:1]),
                     reads=[b_xs[xsl]], writes=[b_junk, b_stat])

            def L2():
                S.op("act", C("activation", out=stat[:, 1:2], in_=stat[:, 0:1], func=AF.Ln, scale=1.0 / DM, bias=EPS),
                     reads=[b_stat], writes=[b_stat])
                S.op("act", C("activation", out=stat[:, 2:3], in_=stat[:, 1:2], func=AF.Exp, scale=-0.5),
                     reads=[b_stat], writes=[b_stat])

            def L3():
                S.op("pool", C("tensor_scalar", out=xn[:], in0=xs[xsl], scalar1=stat[:, 2:3], scalar2=1.0,
                               op0=ALU.mult, op1=ALU.mult), reads=[b_xs[xsl], b_stat], writes=[b_xn])

            def LT(h):
                def f():
                    for kc in range(4 * h, 4 * h + 4):
                        S.op("pe", C("transpose", out=PT[:, 0, (kc % 4) * 128:(kc % 4 + 1) * 128],
                                     in_=xn[:, kc * 128:(kc + 1) * 128], identity=idf[:]),
                             reads=[b_xn, b_idf], writes=[b_pt[0]])
                return f

            def LE(h):
                def f():
                    for kc in range(4 * h, 4 * h + 4):
                        S.op("dve", C("tensor_scalar", out=hT0[:, kc, :], in0=PT[:, 0, (kc % 4) * 128:(kc % 4 + 1) * 128],
                                      scalar1=Gcol[:, kc, b:b + 1], scalar2=SHcol[:, kc, b:b + 1], op0=ALU.mult, op1=ALU.add),
                             reads=[b_pt[0], b_Gcol, b_SHcol], writes=[bht])
                return f
            chains.append((0, [L1, L2, L3, LT(0), LE(0), LT(1), LE(1)]))

            def norm_chain(c0, nh, qs, nchunks, evac):
                n = nh * 64
                st = {}

                def s0():
                    st["pj"] = proj(c0, n)

                def s1():
                    S.op("act", C("activation", out=sq[:, 0:n], in_=PJ[:, st["pj"], 0:n], func=AF.Square),
                         reads=[b_pj[st["pj"]]], writes=[b_sq])

                def s2():
                    S.op("dve", C("tensor_reduce", out=ssq[:, 0:nh], in_=sq[:, 0:n].rearrange("p (h d) -> p h d", d=64),
                                  axis=AX.X, op=ALU.add), reads=[b_sq], writes=[b_ssq])

                def s3():
                    S.op("act", C("activation", out=ssq[:, 0:nh], in_=ssq[:, 0:nh], func=AF.Ln, scale=1.0 / 64, bias=EPS),
                         reads=[b_ssq], writes=[b_ssq])
                    S.op("act", C("activation", out=ssq[:, 0:nh], in_=ssq[:, 0:nh], func=AF.Exp, scale=-0.5),
                         reads=[b_ssq], writes=[b_ssq])

                def s4():
                    pj = st["pj"]
                    S.op("dve", C("tensor_tensor", out=qn[qs][:, 0:n].rearrange("p (h d) -> p h d", d=64),
                                  in0=PJ[:, pj, 0:n].rearrange("p (h d) -> p h d", d=64),
                                  in1=V(ssq[:, 0:1], [[1, nh], [0, 64]]), op=ALU.mult),
                         reads=[b_pj[pj], b_ssq], writes=[b_qn[qs]])

                def s5():
                    for c in range(nchunks):
                        S.op("pe", C("transpose", out=PTb[0][:, c * 128:(c + 1) * 128], in_=qn[qs][:, c * 128:(c + 1) * 128], identity=idb[:]),
                             reads=[b_qn[qs], b_idb], writes=[b_pt[0]])
                return [s0, s1, s2, s3, s4, s5, evac]

            def evac_qa():
                S.op("dve", C("tensor_scalar", out=qTa[:, u].rearrange("p r t -> p (r t)"), in0=PTb[0][:, 0:512],
                              scalar1=gcols[:, 0:1], scalar2=None, op0=ALU.mult),
                     reads=[b_pt[0], b_gcols], writes=[b_qTa[u]])

            def evac_qb():
                for g_ in range(2):
                    hp_ = slice(g_ * 64, (g_ + 1) * 64)
                    S.op("dve", C("tensor_scalar", out=QNt[hp_, u, g_, :], in0=PTb[0][hp_, 0:512],
                                  scalar1=gcols[hp_, 2:3], scalar2=None, op0=ALU.mult),
                         reads=[b_pt[0], b_gcols], writes=[b_QNw[u][g_]])

            def evac_k():
                for g_ in range(2):
                    hp_ = slice(g_ * 64, (g_ + 1) * 64)
                    S.op("dve", C("tensor_scalar", out=kTa[g_][hp_, ra * 128:(ra + 1) * 128], in0=PTb[0][hp_, 0:128],
                                  scalar1=gcols[hp_, 1:2], scalar2=None, op0=ALU.mult),
                         reads=[b_pt[0], b_gcols], writes=[b_kTa[ra]])
                    S.op("dve", C("tensor_scalar", out=kTw[g_][hp_, rw * 128:(rw + 1) * 128], in0=PTb[0][hp_, 256:384],
                                  scalar1=gcols[hp_, 5:6], scalar2=None, op0=ALU.mult),
                         reads=[b_pt[0], b_gcols], writes=[b_kTw[rw]])
                    S.op("dve", C("tensor_scalar", out=KE[g_][hp_, i * 128:(i + 1) * 128], in0=PTb[0][hp_, 128:256],
                                  scalar1=gcols[hp_, 4:5], scalar2=None, op0=ALU.mult),
                         reads=[b_pt[0], b_gcols], writes=[b_kTs[i]])

            stc = {}
            tau0 = (i % 4) * 128

            def c0_():
                stc["pj"] = proj(G_C, 512)

            def c1_():
                S.op("act", C("copy", out=qn[1][:], in_=PJ[:, stc["pj"], :]), reads=[b_pj[stc["pj"]]], writes=[b_qn[1]])

            def c2_():
                for c in range(4):
                    S.op("pe", C("transpose", out=PTb[0][:, c * 128:(c + 1) * 128], in_=qn[1][:, c * 128:(c + 1) * 128], identity=idb[:]),
                         reads=[b_qn[1], b_idb], writes=[b_pt[0]])

            def c3_():
                for (Xt, bX, cb) in ((Xk, b_Xk, 0), (Xv, b_Xv, 256)):
                    S.op("dve", C("tensor_copy", out=Xt[0][0:64, :, 17 + tau0:17 + tau0 + 128],
                                  in_=PTb[0][0:64, cb:cb + 256].rearrange("p (g t) -> p g t", g=2)),
                         reads=[b_pt[0]], writes=[bX[0]])
                    S.op("dve", C("tensor_copy", out=Xt[0][64:128, :, 16 + tau0:16 + tau0 + 128],
                                  in_=PTb[0][64:128, cb:cb + 256].rearrange("p (g t) -> p g t", g=2)),
                         reads=[b_pt[0]], writes=[bX[0]])
            chains.append((8, [c0_, c1_, c2_, c3_]))
            chains.append((10, norm_chain(G_K, 6, 0, 3, evac_k)))

            stv_ = {}

            def v0_():
                stv_["pj"] = proj(G_V, 408)

            def v1_():
                pj = stv_["pj"]
                S.op("dve", C("tensor_copy", out=Va[:, ra, :, 0:64], in_=PJ[:, pj, 0:128].rearrange("p (g d) -> p g d", g=2)),
                     reads=[b_pj[pj]], writes=[b_Va[ra]])
                S.op("dve", C("tensor_copy", out=Vs[:, i, :, 0:64], in_=PJ[:, pj, 128:256].rearrange("p (g d) -> p g d", g=2)),
                     reads=[b_pj[pj]], writes=[b_Vs[i]])
                S.op("dve", C("tensor_copy", out=Vw[:, rw, :, 0:64], in_=PJ[:, pj, 256:384].rearrange("p (g d) -> p g d", g=2)),
                     reads=[b_pj[pj]], writes=[b_Vw[rw]])
                S.op("dve", C("tensor_tensor", out=gtmp[:], in0=PJ[:, pj, 384:408], in1=bgate[:], op=ALU.add),
                     reads=[b_pj[pj], b_bgate], writes=[b_gtmp])

            def v2_():
                S.op("act", C("activation", out=gtmp[:], in_=gtmp[:], func=AF.Exp, scale=-1.0), reads=[b_gtmp], writes=[b_gtmp])

            def v3_():
                S.op("dve", C("tensor_scalar", out=gtmp[:], in0=gtmp[:], scalar1=1.0, scalar2=None, op0=ALU.add),
                     reads=[b_gtmp], writes=[b_gtmp])
                S.op("dve", C("reciprocal", out=gts[:, u, :], in_=gtmp[:]), reads=[b_gtmp], writes=[b_gts[u]])
            chains.append((12, [v0_, v1_, v2_, v3_]))

            c0x = 128 * (i % 4)

            def cm(part):
                (Xt, bX, w1, bw1) = ((Xk, b_Xk, w1k, b_w1k), (Xv, b_Xv, w1v, b_w1v))[part // 2]
                jc = part % 2

                def f():
                    for lp in range(16):
                        S.op("pe", C("matmul", CPb[:, part * 16:(part + 1) * 16].rearrange("p (g m) -> p g m", g=2),
                                     lhsT=w1[:, lp, jc * 128:(jc + 1) * 128],
                                     rhs=Xt[0][:, :, c0x + 1 + 2 * lp:c0x + 1 + 2 * lp + 16 * 7 + 1:16],
                                     start=(lp == 0), stop=(lp == 15), skip_group_check=True),
                             reads=[bX[0], bw1], writes=[b_cp], inc=(lp == 15))
                return f

            def cm_act():
                for part in range(4):
                    nb1 = (nb1k, nb1v)[part // 2]
                    S.op("act", C("activation", out=hsig[:, part * 16:(part + 1) * 16], in_=CPb[:, part * 16:(part + 1) * 16],
                                  func=AF.Exp, scale=-1.0, bias=nb1[:, part % 2:part % 2 + 1]),
                         reads=[b_cp, b_nb1], writes=[b_hsig])

            def cm_dve1():
                S.op("dve", C("tensor_scalar", out=hsig[:], in0=hsig[:], scalar1=1.0, scalar2=None, op0=ALU.add),
                     reads=[b_hsig], writes=[b_hsig])
                S.op("dve", C("reciprocal", out=hsig[:], in_=hsig[:]), reads=[b_hsig], writes=[b_hsig])

            def cm_dve2():
                for part in range(4):
                    hid, bh = ((hidk, b_hidk), (hidv, b_hidv))[part // 2]
                    b1 = (b1k, b1v)[part // 2]
                    S.op("dve", C("scalar_tensor_tensor", out=hid[:, part % 2, :], in0=CPb[:, part * 16:(part + 1) * 16],
                                  scalar=b1[:, part % 2:part % 2 + 1], in1=hsig[:, part * 16:(part + 1) * 16],
                                  op0=ALU.add, op1=ALU.mult),
                         reads=[b_cp, b_b1k, b_b1v, b_hsig], writes=[bh])

            def cm_mm2():
                for jc in range(2):
                    S.op("pe", C("matmul", CPb[0:16, 64:128], lhsT=hidk[:, jc, :], rhs=w2k[:, jc, :],
                                 start=(jc == 0), stop=(jc == 1), skip_group_check=True),
                         reads=[b_hidk, b_w2k], writes=[b_cp], inc=False)
                for jc in range(2):
                    S.op("pe", C("matmul", CPb[0:16, 128:192], lhsT=hidv[:, jc, :], rhs=w2v[:, jc, :],
                                 start=(jc == 0), stop=(jc == 1), skip_group_check=True),
                         reads=[b_hidv, b_w2v], writes=[b_cp], inc=(jc == 1))

            def cm_a2():
                S.op("act", C("activation", out=kjunk[:], in_=CPb[0:16, 64:128], func=AF.Square, accum_out=kc32[:, 0:1]),
                     reads=[b_cp], writes=[b_kjunk, b_kc32])
                S.op("dve", C("tensor_copy", out=vstag[:], in_=CPb[0:16, 128:192]), reads=[b_cp, b_kc32], writes=[b_vstag])

            def cm_a3():
                S.op("act", C("activation", out=kc32[:, 1:2], in_=kc32[:, 0:1], func=AF.Ln, scale=1.0 / 64, bias=EPS),
                     reads=[b_kc32], writes=[b_kc32])
                S.op("act", C("activation", out=kc32[:, 2:3], in_=kc32[:, 1:2], func=AF.Exp, scale=-0.5),
                     reads=[b_kc32], writes=[b_kc32])
                n0 = 8 * i
                T, p0 = n0 // 128, n0 % 128
                m0 = 1 if i == 0 else 0
                for g in range(2):
                    S.dma(C("dma_start", out=VC[p0 + m0:p0 + 8, T, g, 0:64], in_=vstag[g * 8 + m0:(g + 1) * 8, :]),
                          reads=[b_vstag], writes=[b_VC])

            def cm_d3():
                for h2 in range(2):
                    S.op("dve", C("tensor_scalar", out=kndup[:, h2 * 64:(h2 + 1) * 64], in0=CPb[0:16, 64:128],
                                  scalar1=kc32[:, 2:3], scalar2=None, op0=ALU.mult),
                         reads=[b_cp, b_kc32], writes=[b_kndup])

            def cm_t():
                S.op("pe", C("transpose", out=PTb[0][:, 0:16], in_=kndup[:], identity=idb[0:16, 0:16]),
                     reads=[b_kndup, b_idb], writes=[b_pt[0]])

            def cm_e():
                n0 = 8 * i
                for g in range(2):
                    S.op("dve", C("tensor_scalar", out=kcT[g][g * 64:(g + 1) * 64, n0:n0 + 8],
                                  in0=PTb[0][g * 64:(g + 1) * 64, g * 8:(g + 1) * 8],
                                  scalar1=gcols[g * 64:(g + 1) * 64, 3:4], scalar2=None, op0=ALU.mult),
                         reads=[b_pt[0], b_gcols], writes=[b_kcT])
            nop = lambda: None
            import os as _os
            chains.append((12, [cm(0), cm(1), cm(2), cm(3), cm_act, cm_dve1, cm_dve2, cm_mm2, cm_a2, cm_a3, cm_d3, nop, nop, nop, cm_t, cm_e][:int(_os.environ.get('NCM', '99'))]))
            chains.append((15, norm_chain(G_QA, 8, 1, 4, evac_qa)))
            chains.append((18, norm_chain(G_QB, 8, 0, 4, evac_qb)))

            def z_chain(c0, zbuf, bz, col0):
                st = {}

                def s0():
                    st["pj"] = proj(c0, 512)

                def s1():
                    S.op("act", C("activation", out=zbuf, in_=PJ[:, st["pj"], :], func=AF.Exp, scale=-1.0),
                         reads=[b_pj[st["pj"]]], writes=[bz])

                def s2():
                    S.op("act", C("activation", out=zbuf, in_=zbuf, func=AF.Ln, bias=1.0), reads=[bz], writes=[bz])

                def s3():
                    S.op("act", C("activation", out=zbuf, in_=zbuf, func=AF.Exp, scale=-1.0), reads=[bz], writes=[bz])

                def s4():
                    pj = st["pj"]
                    S.op("dve", C("tensor_tensor", out=sz[:, u, col0:col0 + 512], in0=PJ[:, pj, :], in1=zbuf, op=ALU.mult),
                         reads=[b_pj[pj], bz], writes=[b_sz[u]])
                return [s0, s1, s2, s3, s4]
            chains.append((21, z_chain(G_ZA, sq[:, :], b_sq, 0)))
            chains.append((24, z_chain(G_ZB, zt2[:, :], b_zt2, 512)))
            return chains

        NSTEP = 32
        ntile = 4 * n_sb

        def x_load(b, i):
            S.dma(C("dma_start", out=xs[i % 2], in_=x_d[b, i * 128:(i + 1) * 128, :]), writes=[b_xs[i % 2]])

        def gen_steps(b, pc, tl):
            chains = []
            if pc is not None:
                if pc % 4 == 0 and pc > 0:
                    for (Xt, bX) in ((Xk, b_Xk), (Xv, b_Xv)):
                        S.op("pool", C("tensor_copy", out=Xt[0][:, :, 0:17], in_=Xt[0][:, :, 512:529]), reads=[bX[0]], writes=[bX[0]])
                chains += build_PC(b, pc)[:nch]
                if pc + 1 < ntile:
                    chains.append((10, [lambda: x_load(b, pc + 1)]))
            if tl is not None:
                chains.append(tail_chain(b, tl))
            T_ = max(o + len(ch) for o, ch in chains)
            assert T_ <= NSTEP, T_
            for tau in range(NSTEP):
                for o, ch in chains:
                    k = tau - o
                    if 0 <= k < len(ch):
                        ch[k]()
                yield

        def score_tile(lhsT_ap, rhs_ap, extra, reads):
            st = st_i[0] % 2
            st_i[0] += 1
            n = len(extra)
            S.op("pe", C("matmul", ST[:, st, :], lhsT=lhsT_ap, rhs=rhs_ap, start=True, stop=(n == 0)),
                 reads=reads, writes=[b_st[st]], inc=(n == 0))
            for j, (l_ap, r_ap, rd) in enumerate(extra):
                S.op("pe", C("matmul", ST[:, st, :], lhsT=l_ap, rhs=r_ap, start=False, stop=(j == n - 1)),
                     reads=rd, writes=[b_st[st]], inc=(j == n - 1))
            return st

        def exp_tile(st):
            p = pts_i[0] % 2
            pts_i[0] += 1
            S.op("act", C("activation", out=PTs[p][:], in_=ST[:, st, :], func=AF.Exp), reads=[b_st[st]], writes=[b_PTs[p]])
            return p

        def bc4(ap2d):
            return V(ap2d, [[0, 4], [1, 128]])

        def phaseA(b, i, u, nxt):
            gsl = gts[:, u, :]
            jobs = []

            def Bx(Bt, ty, g):
                return [(idr[:], Bt[:, ty, 4 * g:4 * g + 4, :].rearrange("p h t -> p (h t)"), [b_idr, b_BA if Bt is BA else b_BB])]

            def add_branch(g, kts, lhs_fn, rhs_fn, extras_fn, V_t, V_bufs, vidx, oab, post):
                nk = len(kts)
                for j, kt in enumerate(kts):
                    def fscore(kt=kt):
                        l_ap, l_rd = lhs_fn(kt)
                        r_ap, r_rd = rhs_fn(kt)
                        return score_tile(l_ap, r_ap, extras_fn(kt), l_rd + r_rd)

                    def pv(p, kt=kt, j=j):
                        for r in range(4):
                            S.op("pe", C("matmul", OA[:, oab, r * 66:(r + 1) * 66], lhsT=PTs[p][:, r * 128:(r + 1) * 128],
                                         rhs=V_t[:, vidx(kt), g, :], start=(j == 0 and r == 0), stop=(j == nk - 1), skip_group_check=True),
                                 reads=[b_PTs[p], V_bufs(kt)], writes=[b_oa[oab]], inc=(j == nk - 1 and r == 3))
                    jobs.append((fscore, pv, post if j == nk - 1 else None))

            def accumulate(g, oab, dcol, gate_br):
                S.op("dve", C("reciprocal", out=den[:, dcol:dcol + 4], in_=V(OA[:, oab, 64:65], [[66, 4]])),
                     reads=[b_oa[oab]], writes=[b_den])
                S.op("dve", C("tensor_tensor", out=coef[:, dcol:dcol + 4], in0=den[:, dcol:dcol + 4],
                              in1=gsl[:, g * 12 + gate_br:(g + 1) * 12:3], op=ALU.mult),
                     reads=[b_den, b_gts[u]], writes=[b_coef])
                S.op("dve", C("tensor_tensor", out=ytmp[oab][:, :].rearrange("p (r d) -> p r d", r=4),
                              in0=V(OA[:, oab, 0:1], [[66, 4], [1, 64]]),
                              in1=V(coef[:, dcol:dcol + 1], [[1, 4], [0, 64]]), op=ALU.mult),
                     reads=[b_oa[oab], b_coef], writes=[b_ytmp[oab]])
                S.op("dve", C("tensor_tensor", out=yB[:, g, :], in0=yB[:, g, :], in1=ytmp[oab][:, :], op=ALU.add),
                     reads=[b_yB[g], b_ytmp[oab]], writes=[b_yB[g]])

            Tmax = i // 16
            post_b_fns = {}
            for g in range(2):
                hp = slice(g * 64, (g + 1) * 64)
                qb_ap = QNt[hp, u, g, :]
                for T in range(Tmax + 1):
                    def fscore(T=T, hp=hp, qb_ap=qb_ap, g=g):
                        extra = []
                        if T == Tmax:
                            extra.append((idb[:], bc4(CM[:, i % 16, :]), [b_idb, b_CM]))
                        return score_tile(kcT[g][:, T * 128:(T + 1) * 128], QNt[:, u, g, :], extra, [b_kcT, b_QNw[u][g]])

                    def pv(p, T=T, g=g):
                        for r in range(4):
                            S.op("pe", C("matmul", OA[:, g, r * 128:(r + 1) * 128], lhsT=PTs[p][:, r * 128:(r + 1) * 128],
                                         rhs=VC[:, T, g, :], start=(T == 0 and r == 0), stop=(T == Tmax), skip_group_check=True),
                                 reads=[b_PTs[p], b_VC], writes=[b_oa[g]], inc=(T == Tmax and r == 3))

                    def post(st, g=g, hp=hp):
                        S.op("dve", C("tensor_scalar", out=den[:, 0:4], in0=V(OA[:, g, 64:65], [[128, 4]]),
                                      scalar1=1e-30, scalar2=None, op0=ALU.max),
                             reads=[b_oa[g]], writes=[b_den])
                        S.op("dve", C("reciprocal", out=den[:, 0:4], in_=den[:, 0:4]), reads=[b_den], writes=[b_den])
                        S.op("dve", C("tensor_tensor", out=coef[:, 0:4], in0=den[:, 0:4],
                                      in1=gsl[:, g * 12:(g + 1) * 12:3], op=ALU.mult),
                             reads=[b_den, b_gts[u]], writes=[b_coef])
                        pat_ap = PAT[:, 64 - 2 * i + 64:128 - 2 * i + 64]
                        for r in range(4):
                            dst, bdst = (impA, b_impA) if r % 2 == 0 else (impB, b_impB)
                            in0 = OA[:, g, r * 128 + 66:r * 128 + 128]
                            if r == 1:
                                S.op("dve", C("tensor_scalar", out=dst[:, 1:63], in0=in0, scalar1=den[:, r:r + 1], scalar2=None, op0=ALU.mult),
                                     reads=[b_oa[g], b_den], writes=[bdst])
                            else:
                                in1, rd1 = (pat_ap[:, 1:63], b_PAT) if r == 0 else (dst[:, 1:63], bdst)
                                S.op("dve", C("scalar_tensor_tensor", out=dst[:, 1:63], in0=in0, scalar=den[:, r:r + 1],
                                              in1=in1, op0=ALU.mult, op1=ALU.add),
                                     reads=[b_oa[g], b_den, rd1], writes=[bdst])
                        S.op("dve", C("tensor_tensor", out=yB[:, g, :].rearrange("p (r d) -> p r d", r=4),
                                      in0=V(OA[:, g, 0:1], [[128, 4], [1, 64]]), in1=V(coef[:, 0:1], [[1, 4], [0, 64]]), op=ALU.mult),
                             reads=[b_oa[g], b_coef], writes=[b_yB[g]])
                        S.op("dve", C("tensor_tensor", out=score[:, 1:63], in0=impA[:, 1:63], in1=impB[:, 1:63], op=ALU.add),
                             reads=[b_impA, b_impB], writes=[b_score])
                        if i == NT - 1:
                            S.op("dve", C("tensor_copy", out=score[:, 63:64], in_=pat_ap[:, 63:64]), reads=[b_PAT, b_score], writes=[b_score])
                        S.op("dve", C("max", out=m8[:, 0:8], in_=score[:]), reads=[b_score], writes=[b_m8])
                        S.op("dve", C("match_replace", out=wk[:], in_to_replace=m8[:, 0:8], in_values=score[:], imm_value=-1e30),
                             reads=[b_score, b_m8], writes=[b_wk])
                        S.op("dve", C("max", out=m8[:, 8:16], in_=wk[:]), reads=[b_wk], writes=[b_m8])
                        S.op("dve", C("tensor_scalar", out=negmg[g][:, :].rearrange("p (a j) -> p a j", a=2), in0=V(score[:, 0:1], [[0, 2], [1, 64]]),
                                      scalar1=m8[:, 15:16], scalar2=NEG,
                                      op0=ALU.is_lt, op1=ALU.mult), reads=[b_score, b_m8], writes=[b_negm[g]])

                    def post_b(g=g):
                        S.op("pe", C("transpose", out=OA[:, g, 384:512], in_=negmg[g][:], identity=idf[:]),
                             reads=[b_negm[g], b_idf], writes=[b_oa[g]])
                        op_ = slice((1 - g) * 64, (2 - g) * 64)
                        S.op("dve", C("tensor_copy", out=QNt[op_, u, g, :].rearrange("p (r t) -> p r t", r=4), in_=bc4(OA[op_, g, 384:512])),
                             reads=[b_oa[g]], writes=[b_QNw[u][g]])
                    if T == Tmax:
                        post_b_fns[g] = post_b
                    jobs.append((fscore, pv, post if T == Tmax else None))

            for g in range(2):
                hp = slice(g * 64, (g + 1) * 64)
                qa_ap = qTa[hp, u].rearrange("p r t -> p (r t)")

                def a_post(st, g=g):
                    oab = g
                    post_b_fns[g]()
                    S.op("dve", C("tensor_tensor", out=den[:, 12:16], in0=V(OA[:, oab, 64:65], [[66, 4]]),
                                  in1=esink[:, 4 * g:4 * g + 4], op=ALU.add),
                         reads=[b_oa[oab], b_esink], writes=[b_den])
                    S.op("dve", C("reciprocal", out=den[:, 12:16], in_=den[:, 12:16]), reads=[b_den], writes=[b_den])
                    S.op("dve", C("tensor_tensor", out=ytmp[oab][:, :].rearrange("p (r d) -> p r d", r=4),
                                  in0=V(OA[:, oab, 0:1], [[66, 4], [1, 64]]),
                                  in1=V(den[:, 12:13], [[1, 4], [0, 64]]), op=ALU.mult),
                         reads=[b_oa[oab], b_den], writes=[b_ytmp[oab]])
                    S.op("pool", C("tensor_tensor", out=y[:, g * 256:(g + 1) * 256], in0=ytmp[oab][:, :],
                                   in1=sz[:, u, g * 256:(g + 1) * 256], op=ALU.mult),
                         reads=[b_ytmp[oab], b_sz[u]], writes=[b_y])
                add_branch(g, list(range(max(0, i - 1), i + 1)),
                           lambda kt, g=g: (kTa[g][:, (kt % RA) * 128:(kt % RA) * 128 + 128], [b_kTa[kt % RA]]),
                           lambda kt: (qTa[:, u].rearrange("p r t -> p (r t)"), [b_qTa[u]]),
                           lambda kt, g=g: Bx(BA, 0, g) if kt == i else Bx(BA, 1, g),
                           Va, lambda kt: b_Va[kt % RA], lambda kt: kt % RA, g, a_post)
            for g in range(2):
                hp = slice(g * 64, (g + 1) * 64)
                qb_ap = QNt[hp, u, g, :]

                def win_extras(kt, g=g):
                    dk = i - kt
                    if dk == 0:
                        return Bx(BB, 0, g)
                    if dk == 1:
                        return Bx(BB, 1, g)
                    if dk == 4:
                        return [(idb[:], bc4(TRI[:, :]), [b_idb, b_TRI])]
                    return []
                add_branch(g, list(range(max(0, i - 4), i + 1)),
                           lambda kt, g=g: (kTw[g][:, (kt % RW) * 128:(kt % RW) * 128 + 128], [b_kTw[kt % RW]]),
                           lambda kt, g=g: (QNt[:, u, g, :], [b_QNw[u][g]]),
                           win_extras, Vw, lambda kt: b_Vw[kt % RW], lambda kt: kt % RW, g,
                           lambda st, g=g: accumulate(g, g, 8, 2))
            for g in range(2):
                hp = slice(g * 64, (g + 1) * 64)
                qb_ap = QNt[hp, u, g, :]

                def sel_lhs(kt, g=g, hp=hp):
                    return (KE[g][:, kt * 128:(kt + 1) * 128], [b_kTs[kt], b_E])

                def sel_rhs(kt, g=g, qb_ap=qb_ap):
                    return (QNt[:, u, g, :], [b_QNw[u][g]])

                def sel_extras(kt, g=g):
                    if kt == i:
                        return Bx(BB, 0, g)
                    if kt == i - 1:
                        return Bx(BB, 1, g)
                    return []

                def sel_post(st, g=g):
                    accumulate(g, g, 4, 1)
                    S.op("dve", C("tensor_tensor", out=y[:, 512 + g * 256:512 + (g + 1) * 256], in0=yB[:, g, :],
                                  in1=sz[:, u, 512 + g * 256:512 + (g + 1) * 256], op=ALU.mult),
                         reads=[b_yB[g], b_sz[u]], writes=[b_y])
                add_branch(g, list(range(0, i + 1)), sel_lhs, sel_rhs, sel_extras, Vs, lambda kt: b_Vs[kt], lambda kt: kt, g, sel_post)

            pend = None
            nj = len(jobs)
            sdone = 0
            ncmp = 2 * (Tmax + 1)
            for jx, (fscore, pv, post) in enumerate(jobs):
                st = fscore()
                want = ((jx + 1) * NSTEP) // nj
                if jx + 1 >= ncmp + 1:
                    want = max(2, want)
                while sdone < want:
                    next(nxt, None)
                    sdone += 1
                if pend is not None:
                    pst, ppv, ppost = pend
                    p = exp_tile(pst)
                    ppv(p)
                    if ppost is not None:
                        ppost(pst)
                pend = (st, pv, post)
            pst, ppv, ppost = pend
            p = exp_tile(pst)
            ppv(p)
            if ppost is not None:
                ppost(pst)

        def tail_chain(b, i):
            xrs = i % 2

            def t0():
                S.dma(C("dma_start", out=xr[xrs], in_=x_d[b, i * 128:(i + 1) * 128, :]), writes=[b_xr[xrs]])
                for kc in range(8):
                    S.op("pe", C("transpose", out=PTb[1][:, kc * 128:(kc + 1) * 128], in_=y[:, kc * 128:(kc + 1) * 128],
                                 identity=idb[:]), reads=[b_y, b_idb], writes=[b_pt[1]])

            def t1():
                S.op("act", C("copy", out=yT[:, :, :].rearrange("p k t -> p (k t)"), in_=PTb[1][:, :]), reads=[b_pt[1]], writes=[b_yT])
            stt = {}

            def mm(hf):
                def f():
                    pj = next_pj()
                    stt[hf] = pj
                    for kc in range(8):
                        S.op("pe", C("matmul", PJ[:, pj, :], lhsT=yT[:, kc, :], rhs=Wout[:, kc, hf * 512:(hf + 1) * 512],
                                     start=(kc == 0), stop=(kc == 7)),
                             reads=[b_yT, b_wout] + b_wout_l, writes=[b_pj[pj]], inc=(kc == 7))
                return f

            def ml(hf):
                def f():
                    pj = stt[hf]
                    S.op("dve", C("tensor_tensor", out=rtmp[:, :], in0=PJ[:, pj, :],
                                  in1=GATE[:, b, hf * 512:(hf + 1) * 512], op=ALU.mult),
                         reads=[b_pj[pj], b_GATE], writes=[b_rtmp])
                return f

            def t6(hf):
                def f():
                    S.op("dve", C("tensor_tensor", out=xr[xrs][:, hf * 512:(hf + 1) * 512], in0=xr[xrs][:, hf * 512:(hf + 1) * 512],
                                  in1=rtmp[:, :], op=ALU.add),
                         reads=[b_xr[xrs], b_rtmp], writes=[b_xr[xrs]])
                return f

            def t7():
                S.dma(C("dma_start", out=out_d[b, i * 128:(i + 1) * 128, :], in_=xr[xrs]), reads=[b_xr[xrs]], is_output=True)
            return (0, [t0, t1, mm(0), ml(0), t6(0), mm(1), ml(1), t6(1), t7])


        def drain(gen):
            for _ in gen:
                pass

        for b in range(nb):
            if b > 0:
                for g_ in range(2):
                    S.op("pool", C("memset", kcT[g_][:], 0.0), writes=[b_kcT])
                S.op("pool", C("memset", VC[:, :, :, 0:64], 0.0), writes=[b_VC])
                S.op("pool", C("memset", Xk[0][:, :, 0:17], 0.0), writes=[b_Xk[0]])
                S.op("pool", C("memset", Xv[0][:, :, 0:17], 0.0), writes=[b_Xv[0]])
                S.op("pool", C("memset", score[:, 63:64], 0.0), reads=[b_score], writes=[b_score])
            x_load(b, 0)
            drain(gen_steps(b, 0, None))
            for i in range(ntile):
                pc_ = i + 1 if i + 1 < ntile else None
                tl_ = i - 1 if i >= 1 else None
                nxt = gen_steps(b, pc_, tl_) if (pc_ is not None or tl_ is not None) else iter(())
                if "A" in stages and i < n_a:
                    phaseA(b, i, i % 2, nxt)
                drain(nxt)
            drain(gen_steps(b, None, ntile - 1))
        if dumps:
            L = dict(GATE=(GATE, [b_GATE]), Gcol=(Gcol, [b_Gcol]), SHcol=(SHcol, [b_SHcol]), BA=(BA, [b_BA]), BB=(BB, [b_BB]),
                     b1k=(b1k, [b_b1k]), b1v=(b1v, [b_b1v]), Win=(Win, [b_win]), Wout=(Wout, [b_wout]), w1k=(w1k, [b_w1k]),
                     qTa=(qTa, b_qTa), kTs=(KE[0], b_kTs),
                     Vs=(Vs, b_Vs), Va=(Va, b_Va), Vw=(Vw, b_Vw), VC=(VC, [b_VC]), sz=(sz, b_sz), gts=(gts, b_gts),
                      y=(y, [b_y]), hT=(hT0, [bht]),
                     Xk=(Xk0, [bxk]), hidk=(hidk, [b_hidk]), score=(score, [b_score]), yB=(yB, b_yB), esink=(esink, [b_esink]),
                     den=(den, [b_den]), coef=(coef, [b_coef]))
            for nm in dumps:
                t, bufs = L[nm]
                shp = list(t.shape)
                dd = nc.dram_tensor("d_" + nm, shp, t.dtype, kind="ExternalOutput").ap()
                full = t[tuple(slice(None) for _ in shp)]
                S.dma(C("dma_start", out=dd, in_=full), reads=list(bufs), is_output=True)
        S.emit()
    return nc


def _t5_bucket(dist):
    n = np.maximum(dist, 0)
    nf = np.maximum(n, 1).astype(np.float32)
    large = 16 + (np.log(nf / np.float32(16)) / np.float32(np.log(128 / 16)) * np.float32(16)).astype(np.int32)
    large = np.minimum(large, 31)
    return np.where(n < 16, n, large)


def _constants():
    bf = ml_dtypes.bfloat16
    sl = np.arange(128)[:, None]
    tl = np.arange(128)[None, :]
    d_diag = tl - sl
    d_prev = 128 + tl - sl
    idx = np.stack([_t5_bucket(d_diag), _t5_bucket(d_prev)], 0)
    maskAB = np.zeros((128, 4, 128), np.float32)
    maskAB[:, 0, :] = np.where(d_diag >= 0, 0.0, NEG)
    maskAB[:, 1, :] = np.where(d_prev < 128, 0.0, NEG)
    maskAB[:, 2, :] = np.where(d_diag >= 0, 0.0, NEG)
    maskAB[:, 3, :] = 0.0
    E = (np.arange(SEQ)[None, :] // 64 == (np.arange(128) % 64)[:, None]).astype(np.float32).astype(bf)
    nl = np.arange(128)[:, None, None]
    o = np.arange(16)[None, :, None]
    t3 = np.arange(128)[None, None, :]
    cmask = np.where(16 * nl + 15 <= 128 * o + t3, 0.0, NEG).astype(np.float32).astype(bf)
    tri = np.where(sl > tl, 0.0, NEG).astype(np.float32).astype(bf)
    pat = np.zeros((128, 192), np.float32)
    pat[:, 0] = BONUS
    hi = (np.arange(128) >= 64).astype(np.int64)
    pat[np.arange(128), 64 + 63 + hi] = BONUS
    pat[np.arange(128), 64 + 64 + hi] = BONUS
    pat = pat.astype(bf)
    npr = np.arange(256)
    n = npr - 1
    c_lo = 16 * n
    s_lo = 64 * np.arange(64)
    ov = np.clip(np.minimum(c_lo[:, None] + 32, s_lo[None, :] + 64) - np.maximum(c_lo[:, None], s_lo[None, :]), 0, None) / 32.0
    ov[0, :] = 0.0
    ov = np.ascontiguousarray(ov.reshape(2, 128, 64).transpose(1, 0, 2)[:, :, 1:63]).astype(np.float32).astype(bf)
    return idx, maskAB, E, cmask, tri, pat, ov


def _perm_cols():
    o_qa, o_ka, o_va, o_za, o_qb, o_kc, o_vc, o_ks, o_vs, o_kw, o_vw, o_zb, o_gb = (
        0, 512, 640, 768, 1280, 1792, 1920, 2048, 2176, 2304, 2432, 2560, 3072)
    r64 = np.arange(64)
    cols = []
    for base in (o_qa, o_qb):
        for r in range(4):
            cols += [base + r * 64 + r64, base + (4 + r) * 64 + r64]
    cols += [o_ka + np.arange(128), o_ks + np.arange(128), o_kw + np.arange(128)]
    for base in (o_kc, o_vc):
        for g in range(2):
            cols += [base + g * 64 + r64, base + g * 64 + r64]
    cols += [o_va + np.arange(128), o_vs + np.arange(128), o_vw + np.arange(128), o_gb + np.arange(24)]
    cols += [o_za + np.arange(512), o_zb + np.arange(512)]
    cols = np.concatenate(cols)
    assert cols.shape[0] == WCOLS
    return cols


_NC_CACHE = {}


def kernel(x, c, w_ada, b_ada, norm_gain, w_in, b_nsa_gate, q_gain_a, k_gain_a, sinks,
           q_gain_b, k_gain_cmp, k_gain_sel, k_gain_win, cmp_pos_k, cmp_pos_v,
           w_cmp_k1, w_cmp_k2, w_cmp_v1, w_cmp_v2, w_out, rel_bias):
    f = lambda a: np.ascontiguousarray(np.asarray(a, dtype=np.float32))
    x = f(x); c = f(c); w_ada = f(w_ada)[0]; b_ada = f(b_ada)[0]; norm_gain = f(norm_gain)[0]
    w_in = f(w_in)[0]; b_nsa_gate = f(b_nsa_gate)[0]; sinks = f(sinks)[0]
    rel_bias = f(rel_bias); w_out = f(w_out)[0]
    idx, maskAB, E, cmask, tri, pat, ov = _constants()
    biasAB = rel_bias[idx]
    biasA = np.ascontiguousarray(biasAB[..., 0:8].transpose(1, 0, 3, 2))
    biasB = np.ascontiguousarray(biasAB[..., 8:16].transpose(1, 0, 3, 2))
    c31 = np.ascontiguousarray(rel_bias[31:32, 8:16])
    gcols = np.stack([np.tile(f(g)[0], 2) for g in (q_gain_a, k_gain_a, q_gain_b, k_gain_cmp, k_gain_sel, k_gain_win)], 1)
    shared = {
        "w_ada": w_ada,
        "bada_col": np.ascontiguousarray(b_ada[0:2048].reshape(16, 128).T),
        "bada_gate": np.ascontiguousarray(b_ada[2048:3072].reshape(1, DM)),
        "ng_col": np.ascontiguousarray(norm_gain.reshape(8, 128).T),
        "w_in_p": np.ascontiguousarray(w_in[:, _perm_cols()]),
        "bgate": b_nsa_gate.reshape(1, 24),
        "gcols": np.ascontiguousarray(gcols),
        "sinks": sinks.reshape(1, 8),
        "posk": np.ascontiguousarray(f(cmp_pos_k)[0].reshape(16, 128).T),
        "posv": np.ascontiguousarray(f(cmp_pos_v)[0].reshape(16, 128).T),
        "w1k": f(w_cmp_k1)[0], "w1v": f(w_cmp_v1)[0], "w2k": f(w_cmp_k2)[0], "w2v": f(w_cmp_v2)[0],
        "w_out": w_out, "biasA": biasA, "biasB": biasB, "c31": c31, "maskAB": maskAB,
        "Emat": E, "cmask": cmask, "tri": tri, "pat": pat, "ov": ov,
    }
    in_maps = []
    for core in range(NCORES):
        m = dict(shared)
        m["x"] = x[NB * core:NB * (core + 1)]
        cc = c[NB * core:NB * (core + 1)]
        m["cT"] = np.ascontiguousarray(cc.reshape(NB, 8, 128).transpose(2, 1, 0))
        in_maps.append(m)
    if "nc" not in _NC_CACHE:
        _NC_CACHE["nc"] = build_program()
    res = run_bass_kernel_spmd(_NC_CACHE["nc"], in_maps, core_ids=list(range(NCORES)))
    return np.concatenate([np.asarray(r["out"], dtype=np.float32) for r in res.results], axis=0)
```

```python
import numpy as np
import ml_dtypes
from contextlib import ExitStack
import concourse.bass as bass
import concourse.mybir as mybir
from concourse.bass_utils import run_bass_kernel_spmd

F32 = mybir.dt.float32
BF16 = mybir.dt.bfloat16
F32R = mybir.dt.float32r
AF = mybir.ActivationFunctionType
ALU = mybir.AluOpType
AX = mybir.AxisListType

NCORES = 8
SEQ = 4096
DM = 1024
NT = SEQ // 128
NB = 2
WCOLS = 3352
NEG = -30000.0
BONUS = 1.0e4
EPS = 1e-6
G_QA, G_QB, G_K, G_C, G_V, G_ZA, G_ZB = 0, 512, 1024, 1408, 1920, 2328, 2840
RA, RW = 3, 6


class Buf:
    __slots__ = ("w", "r", "dsem", "dcount", "name")

    def __init__(self, name=""):
        self.w = None
        self.r = {}
        self.dsem = None
        self.dcount = 0
        self.name = name


class Sched:
    ENG = ("pe", "act", "dve", "pool", "sp")

    def __init__(self, nc, same_engine_sync=True):
        self.nc = nc
        self.ops = {e: [] for e in self.ENG}
        self.count = {e: 0 for e in self.ENG}
        self.waited = {e: {} for e in self.ENG}
        self.sems = {}
        self.dma_sems = []
        self.same_engine_sync = same_engine_sync
        self.out_waits = []

    def _need(self, eng, dep, needs):
        if dep is None:
            return
        k, v = dep
        if k == eng:
            if eng in ("pe", "sp"):
                return
            if not self.same_engine_sync:
                return
        if self.waited[eng].get(k, 0) >= v:
            return
        if needs.get(k, 0) < v:
            needs[k] = v

    def _emit_waits(self, eng, needs):
        for k, v in needs.items():
            self.ops[eng].append(("w", k, v))
            self.waited[eng][k] = v

    def op(self, eng, fn, reads=(), writes=(), inc=True):
        needs = {}
        for b in reads:
            self._need(eng, b.w, needs)
        for b in writes:
            self._need(eng, b.w, needs)
            for k, v in b.r.items():
                self._need(eng, (k, v), needs)
        self._emit_waits(eng, needs)
        if inc:
            self.count[eng] += 1
            val = self.count[eng]
        else:
            val = self.count[eng] + 1
        self.ops[eng].append(("op", fn, [(eng, 1)] if inc else []))
        for b in writes:
            b.w = (eng, val)
            b.r = {}
        for b in reads:
            if b.r.get(eng, 0) < val:
                b.r[eng] = val
        return val

    def dma(self, fn, reads=(), writes=(), q="sp", is_output=False):
        needs = {}
        for b in reads:
            self._need(q, b.w, needs)
        for b in writes:
            self._need(q, b.w, needs)
            for k, v in b.r.items():
                self._need(q, (k, v), needs)
        self._emit_waits(q, needs)
        owner = writes[0] if writes else reads[0]
        if owner.dsem is None:
            owner.dsem = "dma%d" % len(self.dma_sems)
            self.dma_sems.append(owner.dsem)
        owner.dcount += 16
        k, v = owner.dsem, owner.dcount
        self.ops[q].append(("op", fn, [(k, 16)]))
        for b in writes:
            b.w = (k, v)
            b.r = {}
        for b in reads:
            b.r[k] = v
        if is_output:
            self.out_waits.append((k, v))

    def emit(self):
        nc = self.nc
        with ExitStack() as es:
            for e in self.ENG:
                self.sems[e] = es.enter_context(nc.semaphore("prog_" + e))
            for k in self.dma_sems:
                self.sems[k] = es.enter_context(nc.semaphore(k))
            fin = {}
            for k, v in self.out_waits:
                fin[k] = max(fin.get(k, 0), v)
            for k, v in fin.items():
                self.ops["sp"].append(("w", k, v))
            block = es.enter_context(nc.Block())
            sems = self.sems
            ops = self.ops

            FUSE = ("activation", "copy", "tensor_tensor", "tensor_copy", "reciprocal", "tensor_reduce", "memset")

            def run(engine, lst, fuse=False):
                pend = []
                for item in lst:
                    if item[0] == "w":
                        pend.append(item)
                        continue
                    name, args, kw = item[1]
                    can = fuse and pend and name in FUSE and kw.get("accum_out") is None
                    for w in (pend[:-1] if can else pend):
                        engine.wait_ge(sems[w[1]], w[2])
                    ins = getattr(engine, name)(*args, **kw)
                    if can:
                        ins._wait_ge(sems[pend[-1][1]], pend[-1][2])
                    pend = []
                    for (k, n) in item[2]:
                        ins.then_inc(sems[k], n)
                for w in pend:
                    engine.wait_ge(sems[w[1]], w[2])

            @block.sync
            def _(e):
                run(e, ops["sp"])

            @block.tensor
            def _(e):
                run(e, ops["pe"])

            @block.scalar
            def _(e):
                run(e, ops["act"], fuse=True)

            @block.vector
            def _(e):
                run(e, ops["dve"], fuse=True)

            @block.gpsimd
            def _(e):
                run(e, ops["pool"])


def C(name, *args, **kw):
    return (name, args, kw)


def V(ap, dims):
    return bass.AP(tensor=ap.tensor, offset=ap.offset, ap=[list(ap.ap[0])] + [list(d) for d in dims])


def build_program(nb=NB, n_sb=NT // 4, stages="PCA", n_a=999, dumps=None, pro=9, nch=99):
    nc = bass.Bass("TRN2", target_bir_lowering=False)
    S = Sched(nc)

    def din(name, shape, dt=F32):
        return nc.dram_tensor(name, list(shape), dt, kind="ExternalInput").ap()

    x_d = din("x", [NB, SEQ, DM])
    cT_d = din("cT", [128, 8, NB])
    wada_d = din("w_ada", [DM, 3 * DM])
    badac_d = din("bada_col", [128, 16])
    badag_d = din("bada_gate", [1, DM])
    ngc_d = din("ng_col", [128, 8])
    win_d = din("w_in_p", [DM, WCOLS])
    bgate_d = din("bgate", [1, 24])
    gcols_d = din("gcols", [128, 6])
    sinks_d = din("sinks", [1, 8])
    posk_d = din("posk", [128, 16])
    posv_d = din("posv", [128, 16])
    w1k_d = din("w1k", [2048, 256])
    w1v_d = din("w1v", [2048, 256])
    w2k_d = din("w2k", [256, 64])
    w2v_d = din("w2v", [256, 64])
    wout_d = din("w_out", [DM, DM])
    biasA_d = din("biasA", [128, 2, 8, 128])
    biasB_d = din("biasB", [128, 2, 8, 128])
    c31_d = din("c31", [1, 8])
    maskAB_d = din("maskAB", [128, 4, 128])
    E_d = din("Emat", [128, SEQ], BF16)
    cmask_d = din("cmask", [128, 16, 128], BF16)
    tri_d = din("tri", [128, 128], BF16)
    pat_d = din("pat", [128, 192], BF16)
    ov_d = din("ov", [128, 2, 62], BF16)
    out_d = nc.dram_tensor("out", [NB, SEQ, DM], F32, kind="ExternalOutput").ap()

    with ExitStack() as es:
        def sb(name, shape, dt):
            return es.enter_context(nc.sbuf_tensor("s_" + name, list(shape), dt))

        def ps(name, shape, dt):
            return es.enter_context(nc.psum_tensor("p_" + name, list(shape), dt))

        PJ = ps("PJ", [128, 2, 512], F32)
        PT = ps("PT", [128, 2, 512], F32)
        ST = ps("ST", [128, 2, 512], F32)
        OA = ps("OA", [128, 2, 512], F32)
        b_pj = [Buf("pj0"), Buf("pj1")]
        b_pt = [Buf("pt0"), Buf("pt1")]
        b_st = [Buf("st0"), Buf("st1")]
        b_oa = [Buf("oa0"), Buf("oa1")]
        PTb = [PT[:, 0, :].bitcast(BF16), PT[:, 1, :].bitcast(BF16)]

        Win = sb("Win", [128, 8, WCOLS], BF16); b_win = Buf("win")
        b_win_l = [Buf("win%d" % i_) for i_ in range(16)]
        Wout = sb("Wout", [128, 8, DM], BF16); b_wout = Buf("wout")
        w1k = sb("w1k", [128, 16, 256], BF16); b_w1k = Buf()
        w1v = sb("w1v", [128, 16, 256], BF16); b_w1v = Buf()
        w2k = sb("w2k", [128, 2, 64], BF16); b_w2k = Buf()
        w2v = sb("w2v", [128, 2, 64], BF16); b_w2v = Buf()
        b1k = sb("b1k", [128, 2], F32); b_b1k = Buf()
        b1v = sb("b1v", [128, 2], F32); b_b1v = Buf()
        STG = [sb("stg0", [128, 2048], F32), sb("stg1", [128, 2048], F32)]
        b_stg = [[Buf("s00"), Buf("s01")], [Buf("s10"), Buf("s11")]]
        kTa = [sb("kTa0", [128, RA * 128], BF16), sb("kTa1", [128, RA * 128], BF16)]; b_kTa = [Buf() for _ in range(RA)]
        kTw = [sb("kTw0", [128, RW * 128], BF16), sb("kTw1", [128, RW * 128], BF16)]; b_kTw = [Buf() for _ in range(RW)]
        KE = [sb("KE0", [128, SEQ], BF16), sb("KE1", [128, SEQ], BF16)]
        b_kTs = [Buf() for _ in range(NT)]
        Va = sb("Va", [128, RA, 2, 66], BF16); b_Va = [Buf() for _ in range(RA)]
        Vw = sb("Vw", [128, RW, 2, 66], BF16); b_Vw = [Buf() for _ in range(RW)]
        Vs = sb("Vs", [128, NT, 2, 66], BF16); b_Vs = [Buf() for _ in range(NT)]
        XW = 529
        Xk0 = sb("Xk0", [128, 2, XW], BF16)
        Xv0 = sb("Xv0", [128, 2, XW], BF16)
        Xk = [Xk0, Xk0]
        Xv = [Xv0, Xv0]
        bxk, bxv = Buf(), Buf()
        b_Xk = [bxk, bxk]
        b_Xv = [bxv, bxv]
        kcT = [sb("kcT0", [128, 256], BF16), sb("kcT1", [128, 256], BF16)]; b_kcT = Buf("kcT")
        VC = sb("VC", [128, 2, 2, 128], BF16); b_VC = Buf("VC")
        BA = sb("BA", [128, 2, 8, 128], F32R); b_BA = Buf()
        BB = sb("BB", [128, 2, 8, 128], F32R); b_BB = Buf()
        b_E = Buf()
        CM = sb("CM", [128, 16, 128], BF16); b_CM = Buf()
        TRI = sb("TRI", [128, 128], BF16); b_TRI = Buf()
        PAT = sb("PAT", [128, 192], BF16); b_PAT = Buf()
        GATE = sb("GATE", [128, NB, DM], F32); b_GATE = Buf()
        Gcol = sb("Gcol", [128, 8, NB], F32); b_Gcol = Buf()
        SHcol = sb("SHcol", [128, 8, NB], F32); b_SHcol = Buf()
        idf = sb("idf", [128, 128], F32); b_idf = Buf()
        idb = sb("idb", [128, 128], BF16); b_idb = Buf()
        idr = sb("idr", [128, 128], F32R); b_idr = Buf()
        gcols = sb("gcols", [128, 6], F32); b_gcols = Buf()
        esink = sb("esink", [128, 8], F32); b_esink = Buf()
        bgate = sb("bgate", [128, 24], F32); b_bgate = Buf()
        c31 = sb("c31", [128, 8], F32); b_c31 = Buf()


        stat = sb("stat", [128, 4], F32); b_stat = Buf()
        hT0 = sb("hT0", [128, 8, 128], BF16)
        hT = [hT0, hT0]
        bht = Buf()
        b_hT = [bht, bht]

        ssq = sb("ssq", [128, 16], F32); b_ssq = Buf()
        qn = [sb("qn0", [128, 512], BF16), sb("qn1", [128, 512], BF16)]
        b_qn = [Buf(), Buf()]
        qTa = sb("qTa", [128, 2, 4, 128], BF16); b_qTa = [Buf() for _ in range(2)]
        sz = sb("sz", [128, 2, DM], BF16); b_sz = [Buf() for _ in range(2)]
        scb = KE[0][:, :].bitcast(F32).rearrange("p (k b m) -> p k b m", k=8, b=NB)
        gts = sb("gts", [128, 2, 24], F32); b_gts = [Buf() for _ in range(2)]
        gtmp = sb("gtmp", [128, 24], F32); b_gtmp = Buf()
        zt2 = sb("zt2", [128, 512], F32); b_zt2 = Buf()
        hsig = sb("hsig", [128, 64], F32); b_hsig = Buf()
        nb1k = sb("nb1k", [128, 2], F32)
        nb1v = sb("nb1v", [128, 2], F32); b_nb1 = Buf()
        hidk = sb("hidk", [128, 2, 16], BF16); b_hidk = Buf()
        hidv = sb("hidv", [128, 2, 16], BF16); b_hidv = Buf()
        kc32 = sb("kc32", [16, 4], F32); b_kc32 = Buf()
        kndup = sb("kndup", [16, 128], BF16); b_kndup = Buf()
        kjunk = sb("kjunk", [16, 64], F32); b_kjunk = Buf()
        vstag = sb("vstag", [16, 64], BF16); b_vstag = Buf()
        PTs = [sb("PTs%d" % i, [128, 512], BF16) for i in range(2)]
        b_PTs = [Buf() for _ in range(2)]
        den = sb("den", [128, 16], F32); b_den = Buf()
        coef = sb("coef", [128, 16], F32); b_coef = Buf()
        impA = sb("impA", [128, 64], F32); b_impA = Buf()
        impB = sb("impB", [128, 64], F32); b_impB = Buf()
        score = sb("score", [128, 64], F32); b_score = Buf()
        wk = sb("wk", [128, 64], F32); b_wk = Buf()
        m8 = sb("m8", [128, 16], F32); b_m8 = Buf()
        negmg = [sb("negm0", [128, 128], F32), sb("negm1", [128, 128], F32)]
        b_negm = [Buf(), Buf()]
        QNt = sb("QNw", [128, 2, 2, 512], BF16)
        b_QNw = [[Buf(), Buf()], [Buf(), Buf()]]
        yB = sb("yB", [128, 2, 256], F32); b_yB = [Buf(), Buf()]
        maskAB = yB[:, :, :].rearrange("p a (b c) -> p (a b) c", b=2); b_maskAB = b_yB[0]
        ytmp = [sb("ytmp0", [128, 256], F32), sb("ytmp1", [128, 256], F32)]
        b_ytmp = [Buf(), Buf()]
        y = sb("y", [128, DM], BF16); b_y = Buf()
        sq = sb("sq", [128, 512], F32); b_sq = Buf()
        junk = sq[:, :].bitcast(BF16); b_junk = b_sq
        yT = sb("yT", [128, 8, 128], BF16); b_yT = Buf()
        rtmp = sb("rtmp", [128, 512], F32); b_rtmp = Buf()
        xn = sb("xn", [128, DM], F32); b_xn = Buf()
        scol = sb("scol", [128, 8, NB], F32); b_scol = Buf()

        modc = sb("modc", [128, 16, NB], F32); b_modc = Buf()
        posk = sb("posk", [128, 16], F32); b_posk = Buf()
        posv = sb("posv", [128, 16], F32); b_posv = Buf()
        poskb = sb("poskb", [128, 16], BF16); b_poskb = Buf()
        posvb = sb("posvb", [128, 16], BF16); b_posvb = Buf()
        badac = sb("badac", [128, 16], F32); b_badac = Buf()
        ngc = sb("ngc", [128, 8], F32); b_ngc = Buf()
        badag = sb("badag", [128, DM], F32) if False else None

        xs = [STG[0][:, 0:1024], STG[0][:, 1024:2048]]
        b_xs = b_stg[0]
        xr = [STG[1][:, 0:1024], STG[1][:, 1024:2048]]
        b_xr = b_stg[1]

        def pbc(d_ap, n):
            return bass.AP(tensor=d_ap.tensor, offset=d_ap.offset, ap=[[0, 128], [1, n]])

        if dumps:
            for (t_, bl_) in ((KE[0], b_kTs), (Vs, b_Vs), (Va, b_Va), (Vw, b_Vw), (qTa, b_qTa),
                              (sz, b_sz), (gts, b_gts), (hT0, [bht]), (QNt, b_QNw[0] + b_QNw[1]), (y, [b_y]),
                              (hidk, [b_hidk]), (score, [b_score]), (yB, b_yB), (den, [b_den]), (coef, [b_coef])):
                shp_ = list(t_.shape)
                S.op("pool", C("memset", t_[tuple(slice(None) for _ in shp_)], 0.0), writes=list(bl_))
        def ld(dst_ap, src_ap, buf):
            S.dma(C("dma_start", out=dst_ap, in_=src_ap), writes=[buf])

        ld(CM[:], cmask_d, b_CM)
        ld(TRI[:], tri_d, b_TRI)
        ld(PAT[:], pat_d, b_PAT)
        ld(gcols[:], gcols_d, b_gcols)
        ld(esink[:], pbc(sinks_d, 8), b_esink)
        ld(bgate[:], pbc(bgate_d, 24), b_bgate)
        ld(c31[:], pbc(c31_d, 8), b_c31)

        ld(scol[:], cT_d, b_scol)
        ld(posk[:], posk_d, b_posk)
        ld(posv[:], posv_d, b_posv)
        ld(badac[:], badac_d, b_badac)
        ld(ngc[:], ngc_d, b_ngc)
        S.op("pool", C("memset", idf[:], 0.0), writes=[b_idf])
        S.op("pool", C("affine_select", out=idf[:], in_=idf[:], pattern=[[-1, 128]], compare_op=ALU.not_equal,
                                                fill=1.0, base=0, channel_multiplier=1), reads=[b_idf], writes=[b_idf])
        S.op("dve", C("tensor_copy", out=idb[:], in_=idf[:]), reads=[b_idf], writes=[b_idb])
        S.op("dve", C("tensor_copy", out=idr[:], in_=idf[:]), reads=[b_idf], writes=[b_idr])
        S.op("act", C("activation", out=esink[:], in_=esink[:], func=AF.Exp), reads=[b_esink], writes=[b_esink])
        S.op("dve", C("tensor_scalar", out=gcols[:, 0:1], in0=gcols[:, 0:1], scalar1=0.125, scalar2=None, op0=ALU.mult),
             reads=[b_gcols], writes=[b_gcols])
        S.op("dve", C("tensor_scalar", out=gcols[:, 2:3], in0=gcols[:, 2:3], scalar1=0.125, scalar2=None, op0=ALU.mult),
             reads=[b_gcols], writes=[b_gcols])
        S.op("dve", C("tensor_copy", out=poskb[:], in_=posk[:]), reads=[b_posk], writes=[b_poskb])
        S.op("dve", C("tensor_copy", out=posvb[:], in_=posv[:]), reads=[b_posv], writes=[b_posvb])
        S.op("pool", C("memset", score[:], 0.0), writes=[b_score])
        S.op("pool", C("memset", score[:, 0:1], BONUS), reads=[b_score], writes=[b_score])
        for g_ in range(2):
            S.op("pool", C("memset", kcT[g_][:], 0.0), writes=[b_kcT])
            S.op("pool", C("memset", kTa[g_][:], 0.0), writes=b_kTa)
            S.op("pool", C("memset", kTw[g_][:], 0.0), writes=b_kTw)
        S.op("pool", C("memset", QNt[:, :, :, :], 0.0), writes=b_QNw[0] + b_QNw[1])
        S.op("pool", C("memset", VC[:], 0.0), writes=[b_VC])
        S.op("pool", C("memset", VC[:, :, :, 64:65], 1.0), reads=[b_VC], writes=[b_VC])
        S.op("pool", C("memset", VC[0:1, 0, :, 64:65], 0.0), reads=[b_VC], writes=[b_VC])
        for g in range(2):
            S.dma(C("dma_start", out=VC[:, :, g, 66:128], in_=ov_d), writes=[b_VC])
        for (Vt_, bV_) in ((Va, b_Va), (Vw, b_Vw), (Vs, b_Vs)):
            S.op("pool", C("memset", Vt_[:, :, :, 64:65], 1.0), writes=bV_)
            S.op("pool", C("memset", Vt_[:, :, :, 65:66], 0.0), writes=bV_)
        for s_ in range(1):
            S.op("pool", C("memset", Xk[s_][:], 0.0), writes=[b_Xk[s_]])
            S.op("pool", C("memset", Xv[s_][:], 0.0), writes=[b_Xv[s_]])

        stg_i = [0]

        def stage_load(src_ap, ncols, view=None):
            s_ = stg_i[0] % 2
            stg_i[0] += 1
            dst = STG[s_][:, 0:ncols] if view is None else view(STG[s_])
            S.dma(C("dma_start", out=dst, in_=src_ap), writes=b_stg[s_])
            return s_

        def cast_from_stage(eng, s_, dst_ap, ncols, dst_buf, src_view=None):
            src = STG[s_][:, 0:ncols] if src_view is None else src_view(STG[s_])
            S.op(eng, C("copy" if eng == "act" else "tensor_copy", out=dst_ap, in_=src), reads=b_stg[s_], writes=[dst_buf])

        ADA = pro >= 2
        S.op("act", C("activation", out=scol[:], in_=scol[:], func=AF.Silu), reads=[b_scol], writes=[b_scol])
        S.op("dve", C("tensor_copy", out=scb, in_=V(scol[:, 0, 0:1], [[NB, 8], [1, NB], [0, 128]])),
             reads=[b_scol], writes=[b_E])
        pj_i = [0]

        def next_pj():
            i_ = pj_i[0] % 2
            pj_i[0] += 1
            return i_

        for cg in range(12 if ADA else 0):
            src = wada_d[:, cg * 256:(cg + 1) * 256].rearrange("(kc p) c -> p kc c", p=128)
            s_ = stage_load(src, 2048, view=lambda t: t[:, :].rearrange("p (kc c) -> p kc c", kc=8))
            stv = STG[s_][:, :].rearrange("p (kc c) -> p kc c", kc=8)
            if cg < 8:
                for cc in range(2):
                    ch = cg * 2 + cc
                    pj = next_pj()
                    for kc in range(8):
                        S.op("pe", C("matmul",
                            PJ[:, pj, 0:NB], lhsT=stv[:, kc, cc * 128:(cc + 1) * 128], rhs=scol[:, kc, :],
                            start=(kc == 0), stop=(kc == 7)),
                            reads=b_stg[s_] + [b_scol], writes=[b_pj[pj]], inc=(kc == 7))
                    S.op("dve", C("tensor_scalar",
                        out=modc[:, ch, :], in0=PJ[:, pj, 0:NB], scalar1=badac[:, ch:ch + 1], scalar2=None, op0=ALU.add),
                        reads=[b_pj[pj], b_badac], writes=[b_modc])
            else:
                for b in range(NB):
                    pj = next_pj()
                    for kc in range(8):
                        S.op("pe", C("matmul",
                            PJ[:, pj, 0:256], lhsT=scb[:, kc, b, :], rhs=stv[:, kc, :],
                            start=(kc == 0), stop=(kc == 7)),
                            reads=b_stg[s_] + [b_E], writes=[b_pj[pj]], inc=(kc == 7))
                    c0 = (cg - 8) * 256
                    S.op("dve", C("tensor_copy", out=GATE[:, b, c0:c0 + 256], in_=PJ[:, pj, 0:256]),
                         reads=[b_pj[pj]], writes=[b_GATE])
        s_ = stage_load(pbc(badag_d, DM), DM)
        for b in range(NB if ADA else 0):
            S.op("dve", C("tensor_tensor", out=GATE[:, b, :], in0=GATE[:, b, :], in1=STG[s_][:, 0:DM], op=ALU.add),
                 reads=b_stg[s_] + [b_GATE], writes=[b_GATE])
        S.op("dve", C("tensor_copy", out=SHcol[:], in_=modc[:, 0:8, :]), reads=[b_modc], writes=[b_SHcol])
        S.op("dve", C("tensor_scalar", out=Gcol[:], in0=modc[:, 8:16, :], scalar1=1.0, scalar2=None, op0=ALU.add),
             reads=[b_modc], writes=[b_Gcol])
        S.op("dve", C("tensor_tensor", out=Gcol[:], in0=Gcol[:], in1=V(ngc[:, 0:1], [[1, 8], [0, NB]]), op=ALU.mult),
             reads=[b_Gcol, b_ngc], writes=[b_Gcol])

        S.dma(C("dma_start", out=KE[0][64:128, :], in_=E_d[64:128, :]), writes=[b_E])
        S.dma(C("dma_start", out=KE[1][0:64, :], in_=E_d[0:64, :]), writes=[b_E])
        for kc in range(8 if pro >= 3 else 0):
            for hx, (c0, c1) in enumerate(((0, 1676), (1676, WCOLS))):
                S.dma(C("dma_start", out=Win[:, kc, c0:c1], in_=win_d[kc * 128:(kc + 1) * 128, c0:c1]),
                      writes=[b_win_l[kc * 2 + hx]], q="pool")
        for kc in range(0, 8 if pro >= 3 else 0, 2):
            src = wout_d[kc * 128:(kc + 2) * 128, :].rearrange("(a p) c -> p a c", p=128)
            s_ = stage_load(src, 2048, view=lambda t: t[:, :].rearrange("p (a c) -> p a c", a=2))
            cast_from_stage("act", s_, Wout[:, kc:kc + 2, :], 2048, b_wout,
                            src_view=lambda t: t[:, :].rearrange("p (a c) -> p a c", a=2))
        for (wd, wsb, wb) in ((w1k_d, w1k, b_w1k), (w1v_d, w1v, b_w1v)):
            for hlf in range(2 if pro >= 3 else 0):
                src = wd[hlf * 1024:(hlf + 1) * 1024, :].rearrange("(lp p) j -> p lp j", p=128)
                s_ = stage_load(src, 2048, view=lambda t: t[:, :].rearrange("p (lp j) -> p lp j", lp=8))
                cast_from_stage("dve", s_, wsb[:, hlf * 8:(hlf + 1) * 8, :], 2048, wb,
                                src_view=lambda t: t[:, :].rearrange("p (lp j) -> p lp j", lp=8))
        for (wd, wsb, wb) in ((w2k_d, w2k, b_w2k), (w2v_d, w2v, b_w2v)) if pro >= 3 else ():
            src = wd.rearrange("(jc p) d -> p jc d", p=128)
            s_ = stage_load(src, 128, view=lambda t: t[:, 0:128].rearrange("p (jc d) -> p jc d", jc=2))
            cast_from_stage("dve", s_, wsb[:], 128, wb, src_view=lambda t: t[:, 0:128].rearrange("p (jc d) -> p jc d", jc=2))
        S.dma(C("dma_start", out=maskAB, in_=maskAB_d), writes=b_yB)
        for (bd, Bt, bB, m0, isB) in ((biasA_d, BA, b_BA, 0, False), (biasB_d, BB, b_BB, 2, True)) if pro >= 4 else ():
            s_ = stage_load(bd, 2048, view=lambda t: t[:, :].rearrange("p (a h c) -> p a h c", a=2, h=8))
            stv = STG[s_][:, :].rearrange("p (a h c) -> p a h c", a=2, h=8)
            for ty in range(2):
                S.op("dve", C("tensor_tensor",
                    out=stv[:, ty], in0=stv[:, ty], in1=V(maskAB[:, m0 + ty, 0:1], [[0, 8], [1, 128]]), op=ALU.add),
                    reads=b_stg[s_] + b_yB, writes=b_stg[s_])
                if isB:
                    S.op("dve", C("tensor_tensor",
                        out=stv[:, ty], in0=stv[:, ty], in1=V(c31[:, 0:1], [[1, 8], [0, 128]]), op=ALU.subtract),
                        reads=b_stg[s_] + [b_c31], writes=b_stg[s_])
            S.op("dve", C("tensor_copy", out=Bt[:, :, :, :], in_=stv), reads=b_stg[s_], writes=[bB])
        for (wsb, wb, pb, bpb, b1, bb1) in ((w1k, b_w1k, poskb, b_poskb, b1k, b_b1k), (w1v, b_w1v, posvb, b_posvb, b1v, b_b1v)) if pro >= 5 else ():
            for jc in range(2):
                pj = next_pj()
                for lp in range(16):
                    S.op("pe", C("matmul",
                        PJ[:, pj, 0:2], lhsT=wsb[:, lp, jc * 128:(jc + 1) * 128], rhs=V(pb[:, lp:lp + 1], [[0, 2]]),
                        start=(lp == 0), stop=(lp == 15)), reads=[wb, bpb], writes=[b_pj[pj]], inc=(lp == 15))
                S.op("dve", C("tensor_copy", out=b1[:, jc:jc + 1], in_=PJ[:, pj, 0:1]),
                     reads=[b_pj[pj]], writes=[bb1])

        if pro >= 5:
            S.op("dve", C("tensor_scalar", out=nb1k[:], in0=b1k[:], scalar1=-1.0, scalar2=None, op0=ALU.mult), reads=[b_b1k], writes=[b_nb1])
            S.op("dve", C("tensor_scalar", out=nb1v[:], in0=b1v[:], scalar1=-1.0, scalar2=None, op0=ALU.mult), reads=[b_b1v], writes=[b_nb1])
        st_i = [0]
        pts_i = [0]

        def build_PC(b, i):
            u = i % 2
            xsl = i % 2
            ra, rw = i % RA, i % RW
            CPb = PT[:, 1, :]
            CPbb = PTb[1]
            b_cp = b_pt[1]
            chains = []

            def proj(c0, n):
                pj = next_pj()
                for kc in range(8):
                    S.op("pe", C("matmul", PJ[:, pj, 0:n], lhsT=hT0[:, kc, :], rhs=Win[:, kc, c0:c0 + n],
                                 start=(kc == 0), stop=(kc == 7)),
                         reads=[bht, b_win] + b_win_l, writes=[b_pj[pj]], inc=(kc == 7))
                return pj

            def L0():
                S.dma(C("dma_start", out=xs[xsl], in_=x_d[b, i * 128:(i + 1) * 128, :]), writes=[b_xs[xsl]])

            def L1():
                S.op("act", C("activation", out=junk, in_=xs[xsl], func=AF.Square, accum_out=stat[:, 0:1]),
                     reads=[b_xs[xsl]], writes=[b_junk, b_stat])

            def L2():
                S.op("act", C("activation", out=stat[:, 1:2], in_=stat[:, 0:1], func=AF.Ln, scale=1.0 / DM, bias=EPS),
                     reads=[b_stat], writes=[b_stat])
                S.op("act", C("activation", out=stat[:, 2:3], in_=stat[:, 1:2], func=AF.Exp, scale=-0.5),
                     reads=[b_stat], writes=[b_stat])

            def L3():
                S.op("pool", C("tensor_scalar", out=xn[:], in0=xs[xsl], scalar1=stat[:, 2:3], scalar2=1.0,
                               op0=ALU.mult, op1=ALU.mult), reads=[b_xs[xsl], b_stat], writes=[b_xn])

            def LT(h):
                def f():
                    for kc in range(4 * h, 4 * h + 4):
                        S.op("pe", C("transpose", out=PT[:, 0, (kc % 4) * 128:(kc % 4 + 1) * 128],
                                     in_=xn[:, kc * 128:(kc + 1) * 128], identity=idf[:]),
                             reads=[b_xn, b_idf], writes=[b_pt[0]])
                return f

            def LE(h):
                def f():
                    for kc in range(4 * h, 4 * h + 4):
                        S.op("dve", C("tensor_scalar", out=hT0[:, kc, :], in0=PT[:, 0, (kc % 4) * 128:(kc % 4 + 1) * 128],
                                      scalar1=Gcol[:, kc, b:b + 1], scalar2=SHcol[:, kc, b:b + 1], op0=ALU.mult, op1=ALU.add),
                             reads=[b_pt[0], b_Gcol, b_SHcol], writes=[bht])
                return f
            chains.append((0, [L1, L2, L3, LT(0), LE(0), LT(1), LE(1)]))

            def norm_chain(c0, nh, qs, nchunks, evac):
                n = nh * 64
                st = {}

                def s0():
                    st["pj"] = proj(c0, n)

                def s1():
                    S.op("act", C("activation", out=sq[:, 0:n], in_=PJ[:, st["pj"], 0:n], func=AF.Square),
                         reads=[b_pj[st["pj"]]], writes=[b_sq])

                def s2():
                    S.op("dve", C("tensor_reduce", out=ssq[:, 0:nh], in_=sq[:, 0:n].rearrange("p (h d) -> p h d", d=64),
                                  axis=AX.X, op=ALU.add), reads=[b_sq], writes=[b_ssq])

                def s3():
                    S.op("act", C("activation", out=ssq[:, 0:nh], in_=ssq[:, 0:nh], func=AF.Ln, scale=1.0 / 64, bias=EPS),
                         reads=[b_ssq], writes=[b_ssq])
                    S.op("act", C("activation", out=ssq[:, 0:nh], in_=ssq[:, 0:nh], func=AF.Exp, scale=-0.5),
                         reads=[b_ssq], writes=[b_ssq])

                def s4():
                    pj = st["pj"]
                    S.op("dve", C("tensor_tensor", out=qn[qs][:, 0:n].rearrange("p (h d) -> p h d", d=64),
                                  in0=PJ[:, pj, 0:n].rearrange("p (h d) -> p h d", d=64),
                                  in1=V(ssq[:, 0:1], [[1, nh], [0, 64]]), op=ALU.mult),
                         reads=[b_pj[pj], b_ssq], writes=[b_qn[qs]])

                def s5():
                    for c in range(nchunks):
                        S.op("pe", C("transpose", out=PTb[0][:, c * 128:(c + 1) * 128], in_=qn[qs][:, c * 128:(c + 1) * 128], identity=idb[:]),
                             reads=[b_qn[qs], b_idb], writes=[b_pt[0]])
                return [s0, s1, s2, s3, s4, s5, evac]

            def evac_qa():
                S.op("dve", C("tensor_scalar", out=qTa[:, u].rearrange("p r t -> p (r t)"), in0=PTb[0][:, 0:512],
                              scalar1=gcols[:, 0:1], scalar2=None, op0=ALU.mult),
                     reads=[b_pt[0], b_gcols], writes=[b_qTa[u]])

            def evac_qb():
                for g_ in range(2):
                    hp_ = slice(g_ * 64, (g_ + 1) * 64)
                    S.op("dve", C("tensor_scalar", out=QNt[hp_, u, g_, :], in0=PTb[0][hp_, 0:512],
                                  scalar1=gcols[hp_, 2:3], scalar2=None, op0=ALU.mult),
                         reads=[b_pt[0], b_gcols], writes=[b_QNw[u][g_]])

            def evac_k():
                for g_ in range(2):
                    hp_ = slice(g_ * 64, (g_ + 1) * 64)
                    S.op("dve", C("tensor_scalar", out=kTa[g_][hp_, ra * 128:(ra + 1) * 128], in0=PTb[0][hp_, 0:128],
                                  scalar1=gcols[hp_, 1:2], scalar2=None, op0=ALU.mult),
                         reads=[b_pt[0], b_gcols], writes=[b_kTa[ra]])
                    S.op("dve", C("tensor_scalar", out=kTw[g_][hp_, rw * 128:(rw + 1) * 128], in0=PTb[0][hp_, 256:384],
                                  scalar1=gcols[hp_, 5:6], scalar2=None, op0=ALU.mult),
                         reads=[b_pt[0], b_gcols], writes=[b_kTw[rw]])
                    S.op("dve", C("tensor_scalar", out=KE[g_][hp_, i * 128:(i + 1) * 128], in0=PTb[0][hp_, 128:256],
                                  scalar1=gcols[hp_, 4:5], scalar2=None, op0=ALU.mult),
                         reads=[b_pt[0], b_gcols], writes=[b_kTs[i]])

            stc = {}
            tau0 = (i % 4) * 128

            def c0_():
                stc["pj"] = proj(G_C, 512)

            def c1_():
                S.op("act", C("copy", out=qn[1][:], in_=PJ[:, stc["pj"], :]), reads=[b_pj[stc["pj"]]], writes=[b_qn[1]])

            def c2_():
                for c in range(4):
                    S.op("pe", C("transpose", out=PTb[0][:, c * 128:(c + 1) * 128], in_=qn[1][:, c * 128:(c + 1) * 128], identity=idb[:]),
                         reads=[b_qn[1], b_idb], writes=[b_pt[0]])

            def c3_():
                for (Xt, bX, cb) in ((Xk, b_Xk, 0), (Xv, b_Xv, 256)):
                    S.op("dve", C("tensor_copy", out=Xt[0][0:64, :, 17 + tau0:17 + tau0 + 128],
                                  in_=PTb[0][0:64, cb:cb + 256].rearrange("p (g t) -> p g t", g=2)),
                         reads=[b_pt[0]], writes=[bX[0]])
                    S.op("dve", C("tensor_copy", out=Xt[0][64:128, :, 16 + tau0:16 + tau0 + 128],
                                  in_=PTb[0][64:128, cb:cb + 256].rearrange("p (g t) -> p g t", g=2)),
                         reads=[b_pt[0]], writes=[bX[0]])
            chains.append((8, [c0_, c1_, c2_, c3_]))
            chains.append((10, norm_chain(G_K, 6, 0, 3, evac_k)))

            stv_ = {}

            def v0_():
                stv_["pj"] = proj(G_V, 408)

            def v1_():
                pj = stv_["pj"]
                S.op("dve", C("tensor_copy", out=Va[:, ra, :, 0:64], in_=PJ[:, pj, 0:128].rearrange("p (g d) -> p g d", g=2)),
                     reads=[b_pj[pj]], writes=[b_Va[ra]])
                S.op("dve", C("tensor_copy", out=Vs[:, i, :, 0:64], in_=PJ[:, pj, 128:256].rearrange("p (g d) -> p g d", g=2)),
                     reads=[b_pj[pj]], writes=[b_Vs[i]])
                S.op("dve", C("tensor_copy", out=Vw[:, rw, :, 0:64], in_=PJ[:, pj, 256:384].rearrange("p (g d) -> p g d", g=2)),
                     reads=[b_pj[pj]], writes=[b_Vw[rw]])
                S.op("dve", C("tensor_tensor", out=gtmp[:], in0=PJ[:, pj, 384:408], in1=bgate[:], op=ALU.add),
                     reads=[b_pj[pj], b_bgate], writes=[b_gtmp])

            def v2_():
                S.op("act", C("activation", out=gtmp[:], in_=gtmp[:], func=AF.Exp, scale=-1.0), reads=[b_gtmp], writes=[b_gtmp])

            def v3_():
                S.op("dve", C("tensor_scalar", out=gtmp[:], in0=gtmp[:], scalar1=1.0, scalar2=None, op0=ALU.add),
                     reads=[b_gtmp], writes=[b_gtmp])
                S.op("dve", C("reciprocal", out=gts[:, u, :], in_=gtmp[:]), reads=[b_gtmp], writes=[b_gts[u]])
            chains.append((12, [v0_, v1_, v2_, v3_]))

            c0x = 128 * (i % 4)

            def cm(part):
                (Xt, bX, w1, bw1) = ((Xk, b_Xk, w1k, b_w1k), (Xv, b_Xv, w1v, b_w1v))[part // 2]
                jc = part % 2

                def f():
                    for lp in range(16):
                        S.op("pe", C("matmul", CPb[:, part * 16:(part + 1) * 16].rearrange("p (g m) -> p g m", g=2),
                                     lhsT=w1[:, lp, jc * 128:(jc + 1) * 128],
                                     rhs=Xt[0][:, :, c0x + 1 + 2 * lp:c0x + 1 + 2 * lp + 16 * 7 + 1:16],
                                     start=(lp == 0), stop=(lp == 15), skip_group_check=True),
                             reads=[bX[0], bw1], writes=[b_cp], inc=(lp == 15))
                return f

            def cm_act():
                for part in range(4):
                    nb1 = (nb1k, nb1v)[part // 2]
                    S.op("act", C("activation", out=hsig[:, part * 16:(part + 1) * 16], in_=CPb[:, part * 16:(part + 1) * 16],
                                  func=AF.Exp, scale=-1.0, bias=nb1[:, part % 2:part % 2 + 1]),
                         reads=[b_cp, b_nb1], writes=[b_hsig])

            def cm_dve1():
                S.op("dve", C("tensor_scalar", out=hsig[:], in0=hsig[:], scalar1=1.0, scalar2=None, op0=ALU.add),
                     reads=[b_hsig], writes=[b_hsig])
                S.op("dve", C("reciprocal", out=hsig[:], in_=hsig[:]), reads=[b_hsig], writes=[b_hsig])

            def cm_dve2():
                for part in range(4):
                    hid, bh = ((hidk, b_hidk), (hidv, b_hidv))[part // 2]
                    b1 = (b1k, b1v)[part // 2]
                    S.op("dve", C("scalar_tensor_tensor", out=hid[:, part % 2, :], in0=CPb[:, part * 16:(part + 1) * 16],
                                  scalar=b1[:, part % 2:part % 2 + 1], in1=hsig[:, part * 16:(part + 1) * 16],
                                  op0=ALU.add, op1=ALU.mult),
                         reads=[b_cp, b_b1k, b_b1v, b_hsig], writes=[bh])

            def cm_mm2():
                for jc in range(2):
                    S.op("pe", C("matmul", CPb[0:16, 64:128], lhsT=hidk[:, jc, :], rhs=w2k[:, jc, :],
                                 start=(jc == 0), stop=(jc == 1), skip_group_check=True),
                         reads=[b_hidk, b_w2k], writes=[b_cp], inc=False)
                for jc in range(2):
                    S.op("pe", C("matmul", CPb[0:16, 128:192], lhsT=hidv[:, jc, :], rhs=w2v[:, jc, :],
                                 start=(jc == 0), stop=(jc == 1), skip_group_check=True),
                         reads=[b_hidv, b_w2v], writes=[b_cp], inc=(jc == 1))

            def cm_a2():
                S.op("act", C("activation", out=kjunk[:], in_=CPb[0:16, 64:128], func=AF.Square, accum_out=kc32[:, 0:1]),
                     reads=[b_cp], writes=[b_kjunk, b_kc32])
                S.op("dve", C("tensor_copy", out=vstag[:], in_=CPb[0:16, 128:192]), reads=[b_cp, b_kc32], writes=[b_vstag])

            def cm_a3():
                S.op("act", C("activation", out=kc32[:, 1:2], in_=kc32[:, 0:1], func=AF.Ln, scale=1.0 / 64, bias=EPS),
                     reads=[b_kc32], writes=[b_kc32])
                S.op("act", C("activation", out=kc32[:, 2:3], in_=kc32[:, 1:2], func=AF.Exp, scale=-0.5),
                     reads=[b_kc32], writes=[b_kc32])
                n0 = 8 * i
                T, p0 = n0 // 128, n0 % 128
                m0 = 1 if i == 0 else 0
                for g in range(2):
                    S.dma(C("dma_start", out=VC[p0 + m0:p0 + 8, T, g, 0:64], in_=vstag[g * 8 + m0:(g + 1) * 8, :]),
                          reads=[b_vstag], writes=[b_VC])

            def cm_d3():
                for h2 in range(2):
                    S.op("dve", C("tensor_scalar", out=kndup[:, h2 * 64:(h2 + 1) * 64], in0=CPb[0:16, 64:128],
                                  scalar1=kc32[:, 2:3], scalar2=None, op0=ALU.mult),
                         reads=[b_cp, b_kc32], writes=[b_kndup])

            def cm_t():
                S.op("pe", C("transpose", out=PTb[0][:, 0:16], in_=kndup[:], identity=idb[0:16, 0:16]),
                     reads=[b_kndup, b_idb], writes=[b_pt[0]])

            def cm_e():
                n0 = 8 * i
                for g in range(2):
                    S.op("dve", C("tensor_scalar", out=kcT[g][g * 64:(g + 1) * 64, n0:n0 + 8],
                                  in0=PTb[0][g * 64:(g + 1) * 64, g * 8:(g + 1) * 8],
                                  scalar1=gcols[g * 64:(g + 1) * 64, 3:4], scalar2=None, op0=ALU.mult),
                         reads=[b_pt[0], b_gcols], writes=[b_kcT])
            nop = lambda: None
            import os as _os
            chains.append((12, [cm(0), cm(1), cm(2), cm(3), cm_act, cm_dve1, cm_dve2, cm_mm2, cm_a2, cm_a3, cm_d3, nop, nop, nop, cm_t, cm_e][:int(_os.environ.get('NCM', '99'))]))
            chains.append((15, norm_chain(G_QA, 8, 1, 4, evac_qa)))
            chains.append((18, norm_chain(G_QB, 8, 0, 4, evac_qb)))

            def z_chain(c0, zbuf, bz, col0):
                st = {}

                def s0():
                    st["pj"] = proj(c0, 512)

                def s1():
                    S.op("act", C("activation", out=zbuf, in_=PJ[:, st["pj"], :], func=AF.Exp, scale=-1.0),
                         reads=[b_pj[st["pj"]]], writes=[bz])

                def s2():
                    S.op("act", C("activation", out=zbuf, in_=zbuf, func=AF.Ln, bias=1.0), reads=[bz], writes=[bz])

                def s3():
                    S.op("act", C("activation", out=zbuf, in_=zbuf, func=AF.Exp, scale=-1.0), reads=[bz], writes=[bz])

                def s4():
                    pj = st["pj"]
                    S.op("dve", C("tensor_tensor", out=sz[:, u, col0:col0 + 512], in0=PJ[:, pj, :], in1=zbuf, op=ALU.mult),
                         reads=[b_pj[pj], bz], writes=[b_sz[u]])
                return [s0, s1, s2, s3, s4]
            chains.append((21, z_chain(G_ZA, sq[:, :], b_sq, 0)))
            chains.append((24, z_chain(G_ZB, zt2[:, :], b_zt2, 512)))
            return chains

        NSTEP = 32
        ntile = 4 * n_sb

        def x_load(b, i):
            S.dma(C("dma_start", out=xs[i % 2], in_=x_d[b, i * 128:(i + 1) * 128, :]), writes=[b_xs[i % 2]])

        def gen_steps(b, pc, tl):
            chains = []
            if pc is not None:
                if pc % 4 == 0 and pc > 0:
                    for (Xt, bX) in ((Xk, b_Xk), (Xv, b_Xv)):
                        S.op("pool", C("tensor_copy", out=Xt[0][:, :, 0:17], in_=Xt[0][:, :, 512:529]), reads=[bX[0]], writes=[bX[0]])
                chains += build_PC(b, pc)[:nch]
                if pc + 1 < ntile:
                    chains.append((10, [lambda: x_load(b, pc + 1)]))
            if tl is not None:
                chains.append(tail_chain(b, tl))
            T_ = max(o + len(ch) for o, ch in chains)
            assert T_ <= NSTEP, T_
            for tau in range(NSTEP):
                for o, ch in chains:
                    k = tau - o
                    if 0 <= k < len(ch):
                        ch[k]()
                yield

        def score_tile(lhsT_ap, rhs_ap, extra, reads):
            st = st_i[0] % 2
            st_i[0] += 1
            n = len(extra)
            S.op("pe", C("matmul", ST[:, st, :], lhsT=lhsT_ap, rhs=rhs_ap, start=True, stop=(n == 0)),
                 reads=reads, writes=[b_st[st]], inc=(n == 0))
            for j, (l_ap, r_ap, rd) in enumerate(extra):
                S.op("pe", C("matmul", ST[:, st, :], lhsT=l_ap, rhs=r_ap, start=False, stop=(j == n - 1)),
                     reads=rd, writes=[b_st[st]], inc=(j == n - 1))
            return st

        def exp_tile(st):
            p = pts_i[0] % 2
            pts_i[0] += 1
            S.op("act", C("activation", out=PTs[p][:], in_=ST[:, st, :], func=AF.Exp), reads=[b_st[st]], writes=[b_PTs[p]])
            return p

        def bc4(ap2d):
            return V(ap2d, [[0, 4], [1, 128]])

        def phaseA(b, i, u, nxt):
            gsl = gts[:, u, :]
            jobs = []

            def Bx(Bt, ty, g):
                return [(idr[:], Bt[:, ty, 4 * g:4 * g + 4, :].rearrange("p h t -> p (h t)"), [b_idr, b_BA if Bt is BA else b_BB])]

            def add_branch(g, kts, lhs_fn, rhs_fn, extras_fn, V_t, V_bufs, vidx, oab, post):
                nk = len(kts)
                for j, kt in enumerate(kts):
                    def fscore(kt=kt):
                        l_ap, l_rd = lhs_fn(kt)
                        r_ap, r_rd = rhs_fn(kt)
                        return score_tile(l_ap, r_ap, extras_fn(kt), l_rd + r_rd)

                    def pv(p, kt=kt, j=j):
                        for r in range(4):
                            S.op("pe", C("matmul", OA[:, oab, r * 66:(r + 1) * 66], lhsT=PTs[p][:, r * 128:(r + 1) * 128],
                                         rhs=V_t[:, vidx(kt), g, :], start=(j == 0 and r == 0), stop=(j == nk - 1), skip_group_check=True),
                                 reads=[b_PTs[p], V_bufs(kt)], writes=[b_oa[oab]], inc=(j == nk - 1 and r == 3))
                    jobs.append((fscore, pv, post if j == nk - 1 else None))

            def accumulate(g, oab, dcol, gate_br):
                S.op("dve", C("reciprocal", out=den[:, dcol:dcol + 4], in_=V(OA[:, oab, 64:65], [[66, 4]])),
                     reads=[b_oa[oab]], writes=[b_den])
                S.op("dve", C("tensor_tensor", out=coef[:, dcol:dcol + 4], in0=den[:, dcol:dcol + 4],
                              in1=gsl[:, g * 12 + gate_br:(g + 1) * 12:3], op=ALU.mult),
                     reads=[b_den, b_gts[u]], writes=[b_coef])
                S.op("dve", C("tensor_tensor", out=ytmp[oab][:, :].rearrange("p (r d) -> p r d", r=4),
                              in0=V(OA[:, oab, 0:1], [[66, 4], [1, 64]]),
                              in1=V(coef[:, dcol:dcol + 1], [[1, 4], [0, 64]]), op=ALU.mult),
                     reads=[b_oa[oab], b_coef], writes=[b_ytmp[oab]])
                S.op("dve", C("tensor_tensor", out=yB[:, g, :], in0=yB[:, g, :], in1=ytmp[oab][:, :], op=ALU.add),
                     reads=[b_yB[g], b_ytmp[oab]], writes=[b_yB[g]])

            Tmax = i // 16
            post_b_fns = {}
            for g in range(2):
                hp = slice(g * 64, (g + 1) * 64)
                qb_ap = QNt[hp, u, g, :]
                for T in range(Tmax + 1):
                    def fscore(T=T, hp=hp, qb_ap=qb_ap, g=g):
                        extra = []
                        if T == Tmax:
                            extra.append((idb[:], bc4(CM[:, i % 16, :]), [b_idb, b_CM]))
                        return score_tile(kcT[g][:, T * 128:(T + 1) * 128], QNt[:, u, g, :], extra, [b_kcT, b_QNw[u][g]])

                    def pv(p, T=T, g=g):
                        for r in range(4):
                            S.op("pe", C("matmul", OA[:, g, r * 128:(r + 1) * 128], lhsT=PTs[p][:, r * 128:(r + 1) * 128],
                                         rhs=VC[:, T, g, :], start=(T == 0 and r == 0), stop=(T == Tmax), skip_group_check=True),
                                 reads=[b_PTs[p], b_VC], writes=[b_oa[g]], inc=(T == Tmax and r == 3))

                    def post(st, g=g, hp=hp):
                        S.op("dve", C("tensor_scalar", out=den[:, 0:4], in0=V(OA[:, g, 64:65], [[128, 4]]),
                                      scalar1=1e-30, scalar2=None, op0=ALU.max),
                             reads=[b_oa[g]], writes=[b_den])
                        S.op("dve", C("reciprocal", out=den[:, 0:4], in_=den[:, 0:4]), reads=[b_den], writes=[b_den])
                        S.op("dve", C("tensor_tensor", out=coef[:, 0:4], in0=den[:, 0:4],
                                      in1=gsl[:, g * 12:(g + 1) * 12:3], op=ALU.mult),
                             reads=[b_den, b_gts[u]], writes=[b_coef])
                        pat_ap = PAT[:, 64 - 2 * i + 64:128 - 2 * i + 64]
                        for r in range(4):
                            dst, bdst = (impA, b_impA) if r % 2 == 0 else (impB, b_impB)
                            in0 = OA[:, g, r * 128 + 66:r * 128 + 128]
                            if r == 1:
                                S.op("dve", C("tensor_scalar", out=dst[:, 1:63], in0=in0, scalar1=den[:, r:r + 1], scalar2=None, op0=ALU.mult),
                                     reads=[b_oa[g], b_den], writes=[bdst])
                            else:
                                in1, rd1 = (pat_ap[:, 1:63], b_PAT) if r == 0 else (dst[:, 1:63], bdst)
                                S.op("dve", C("scalar_tensor_tensor", out=dst[:, 1:63], in0=in0, scalar=den[:, r:r + 1],
                                              in1=in1, op0=ALU.mult, op1=ALU.add),
                                     reads=[b_oa[g], b_den, rd1], writes=[bdst])
                        S.op("dve", C("tensor_tensor", out=yB[:, g, :].rearrange("p (r d) -> p r d", r=4),
                                      in0=V(OA[:, g, 0:1], [[128, 4], [1, 64]]), in1=V(coef[:, 0:1], [[1, 4], [0, 64]]), op=ALU.mult),
                             reads=[b_oa[g], b_coef], writes=[b_yB[g]])
                        S.op("dve", C("tensor_tensor", out=score[:, 1:63], in0=impA[:, 1:63], in1=impB[:, 1:63], op=ALU.add),
                             reads=[b_impA, b_impB], writes=[b_score])
                        if i == NT - 1:
                            S.op("dve", C("tensor_copy", out=score[:, 63:64], in_=pat_ap[:, 63:64]), reads=[b_PAT, b_score], writes=[b_score])
                        S.op("dve", C("max", out=m8[:, 0:8], in_=score[:]), reads=[b_score], writes=[b_m8])
                        S.op("dve", C("match_replace", out=wk[:], in_to_replace=m8[:, 0:8], in_values=score[:], imm_value=-1e30),
                             reads=[b_score, b_m8], writes=[b_wk])
                        S.op("dve", C("max", out=m8[:, 8:16], in_=wk[:]), reads=[b_wk], writes=[b_m8])
                        S.op("dve", C("tensor_scalar", out=negmg[g][:, :].rearrange("p (a j) -> p a j", a=2), in0=V(score[:, 0:1], [[0, 2], [1, 64]]),
                                      scalar1=m8[:, 15:16], scalar2=NEG,
                                      op0=ALU.is_lt, op1=ALU.mult), reads=[b_score, b_m8], writes=[b_negm[g]])

                    def post_b(g=g):
                        S.op("pe", C("transpose", out=OA[:, g, 384:512], in_=negmg[g][:], identity=idf[:]),
                             reads=[b_negm[g], b_idf], writes=[b_oa[g]])
                        op_ = slice((1 - g) * 64, (2 - g) * 64)
                        S.op("dve", C("tensor_copy", out=QNt[op_, u, g, :].rearrange("p (r t) -> p r t", r=4), in_=bc4(OA[op_, g, 384:512])),
                             reads=[b_oa[g]], writes=[b_QNw[u][g]])
                    if T == Tmax:
                        post_b_fns[g] = post_b
                    jobs.append((fscore, pv, post if T == Tmax else None))

            for g in range(2):
                hp = slice(g * 64, (g + 1) * 64)
                qa_ap = qTa[hp, u].rearrange("p r t -> p (r t)")

                def a_post(st, g=g):
                    oab = g
                    post_b_fns[g]()
                    S.op("dve", C("tensor_tensor", out=den[:, 12:16], in0=V(OA[:, oab, 64:65], [[66, 4]]),
                                  in1=esink[:, 4 * g:4 * g + 4], op=ALU.add),
                         reads=[b_oa[oab], b_esink], writes=[b_den])
                    S.op("dve", C("reciprocal", out=den[:, 12:16], in_=den[:, 12:16]), reads=[b_den], writes=[b_den])
                    S.op("dve", C("tensor_tensor", out=ytmp[oab][:, :].rearrange("p (r d) -> p r d", r=4),
                                  in0=V(OA[:, oab, 0:1], [[66, 4], [1, 64]]),
                                  in1=V(den[:, 12:13], [[1, 4], [0, 64]]), op=ALU.mult),
                         reads=[b_oa[oab], b_den], writes=[b_ytmp[oab]])
                    S.op("pool", C("tensor_tensor", out=y[:, g * 256:(g + 1) * 256], in0=ytmp[oab][:, :],
                                   in1=sz[:, u, g * 256:(g + 1) * 256], op=ALU.mult),
                         reads=[b_ytmp[oab], b_sz[u]], writes=[b_y])
                add_branch(g, list(range(max(0, i - 1), i + 1)),
                           lambda kt, g=g: (kTa[g][:, (kt % RA) * 128:(kt % RA) * 128 + 128], [b_kTa[kt % RA]]),
                           lambda kt: (qTa[:, u].rearrange("p r t -> p (r t)"), [b_qTa[u]]),
                           lambda kt, g=g: Bx(BA, 0, g) if kt == i else Bx(BA, 1, g),
                           Va, lambda kt: b_Va[kt % RA], lambda kt: kt % RA, g, a_post)
            for g in range(2):
                hp = slice(g * 64, (g + 1) * 64)
                qb_ap = QNt[hp, u, g, :]

                def win_extras(kt, g=g):
                    dk = i - kt
                    if dk == 0:
                        return Bx(BB, 0, g)
                    if dk == 1:
                        return Bx(BB, 1, g)
                    if dk == 4:
                        return [(idb[:], bc4(TRI[:, :]), [b_idb, b_TRI])]
                    return []
                add_branch(g, list(range(max(0, i - 4), i + 1)),
                           lambda kt, g=g: (kTw[g][:, (kt % RW) * 128:(kt % RW) * 128 + 128], [b_kTw[kt % RW]]),
                           lambda kt, g=g: (QNt[:, u, g, :], [b_QNw[u][g]]),
                           win_extras, Vw, lambda kt: b_Vw[kt % RW], lambda kt: kt % RW, g,
                           lambda st, g=g: accumulate(g, g, 8, 2))
            for g in range(2):
                hp = slice(g * 64, (g + 1) * 64)
                qb_ap = QNt[hp, u, g, :]

                def sel_lhs(kt, g=g, hp=hp):
                    return (KE[g][:, kt * 128:(kt + 1) * 128], [b_kTs[kt], b_E])

                def sel_rhs(kt, g=g, qb_ap=qb_ap):
                    return (QNt[:, u, g, :], [b_QNw[u][g]])

                def sel_extras(kt, g=g):
                    if kt == i:
                        return Bx(BB, 0, g)
                    if kt == i - 1:
                        return Bx(BB, 1, g)
                    return []

                def sel_post(st, g=g):
                    accumulate(g, g, 4, 1)
                    S.op("dve", C("tensor_tensor", out=y[:, 512 + g * 256:512 + (g + 1) * 256], in0=yB[:, g, :],
                                  in1=sz[:, u, 512 + g * 256:512 + (g + 1) * 256], op=ALU.mult),
                         reads=[b_yB[g], b_sz[u]], writes=[b_y])
                add_branch(g, list(range(0, i + 1)), sel_lhs, sel_rhs, sel_extras, Vs, lambda kt: b_Vs[kt], lambda kt: kt, g, sel_post)

            pend = None
            nj = len(jobs)
            sdone = 0
            ncmp = 2 * (Tmax + 1)
            for jx, (fscore, pv, post) in enumerate(jobs):
                st = fscore()
                want = ((jx + 1) * NSTEP) // nj
                if jx + 1 >= ncmp + 1:
                    want = max(2, want)
                while sdone < want:
                    next(nxt, None)
                    sdone += 1
                if pend is not None:
                    pst, ppv, ppost = pend
                    p = exp_tile(pst)
                    ppv(p)
                    if ppost is not None:
                        ppost(pst)
                pend = (st, pv, post)
            pst, ppv, ppost = pend
            p = exp_tile(pst)
            ppv(p)
            if ppost is not None:
                ppost(pst)

        def tail_chain(b, i):
            xrs = i % 2

            def t0():
                S.dma(C("dma_start", out=xr[xrs], in_=x_d[b, i * 128:(i + 1) * 128, :]), writes=[b_xr[xrs]])
                for kc in range(8):
                    S.op("pe", C("transpose", out=PTb[1][:, kc * 128:(kc + 1) * 128], in_=y[:, kc * 128:(kc + 1) * 128],
                                 identity=idb[:]), reads=[b_y, b_idb], writes=[b_pt[1]])

            def t1():
                S.op("act", C("copy", out=yT[:, :, :].rearrange("p k t -> p (k t)"), in_=PTb[1][:, :]), reads=[b_pt[1]], writes=[b_yT])
            stt = {}

            def mm(hf):
                def f():
                    pj = next_pj()
                    stt[hf] = pj
                    for kc in range(8):
                        S.op("pe", C("matmul", PJ[:, pj, :], lhsT=yT[:, kc, :], rhs=Wout[:, kc, hf * 512:(hf + 1) * 512],
                                     start=(kc == 0), stop=(kc == 7)),
                             reads=[b_yT, b_wout], writes=[b_pj[pj]], inc=(kc == 7))
                return f

            def ml(hf):
                def f():
                    pj = stt[hf]
                    S.op("dve", C("tensor_tensor", out=rtmp[:, :], in0=PJ[:, pj, :],
                                  in1=GATE[:, b, hf * 512:(hf + 1) * 512], op=ALU.mult),
                         reads=[b_pj[pj], b_GATE], writes=[b_rtmp])
                return f

            def t6(hf):
                def f():
                    S.op("dve", C("tensor_tensor", out=xr[xrs][:, hf * 512:(hf + 1) * 512], in0=xr[xrs][:, hf * 512:(hf + 1) * 512],
                                  in1=rtmp[:, :], op=ALU.add),
                         reads=[b_xr[xrs], b_rtmp], writes=[b_xr[xrs]])
                return f

            def t7():
                S.dma(C("dma_start", out=out_d[b, i * 128:(i + 1) * 128, :], in_=xr[xrs]), reads=[b_xr[xrs]], is_output=True)
            return (0, [t0, t1, mm(0), ml(0), t6(0), mm(1), ml(1), t6(1), t7])


        def drain(gen):
            for _ in gen:
                pass

        for b in range(nb):
            if b > 0:
                for g_ in range(2):
                    S.op("pool", C("memset", kcT[g_][:], 0.0), writes=[b_kcT])
                S.op("pool", C("memset", VC[:, :, :, 0:64], 0.0), writes=[b_VC])
                S.op("pool", C("memset", Xk[0][:, :, 0:17], 0.0), writes=[b_Xk[0]])
                S.op("pool", C("memset", Xv[0][:, :, 0:17], 0.0), writes=[b_Xv[0]])
                S.op("pool", C("memset", score[:, 63:64], 0.0), reads=[b_score], writes=[b_score])
            x_load(b, 0)
            drain(gen_steps(b, 0, None))
            for i in range(ntile):
                pc_ = i + 1 if i + 1 < ntile else None
                tl_ = i - 1 if i >= 1 else None
                nxt = gen_steps(b, pc_, tl_) if (pc_ is not None or tl_ is not None) else iter(())
                if "A" in stages and i < n_a:
                    phaseA(b, i, i % 2, nxt)
                drain(nxt)
            drain(gen_steps(b, None, ntile - 1))
        if dumps:
            L = dict(GATE=(GATE, [b_GATE]), Gcol=(Gcol, [b_Gcol]), SHcol=(SHcol, [b_SHcol]), BA=(BA, [b_BA]), BB=(BB, [b_BB]),
                     b1k=(b1k, [b_b1k]), b1v=(b1v, [b_b1v]), Win=(Win, [b_win]), Wout=(Wout, [b_wout]), w1k=(w1k, [b_w1k]),
                     qTa=(qTa, b_qTa), kTs=(KE[0], b_kTs),
                     Vs=(Vs, b_Vs), Va=(Va, b_Va), Vw=(Vw, b_Vw), VC=(VC, [b_VC]), sz=(sz, b_sz), gts=(gts, b_gts),
                      y=(y, [b_y]), hT=(hT0, [bht]),
                     Xk=(Xk0, [bxk]), hidk=(hidk, [b_hidk]), score=(score, [b_score]), yB=(yB, b_yB), esink=(esink, [b_esink]),
                     den=(den, [b_den]), coef=(coef, [b_coef]))
            for nm in dumps:
                t, bufs = L[nm]
                shp = list(t.shape)
                dd = nc.dram_tensor("d_" + nm, shp, t.dtype, kind="ExternalOutput").ap()
                full = t[tuple(slice(None) for _ in shp)]
                S.dma(C("dma_start", out=dd, in_=full), reads=list(bufs), is_output=True)
        S.emit()
    return nc


def _t5_bucket(dist):
    n = np.maximum(dist, 0)
    nf = np.maximum(n, 1).astype(np.float32)
    large = 16 + (np.log(nf / np.float32(16)) / np.float32(np.log(128 / 16)) * np.float32(16)).astype(np.int32)
    large = np.minimum(large, 31)
    return np.where(n < 16, n, large)


def _constants():
    bf = ml_dtypes.bfloat16
    sl = np.arange(128)[:, None]
    tl = np.arange(128)[None, :]
    d_diag = tl - sl
    d_prev = 128 + tl - sl
    idx = np.stack([_t5_bucket(d_diag), _t5_bucket(d_prev)], 0)
    maskAB = np.zeros((128, 4, 128), np.float32)
    maskAB[:, 0, :] = np.where(d_diag >= 0, 0.0, NEG)
    maskAB[:, 1, :] = np.where(d_prev < 128, 0.0, NEG)
    maskAB[:, 2, :] = np.where(d_diag >= 0, 0.0, NEG)
    maskAB[:, 3, :] = 0.0
    E = (np.arange(SEQ)[None, :] // 64 == (np.arange(128) % 64)[:, None]).astype(np.float32).astype(bf)
    nl = np.arange(128)[:, None, None]
    o = np.arange(16)[None, :, None]
    t3 = np.arange(128)[None, None, :]
    cmask = np.where(16 * nl + 15 <= 128 * o + t3, 0.0, NEG).astype(np.float32).astype(bf)
    tri = np.where(sl > tl, 0.0, NEG).astype(np.float32).astype(bf)
    pat = np.zeros((128, 192), np.float32)
    pat[:, 0] = BONUS
    hi = (np.arange(128) >= 64).astype(np.int64)
    pat[np.arange(128), 64 + 63 + hi] = BONUS
    pat[np.arange(128), 64 + 64 + hi] = BONUS
    pat = pat.astype(bf)
    npr = np.arange(256)
    n = npr - 1
    c_lo = 16 * n
    s_lo = 64 * np.arange(64)
    ov = np.clip(np.minimum(c_lo[:, None] + 32, s_lo[None, :] + 64) - np.maximum(c_lo[:, None], s_lo[None, :]), 0, None) / 32.0
    ov[0, :] = 0.0
    ov = np.ascontiguousarray(ov.reshape(2, 128, 64).transpose(1, 0, 2)[:, :, 1:63]).astype(np.float32).astype(bf)
    return idx, maskAB, E, cmask, tri, pat, ov


def _perm_cols():
    o_qa, o_ka, o_va, o_za, o_qb, o_kc, o_vc, o_ks, o_vs, o_kw, o_vw, o_zb, o_gb = (
        0, 512, 640, 768, 1280, 1792, 1920, 2048, 2176, 2304, 2432, 2560, 3072)
    r64 = np.arange(64)
    cols = []
    for base in (o_qa, o_qb):
        for r in range(4):
            cols += [base + r * 64 + r64, base + (4 + r) * 64 + r64]
    cols += [o_ka + np.arange(128), o_ks + np.arange(128), o_kw + np.arange(128)]
    for base in (o_kc, o_vc):
        for g in range(2):
            cols += [base + g * 64 + r64, base + g * 64 + r64]
    cols += [o_va + np.arange(128), o_vs + np.arange(128), o_vw + np.arange(128), o_gb + np.arange(24)]
    cols += [o_za + np.arange(512), o_zb + np.arange(512)]
    cols = np.concatenate(cols)
    assert cols.shape[0] == WCOLS
    return cols


_NC_CACHE = {}


def kernel(x, c, w_ada, b_ada, norm_gain, w_in, b_nsa_gate, q_gain_a, k_gain_a, sinks,
           q_gain_b, k_gain_cmp, k_gain_sel, k_gain_win, cmp_pos_k, cmp_pos_v,
           w_cmp_k1, w_cmp_k2, w_cmp_v1, w_cmp_v2, w_out, rel_bias):
    f = lambda a: np.ascontiguousarray(np.asarray(a, dtype=np.float32))
    x = f(x); c = f(c); w_ada = f(w_ada)[0]; b_ada = f(b_ada)[0]; norm_gain = f(norm_gain)[0]
    w_in = f(w_in)[0]; b_nsa_gate = f(b_nsa_gate)[0]; sinks = f(sinks)[0]
    rel_bias = f(rel_bias); w_out = f(w_out)[0]
    idx, maskAB, E, cmask, tri, pat, ov = _constants()
    biasAB = rel_bias[idx]
    biasA = np.ascontiguousarray(biasAB[..., 0:8].transpose(1, 0, 3, 2))
    biasB = np.ascontiguousarray(biasAB[..., 8:16].transpose(1, 0, 3, 2))
    c31 = np.ascontiguousarray(rel_bias[31:32, 8:16])
    gcols = np.stack([np.tile(f(g)[0], 2) for g in (q_gain_a, k_gain_a, q_gain_b, k_gain_cmp, k_gain_sel, k_gain_win)], 1)
    shared = {
        "w_ada": w_ada,
        "bada_col": np.ascontiguousarray(b_ada[0:2048].reshape(16, 128).T),
        "bada_gate": np.ascontiguousarray(b_ada[2048:3072].reshape(1, DM)),
        "ng_col": np.ascontiguousarray(norm_gain.reshape(8, 128).T),
        "w_in_p": np.ascontiguousarray(w_in[:, _perm_cols()]),
        "bgate": b_nsa_gate.reshape(1, 24),
        "gcols": np.ascontiguousarray(gcols),
        "sinks": sinks.reshape(1, 8),
        "posk": np.ascontiguousarray(f(cmp_pos_k)[0].reshape(16, 128).T),
        "posv": np.ascontiguousarray(f(cmp_pos_v)[0].reshape(16, 128).T),
        "w1k": f(w_cmp_k1)[0], "w1v": f(w_cmp_v1)[0], "w2k": f(w_cmp_k2)[0], "w2v": f(w_cmp_v2)[0],
        "w_out": w_out, "biasA": biasA, "biasB": biasB, "c31": c31, "maskAB": maskAB,
        "Emat": E, "cmask": cmask, "tri": tri, "pat": pat, "ov": ov,
    }
    in_maps = []
    for core in range(NCORES):
        m = dict(shared)
        m["x"] = x[NB * core:NB * (core + 1)]
        cc = c[NB * core:NB * (core + 1)]
        m["cT"] = np.ascontiguousarray(cc.reshape(NB, 8, 128).transpose(2, 1, 0))
        in_maps.append(m)
    if "nc" not in _NC_CACHE:
        _NC_CACHE["nc"] = build_program()
    res = run_bass_kernel_spmd(_NC_CACHE["nc"], in_maps, core_ids=list(range(NCORES)))
    return np.concatenate([np.asarray(r["out"], dtype=np.float32) for r in res.results], axis=0)
```

```python
import numpy as np
import ml_dtypes
from contextlib import ExitStack
import concourse.bass as bass
import concourse.mybir as mybir
from concourse.bass_utils import run_bass_kernel_spmd

F32 = mybir.dt.float32
BF16 = mybir.dt.bfloat16
F32R = mybir.dt.float32r
AF = mybir.ActivationFunctionType
ALU = mybir.AluOpType
AX = mybir.AxisListType

NCORES = 8
SEQ = 4096
DM = 1024
NT = SEQ // 128
NB = 2
WCOLS = 3352
NEG = -30000.0
BONUS = 1.0e4
EPS = 1e-6
G_QA, G_QB, G_K, G_C, G_V, G_ZA, G_ZB = 0, 512, 1024, 1408, 1920, 2328, 2840
RA, RW = 3, 6


class Buf:
    __slots__ = ("w", "r", "dsem", "dcount", "name")

    def __init__(self, name=""):
        self.w = None
        self.r = {}
        self.dsem = None
        self.dcount = 0
        self.name = name


class Sched:
    ENG = ("pe", "act", "dve", "pool", "sp")

    def __init__(self, nc, same_engine_sync=True):
        self.nc = nc
        self.ops = {e: [] for e in self.ENG}
        self.count = {e: 0 for e in self.ENG}
        self.waited = {e: {} for e in self.ENG}
        self.sems = {}
        self.dma_sems = []
        self.same_engine_sync = same_engine_sync
        self.out_waits = []

    def _need(self, eng, dep, needs):
        if dep is None:
            return
        k, v = dep
        if k == eng:
            if eng in ("pe", "sp"):
                return
            if not self.same_engine_sync:
                return
        if self.waited[eng].get(k, 0) >= v:
            return
        if needs.get(k, 0) < v:
            needs[k] = v

    def _emit_waits(self, eng, needs):
        for k, v in needs.items():
            self.ops[eng].append(("w", k, v))
            self.waited[eng][k] = v

    def op(self, eng, fn, reads=(), writes=(), inc=True):
        needs = {}
        for b in reads:
            self._need(eng, b.w, needs)
        for b in writes:
            self._need(eng, b.w, needs)
            for k, v in b.r.items():
                self._need(eng, (k, v), needs)
        self._emit_waits(eng, needs)
        if inc:
            self.count[eng] += 1
            val = self.count[eng]
        else:
            val = self.count[eng] + 1
        self.ops[eng].append(("op", fn, [(eng, 1)] if inc else []))
        for b in writes:
            b.w = (eng, val)
            b.r = {}
        for b in reads:
            if b.r.get(eng, 0) < val:
                b.r[eng] = val
        return val

    def dma(self, fn, reads=(), writes=(), q="sp", is_output=False):
        needs = {}
        for b in reads:
            self._need(q, b.w, needs)
        for b in writes:
            self._need(q, b.w, needs)
            for k, v in b.r.items():
                self._need(q, (k, v), needs)
        self._emit_waits(q, needs)
        owner = writes[0] if writes else reads[0]
        if owner.dsem is None:
            owner.dsem = "dma%d" % len(self.dma_sems)
            self.dma_sems.append(owner.dsem)
        owner.dcount += 16
        k, v = owner.dsem, owner.dcount
        self.ops[q].append(("op", fn, [(k, 16)]))
        for b in writes:
            b.w = (k, v)
            b.r = {}
        for b in reads:
            b.r[k] = v
        if is_output:
            self.out_waits.append((k, v))

    def emit(self):
        nc = self.nc
        with ExitStack() as es:
            for e in self.ENG:
                self.sems[e] = es.enter_context(nc.semaphore("prog_" + e))
            for k in self.dma_sems:
                self.sems[k] = es.enter_context(nc.semaphore(k))
            fin = {}
            for k, v in self.out_waits:
                fin[k] = max(fin.get(k, 0), v)
            for k, v in fin.items():
                self.ops["sp"].append(("w", k, v))
            block = es.enter_context(nc.Block())
            sems = self.sems
            ops = self.ops

            FUSE = ("activation", "copy", "tensor_tensor", "tensor_copy", "reciprocal", "tensor_reduce", "memset",
                    "tensor_scalar", "scalar_tensor_tensor", "max")

            def run(engine, lst, fuse=False):
                pend = []
                for item in lst:
                    if item[0] == "w":
                        pend.append(item)
                        continue
                    name, args, kw = item[1]
                    can = fuse and pend and name in FUSE and kw.get("accum_out") is None
                    for w in (pend[:-1] if can else pend):
                        engine.wait_ge(sems[w[1]], w[2])
                    ins = getattr(engine, name)(*args, **kw)
                    if can:
                        ins._wait_ge(sems[pend[-1][1]], pend[-1][2])
                    pend = []
                    for (k, n) in item[2]:
                        ins.then_inc(sems[k], n)
                for w in pend:
                    engine.wait_ge(sems[w[1]], w[2])

            @block.sync
            def _(e):
                run(e, ops["sp"])

            @block.tensor
            def _(e):
                run(e, ops["pe"])

            @block.scalar
            def _(e):
                run(e, ops["act"], fuse=True)

            @block.vector
            def _(e):
                run(e, ops["dve"], fuse=True)

            @block.gpsimd
            def _(e):
                run(e, ops["pool"])


def C(name, *args, **kw):
    return (name, args, kw)


def V(ap, dims):
    return bass.AP(tensor=ap.tensor, offset=ap.offset, ap=[list(ap.ap[0])] + [list(d) for d in dims])


def build_program(nb=NB, n_sb=NT // 4, stages="PCA", n_a=999, dumps=None, pro=9, nch=99):
    nc = bass.Bass("TRN2", target_bir_lowering=False)
    S = Sched(nc)

    def din(name, shape, dt=F32):
        return nc.dram_tensor(name, list(shape), dt, kind="ExternalInput").ap()

    x_d = din("x", [NB, SEQ, DM])
    cT_d = din("cT", [128, 8, NB])
    wada_d = din("w_ada", [DM, 3 * DM])
    badac_d = din("bada_col", [128, 16])
    badag_d = din("bada_gate", [1, DM])
    ngc_d = din("ng_col", [128, 8])
    win_d = din("w_in_p", [DM, WCOLS])
    bgate_d = din("bgate", [1, 24])
    gcols_d = din("gcols", [128, 6])
    sinks_d = din("sinks", [1, 8])
    posk_d = din("posk", [128, 16])
    posv_d = din("posv", [128, 16])
    w1k_d = din("w1k", [2048, 256])
    w1v_d = din("w1v", [2048, 256])
    w2k_d = din("w2k", [256, 64])
    w2v_d = din("w2v", [256, 64])
    wout_d = din("w_out", [DM, DM])
    biasA_d = din("biasA", [128, 2, 8, 128])
    biasB_d = din("biasB", [128, 2, 8, 128])
    c31_d = din("c31", [1, 8])
    maskAB_d = din("maskAB", [128, 4, 128])
    E_d = din("Emat", [128, SEQ], BF16)
    cmask_d = din("cmask", [128, 16, 128], BF16)
    tri_d = din("tri", [128, 128], BF16)
    pat_d = din("pat", [128, 192], BF16)
    ov_d = din("ov", [128, 2, 62], BF16)
    out_d = nc.dram_tensor("out", [NB, SEQ, DM], F32, kind="ExternalOutput").ap()

    with ExitStack() as es:
        def sb(name, shape, dt):
            return es.enter_context(nc.sbuf_tensor("s_" + name, list(shape), dt))

        def ps(name, shape, dt):
            return es.enter_context(nc.psum_tensor("p_" + name, list(shape), dt))

        PJ = ps("PJ", [128, 2, 512], F32)
        PT = ps("PT", [128, 2, 512], F32)
        ST = ps("ST", [128, 2, 512], F32)
        OA = ps("OA", [128, 2, 512], F32)
        b_pj = [Buf("pj0"), Buf("pj1")]
        b_pt = [Buf("pt0"), Buf("pt1")]
        b_st = [Buf("st0"), Buf("st1")]
        b_oa = [Buf("oa0"), Buf("oa1")]
        PTb = [PT[:, 0, :].bitcast(BF16), PT[:, 1, :].bitcast(BF16)]

        Win = sb("Win", [128, 8, WCOLS], BF16); b_win = Buf("win")
        b_win_l = [Buf("win%d" % i_) for i_ in range(16)]
        Wout = sb("Wout", [128, 8, DM], BF16); b_wout = Buf("wout")
        b_wout_l = [Buf("wout%d" % i_) for i_ in range(8)]
        w1k = sb("w1k", [128, 16, 256], BF16); b_w1k = Buf()
        w1v = sb("w1v", [128, 16, 256], BF16); b_w1v = Buf()
        w2k = sb("w2k", [128, 2, 64], BF16); b_w2k = Buf()
        w2v = sb("w2v", [128, 2, 64], BF16); b_w2v = Buf()
        b1k = sb("b1k", [128, 2], F32); b_b1k = Buf()
        b1v = sb("b1v", [128, 2], F32); b_b1v = Buf()
        STG = [sb("stg0", [128, 2048], F32), sb("stg1", [128, 2048], F32)]
        b_stg = [[Buf("s00"), Buf("s01")], [Buf("s10"), Buf("s11")]]
        kTa = [sb("kTa0", [128, RA * 128], BF16), sb("kTa1", [128, RA * 128], BF16)]; b_kTa = [Buf() for _ in range(RA)]
        kTw = [sb("kTw0", [128, RW * 128], BF16), sb("kTw1", [128, RW * 128], BF16)]; b_kTw = [Buf() for _ in range(RW)]
        KE = [sb("KE0", [128, SEQ], BF16), sb("KE1", [128, SEQ], BF16)]
        b_kTs = [Buf() for _ in range(NT)]
        Va = sb("Va", [128, RA, 2, 66], BF16); b_Va = [Buf() for _ in range(RA)]
        Vw = sb("Vw", [128, RW, 2, 66], BF16); b_Vw = [Buf() for _ in range(RW)]
        Vs = sb("Vs", [128, NT, 2, 66], BF16); b_Vs = [Buf() for _ in range(NT)]
        XW = 529
        Xk0 = sb("Xk0", [128, 2, XW], BF16)
        Xv0 = sb("Xv0", [128, 2, XW], BF16)
        Xk = [Xk0, Xk0]
        Xv = [Xv0, Xv0]
        bxk, bxv = Buf(), Buf()
        b_Xk = [bxk, bxk]
        b_Xv = [bxv, bxv]
        kcT = [sb("kcT0", [128, 256], BF16), sb("kcT1", [128, 256], BF16)]; b_kcT = Buf("kcT")
        VC = sb("VC", [128, 2, 2, 128], BF16); b_VC = Buf("VC")
        BA = sb("BA", [128, 2, 8, 128], F32R); b_BA = Buf()
        BB = sb("BB", [128, 2, 8, 128], F32R); b_BB = Buf()
        b_E = Buf()
        CM = sb("CM", [128, 16, 128], BF16); b_CM = Buf()
        TRI = sb("TRI", [128, 128], BF16); b_TRI = Buf()
        PAT = sb("PAT", [128, 192], BF16); b_PAT = Buf()
        GATE = sb("GATE", [128, NB, DM], F32); b_GATE = Buf()
        Gcol = sb("Gcol", [128, 8, NB], F32); b_Gcol = Buf()
        SHcol = sb("SHcol", [128, 8, NB], F32); b_SHcol = Buf()
        idf = sb("idf", [128, 128], F32); b_idf = Buf()
        idb = sb("idb", [128, 128], BF16); b_idb = Buf()
        idr = sb("idr", [128, 128], F32R); b_idr = Buf()
        gcols = sb("gcols", [128, 6], F32); b_gcols = Buf()
        esink = sb("esink", [128, 8], F32); b_esink = Buf()
        bgate = sb("bgate", [128, 24], F32); b_bgate = Buf()
        c31 = sb("c31", [128, 8], F32); b_c31 = Buf()


        stat = sb("stat", [128, 4], F32); b_stat = Buf()
        hT0 = sb("hT0", [128, 8, 128], BF16)
        hT = [hT0, hT0]
        bht = Buf()
        b_hT = [bht, bht]

        ssq = sb("ssq", [128, 16], F32); b_ssq = Buf()
        qn = [sb("qn0", [128, 512], BF16), sb("qn1", [128, 512], BF16)]
        b_qn = [Buf(), Buf()]
        qTa = sb("qTa", [128, 2, 4, 128], BF16); b_qTa = [Buf() for _ in range(2)]
        sz = sb("sz", [128, 2, DM], BF16); b_sz = [Buf() for _ in range(2)]
        scb = KE[0][:, :].bitcast(F32).rearrange("p (k b m) -> p k b m", k=8, b=NB)
        gts = sb("gts", [128, 2, 24], F32); b_gts = [Buf() for _ in range(2)]
        gtmp = sb("gtmp", [128, 24], F32); b_gtmp = Buf()
        zt2 = sb("zt2", [128, 512], F32); b_zt2 = Buf()
        hsig = sb("hsig", [128, 64], F32); b_hsig = Buf()
        nb1k = sb("nb1k", [128, 2], F32)
        nb1v = sb("nb1v", [128, 2], F32); b_nb1 = Buf()
        hidk = sb("hidk", [128, 2, 16], BF16); b_hidk = Buf()
        hidv = sb("hidv", [128, 2, 16], BF16); b_hidv = Buf()
        kc32 = sb("kc32", [16, 4], F32); b_kc32 = Buf()
        kndup = sb("kndup", [16, 128], BF16); b_kndup = Buf()
        kjunk = sb("kjunk", [16, 64], F32); b_kjunk = Buf()
        vstag = sb("vstag", [16, 64], BF16); b_vstag = Buf()
        PTs = [sb("PTs%d" % i, [128, 512], BF16) for i in range(2)]
        b_PTs = [Buf() for _ in range(2)]
        den = sb("den", [128, 16], F32); b_den = Buf()
        coef = sb("coef", [128, 16], F32); b_coef = Buf()
        impA = sb("impA", [128, 64], F32); b_impA = Buf()
        impB = sb("impB", [128, 64], F32); b_impB = Buf()
        score = sb("score", [128, 64], F32); b_score = Buf()
        wk = sb("wk", [128, 64], F32); b_wk = Buf()
        m8 = sb("m8", [128, 16], F32); b_m8 = Buf()
        negmg = [sb("negm0", [128, 128], F32), sb("negm1", [128, 128], F32)]
        b_negm = [Buf(), Buf()]
        QNt = sb("QNw", [128, 2, 2, 512], BF16)
        b_QNw = [[Buf(), Buf()], [Buf(), Buf()]]
        yB = sb("yB", [128, 2, 256], F32); b_yB = [Buf(), Buf()]
        maskAB = yB[:, :, :].rearrange("p a (b c) -> p (a b) c", b=2); b_maskAB = b_yB[0]
        ytmp = [sb("ytmp0", [128, 256], F32), sb("ytmp1", [128, 256], F32)]
        b_ytmp = [Buf(), Buf()]
        y = sb("y", [128, DM], BF16); b_y = Buf()
        sq = sb("sq", [128, 512], F32); b_sq = Buf()
        junk = sq[:, :].bitcast(BF16); b_junk = b_sq
        yT = sb("yT", [128, 8, 128], BF16); b_yT = Buf()
        rtmp = sb("rtmp", [128, 512], F32); b_rtmp = Buf()
        xn = sb("xn", [128, DM], F32); b_xn = Buf()
        scol = sb("scol", [128, 8, NB], F32); b_scol = Buf()

        modc = sb("modc", [128, 16, NB], F32); b_modc = Buf()
        posk = sb("posk", [128, 16], F32); b_posk = Buf()
        posv = sb("posv", [128, 16], F32); b_posv = Buf()
        poskb = sb("poskb", [128, 16], BF16); b_poskb = Buf()
        posvb = sb("posvb", [128, 16], BF16); b_posvb = Buf()
        badac = sb("badac", [128, 16], F32); b_badac = Buf()
        ngc = sb("ngc", [128, 8], F32); b_ngc = Buf()
        badag = sb("badag", [128, DM], F32) if False else None

        xs = [STG[0][:, 0:1024], STG[0][:, 1024:2048]]
        b_xs = b_stg[0]
        xr = [STG[1][:, 0:1024], STG[1][:, 1024:2048]]
        b_xr = b_stg[1]

        def pbc(d_ap, n):
            return bass.AP(tensor=d_ap.tensor, offset=d_ap.offset, ap=[[0, 128], [1, n]])

        if dumps:
            for (t_, bl_) in ((KE[0], b_kTs), (Vs, b_Vs), (Va, b_Va), (Vw, b_Vw), (qTa, b_qTa),
                              (sz, b_sz), (gts, b_gts), (hT0, [bht]), (QNt, b_QNw[0] + b_QNw[1]), (y, [b_y]),
                              (hidk, [b_hidk]), (score, [b_score]), (yB, b_yB), (den, [b_den]), (coef, [b_coef])):
                shp_ = list(t_.shape)
                S.op("pool", C("memset", t_[tuple(slice(None) for _ in shp_)], 0.0), writes=list(bl_))
        def ld(dst_ap, src_ap, buf):
            S.dma(C("dma_start", out=dst_ap, in_=src_ap), writes=[buf])

        ld(CM[:], cmask_d, b_CM)
        ld(TRI[:], tri_d, b_TRI)
        ld(PAT[:], pat_d, b_PAT)
        ld(gcols[:], gcols_d, b_gcols)
        ld(esink[:], pbc(sinks_d, 8), b_esink)
        ld(bgate[:], pbc(bgate_d, 24), b_bgate)
        ld(c31[:], pbc(c31_d, 8), b_c31)

        ld(scol[:], cT_d, b_scol)
        ld(posk[:], posk_d, b_posk)
        ld(posv[:], posv_d, b_posv)
        ld(badac[:], badac_d, b_badac)
        ld(ngc[:], ngc_d, b_ngc)
        S.op("pool", C("memset", idf[:], 0.0), writes=[b_idf])
        S.op("pool", C("affine_select", out=idf[:], in_=idf[:], pattern=[[-1, 128]], compare_op=ALU.not_equal,
                                                fill=1.0, base=0, channel_multiplier=1), reads=[b_idf], writes=[b_idf])
        S.op("dve", C("tensor_copy", out=idb[:], in_=idf[:]), reads=[b_idf], writes=[b_idb])
        S.op("dve", C("tensor_copy", out=idr[:], in_=idf[:]), reads=[b_idf], writes=[b_idr])
        S.op("act", C("activation", out=esink[:], in_=esink[:], func=AF.Exp), reads=[b_esink], writes=[b_esink])
        S.op("dve", C("tensor_scalar", out=gcols[:, 0:1], in0=gcols[:, 0:1], scalar1=0.125, scalar2=None, op0=ALU.mult),
             reads=[b_gcols], writes=[b_gcols])
        S.op("dve", C("tensor_scalar", out=gcols[:, 2:3], in0=gcols[:, 2:3], scalar1=0.125, scalar2=None, op0=ALU.mult),
             reads=[b_gcols], writes=[b_gcols])
        S.op("dve", C("tensor_copy", out=poskb[:], in_=posk[:]), reads=[b_posk], writes=[b_poskb])
        S.op("dve", C("tensor_copy", out=posvb[:], in_=posv[:]), reads=[b_posv], writes=[b_posvb])
        S.op("pool", C("memset", score[:], 0.0), writes=[b_score])
        S.op("pool", C("memset", score[:, 0:1], BONUS), reads=[b_score], writes=[b_score])
        for g_ in range(2):
            S.op("pool", C("memset", kcT[g_][:], 0.0), writes=[b_kcT])
            S.op("pool", C("memset", kTa[g_][:], 0.0), writes=b_kTa)
            S.op("pool", C("memset", kTw[g_][:], 0.0), writes=b_kTw)
        S.op("pool", C("memset", QNt[:, :, :, :], 0.0), writes=b_QNw[0] + b_QNw[1])
        S.op("pool", C("memset", VC[:], 0.0), writes=[b_VC])
        S.op("pool", C("memset", VC[:, :, :, 64:65], 1.0), reads=[b_VC], writes=[b_VC])
        S.op("pool", C("memset", VC[0:1, 0, :, 64:65], 0.0), reads=[b_VC], writes=[b_VC])
        for g in range(2):
            S.dma(C("dma_start", out=VC[:, :, g, 66:128], in_=ov_d), writes=[b_VC])
        for (Vt_, bV_) in ((Va, b_Va), (Vw, b_Vw), (Vs, b_Vs)):
            S.op("pool", C("memset", Vt_[:, :, :, 64:65], 1.0), writes=bV_)
            S.op("pool", C("memset", Vt_[:, :, :, 65:66], 0.0), writes=bV_)
        for s_ in range(1):
            S.op("pool", C("memset", Xk[s_][:], 0.0), writes=[b_Xk[s_]])
            S.op("pool", C("memset", Xv[s_][:], 0.0), writes=[b_Xv[s_]])

        stg_i = [0]

        def stage_load(src_ap, ncols, view=None):
            s_ = stg_i[0] % 2
            stg_i[0] += 1
            dst = STG[s_][:, 0:ncols] if view is None else view(STG[s_])
            S.dma(C("dma_start", out=dst, in_=src_ap), writes=b_stg[s_])
            return s_

        def cast_from_stage(eng, s_, dst_ap, ncols, dst_buf, src_view=None):
            src = STG[s_][:, 0:ncols] if src_view is None else src_view(STG[s_])
            S.op(eng, C("copy" if eng == "act" else "tensor_copy", out=dst_ap, in_=src), reads=b_stg[s_], writes=[dst_buf])

        ADA = pro >= 2
        S.op("act", C("activation", out=scol[:], in_=scol[:], func=AF.Silu), reads=[b_scol], writes=[b_scol])
        S.op("dve", C("tensor_copy", out=scb, in_=V(scol[:, 0, 0:1], [[NB, 8], [1, NB], [0, 128]])),
             reads=[b_scol], writes=[b_E])
        pj_i = [0]

        def next_pj():
            i_ = pj_i[0] % 2
            pj_i[0] += 1
            return i_

        for cg in range(12 if ADA else 0):
            src = wada_d[:, cg * 256:(cg + 1) * 256].rearrange("(kc p) c -> p kc c", p=128)
            s_ = stage_load(src, 2048, view=lambda t: t[:, :].rearrange("p (kc c) -> p kc c", kc=8))
            stv = STG[s_][:, :].rearrange("p (kc c) -> p kc c", kc=8)
            if cg < 8:
                for cc in range(2):
                    ch = cg * 2 + cc
                    pj = next_pj()
                    for kc in range(8):
                        S.op("pe", C("matmul",
                            PJ[:, pj, 0:NB], lhsT=stv[:, kc, cc * 128:(cc + 1) * 128], rhs=scol[:, kc, :],
                            start=(kc == 0), stop=(kc == 7)),
                            reads=b_stg[s_] + [b_scol], writes=[b_pj[pj]], inc=(kc == 7))
                    S.op("dve", C("tensor_scalar",
                        out=modc[:, ch, :], in0=PJ[:, pj, 0:NB], scalar1=badac[:, ch:ch + 1], scalar2=None, op0=ALU.add),
                        reads=[b_pj[pj], b_badac], writes=[b_modc])
            else:
                for b in range(NB):
                    pj = next_pj()
                    for kc in range(8):
                        S.op("pe", C("matmul",
                            PJ[:, pj, 0:256], lhsT=scb[:, kc, b, :], rhs=stv[:, kc, :],
                            start=(kc == 0), stop=(kc == 7)),
                            reads=b_stg[s_] + [b_E], writes=[b_pj[pj]], inc=(kc == 7))
                    c0 = (cg - 8) * 256
                    S.op("dve", C("tensor_copy", out=GATE[:, b, c0:c0 + 256], in_=PJ[:, pj, 0:256]),
                         reads=[b_pj[pj]], writes=[b_GATE])
        s_ = stage_load(pbc(badag_d, DM), DM)
        for b in range(NB if ADA else 0):
            S.op("dve", C("tensor_tensor", out=GATE[:, b, :], in0=GATE[:, b, :], in1=STG[s_][:, 0:DM], op=ALU.add),
                 reads=b_stg[s_] + [b_GATE], writes=[b_GATE])
        S.op("dve", C("tensor_copy", out=SHcol[:], in_=modc[:, 0:8, :]), reads=[b_modc], writes=[b_SHcol])
        S.op("dve", C("tensor_scalar", out=Gcol[:], in0=modc[:, 8:16, :], scalar1=1.0, scalar2=None, op0=ALU.add),
             reads=[b_modc], writes=[b_Gcol])
        S.op("dve", C("tensor_tensor", out=Gcol[:], in0=Gcol[:], in1=V(ngc[:, 0:1], [[1, 8], [0, NB]]), op=ALU.mult),
             reads=[b_Gcol, b_ngc], writes=[b_Gcol])

        S.dma(C("dma_start", out=KE[0][64:128, :], in_=E_d[64:128, :]), writes=[b_E])
        S.dma(C("dma_start", out=KE[1][0:64, :], in_=E_d[0:64, :]), writes=[b_E])
        for kc in range(8 if pro >= 3 else 0):
            for hx, (c0, c1) in enumerate(((0, 1676), (1676, WCOLS))):
                S.dma(C("dma_start", out=Win[:, kc, c0:c1], in_=win_d[kc * 128:(kc + 1) * 128, c0:c1]),
                      writes=[b_win_l[kc * 2 + hx]], q="pool")
        if pro >= 3:
            for (wd, wsb, wb) in ((w1k_d, w1k, b_w1k), (w1v_d, w1v, b_w1v)):
                for hlf in range(2):
                    S.dma(C("dma_start", out=wsb[:, hlf * 8:(hlf + 1) * 8, :],
                            in_=wd[hlf * 1024:(hlf + 1) * 1024, :].rearrange("(lp p) j -> p lp j", p=128)), writes=[wb], q="pool")
            for (wd, wsb, wb) in ((w2k_d, w2k, b_w2k), (w2v_d, w2v, b_w2v)):
                S.dma(C("dma_start", out=wsb[:], in_=wd.rearrange("(jc p) d -> p jc d", p=128)), writes=[wb], q="pool")
            for kc in range(8):
                S.dma(C("dma_start", out=Wout[:, kc, :], in_=wout_d[kc * 128:(kc + 1) * 128, :]), writes=[b_wout_l[kc]], q="pool")
        S.dma(C("dma_start", out=maskAB, in_=maskAB_d), writes=b_yB)
        for (bd, Bt, bB, m0, isB) in ((biasA_d, BA, b_BA, 0, False), (biasB_d, BB, b_BB, 2, True)) if pro >= 4 else ():
            s_ = stage_load(bd, 2048, view=lambda t: t[:, :].rearrange("p (a h c) -> p a h c", a=2, h=8))
            stv = STG[s_][:, :].rearrange("p (a h c) -> p a h c", a=2, h=8)
            for ty in range(2):
                S.op("dve", C("tensor_tensor",
                    out=stv[:, ty], in0=stv[:, ty], in1=V(maskAB[:, m0 + ty, 0:1], [[0, 8], [1, 128]]), op=ALU.add),
                    reads=b_stg[s_] + b_yB, writes=b_stg[s_])
                if isB:
                    S.op("dve", C("tensor_tensor",
                        out=stv[:, ty], in0=stv[:, ty], in1=V(c31[:, 0:1], [[1, 8], [0, 128]]), op=ALU.subtract),
                        reads=b_stg[s_] + [b_c31], writes=b_stg[s_])
            S.op("dve", C("tensor_copy", out=Bt[:, :, :, :], in_=stv), reads=b_stg[s_], writes=[bB])
        for (wsb, wb, pb, bpb, b1, bb1) in ((w1k, b_w1k, poskb, b_poskb, b1k, b_b1k), (w1v, b_w1v, posvb, b_posvb, b1v, b_b1v)) if pro >= 5 else ():
            for jc in range(2):
                pj = next_pj()
                for lp in range(16):
                    S.op("pe", C("matmul",
                        PJ[:, pj, 0:2], lhsT=wsb[:, lp, jc * 128:(jc + 1) * 128], rhs=V(pb[:, lp:lp + 1], [[0, 2]]),
                        start=(lp == 0), stop=(lp == 15)), reads=[wb, bpb], writes=[b_pj[pj]], inc=(lp == 15))
                S.op("dve", C("tensor_copy", out=b1[:, jc:jc + 1], in_=PJ[:, pj, 0:1]),
                     reads=[b_pj[pj]], writes=[bb1])

        if pro >= 5:
            S.op("dve", C("tensor_scalar", out=nb1k[:], in0=b1k[:], scalar1=-1.0, scalar2=None, op0=ALU.mult), reads=[b_b1k], writes=[b_nb1])
            S.op("dve", C("tensor_scalar", out=nb1v[:], in0=b1v[:], scalar1=-1.0, scalar2=None, op0=ALU.mult), reads=[b_b1v], writes=[b_nb1])
        st_i = [0]
        pts_i = [0]

        def build_PC(b, i):
            u = i % 2
            xsl = i % 2
            ra, rw = i % RA, i % RW
            CPb = PT[:, 1, :]
            CPbb = PTb[1]
            b_cp = b_pt[1]
            chains = []

            def proj(c0, n):
                pj = next_pj()
                for kc in range(8):
                    S.op("pe", C("matmul", PJ[:, pj, 0:n], lhsT=hT0[:, kc, :], rhs=Win[:, kc, c0:c0 + n],
                                 start=(kc == 0), stop=(kc == 7)),
                         reads=[bht, b_win] + b_win_l, writes=[b_pj[pj]], inc=(kc == 7))
                return pj

            def L0():
                S.dma(C("dma_start", out=xs[xsl], in_=x_d[b, i * 128:(i + 1) * 128, :]), writes=[b_xs[xsl]])

            def L1():
                S.op("act", C("activation", out=junk, in_=xs[xsl], func=AF.Square, accum_out=stat[:, 0:1]),
                     reads=[b_xs[xsl]], writes=[b_junk, b_stat])

            def L2():
                S.op("act", C("activation", out=stat[:, 1:2], in_=stat[:, 0:1], func=AF.Ln, scale=1.0 / DM, bias=EPS),
                     reads=[b_stat], writes=[b_stat])
                S.op("act", C("activation", out=stat[:, 2:3], in_=stat[:, 1:2], func=AF.Exp, scale=-0.5),
                     reads=[b_stat], writes=[b_stat])

            def L3():
                S.op("pool", C("tensor_scalar", out=xn[:], in0=xs[xsl], scalar1=stat[:, 2:3], scalar2=1.0,
                               op0=ALU.mult, op1=ALU.mult), reads=[b_xs[xsl], b_stat], writes=[b_xn])

            def LT(h):
                def f():
                    for kc in range(4 * h, 4 * h + 4):
                        S.op("pe", C("transpose", out=PT[:, 0, (kc % 4) * 128:(kc % 4 + 1) * 128],
                                     in_=xn[:, kc * 128:(kc + 1) * 128], identity=idf[:]),
                             reads=[b_xn, b_idf], writes=[b_pt[0]])
                return f

            def LE(h):
                def f():
                    for kc in range(4 * h, 4 * h + 4):
                        S.op("dve", C("tensor_scalar", out=hT0[:, kc, :], in0=PT[:, 0, (kc % 4) * 128:(kc % 4 + 1) * 128],
                                      scalar1=Gcol[:, kc, b:b + 1], scalar2=SHcol[:, kc, b:b + 1], op0=ALU.mult, op1=ALU.add),
                             reads=[b_pt[0], b_Gcol, b_SHcol], writes=[bht])
                return f
            chains.append((0, [L1, L2, L3, LT(0), LE(0), LT(1), LE(1)]))

            def norm_chain(c0, nh, qs, nchunks, evac):
                n = nh * 64
                st = {}

                def s0():
                    st["pj"] = proj(c0, n)

                def s1():
                    S.op("act", C("activation", out=sq[:, 0:n], in_=PJ[:, st["pj"], 0:n], func=AF.Square),
                         reads=[b_pj[st["pj"]]], writes=[b_sq])

                def s2():
                    S.op("dve", C("tensor_reduce", out=ssq[:, 0:nh], in_=sq[:, 0:n].rearrange("p (h d) -> p h d", d=64),
                                  axis=AX.X, op=ALU.add), reads=[b_sq], writes=[b_ssq])

                def s3():
                    S.op("act", C("activation", out=ssq[:, 0:nh], in_=ssq[:, 0:nh], func=AF.Ln, scale=1.0 / 64, bias=EPS),
                         reads=[b_ssq], writes=[b_ssq])
                    S.op("act", C("activation", out=ssq[:, 0:nh], in_=ssq[:, 0:nh], func=AF.Exp, scale=-0.5),
                         reads=[b_ssq], writes=[b_ssq])

                def s4():
                    pj = st["pj"]
                    S.op("dve", C("tensor_tensor", out=qn[qs][:, 0:n].rearrange("p (h d) -> p h d", d=64),
                                  in0=PJ[:, pj, 0:n].rearrange("p (h d) -> p h d", d=64),
                                  in1=V(ssq[:, 0:1], [[1, nh], [0, 64]]), op=ALU.mult),
                         reads=[b_pj[pj], b_ssq], writes=[b_qn[qs]])

                def s5():
                    for c in range(nchunks):
                        S.op("pe", C("transpose", out=PTb[0][:, c * 128:(c + 1) * 128], in_=qn[qs][:, c * 128:(c + 1) * 128], identity=idb[:]),
                             reads=[b_qn[qs], b_idb], writes=[b_pt[0]])
                return [s0, s1, s2, s3, s4, s5, evac]

            def evac_qa():
                S.op("dve", C("tensor_scalar", out=qTa[:, u].rearrange("p r t -> p (r t)"), in0=PTb[0][:, 0:512],
                              scalar1=gcols[:, 0:1], scalar2=None, op0=ALU.mult),
                     reads=[b_pt[0], b_gcols], writes=[b_qTa[u]])

            def evac_qb():
                for g_ in range(2):
                    hp_ = slice(g_ * 64, (g_ + 1) * 64)
                    S.op("dve", C("tensor_scalar", out=QNt[hp_, u, g_, :], in0=PTb[0][hp_, 0:512],
                                  scalar1=gcols[hp_, 2:3], scalar2=None, op0=ALU.mult),
                         reads=[b_pt[0], b_gcols], writes=[b_QNw[u][g_]])

            def evac_k():
                for g_ in range(2):
                    hp_ = slice(g_ * 64, (g_ + 1) * 64)
                    S.op("dve", C("tensor_scalar", out=kTa[g_][hp_, ra * 128:(ra + 1) * 128], in0=PTb[0][hp_, 0:128],
                                  scalar1=gcols[hp_, 1:2], scalar2=None, op0=ALU.mult),
                         reads=[b_pt[0], b_gcols], writes=[b_kTa[ra]])
                    S.op("dve", C("tensor_scalar", out=kTw[g_][hp_, rw * 128:(rw + 1) * 128], in0=PTb[0][hp_, 256:384],
                                  scalar1=gcols[hp_, 5:6], scalar2=None, op0=ALU.mult),
                         reads=[b_pt[0], b_gcols], writes=[b_kTw[rw]])
                    S.op("dve", C("tensor_scalar", out=KE[g_][hp_, i * 128:(i + 1) * 128], in0=PTb[0][hp_, 128:256],
                                  scalar1=gcols[hp_, 4:5], scalar2=None, op0=ALU.mult),
                         reads=[b_pt[0], b_gcols], writes=[b_kTs[i]])

            stc = {}
            tau0 = (i % 4) * 128

            def c0_():
                stc["pj"] = proj(G_C, 512)

            def c1_():
                S.op("act", C("copy", out=qn[1][:], in_=PJ[:, stc["pj"], :]), reads=[b_pj[stc["pj"]]], writes=[b_qn[1]])

            def c2_():
                for c in range(4):
                    S.op("pe", C("transpose", out=PTb[0][:, c * 128:(c + 1) * 128], in_=qn[1][:, c * 128:(c + 1) * 128], identity=idb[:]),
                         reads=[b_qn[1], b_idb], writes=[b_pt[0]])

            def c3_():
                for (Xt, bX, cb) in ((Xk, b_Xk, 0), (Xv, b_Xv, 256)):
                    S.op("dve", C("tensor_copy", out=Xt[0][0:64, :, 17 + tau0:17 + tau0 + 128],
                                  in_=PTb[0][0:64, cb:cb + 256].rearrange("p (g t) -> p g t", g=2)),
                         reads=[b_pt[0]], writes=[bX[0]])
                    S.op("dve", C("tensor_copy", out=Xt[0][64:128, :, 16 + tau0:16 + tau0 + 128],
                                  in_=PTb[0][64:128, cb:cb + 256].rearrange("p (g t) -> p g t", g=2)),
                         reads=[b_pt[0]], writes=[bX[0]])
            chains.append((8, [c0_, c1_, c2_, c3_]))
            chains.append((10, norm_chain(G_K, 6, 0, 3, evac_k)))

            stv_ = {}

            def v0_():
                stv_["pj"] = proj(G_V, 408)

            def v1_():
                pj = stv_["pj"]
                S.op("dve", C("tensor_copy", out=Va[:, ra, :, 0:64], in_=PJ[:, pj, 0:128].rearrange("p (g d) -> p g d", g=2)),
                     reads=[b_pj[pj]], writes=[b_Va[ra]])
                S.op("dve", C("tensor_copy", out=Vs[:, i, :, 0:64], in_=PJ[:, pj, 128:256].rearrange("p (g d) -> p g d", g=2)),
                     reads=[b_pj[pj]], writes=[b_Vs[i]])
                S.op("dve", C("tensor_copy", out=Vw[:, rw, :, 0:64], in_=PJ[:, pj, 256:384].rearrange("p (g d) -> p g d", g=2)),
                     reads=[b_pj[pj]], writes=[b_Vw[rw]])
                S.op("dve", C("tensor_tensor", out=gtmp[:], in0=PJ[:, pj, 384:408], in1=bgate[:], op=ALU.add),
                     reads=[b_pj[pj], b_bgate], writes=[b_gtmp])

            def v2_():
                S.op("act", C("activation", out=gtmp[:], in_=gtmp[:], func=AF.Exp, scale=-1.0), reads=[b_gtmp], writes=[b_gtmp])

            def v3_():
                S.op("dve", C("tensor_scalar", out=gtmp[:], in0=gtmp[:], scalar1=1.0, scalar2=None, op0=ALU.add),
                     reads=[b_gtmp], writes=[b_gtmp])
                S.op("dve", C("reciprocal", out=gts[:, u, :], in_=gtmp[:]), reads=[b_gtmp], writes=[b_gts[u]])
            chains.append((12, [v0_, v1_, v2_, v3_]))

            c0x = 128 * (i % 4)

            def cm(part):
                (Xt, bX, w1, bw1) = ((Xk, b_Xk, w1k, b_w1k), (Xv, b_Xv, w1v, b_w1v))[part // 2]
                jc = part % 2

                def f():
                    for lp in range(16):
                        S.op("pe", C("matmul", CPb[:, part * 16:(part + 1) * 16].rearrange("p (g m) -> p g m", g=2),
                                     lhsT=w1[:, lp, jc * 128:(jc + 1) * 128],
                                     rhs=Xt[0][:, :, c0x + 1 + 2 * lp:c0x + 1 + 2 * lp + 16 * 7 + 1:16],
                                     start=(lp == 0), stop=(lp == 15), skip_group_check=True),
                             reads=[bX[0], bw1], writes=[b_cp], inc=(lp == 15))
                return f

            def cm_act():
                for part in range(4):
                    nb1 = (nb1k, nb1v)[part // 2]
                    S.op("act", C("activation", out=hsig[:, part * 16:(part + 1) * 16], in_=CPb[:, part * 16:(part + 1) * 16],
                                  func=AF.Exp, scale=-1.0, bias=nb1[:, part % 2:part % 2 + 1]),
                         reads=[b_cp, b_nb1], writes=[b_hsig])

            def cm_dve1():
                S.op("dve", C("tensor_scalar", out=hsig[:], in0=hsig[:], scalar1=1.0, scalar2=None, op0=ALU.add),
                     reads=[b_hsig], writes=[b_hsig])
                S.op("dve", C("reciprocal", out=hsig[:], in_=hsig[:]), reads=[b_hsig], writes=[b_hsig])

            def cm_dve2():
                for part in range(4):
                    hid, bh = ((hidk, b_hidk), (hidv, b_hidv))[part // 2]
                    b1 = (b1k, b1v)[part // 2]
                    S.op("dve", C("scalar_tensor_tensor", out=hid[:, part % 2, :], in0=CPb[:, part * 16:(part + 1) * 16],
                                  scalar=b1[:, part % 2:part % 2 + 1], in1=hsig[:, part * 16:(part + 1) * 16],
                                  op0=ALU.add, op1=ALU.mult),
                         reads=[b_cp, b_b1k, b_b1v, b_hsig], writes=[bh])

            def cm_mm2():
                for jc in range(2):
                    S.op("pe", C("matmul", CPb[0:16, 64:128], lhsT=hidk[:, jc, :], rhs=w2k[:, jc, :],
                                 start=(jc == 0), stop=(jc == 1), skip_group_check=True),
                         reads=[b_hidk, b_w2k], writes=[b_cp], inc=False)
                for jc in range(2):
                    S.op("pe", C("matmul", CPb[0:16, 128:192], lhsT=hidv[:, jc, :], rhs=w2v[:, jc, :],
                                 start=(jc == 0), stop=(jc == 1), skip_group_check=True),
                         reads=[b_hidv, b_w2v], writes=[b_cp], inc=(jc == 1))

            def cm_a2():
                S.op("act", C("activation", out=kjunk[:], in_=CPb[0:16, 64:128], func=AF.Square, accum_out=kc32[:, 0:1]),
                     reads=[b_cp], writes=[b_kjunk, b_kc32])
                S.op("dve", C("tensor_copy", out=vstag[:], in_=CPb[0:16, 128:192]), reads=[b_cp, b_kc32], writes=[b_vstag])

            def cm_a3():
                S.op("act", C("activation", out=kc32[:, 1:2], in_=kc32[:, 0:1], func=AF.Ln, scale=1.0 / 64, bias=EPS),
                     reads=[b_kc32], writes=[b_kc32])
                S.op("act", C("activation", out=kc32[:, 2:3], in_=kc32[:, 1:2], func=AF.Exp, scale=-0.5),
                     reads=[b_kc32], writes=[b_kc32])
                n0 = 8 * i
                T, p0 = n0 // 128, n0 % 128
                m0 = 1 if i == 0 else 0
                for g in range(2):
                    S.dma(C("dma_start", out=VC[p0 + m0:p0 + 8, T, g, 0:64], in_=vstag[g * 8 + m0:(g + 1) * 8, :]),
                          reads=[b_vstag], writes=[b_VC])

            def cm_d3():
                for h2 in range(2):
                    S.op("dve", C("tensor_scalar", out=kndup[:, h2 * 64:(h2 + 1) * 64], in0=CPb[0:16, 64:128],
                                  scalar1=kc32[:, 2:3], scalar2=None, op0=ALU.mult),
                         reads=[b_cp, b_kc32], writes=[b_kndup])

            def cm_t():
                S.op("pe", C("transpose", out=PTb[0][:, 0:16], in_=kndup[:], identity=idb[0:16, 0:16]),
                     reads=[b_kndup, b_idb], writes=[b_pt[0]])

            def cm_e():
                n0 = 8 * i
                for g in range(2):
                    S.op("dve", C("tensor_scalar", out=kcT[g][g * 64:(g + 1) * 64, n0:n0 + 8],
                                  in0=PTb[0][g * 64:(g + 1) * 64, g * 8:(g + 1) * 8],
                                  scalar1=gcols[g * 64:(g + 1) * 64, 3:4], scalar2=None, op0=ALU.mult),
                         reads=[b_pt[0], b_gcols], writes=[b_kcT])
            nop = lambda: None
            import os as _os
            chains.append((12, [cm(0), cm(1), cm(2), cm(3), cm_act, cm_dve1, cm_dve2, cm_mm2, cm_a2, cm_a3, cm_d3, nop, nop, nop, cm_t, cm_e][:int(_os.environ.get('NCM', '99'))]))
            chains.append((15, norm_chain(G_QA, 8, 1, 4, evac_qa)))
            chains.append((18, norm_chain(G_QB, 8, 0, 4, evac_qb)))

            def z_chain(c0, zbuf, bz, col0):
                st = {}

                def s0():
                    st["pj"] = proj(c0, 512)

                def s1():
                    S.op("act", C("activation", out=zbuf, in_=PJ[:, st["pj"], :], func=AF.Exp, scale=-1.0),
                         reads=[b_pj[st["pj"]]], writes=[bz])

                def s2():
                    S.op("act", C("activation", out=zbuf, in_=zbuf, func=AF.Ln, bias=1.0), reads=[bz], writes=[bz])

                def s3():
                    S.op("act", C("activation", out=zbuf, in_=zbuf, func=AF.Exp, scale=-1.0), reads=[bz], writes=[bz])

                def s4():
                    pj = st["pj"]
                    S.op("dve", C("tensor_tensor", out=sz[:, u, col0:col0 + 512], in0=PJ[:, pj, :], in1=zbuf, op=ALU.mult),
                         reads=[b_pj[pj], bz], writes=[b_sz[u]])
                return [s0, s1, s2, s3, s4]
            chains.append((21, z_chain(G_ZA, sq[:, :], b_sq, 0)))
            chains.append((24, z_chain(G_ZB, zt2[:, :], b_zt2, 512)))
            return chains

        NSTEP = 32
        ntile = 4 * n_sb

        def x_load(b, i):
            S.dma(C("dma_start", out=xs[i % 2], in_=x_d[b, i * 128:(i + 1) * 128, :]), writes=[b_xs[i % 2]])

        def gen_steps(b, pc, tl):
            chains = []
            if pc is not None:
                if pc % 4 == 0 and pc > 0:
                    for (Xt, bX) in ((Xk, b_Xk), (Xv, b_Xv)):
                        S.op("pool", C("tensor_copy", out=Xt[0][:, :, 0:17], in_=Xt[0][:, :, 512:529]), reads=[bX[0]], writes=[bX[0]])
                chains += build_PC(b, pc)[:nch]
                if pc + 1 < ntile:
                    chains.append((10, [lambda: x_load(b, pc + 1)]))
            if tl is not None:
                chains.append(tail_chain(b, tl))
            T_ = max(o + len(ch) for o, ch in chains)
            assert T_ <= NSTEP, T_
            for tau in range(NSTEP):
                for o, ch in chains:
                    k = tau - o
                    if 0 <= k < len(ch):
                        ch[k]()
                yield

        def score_tile(lhsT_ap, rhs_ap, extra, reads):
            st = st_i[0] % 2
            st_i[0] += 1
            n = len(extra)
            S.op("pe", C("matmul", ST[:, st, :], lhsT=lhsT_ap, rhs=rhs_ap, start=True, stop=(n == 0)),
                 reads=reads, writes=[b_st[st]], inc=(n == 0))
            for j, (l_ap, r_ap, rd) in enumerate(extra):
                S.op("pe", C("matmul", ST[:, st, :], lhsT=l_ap, rhs=r_ap, start=False, stop=(j == n - 1)),
                     reads=rd, writes=[b_st[st]], inc=(j == n - 1))
            return st

        def exp_tile(st):
            p = pts_i[0] % 2
            pts_i[0] += 1
            S.op("act", C("activation", out=PTs[p][:], in_=ST[:, st, :], func=AF.Exp), reads=[b_st[st]], writes=[b_PTs[p]])
            return p

        def bc4(ap2d):
            return V(ap2d, [[0, 4], [1, 128]])

        def phaseA(b, i, u, nxt):
            gsl = gts[:, u, :]
            jobs = []

            def Bx(Bt, ty, g):
                return [(idr[:], Bt[:, ty, 4 * g:4 * g + 4, :].rearrange("p h t -> p (h t)"), [b_idr, b_BA if Bt is BA else b_BB])]

            def add_branch(g, kts, lhs_fn, rhs_fn, extras_fn, V_t, V_bufs, vidx, oab, post):
                nk = len(kts)
                for j, kt in enumerate(kts):
                    def fscore(kt=kt):
                        l_ap, l_rd = lhs_fn(kt)
                        r_ap, r_rd = rhs_fn(kt)
                        return score_tile(l_ap, r_ap, extras_fn(kt), l_rd + r_rd)

                    def pv(p, kt=kt, j=j):
                        for r in range(4):
                            S.op("pe", C("matmul", OA[:, oab, r * 66:(r + 1) * 66], lhsT=PTs[p][:, r * 128:(r + 1) * 128],
                                         rhs=V_t[:, vidx(kt), g, :], start=(j == 0 and r == 0), stop=(j == nk - 1), skip_group_check=True),
                                 reads=[b_PTs[p], V_bufs(kt)], writes=[b_oa[oab]], inc=(j == nk - 1 and r == 3))
                    jobs.append((fscore, pv, post if j == nk - 1 else None))

            def accumulate(g, oab, dcol, gate_br):
                S.op("dve", C("reciprocal", out=den[:, dcol:dcol + 4], in_=V(OA[:, oab, 64:65], [[66, 4]])),
                     reads=[b_oa[oab]], writes=[b_den])
                S.op("dve", C("tensor_tensor", out=coef[:, dcol:dcol + 4], in0=den[:, dcol:dcol + 4],
                              in1=gsl[:, g * 12 + gate_br:(g + 1) * 12:3], op=ALU.mult),
                     reads=[b_den, b_gts[u]], writes=[b_coef])
                S.op("dve", C("tensor_tensor", out=ytmp[oab][:, :].rearrange("p (r d) -> p r d", r=4),
                              in0=V(OA[:, oab, 0:1], [[66, 4], [1, 64]]),
                              in1=V(coef[:, dcol:dcol + 1], [[1, 4], [0, 64]]), op=ALU.mult),
                     reads=[b_oa[oab], b_coef], writes=[b_ytmp[oab]])
                S.op("dve", C("tensor_tensor", out=yB[:, g, :], in0=yB[:, g, :], in1=ytmp[oab][:, :], op=ALU.add),
                     reads=[b_yB[g], b_ytmp[oab]], writes=[b_yB[g]])

            Tmax = i // 16
            post_b_fns = {}
            for g in range(2):
                hp = slice(g * 64, (g + 1) * 64)
                qb_ap = QNt[hp, u, g, :]
                for T in range(Tmax + 1):
                    def fscore(T=T, hp=hp, qb_ap=qb_ap, g=g):
                        extra = []
                        if T == Tmax:
                            extra.append((idb[:], bc4(CM[:, i % 16, :]), [b_idb, b_CM]))
                        return score_tile(kcT[g][:, T * 128:(T + 1) * 128], QNt[:, u, g, :], extra, [b_kcT, b_QNw[u][g]])

                    def pv(p, T=T, g=g):
                        for r in range(4):
                            S.op("pe", C("matmul", OA[:, g, r * 128:(r + 1) * 128], lhsT=PTs[p][:, r * 128:(r + 1) * 128],
                                         rhs=VC[:, T, g, :], start=(T == 0 and r == 0), stop=(T == Tmax), skip_group_check=True),
                                 reads=[b_PTs[p], b_VC], writes=[b_oa[g]], inc=(T == Tmax and r == 3))

                    def post(st, g=g, hp=hp):
                        S.op("dve", C("tensor_scalar", out=den[:, 0:4], in0=V(OA[:, g, 64:65], [[128, 4]]),
                                      scalar1=1e-30, scalar2=None, op0=ALU.max),
                             reads=[b_oa[g]], writes=[b_den])
                        S.op("dve", C("reciprocal", out=den[:, 0:4], in_=den[:, 0:4]), reads=[b_den], writes=[b_den])
                        S.op("dve", C("tensor_tensor", out=coef[:, 0:4], in0=den[:, 0:4],
                                      in1=gsl[:, g * 12:(g + 1) * 12:3], op=ALU.mult),
                             reads=[b_den, b_gts[u]], writes=[b_coef])
                        pat_ap = PAT[:, 64 - 2 * i + 64:128 - 2 * i + 64]
                        for r in range(4):
                            dst, bdst = (impA, b_impA) if r % 2 == 0 else (impB, b_impB)
                            in0 = OA[:, g, r * 128 + 66:r * 128 + 128]
                            if r == 1:
                                S.op("dve", C("tensor_scalar", out=dst[:, 1:63], in0=in0, scalar1=den[:, r:r + 1], scalar2=None, op0=ALU.mult),
                                     reads=[b_oa[g], b_den], writes=[bdst])
                            else:
                                in1, rd1 = (pat_ap[:, 1:63], b_PAT) if r == 0 else (dst[:, 1:63], bdst)
                                S.op("dve", C("scalar_tensor_tensor", out=dst[:, 1:63], in0=in0, scalar=den[:, r:r + 1],
                                              in1=in1, op0=ALU.mult, op1=ALU.add),
                                     reads=[b_oa[g], b_den, rd1], writes=[bdst])
                        S.op("dve", C("tensor_tensor", out=yB[:, g, :].rearrange("p (r d) -> p r d", r=4),
                                      in0=V(OA[:, g, 0:1], [[128, 4], [1, 64]]), in1=V(coef[:, 0:1], [[1, 4], [0, 64]]), op=ALU.mult),
                             reads=[b_oa[g], b_coef], writes=[b_yB[g]])
                        S.op("dve", C("tensor_tensor", out=score[:, 1:63], in0=impA[:, 1:63], in1=impB[:, 1:63], op=ALU.add),
                             reads=[b_impA, b_impB], writes=[b_score])
                        if i == NT - 1:
                            S.op("dve", C("tensor_copy", out=score[:, 63:64], in_=pat_ap[:, 63:64]), reads=[b_PAT, b_score], writes=[b_score])
                        S.op("dve", C("max", out=m8[:, 0:8], in_=score[:]), reads=[b_score], writes=[b_m8])
                        S.op("dve", C("match_replace", out=wk[:], in_to_replace=m8[:, 0:8], in_values=score[:], imm_value=-1e30),
                             reads=[b_score, b_m8], writes=[b_wk])
                        S.op("dve", C("max", out=m8[:, 8:16], in_=wk[:]), reads=[b_wk], writes=[b_m8])
                        S.op("dve", C("tensor_scalar", out=negmg[g][:, :].rearrange("p (a j) -> p a j", a=2), in0=V(score[:, 0:1], [[0, 2], [1, 64]]),
                                      scalar1=m8[:, 15:16], scalar2=NEG,
                                      op0=ALU.is_lt, op1=ALU.mult), reads=[b_score, b_m8], writes=[b_negm[g]])

                    def post_b(g=g):
                        S.op("pe", C("transpose", out=OA[:, g, 384:512], in_=negmg[g][:], identity=idf[:]),
                             reads=[b_negm[g], b_idf], writes=[b_oa[g]])
                        op_ = slice((1 - g) * 64, (2 - g) * 64)
                        S.op("dve", C("tensor_copy", out=QNt[op_, u, g, :].rearrange("p (r t) -> p r t", r=4), in_=bc4(OA[op_, g, 384:512])),
                             reads=[b_oa[g]], writes=[b_QNw[u][g]])
                    if T == Tmax:
                        post_b_fns[g] = post_b
                    jobs.append((fscore, pv, post if T == Tmax else None))

            for g in range(2):
                hp = slice(g * 64, (g + 1) * 64)
                qa_ap = qTa[hp, u].rearrange("p r t -> p (r t)")

                def a_post(st, g=g):
                    oab = g
                    post_b_fns[g]()
                    S.op("dve", C("tensor_tensor", out=den[:, 12:16], in0=V(OA[:, oab, 64:65], [[66, 4]]),
                                  in1=esink[:, 4 * g:4 * g + 4], op=ALU.add),
                         reads=[b_oa[oab], b_esink], writes=[b_den])
                    S.op("dve", C("reciprocal", out=den[:, 12:16], in_=den[:, 12:16]), reads=[b_den], writes=[b_den])
                    S.op("dve", C("tensor_tensor", out=ytmp[oab][:, :].rearrange("p (r d) -> p r d", r=4),
                                  in0=V(OA[:, oab, 0:1], [[66, 4], [1, 64]]),
                                  in1=V(den[:, 12:13], [[1, 4], [0, 64]]), op=ALU.mult),
                         reads=[b_oa[oab], b_den], writes=[b_ytmp[oab]])
                    S.op("pool", C("tensor_tensor", out=y[:, g * 256:(g + 1) * 256], in0=ytmp[oab][:, :],
                                   in1=sz[:, u, g * 256:(g + 1) * 256], op=ALU.mult),
                         reads=[b_ytmp[oab], b_sz[u]], writes=[b_y])
                add_branch(g, list(range(max(0, i - 1), i + 1)),
                           lambda kt, g=g: (kTa[g][:, (kt % RA) * 128:(kt % RA) * 128 + 128], [b_kTa[kt % RA]]),
                           lambda kt: (qTa[:, u].rearrange("p r t -> p (r t)"), [b_qTa[u]]),
                           lambda kt, g=g: Bx(BA, 0, g) if kt == i else Bx(BA, 1, g),
                           Va, lambda kt: b_Va[kt % RA], lambda kt: kt % RA, g, a_post)
            for g in range(2):
                hp = slice(g * 64, (g + 1) * 64)
                qb_ap = QNt[hp, u, g, :]

                def win_extras(kt, g=g):
                    dk = i - kt
                    if dk == 0:
                        return Bx(BB, 0, g)
                    if dk == 1:
                        return Bx(BB, 1, g)
                    if dk == 4:
                        return [(idb[:], bc4(TRI[:, :]), [b_idb, b_TRI])]
                    return []
                add_branch(g, list(range(max(0, i - 4), i + 1)),
                           lambda kt, g=g: (kTw[g][:, (kt % RW) * 128:(kt % RW) * 128 + 128], [b_kTw[kt % RW]]),
                           lambda kt, g=g: (QNt[:, u, g, :], [b_QNw[u][g]]),
                           win_extras, Vw, lambda kt: b_Vw[kt % RW], lambda kt: kt % RW, g,
                           lambda st, g=g: accumulate(g, g, 8, 2))
            for g in range(2):
                hp = slice(g * 64, (g + 1) * 64)
                qb_ap = QNt[hp, u, g, :]

                def sel_lhs(kt, g=g, hp=hp):
                    return (KE[g][:, kt * 128:(kt + 1) * 128], [b_kTs[kt], b_E])

                def sel_rhs(kt, g=g, qb_ap=qb_ap):
                    return (QNt[:, u, g, :], [b_QNw[u][g]])

                def sel_extras(kt, g=g):
                    if kt == i:
                        return Bx(BB, 0, g)
                    if kt == i - 1:
                        return Bx(BB, 1, g)
                    return []

                def sel_post(st, g=g):
                    accumulate(g, g, 4, 1)
                    S.op("dve", C("tensor_tensor", out=y[:, 512 + g * 256:512 + (g + 1) * 256], in0=yB[:, g, :],
                                  in1=sz[:, u, 512 + g * 256:512 + (g + 1) * 256], op=ALU.mult),
                         reads=[b_yB[g], b_sz[u]], writes=[b_y])
                add_branch(g, list(range(0, i + 1)), sel_lhs, sel_rhs, sel_extras, Vs, lambda kt: b_Vs[kt], lambda kt: kt, g, sel_post)

            pend = None
            nj = len(jobs)
            sdone = 0
            ncmp = 2 * (Tmax + 1)
            for jx, (fscore, pv, post) in enumerate(jobs):
                st = fscore()
                want = ((jx + 1) * NSTEP) // nj
                if jx + 1 >= ncmp + 1:
                    want = max(2, want)
                while sdone < want:
                    next(nxt, None)
                    sdone += 1
                if pend is not None:
                    pst, ppv, ppost = pend
                    p = exp_tile(pst)
                    ppv(p)
                    if ppost is not None:
                        ppost(pst)
                pend = (st, pv, post)
            pst, ppv, ppost = pend
            p = exp_tile(pst)
            ppv(p)
            if ppost is not None:
                ppost(pst)

        def tail_chain(b, i):
            xrs = i % 2

            def t0():
                S.dma(C("dma_start", out=xr[xrs], in_=x_d[b, i * 128:(i + 1) * 128, :]), writes=[b_xr[xrs]])
                for kc in range(8):
                    S.op("pe", C("transpose", out=PTb[1][:, kc * 128:(kc + 1) * 128], in_=y[:, kc * 128:(kc + 1) * 128],
                                 identity=idb[:]), reads=[b_y, b_idb], writes=[b_pt[1]])

            def t1():
                S.op("act", C("copy", out=yT[:, :, :].rearrange("p k t -> p (k t)"), in_=PTb[1][:, :]), reads=[b_pt[1]], writes=[b_yT])
            stt = {}

            def mm(hf):
                def f():
                    pj = next_pj()
                    stt[hf] = pj
                    for kc in range(8):
                        S.op("pe", C("matmul", PJ[:, pj, :], lhsT=yT[:, kc, :], rhs=Wout[:, kc, hf * 512:(hf + 1) * 512],
                                     start=(kc == 0), stop=(kc == 7)),
                             reads=[b_yT, b_wout] + b_wout_l, writes=[b_pj[pj]], inc=(kc == 7))
                return f

            def ml(hf):
                def f():
                    pj = stt[hf]
                    S.op("dve", C("tensor_tensor", out=rtmp[:, :], in0=PJ[:, pj, :],
                                  in1=GATE[:, b, hf * 512:(hf + 1) * 512], op=ALU.mult),
                         reads=[b_pj[pj], b_GATE], writes=[b_rtmp])
                return f

            def t6(hf):
                def f():
                    S.op("dve", C("tensor_tensor", out=xr[xrs][:, hf * 512:(hf + 1) * 512], in0=xr[xrs][:, hf * 512:(hf + 1) * 512],
                                  in1=rtmp[:, :], op=ALU.add),
                         reads=[b_xr[xrs], b_rtmp], writes=[b_xr[xrs]])
                return f

            def t7():
                S.dma(C("dma_start", out=out_d[b, i * 128:(i + 1) * 128, :], in_=xr[xrs]), reads=[b_xr[xrs]], is_output=True)
            return (0, [t0, t1, mm(0), ml(0), t6(0), mm(1), ml(1), t6(1), t7])


        def drain(gen):
            for _ in gen:
                pass

        for b in range(nb):
            if b > 0:
                for g_ in range(2):
                    S.op("pool", C("memset", kcT[g_][:], 0.0), writes=[b_kcT])
                S.op("pool", C("memset", VC[:, :, :, 0:64], 0.0), writes=[b_VC])
                S.op("pool", C("memset", Xk[0][:, :, 0:17], 0.0), writes=[b_Xk[0]])
                S.op("pool", C("memset", Xv[0][:, :, 0:17], 0.0), writes=[b_Xv[0]])
                S.op("pool", C("memset", score[:, 63:64], 0.0), reads=[b_score], writes=[b_score])
            x_load(b, 0)
            drain(gen_steps(b, 0, None))
            for i in range(ntile):
                pc_ = i + 1 if i + 1 < ntile else None
                tl_ = i - 1 if i >= 1 else None
                nxt = gen_steps(b, pc_, tl_) if (pc_ is not None or tl_ is not None) else iter(())
                if "A" in stages and i < n_a:
                    phaseA(b, i, i % 2, nxt)
                drain(nxt)
            drain(gen_steps(b, None, ntile - 1))
        if dumps:
            L = dict(GATE=(GATE, [b_GATE]), Gcol=(Gcol, [b_Gcol]), SHcol=(SHcol, [b_SHcol]), BA=(BA, [b_BA]), BB=(BB, [b_BB]),
                     b1k=(b1k, [b_b1k]), b1v=(b1v, [b_b1v]), Win=(Win, [b_win]), Wout=(Wout, [b_wout]), w1k=(w1k, [b_w1k]),
                     qTa=(qTa, b_qTa), kTs=(KE[0], b_kTs),
                     Vs=(Vs, b_Vs), Va=(Va, b_Va), Vw=(Vw, b_Vw), VC=(VC, [b_VC]), sz=(sz, b_sz), gts=(gts, b_gts),
                      y=(y, [b_y]), hT=(hT0, [bht]),
                     Xk=(Xk0, [bxk]), hidk=(hidk, [b_hidk]), score=(score, [b_score]), yB=(yB, b_yB), esink=(esink, [b_esink]),
                     den=(den, [b_den]), coef=(coef, [b_coef]))
            for nm in dumps:
                t, bufs = L[nm]
                shp = list(t.shape)
                dd = nc.dram_tensor("d_" + nm, shp, t.dtype, kind="ExternalOutput").ap()
                full = t[tuple(slice(None) for _ in shp)]
                S.dma(C("dma_start", out=dd, in_=full), reads=list(bufs), is_output=True)
        S.emit()
    return nc


def _t5_bucket(dist):
    n = np.maximum(dist, 0)
    nf = np.maximum(n, 1).astype(np.float32)
    large = 16 + (np.log(nf / np.float32(16)) / np.float32(np.log(128 / 16)) * np.float32(16)).astype(np.int32)
    large = np.minimum(large, 31)
    return np.where(n < 16, n, large)


def _constants():
    bf = ml_dtypes.bfloat16
    sl = np.arange(128)[:, None]
    tl = np.arange(128)[None, :]
    d_diag = tl - sl
    d_prev = 128 + tl - sl
    idx = np.stack([_t5_bucket(d_diag), _t5_bucket(d_prev)], 0)
    maskAB = np.zeros((128, 4, 128), np.float32)
    maskAB[:, 0, :] = np.where(d_diag >= 0, 0.0, NEG)
    maskAB[:, 1, :] = np.where(d_prev < 128, 0.0, NEG)
    maskAB[:, 2, :] = np.where(d_diag >= 0, 0.0, NEG)
    maskAB[:, 3, :] = 0.0
    E = (np.arange(SEQ)[None, :] // 64 == (np.arange(128) % 64)[:, None]).astype(np.float32).astype(bf)
    nl = np.arange(128)[:, None, None]
    o = np.arange(16)[None, :, None]
    t3 = np.arange(128)[None, None, :]
    cmask = np.where(16 * nl + 15 <= 128 * o + t3, 0.0, NEG).astype(np.float32).astype(bf)
    tri = np.where(sl > tl, 0.0, NEG).astype(np.float32).astype(bf)
    pat = np.zeros((128, 192), np.float32)
    pat[:, 0] = BONUS
    hi = (np.arange(128) >= 64).astype(np.int64)
    pat[np.arange(128), 64 + 63 + hi] = BONUS
    pat[np.arange(128), 64 + 64 + hi] = BONUS
    pat = pat.astype(bf)
    npr = np.arange(256)
    n = npr - 1
    c_lo = 16 * n
    s_lo = 64 * np.arange(64)
    ov = np.clip(np.minimum(c_lo[:, None] + 32, s_lo[None, :] + 64) - np.maximum(c_lo[:, None], s_lo[None, :]), 0, None) / 32.0
    ov[0, :] = 0.0
    ov = np.ascontiguousarray(ov.reshape(2, 128, 64).transpose(1, 0, 2)[:, :, 1:63]).astype(np.float32).astype(bf)
    return idx, maskAB, E, cmask, tri, pat, ov


def _perm_cols():
    o_qa, o_ka, o_va, o_za, o_qb, o_kc, o_vc, o_ks, o_vs, o_kw, o_vw, o_zb, o_gb = (
        0, 512, 640, 768, 1280, 1792, 1920, 2048, 2176, 2304, 2432, 2560, 3072)
    r64 = np.arange(64)
    cols = []
    for base in (o_qa, o_qb):
        for r in range(4):
            cols += [base + r * 64 + r64, base + (4 + r) * 64 + r64]
    cols += [o_ka + np.arange(128), o_ks + np.arange(128), o_kw + np.arange(128)]
    for base in (o_kc, o_vc):
        for g in range(2):
            cols += [base + g * 64 + r64, base + g * 64 + r64]
    cols += [o_va + np.arange(128), o_vs + np.arange(128), o_vw + np.arange(128), o_gb + np.arange(24)]
    cols += [o_za + np.arange(512), o_zb + np.arange(512)]
    cols = np.concatenate(cols)
    assert cols.shape[0] == WCOLS
    return cols


_NC_CACHE = {}


def kernel(x, c, w_ada, b_ada, norm_gain, w_in, b_nsa_gate, q_gain_a, k_gain_a, sinks,
           q_gain_b, k_gain_cmp, k_gain_sel, k_gain_win, cmp_pos_k, cmp_pos_v,
           w_cmp_k1, w_cmp_k2, w_cmp_v1, w_cmp_v2, w_out, rel_bias):
    f = lambda a: np.ascontiguousarray(np.asarray(a, dtype=np.float32))
    x = f(x); c = f(c); w_ada = f(w_ada)[0]; b_ada = f(b_ada)[0]; norm_gain = f(norm_gain)[0]
    w_in = f(w_in)[0]; b_nsa_gate = f(b_nsa_gate)[0]; sinks = f(sinks)[0]
    rel_bias = f(rel_bias); w_out = f(w_out)[0]
    idx, maskAB, E, cmask, tri, pat, ov = _constants()
    biasAB = rel_bias[idx]
    biasA = np.ascontiguousarray(biasAB[..., 0:8].transpose(1, 0, 3, 2))
    biasB = np.ascontiguousarray(biasAB[..., 8:16].transpose(1, 0, 3, 2))
    c31 = np.ascontiguousarray(rel_bias[31:32, 8:16])
    gcols = np.stack([np.tile(f(g)[0], 2) for g in (q_gain_a, k_gain_a, q_gain_b, k_gain_cmp, k_gain_sel, k_gain_win)], 1)
    shared = {
        "w_ada": w_ada,
        "bada_col": np.ascontiguousarray(b_ada[0:2048].reshape(16, 128).T),
        "bada_gate": np.ascontiguousarray(b_ada[2048:3072].reshape(1, DM)),
        "ng_col": np.ascontiguousarray(norm_gain.reshape(8, 128).T),
        "w_in_p": np.ascontiguousarray(w_in[:, _perm_cols()]),
        "bgate": b_nsa_gate.reshape(1, 24),
        "gcols": np.ascontiguousarray(gcols),
        "sinks": sinks.reshape(1, 8),
        "posk": np.ascontiguousarray(f(cmp_pos_k)[0].reshape(16, 128).T),
        "posv": np.ascontiguousarray(f(cmp_pos_v)[0].reshape(16, 128).T),
        "w1k": f(w_cmp_k1)[0], "w1v": f(w_cmp_v1)[0], "w2k": f(w_cmp_k2)[0], "w2v": f(w_cmp_v2)[0],
        "w_out": w_out, "biasA": biasA, "biasB": biasB, "c31": c31, "maskAB": maskAB,
        "Emat": E, "cmask": cmask, "tri": tri, "pat": pat, "ov": ov,
    }
    in_maps = []
    for core in range(NCORES):
        m = dict(shared)
        m["x"] = x[NB * core:NB * (core + 1)]
        cc = c[NB * core:NB * (core + 1)]
        m["cT"] = np.ascontiguousarray(cc.reshape(NB, 8, 128).transpose(2, 1, 0))
        in_maps.append(m)
    if "nc" not in _NC_CACHE:
        _NC_CACHE["nc"] = build_program()
    res = run_bass_kernel_spmd(_NC_CACHE["nc"], in_maps, core_ids=list(range(NCORES)))
    return np.concatenate([np.asarray(r["out"], dtype=np.float32) for r in res.results], axis=0)
```

```python
import numpy as np
import ml_dtypes
from contextlib import ExitStack
import concourse.bass as bass
import concourse.mybir as mybir
from concourse.bass_utils import run_bass_kernel_spmd

F32 = mybir.dt.float32
BF16 = mybir.dt.bfloat16
F32R = mybir.dt.float32r
AF = mybir.ActivationFunctionType
ALU = mybir.AluOpType
AX = mybir.AxisListType

NCORES = 8
SEQ = 4096
DM = 1024
NT = SEQ // 128
NB = 2
WCOLS = 3352
NEG = -30000.0
BONUS = 1.0e4
EPS = 1e-6
G_QA, G_QB, G_K, G_C, G_V, G_ZA, G_ZB = 0, 512, 1024, 1408, 1920, 2328, 2840
RA, RW = 3, 6


class Buf:
    __slots__ = ("w", "r", "dsem", "dcount", "name")

    def __init__(self, name=""):
        self.w = None
        self.r = {}
        self.dsem = None
        self.dcount = 0
        self.name = name


class Sched:
    ENG = ("pe", "act", "dve", "pool", "sp")

    def __init__(self, nc, same_engine_sync=True):
        self.nc = nc
        self.ops = {e: [] for e in self.ENG}
        self.count = {e: 0 for e in self.ENG}
        self.waited = {e: {} for e in self.ENG}
        self.sems = {}
        self.dma_sems = []
        self.same_engine_sync = same_engine_sync
        self.out_waits = []

    def _need(self, eng, dep, needs):
        if dep is None:
            return
        k, v = dep
        if k == eng:
            if eng in ("pe", "sp"):
                return
            if not self.same_engine_sync:
                return
        if self.waited[eng].get(k, 0) >= v:
            return
        if needs.get(k, 0) < v:
            needs[k] = v

    def _emit_waits(self, eng, needs):
        for k, v in needs.items():
            self.ops[eng].append(("w", k, v))
            self.waited[eng][k] = v

    def op(self, eng, fn, reads=(), writes=(), inc=True):
        needs = {}
        for b in reads:
            self._need(eng, b.w, needs)
        for b in writes:
            self._need(eng, b.w, needs)
            for k, v in b.r.items():
                self._need(eng, (k, v), needs)
        self._emit_waits(eng, needs)
        if inc:
            self.count[eng] += 1
            val = self.count[eng]
        else:
            val = self.count[eng] + 1
        self.ops[eng].append(("op", fn, [(eng, 1)] if inc else []))
        for b in writes:
            b.w = (eng, val)
            b.r = {}
        for b in reads:
            if b.r.get(eng, 0) < val:
                b.r[eng] = val
        return val

    def dma(self, fn, reads=(), writes=(), q="sp", is_output=False):
        needs = {}
        for b in reads:
            self._need(q, b.w, needs)
        for b in writes:
            self._need(q, b.w, needs)
            for k, v in b.r.items():
                self._need(q, (k, v), needs)
        self._emit_waits(q, needs)
        owner = writes[0] if writes else reads[0]
        if owner.dsem is None:
            owner.dsem = "dma%d" % len(self.dma_sems)
            self.dma_sems.append(owner.dsem)
        owner.dcount += 16
        k, v = owner.dsem, owner.dcount
        self.ops[q].append(("op", fn, [(k, 16)]))
        for b in writes:
            b.w = (k, v)
            b.r = {}
        for b in reads:
            b.r[k] = v
        if is_output:
            self.out_waits.append((k, v))

    def emit(self):
        nc = self.nc
        with ExitStack() as es:
            for e in self.ENG:
                self.sems[e] = es.enter_context(nc.semaphore("prog_" + e))
            for k in self.dma_sems:
                self.sems[k] = es.enter_context(nc.semaphore(k))
            fin = {}
            for k, v in self.out_waits:
                fin[k] = max(fin.get(k, 0), v)
            for k, v in fin.items():
                self.ops["sp"].append(("w", k, v))
            block = es.enter_context(nc.Block())
            sems = self.sems
            ops = self.ops

            FUSE = ("activation", "copy", "tensor_tensor", "tensor_copy", "reciprocal", "tensor_reduce", "memset",
                    "tensor_scalar", "scalar_tensor_tensor", "max", "matmul", "transpose")

            def run(engine, lst, fuse=False):
                pend = []
                for item in lst:
                    if item[0] == "w":
                        pend.append(item)
                        continue
                    name, args, kw = item[1]
                    can = fuse and pend and name in FUSE and kw.get("accum_out") is None
                    for w in (pend[:-1] if can else pend):
                        engine.wait_ge(sems[w[1]], w[2])
                    ins = getattr(engine, name)(*args, **kw)
                    if can:
                        ins._wait_ge(sems[pend[-1][1]], pend[-1][2])
                    pend = []
                    for (k, n) in item[2]:
                        ins.then_inc(sems[k], n)
                for w in pend:
                    engine.wait_ge(sems[w[1]], w[2])

            @block.sync
            def _(e):
                run(e, ops["sp"])

            @block.tensor
            def _(e):
                run(e, ops["pe"], fuse=True)

            @block.scalar
            def _(e):
                run(e, ops["act"], fuse=True)

            @block.vector
            def _(e):
                run(e, ops["dve"], fuse=True)

            @block.gpsimd
            def _(e):
                run(e, ops["pool"])


def C(name, *args, **kw):
    return (name, args, kw)


def V(ap, dims):
    return bass.AP(tensor=ap.tensor, offset=ap.offset, ap=[list(ap.ap[0])] + [list(d) for d in dims])


def build_program(nb=NB, n_sb=NT // 4, stages="PCA", n_a=999, dumps=None, pro=9, nch=99):
    nc = bass.Bass("TRN2", target_bir_lowering=False)
    S = Sched(nc)

    def din(name, shape, dt=F32):
        return nc.dram_tensor(name, list(shape), dt, kind="ExternalInput").ap()

    x_d = din("x", [NB, SEQ, DM])
    cT_d = din("cT", [128, 8, NB])
    wada_d = din("w_ada", [DM, 3 * DM])
    badac_d = din("bada_col", [128, 16])
    badag_d = din("bada_gate", [1, DM])
    ngc_d = din("ng_col", [128, 8])
    win_d = din("w_in_p", [DM, WCOLS])
    bgate_d = din("bgate", [1, 24])
    gcols_d = din("gcols", [128, 6])
    sinks_d = din("sinks", [1, 8])
    posk_d = din("posk", [128, 16])
    posv_d = din("posv", [128, 16])
    w1k_d = din("w1k", [2048, 256])
    w1v_d = din("w1v", [2048, 256])
    w2k_d = din("w2k", [256, 64])
    w2v_d = din("w2v", [256, 64])
    wout_d = din("w_out", [DM, DM])
    biasA_d = din("biasA", [128, 2, 8, 128])
    biasB_d = din("biasB", [128, 2, 8, 128])
    c31_d = din("c31", [1, 8])
    maskAB_d = din("maskAB", [128, 4, 128])
    E_d = din("Emat", [128, SEQ], BF16)
    cmask_d = din("cmask", [128, 16, 128], BF16)
    tri_d = din("tri", [128, 128], BF16)
    pat_d = din("pat", [128, 192], BF16)
    ov_d = din("ov", [128, 2, 62], BF16)
    out_d = nc.dram_tensor("out", [NB, SEQ, DM], F32, kind="ExternalOutput").ap()

    with ExitStack() as es:
        def sb(name, shape, dt):
            return es.enter_context(nc.sbuf_tensor("s_" + name, list(shape), dt))

        def ps(name, shape, dt):
            return es.enter_context(nc.psum_tensor("p_" + name, list(shape), dt))

        PJ = ps("PJ", [128, 2, 512], F32)
        PT = ps("PT", [128, 2, 512], F32)
        ST = ps("ST", [128, 2, 512], F32)
        OA = ps("OA", [128, 2, 512], F32)
        b_pj = [Buf("pj0"), Buf("pj1")]
        b_pt = [Buf("pt0"), Buf("pt1")]
        b_st = [Buf("st0"), Buf("st1")]
        b_oa = [Buf("oa0"), Buf("oa1")]
        PTb = [PT[:, 0, :].bitcast(BF16), PT[:, 1, :].bitcast(BF16)]

        Win = sb("Win", [128, 8, WCOLS], BF16); b_win = Buf("win")
        b_win_l = [Buf("win%d" % i_) for i_ in range(16)]
        Wout = sb("Wout", [128, 8, DM], BF16); b_wout = Buf("wout")
        b_wout_l = [Buf("wout%d" % i_) for i_ in range(8)]
        w1k = sb("w1k", [128, 16, 256], BF16); b_w1k = Buf()
        w1v = sb("w1v", [128, 16, 256], BF16); b_w1v = Buf()
        w2k = sb("w2k", [128, 2, 64], BF16); b_w2k = Buf()
        w2v = sb("w2v", [128, 2, 64], BF16); b_w2v = Buf()
        b1k = sb("b1k", [128, 2], F32); b_b1k = Buf()
        b1v = sb("b1v", [128, 2], F32); b_b1v = Buf()
        STG = [sb("stg0", [128, 2048], F32), sb("stg1", [128, 2048], F32)]
        b_stg = [[Buf("s00"), Buf("s01")], [Buf("s10"), Buf("s11")]]
        kTa = [sb("kTa0", [128, RA * 128], BF16), sb("kTa1", [128, RA * 128], BF16)]; b_kTa = [Buf() for _ in range(RA)]
        kTw = [sb("kTw0", [128, RW * 128], BF16), sb("kTw1", [128, RW * 128], BF16)]; b_kTw = [Buf() for _ in range(RW)]
        KE = [sb("KE0", [128, SEQ], BF16), sb("KE1", [128, SEQ], BF16)]
        b_kTs = [Buf() for _ in range(NT)]
        Va = sb("Va", [128, RA, 2, 66], BF16); b_Va = [Buf() for _ in range(RA)]
        Vw = sb("Vw", [128, RW, 2, 66], BF16); b_Vw = [Buf() for _ in range(RW)]
        Vs = sb("Vs", [128, NT, 2, 66], BF16); b_Vs = [Buf() for _ in range(NT)]
        XW = 529
        Xk0 = sb("Xk0", [128, 2, XW], BF16)
        Xv0 = sb("Xv0", [128, 2, XW], BF16)
        Xk = [Xk0, Xk0]
        Xv = [Xv0, Xv0]
        bxk, bxv = Buf(), Buf()
        b_Xk = [bxk, bxk]
        b_Xv = [bxv, bxv]
        kcT = [sb("kcT0", [128, 256], BF16), sb("kcT1", [128, 256], BF16)]; b_kcT = Buf("kcT")
        VC = sb("VC", [128, 2, 2, 128], BF16); b_VC = Buf("VC")
        BA = sb("BA", [128, 2, 8, 128], F32R); b_BA = Buf()
        BB = sb("BB", [128, 2, 8, 128], F32R); b_BB = Buf()
        b_E = Buf()
        CM = sb("CM", [128, 16, 128], BF16); b_CM = Buf()
        TRI = sb("TRI", [128, 128], BF16); b_TRI = Buf()
        PAT = sb("PAT", [128, 192], BF16); b_PAT = Buf()
        GATE = sb("GATE", [128, NB, DM], F32); b_GATE = Buf()
        Gcol = sb("Gcol", [128, 8, NB], F32); b_Gcol = Buf()
        SHcol = sb("SHcol", [128, 8, NB], F32); b_SHcol = Buf()
        idf = sb("idf", [128, 128], F32); b_idf = Buf()
        idb = sb("idb", [128, 128], BF16); b_idb = Buf()
        idr = sb("idr", [128, 128], F32R); b_idr = Buf()
        gcols = sb("gcols", [128, 6], F32); b_gcols = Buf()
        esink = sb("esink", [128, 8], F32); b_esink = Buf()
        bgate = sb("bgate", [128, 24], F32); b_bgate = Buf()
        c31 = sb("c31", [128, 8], F32); b_c31 = Buf()


        stat = sb("stat", [128, 4], F32); b_stat = Buf()
        hT0 = sb("hT0", [128, 8, 128], BF16)
        hT = [hT0, hT0]
        bht = Buf()
        b_hT = [bht, bht]

        ssq = sb("ssq", [128, 16], F32); b_ssq = Buf()
        qn = [sb("qn0", [128, 512], BF16), sb("qn1", [128, 512], BF16)]
        b_qn = [Buf(), Buf()]
        qTa = sb("qTa", [128, 2, 4, 128], BF16); b_qTa = [Buf() for _ in range(2)]
        sz = sb("sz", [128, 2, DM], BF16); b_sz = [Buf() for _ in range(2)]
        scb = KE[0][:, :].bitcast(F32).rearrange("p (k b m) -> p k b m", k=8, b=NB)
        gts = sb("gts", [128, 2, 24], F32); b_gts = [Buf() for _ in range(2)]
        gtmp = sb("gtmp", [128, 24], F32); b_gtmp = Buf()
        zt2 = sb("zt2", [128, 512], F32); b_zt2 = Buf()
        hsig = sb("hsig", [128, 64], F32); b_hsig = Buf()
        nb1k = sb("nb1k", [128, 2], F32)
        nb1v = sb("nb1v", [128, 2], F32); b_nb1 = Buf()
        hidk = sb("hidk", [128, 2, 16], BF16); b_hidk = Buf()
        hidv = sb("hidv", [128, 2, 16], BF16); b_hidv = Buf()
        kc32 = sb("kc32", [16, 4], F32); b_kc32 = Buf()
        kndup = sb("kndup", [16, 128], BF16); b_kndup = Buf()
        kjunk = sb("kjunk", [16, 64], F32); b_kjunk = Buf()
        vstag = sb("vstag", [16, 64], BF16); b_vstag = Buf()
        PTs = [sb("PTs%d" % i, [128, 512], BF16) for i in range(2)]
        b_PTs = [Buf() for _ in range(2)]
        den = sb("den", [128, 16], F32); b_den = Buf()
        coef = sb("coef", [128, 16], F32); b_coef = Buf()
        impA = sb("impA", [128, 64], F32); b_impA = Buf()
        impB = sb("impB", [128, 64], F32); b_impB = Buf()
        score = sb("score", [128, 64], F32); b_score = Buf()
        wk = sb("wk", [128, 64], F32); b_wk = Buf()
        m8 = sb("m8", [128, 16], F32); b_m8 = Buf()
        negmg = [sb("negm0", [128, 128], F32), sb("negm1", [128, 128], F32)]
        b_negm = [Buf(), Buf()]
        QNt = sb("QNw", [128, 2, 2, 512], BF16)
        b_QNw = [[Buf(), Buf()], [Buf(), Buf()]]
        yB = sb("yB", [128, 2, 256], F32); b_yB = [Buf(), Buf()]
        maskAB = yB[:, :, :].rearrange("p a (b c) -> p (a b) c", b=2); b_maskAB = b_yB[0]
        ytmp = [sb("ytmp0", [128, 256], F32), sb("ytmp1", [128, 256], F32)]
        b_ytmp = [Buf(), Buf()]
        y = sb("y", [128, DM], BF16); b_y = Buf()
        sq = sb("sq", [128, 512], F32); b_sq = Buf()
        junk = sq[:, :].bitcast(BF16); b_junk = b_sq
        yT = sb("yT", [128, 8, 128], BF16); b_yT = Buf()
        rtmp = sb("rtmp", [128, 512], F32); b_rtmp = Buf()
        xn = sb("xn", [128, DM], F32); b_xn = Buf()
        scol = sb("scol", [128, 8, NB], F32); b_scol = Buf()

        modc = sb("modc", [128, 16, NB], F32); b_modc = Buf()
        posk = sb("posk", [128, 16], F32); b_posk = Buf()
        posv = sb("posv", [128, 16], F32); b_posv = Buf()
        poskb = sb("poskb", [128, 16], BF16); b_poskb = Buf()
        posvb = sb("posvb", [128, 16], BF16); b_posvb = Buf()
        badac = sb("badac", [128, 16], F32); b_badac = Buf()
        ngc = sb("ngc", [128, 8], F32); b_ngc = Buf()
        badag = sb("badag", [128, DM], F32) if False else None

        xs = [STG[0][:, 0:1024], STG[0][:, 1024:2048]]
        b_xs = b_stg[0]
        xr = [STG[1][:, 0:1024], STG[1][:, 1024:2048]]
        b_xr = b_stg[1]

        def pbc(d_ap, n):
            return bass.AP(tensor=d_ap.tensor, offset=d_ap.offset, ap=[[0, 128], [1, n]])

        if dumps:
            for (t_, bl_) in ((KE[0], b_kTs), (Vs, b_Vs), (Va, b_Va), (Vw, b_Vw), (qTa, b_qTa),
                              (sz, b_sz), (gts, b_gts), (hT0, [bht]), (QNt, b_QNw[0] + b_QNw[1]), (y, [b_y]),
                              (hidk, [b_hidk]), (score, [b_score]), (yB, b_yB), (den, [b_den]), (coef, [b_coef])):
                shp_ = list(t_.shape)
                S.op("pool", C("memset", t_[tuple(slice(None) for _ in shp_)], 0.0), writes=list(bl_))
        def ld(dst_ap, src_ap, buf):
            S.dma(C("dma_start", out=dst_ap, in_=src_ap), writes=[buf])

        ld(CM[:], cmask_d, b_CM)
        ld(TRI[:], tri_d, b_TRI)
        ld(PAT[:], pat_d, b_PAT)
        ld(gcols[:], gcols_d, b_gcols)
        ld(esink[:], pbc(sinks_d, 8), b_esink)
        ld(bgate[:], pbc(bgate_d, 24), b_bgate)
        ld(c31[:], pbc(c31_d, 8), b_c31)

        ld(scol[:], cT_d, b_scol)
        ld(posk[:], posk_d, b_posk)
        ld(posv[:], posv_d, b_posv)
        ld(badac[:], badac_d, b_badac)
        ld(ngc[:], ngc_d, b_ngc)
        S.op("pool", C("memset", idf[:], 0.0), writes=[b_idf])
        S.op("pool", C("affine_select", out=idf[:], in_=idf[:], pattern=[[-1, 128]], compare_op=ALU.not_equal,
                                                fill=1.0, base=0, channel_multiplier=1), reads=[b_idf], writes=[b_idf])
        S.op("dve", C("tensor_copy", out=idb[:], in_=idf[:]), reads=[b_idf], writes=[b_idb])
        S.op("dve", C("tensor_copy", out=idr[:], in_=idf[:]), reads=[b_idf], writes=[b_idr])
        S.op("act", C("activation", out=esink[:], in_=esink[:], func=AF.Exp), reads=[b_esink], writes=[b_esink])
        S.op("dve", C("tensor_scalar", out=gcols[:, 0:1], in0=gcols[:, 0:1], scalar1=0.125, scalar2=None, op0=ALU.mult),
             reads=[b_gcols], writes=[b_gcols])
        S.op("dve", C("tensor_scalar", out=gcols[:, 2:3], in0=gcols[:, 2:3], scalar1=0.125, scalar2=None, op0=ALU.mult),
             reads=[b_gcols], writes=[b_gcols])
        S.op("dve", C("tensor_copy", out=poskb[:], in_=posk[:]), reads=[b_posk], writes=[b_poskb])
        S.op("dve", C("tensor_copy", out=posvb[:], in_=posv[:]), reads=[b_posv], writes=[b_posvb])
        S.op("pool", C("memset", score[:], 0.0), writes=[b_score])
        S.op("pool", C("memset", score[:, 0:1], BONUS), reads=[b_score], writes=[b_score])
        for g_ in range(2):
            S.op("pool", C("memset", kcT[g_][:], 0.0), writes=[b_kcT])
            S.op("pool", C("memset", kTa[g_][:], 0.0), writes=b_kTa)
            S.op("pool", C("memset", kTw[g_][:], 0.0), writes=b_kTw)
        S.op("pool", C("memset", QNt[:, :, :, :], 0.0), writes=b_QNw[0] + b_QNw[1])
        S.op("pool", C("memset", VC[:], 0.0), writes=[b_VC])
        S.op("pool", C("memset", VC[:, :, :, 64:65], 1.0), reads=[b_VC], writes=[b_VC])
        S.op("pool", C("memset", VC[0:1, 0, :, 64:65], 0.0), reads=[b_VC], writes=[b_VC])
        for g in range(2):
            S.dma(C("dma_start", out=VC[:, :, g, 66:128], in_=ov_d), writes=[b_VC])
        for (Vt_, bV_) in ((Va, b_Va), (Vw, b_Vw), (Vs, b_Vs)):
            S.op("pool", C("memset", Vt_[:, :, :, 64:65], 1.0), writes=bV_)
            S.op("pool", C("memset", Vt_[:, :, :, 65:66], 0.0), writes=bV_)
        for s_ in range(1):
            S.op("pool", C("memset", Xk[s_][:], 0.0), writes=[b_Xk[s_]])
            S.op("pool", C("memset", Xv[s_][:], 0.0), writes=[b_Xv[s_]])

        stg_i = [0]

        def stage_load(src_ap, ncols, view=None):
            s_ = stg_i[0] % 2
            stg_i[0] += 1
            dst = STG[s_][:, 0:ncols] if view is None else view(STG[s_])
            S.dma(C("dma_start", out=dst, in_=src_ap), writes=b_stg[s_])
            return s_

        def cast_from_stage(eng, s_, dst_ap, ncols, dst_buf, src_view=None):
            src = STG[s_][:, 0:ncols] if src_view is None else src_view(STG[s_])
            S.op(eng, C("copy" if eng == "act" else "tensor_copy", out=dst_ap, in_=src), reads=b_stg[s_], writes=[dst_buf])

        ADA = pro >= 2
        S.op("act", C("activation", out=scol[:], in_=scol[:], func=AF.Silu), reads=[b_scol], writes=[b_scol])
        S.op("dve", C("tensor_copy", out=scb, in_=V(scol[:, 0, 0:1], [[NB, 8], [1, NB], [0, 128]])),
             reads=[b_scol], writes=[b_E])
        pj_i = [0]

        def next_pj():
            i_ = pj_i[0] % 2
            pj_i[0] += 1
            return i_

        for cg in range(12 if ADA else 0):
            src = wada_d[:, cg * 256:(cg + 1) * 256].rearrange("(kc p) c -> p kc c", p=128)
            s_ = stage_load(src, 2048, view=lambda t: t[:, :].rearrange("p (kc c) -> p kc c", kc=8))
            stv = STG[s_][:, :].rearrange("p (kc c) -> p kc c", kc=8)
            if cg < 8:
                for cc in range(2):
                    ch = cg * 2 + cc
                    pj = next_pj()
                    for kc in range(8):
                        S.op("pe", C("matmul",
                            PJ[:, pj, 0:NB], lhsT=stv[:, kc, cc * 128:(cc + 1) * 128], rhs=scol[:, kc, :],
                            start=(kc == 0), stop=(kc == 7)),
                            reads=b_stg[s_] + [b_scol], writes=[b_pj[pj]], inc=(kc == 7))
                    S.op("dve", C("tensor_scalar",
                        out=modc[:, ch, :], in0=PJ[:, pj, 0:NB], scalar1=badac[:, ch:ch + 1], scalar2=None, op0=ALU.add),
                        reads=[b_pj[pj], b_badac], writes=[b_modc])
            else:
                for b in range(NB):
                    pj = next_pj()
                    for kc in range(8):
                        S.op("pe", C("matmul",
                            PJ[:, pj, 0:256], lhsT=scb[:, kc, b, :], rhs=stv[:, kc, :],
                            start=(kc == 0), stop=(kc == 7)),
                            reads=b_stg[s_] + [b_E], writes=[b_pj[pj]], inc=(kc == 7))
                    c0 = (cg - 8) * 256
                    S.op("dve", C("tensor_copy", out=GATE[:, b, c0:c0 + 256], in_=PJ[:, pj, 0:256]),
                         reads=[b_pj[pj]], writes=[b_GATE])
        s_ = stage_load(pbc(badag_d, DM), DM)
        for b in range(NB if ADA else 0):
            S.op("dve", C("tensor_tensor", out=GATE[:, b, :], in0=GATE[:, b, :], in1=STG[s_][:, 0:DM], op=ALU.add),
                 reads=b_stg[s_] + [b_GATE], writes=[b_GATE])
        S.op("dve", C("tensor_copy", out=SHcol[:], in_=modc[:, 0:8, :]), reads=[b_modc], writes=[b_SHcol])
        S.op("dve", C("tensor_scalar", out=Gcol[:], in0=modc[:, 8:16, :], scalar1=1.0, scalar2=None, op0=ALU.add),
             reads=[b_modc], writes=[b_Gcol])
        S.op("dve", C("tensor_tensor", out=Gcol[:], in0=Gcol[:], in1=V(ngc[:, 0:1], [[1, 8], [0, NB]]), op=ALU.mult),
             reads=[b_Gcol, b_ngc], writes=[b_Gcol])

        S.dma(C("dma_start", out=KE[0][64:128, :], in_=E_d[64:128, :]), writes=[b_E])
        S.dma(C("dma_start", out=KE[1][0:64, :], in_=E_d[0:64, :]), writes=[b_E])
        for kc in range(8 if pro >= 3 else 0):
            for hx, (c0, c1) in enumerate(((0, 1676), (1676, WCOLS))):
                S.dma(C("dma_start", out=Win[:, kc, c0:c1], in_=win_d[kc * 128:(kc + 1) * 128, c0:c1]),
                      writes=[b_win_l[kc * 2 + hx]], q="pool")
        if pro >= 3:
            for (wd, wsb, wb) in ((w1k_d, w1k, b_w1k), (w1v_d, w1v, b_w1v)):
                for hlf in range(2):
                    S.dma(C("dma_start", out=wsb[:, hlf * 8:(hlf + 1) * 8, :],
                            in_=wd[hlf * 1024:(hlf + 1) * 1024, :].rearrange("(lp p) j -> p lp j", p=128)), writes=[wb], q="pool")
            for (wd, wsb, wb) in ((w2k_d, w2k, b_w2k), (w2v_d, w2v, b_w2v)):
                S.dma(C("dma_start", out=wsb[:], in_=wd.rearrange("(jc p) d -> p jc d", p=128)), writes=[wb], q="pool")
            for kc in range(8):
                S.dma(C("dma_start", out=Wout[:, kc, :], in_=wout_d[kc * 128:(kc + 1) * 128, :]), writes=[b_wout_l[kc]], q="pool")
        S.dma(C("dma_start", out=maskAB, in_=maskAB_d), writes=b_yB)
        for (bd, Bt, bB, m0, isB) in ((biasA_d, BA, b_BA, 0, False), (biasB_d, BB, b_BB, 2, True)) if pro >= 4 else ():
            s_ = stage_load(bd, 2048, view=lambda t: t[:, :].rearrange("p (a h c) -> p a h c", a=2, h=8))
            stv = STG[s_][:, :].rearrange("p (a h c) -> p a h c", a=2, h=8)
            for ty in range(2):
                S.op("dve", C("tensor_tensor",
                    out=stv[:, ty], in0=stv[:, ty], in1=V(maskAB[:, m0 + ty, 0:1], [[0, 8], [1, 128]]), op=ALU.add),
                    reads=b_stg[s_] + b_yB, writes=b_stg[s_])
                if isB:
                    S.op("dve", C("tensor_tensor",
                        out=stv[:, ty], in0=stv[:, ty], in1=V(c31[:, 0:1], [[1, 8], [0, 128]]), op=ALU.subtract),
                        reads=b_stg[s_] + [b_c31], writes=b_stg[s_])
            S.op("dve", C("tensor_copy", out=Bt[:, :, :, :], in_=stv), reads=b_stg[s_], writes=[bB])
        for (wsb, wb, pb, bpb, b1, bb1) in ((w1k, b_w1k, poskb, b_poskb, b1k, b_b1k), (w1v, b_w1v, posvb, b_posvb, b1v, b_b1v)) if pro >= 5 else ():
            for jc in range(2):
                pj = next_pj()
                for lp in range(16):
                    S.op("pe", C("matmul",
                        PJ[:, pj, 0:2], lhsT=wsb[:, lp, jc * 128:(jc + 1) * 128], rhs=V(pb[:, lp:lp + 1], [[0, 2]]),
                        start=(lp == 0), stop=(lp == 15)), reads=[wb, bpb], writes=[b_pj[pj]], inc=(lp == 15))
                S.op("dve", C("tensor_copy", out=b1[:, jc:jc + 1], in_=PJ[:, pj, 0:1]),
                     reads=[b_pj[pj]], writes=[bb1])

        if pro >= 5:
            S.op("dve", C("tensor_scalar", out=nb1k[:], in0=b1k[:], scalar1=-1.0, scalar2=None, op0=ALU.mult), reads=[b_b1k], writes=[b_nb1])
            S.op("dve", C("tensor_scalar", out=nb1v[:], in0=b1v[:], scalar1=-1.0, scalar2=None, op0=ALU.mult), reads=[b_b1v], writes=[b_nb1])
        st_i = [0]
        pts_i = [0]

        def build_PC(b, i):
            u = i % 2
            xsl = i % 2
            ra, rw = i % RA, i % RW
            CPb = PT[:, 1, :]
            CPbb = PTb[1]
            b_cp = b_pt[1]
            chains = []

            def proj(c0, n):
                pj = next_pj()
                for kc in range(8):
                    S.op("pe", C("matmul", PJ[:, pj, 0:n], lhsT=hT0[:, kc, :], rhs=Win[:, kc, c0:c0 + n],
                                 start=(kc == 0), stop=(kc == 7)),
                         reads=[bht, b_win] + b_win_l, writes=[b_pj[pj]], inc=(kc == 7))
                return pj

            def L0():
                S.dma(C("dma_start", out=xs[xsl], in_=x_d[b, i * 128:(i + 1) * 128, :]), writes=[b_xs[xsl]])

            def L1():
                S.op("act", C("activation", out=junk, in_=xs[xsl], func=AF.Square, accum_out=stat[:, 0:1]),
                     reads=[b_xs[xsl]], writes=[b_junk, b_stat])

            def L2():
                S.op("act", C("activation", out=stat[:, 1:2], in_=stat[:, 0:1], func=AF.Ln, scale=1.0 / DM, bias=EPS),
                     reads=[b_stat], writes=[b_stat])
                S.op("act", C("activation", out=stat[:, 2:3], in_=stat[:, 1:2], func=AF.Exp, scale=-0.5),
                     reads=[b_stat], writes=[b_stat])

            def L3():
                S.op("pool", C("tensor_scalar", out=xn[:], in0=xs[xsl], scalar1=stat[:, 2:3], scalar2=1.0,
                               op0=ALU.mult, op1=ALU.mult), reads=[b_xs[xsl], b_stat], writes=[b_xn])

            def LT(h):
                def f():
                    for kc in range(4 * h, 4 * h + 4):
                        S.op("pe", C("transpose", out=PT[:, 0, (kc % 4) * 128:(kc % 4 + 1) * 128],
                                     in_=xn[:, kc * 128:(kc + 1) * 128], identity=idf[:]),
                             reads=[b_xn, b_idf], writes=[b_pt[0]])
                return f

            def LE(h):
                def f():
                    for kc in range(4 * h, 4 * h + 4):
                        S.op("dve", C("tensor_scalar", out=hT0[:, kc, :], in0=PT[:, 0, (kc % 4) * 128:(kc % 4 + 1) * 128],
                                      scalar1=Gcol[:, kc, b:b + 1], scalar2=SHcol[:, kc, b:b + 1], op0=ALU.mult, op1=ALU.add),
                             reads=[b_pt[0], b_Gcol, b_SHcol], writes=[bht])
                return f
            chains.append((0, [L1, L2, L3, LT(0), LE(0), LT(1), LE(1)]))

            def norm_chain(c0, nh, qs, nchunks, evac):
                n = nh * 64
                st = {}

                def s0():
                    st["pj"] = proj(c0, n)

                def s1():
                    S.op("act", C("activation", out=sq[:, 0:n], in_=PJ[:, st["pj"], 0:n], func=AF.Square),
                         reads=[b_pj[st["pj"]]], writes=[b_sq])

                def s2():
                    S.op("dve", C("tensor_reduce", out=ssq[:, 0:nh], in_=sq[:, 0:n].rearrange("p (h d) -> p h d", d=64),
                                  axis=AX.X, op=ALU.add), reads=[b_sq], writes=[b_ssq])

                def s3():
                    S.op("act", C("activation", out=ssq[:, 0:nh], in_=ssq[:, 0:nh], func=AF.Ln, scale=1.0 / 64, bias=EPS),
                         reads=[b_ssq], writes=[b_ssq])
                    S.op("act", C("activation", out=ssq[:, 0:nh], in_=ssq[:, 0:nh], func=AF.Exp, scale=-0.5),
                         reads=[b_ssq], writes=[b_ssq])

                def s4():
                    pj = st["pj"]
                    S.op("dve", C("tensor_tensor", out=qn[qs][:, 0:n].rearrange("p (h d) -> p h d", d=64),
                                  in0=PJ[:, pj, 0:n].rearrange("p (h d) -> p h d", d=64),
                                  in1=V(ssq[:, 0:1], [[1, nh], [0, 64]]), op=ALU.mult),
                         reads=[b_pj[pj], b_ssq], writes=[b_qn[qs]])

                def s5():
                    for c in range(nchunks):
                        S.op("pe", C("transpose", out=PTb[0][:, c * 128:(c + 1) * 128], in_=qn[qs][:, c * 128:(c + 1) * 128], identity=idb[:]),
                             reads=[b_qn[qs], b_idb], writes=[b_pt[0]])
                return [s0, s1, s2, s3, s4, s5, evac]

            def evac_qa():
                S.op("dve", C("tensor_scalar", out=qTa[:, u].rearrange("p r t -> p (r t)"), in0=PTb[0][:, 0:512],
                              scalar1=gcols[:, 0:1], scalar2=None, op0=ALU.mult),
                     reads=[b_pt[0], b_gcols], writes=[b_qTa[u]])

            def evac_qb():
                for g_ in range(2):
                    hp_ = slice(g_ * 64, (g_ + 1) * 64)
                    S.op("dve", C("tensor_scalar", out=QNt[hp_, u, g_, :], in0=PTb[0][hp_, 0:512],
                                  scalar1=gcols[hp_, 2:3], scalar2=None, op0=ALU.mult),
                         reads=[b_pt[0], b_gcols], writes=[b_QNw[u][g_]])

            def evac_k():
                for g_ in range(2):
                    hp_ = slice(g_ * 64, (g_ + 1) * 64)
                    S.op("dve", C("tensor_scalar", out=kTa[g_][hp_, ra * 128:(ra + 1) * 128], in0=PTb[0][hp_, 0:128],
                                  scalar1=gcols[hp_, 1:2], scalar2=None, op0=ALU.mult),
                         reads=[b_pt[0], b_gcols], writes=[b_kTa[ra]])
                    S.op("dve", C("tensor_scalar", out=kTw[g_][hp_, rw * 128:(rw + 1) * 128], in0=PTb[0][hp_, 256:384],
                                  scalar1=gcols[hp_, 5:6], scalar2=None, op0=ALU.mult),
                         reads=[b_pt[0], b_gcols], writes=[b_kTw[rw]])
                    S.op("dve", C("tensor_scalar", out=KE[g_][hp_, i * 128:(i + 1) * 128], in0=PTb[0][hp_, 128:256],
                                  scalar1=gcols[hp_, 4:5], scalar2=None, op0=ALU.mult),
                         reads=[b_pt[0], b_gcols], writes=[b_kTs[i]])

            stc = {}
            tau0 = (i % 4) * 128

            def c0_():
                stc["pj"] = proj(G_C, 512)

            def c1_():
                S.op("act", C("copy", out=qn[1][:], in_=PJ[:, stc["pj"], :]), reads=[b_pj[stc["pj"]]], writes=[b_qn[1]])

            def c2_():
                for c in range(4):
                    S.op("pe", C("transpose", out=PTb[0][:, c * 128:(c + 1) * 128], in_=qn[1][:, c * 128:(c + 1) * 128], identity=idb[:]),
                         reads=[b_qn[1], b_idb], writes=[b_pt[0]])

            def c3_():
                for (Xt, bX, cb) in ((Xk, b_Xk, 0), (Xv, b_Xv, 256)):
                    S.op("dve", C("tensor_copy", out=Xt[0][0:64, :, 17 + tau0:17 + tau0 + 128],
                                  in_=PTb[0][0:64, cb:cb + 256].rearrange("p (g t) -> p g t", g=2)),
                         reads=[b_pt[0]], writes=[bX[0]])
                    S.op("dve", C("tensor_copy", out=Xt[0][64:128, :, 16 + tau0:16 + tau0 + 128],
                                  in_=PTb[0][64:128, cb:cb + 256].rearrange("p (g t) -> p g t", g=2)),
                         reads=[b_pt[0]], writes=[bX[0]])
            chains.append((8, [c0_, c1_, c2_, c3_]))
            chains.append((10, norm_chain(G_K, 6, 0, 3, evac_k)))

            stv_ = {}

            def v0_():
                stv_["pj"] = proj(G_V, 408)

            def v1_():
                pj = stv_["pj"]
                S.op("dve", C("tensor_copy", out=Va[:, ra, :, 0:64], in_=PJ[:, pj, 0:128].rearrange("p (g d) -> p g d", g=2)),
                     reads=[b_pj[pj]], writes=[b_Va[ra]])
                S.op("dve", C("tensor_copy", out=Vs[:, i, :, 0:64], in_=PJ[:, pj, 128:256].rearrange("p (g d) -> p g d", g=2)),
                     reads=[b_pj[pj]], writes=[b_Vs[i]])
                S.op("dve", C("tensor_copy", out=Vw[:, rw, :, 0:64], in_=PJ[:, pj, 256:384].rearrange("p (g d) -> p g d", g=2)),
                     reads=[b_pj[pj]], writes=[b_Vw[rw]])
                S.op("dve", C("tensor_tensor", out=gtmp[:], in0=PJ[:, pj, 384:408], in1=bgate[:], op=ALU.add),
                     reads=[b_pj[pj], b_bgate], writes=[b_gtmp])

            def v2_():
                S.op("act", C("activation", out=gtmp[:], in_=gtmp[:], func=AF.Exp, scale=-1.0), reads=[b_gtmp], writes=[b_gtmp])

            def v3_():
                S.op("dve", C("tensor_scalar", out=gtmp[:], in0=gtmp[:], scalar1=1.0, scalar2=None, op0=ALU.add),
                     reads=[b_gtmp], writes=[b_gtmp])
                S.op("dve", C("reciprocal", out=gts[:, u, :], in_=gtmp[:]), reads=[b_gtmp], writes=[b_gts[u]])
            chains.append((12, [v0_, v1_, v2_, v3_]))

            c0x = 128 * (i % 4)

            def cm(part):
                (Xt, bX, w1, bw1) = ((Xk, b_Xk, w1k, b_w1k), (Xv, b_Xv, w1v, b_w1v))[part // 2]
                jc = part % 2

                def f():
                    for lp in range(16):
                        S.op("pe", C("matmul", CPb[:, part * 16:(part + 1) * 16].rearrange("p (g m) -> p g m", g=2),
                                     lhsT=w1[:, lp, jc * 128:(jc + 1) * 128],
                                     rhs=Xt[0][:, :, c0x + 1 + 2 * lp:c0x + 1 + 2 * lp + 16 * 7 + 1:16],
                                     start=(lp == 0), stop=(lp == 15), skip_group_check=True),
                             reads=[bX[0], bw1], writes=[b_cp], inc=(lp == 15))
                return f

            def cm_act():
                for part in range(4):
                    nb1 = (nb1k, nb1v)[part // 2]
                    S.op("act", C("activation", out=hsig[:, part * 16:(part + 1) * 16], in_=CPb[:, part * 16:(part + 1) * 16],
                                  func=AF.Exp, scale=-1.0, bias=nb1[:, part % 2:part % 2 + 1]),
                         reads=[b_cp, b_nb1], writes=[b_hsig])

            def cm_dve1():
                S.op("dve", C("tensor_scalar", out=hsig[:], in0=hsig[:], scalar1=1.0, scalar2=None, op0=ALU.add),
                     reads=[b_hsig], writes=[b_hsig])
                S.op("dve", C("reciprocal", out=hsig[:], in_=hsig[:]), reads=[b_hsig], writes=[b_hsig])

            def cm_dve2():
                for part in range(4):
                    hid, bh = ((hidk, b_hidk), (hidv, b_hidv))[part // 2]
                    b1 = (b1k, b1v)[part // 2]
                    S.op("dve", C("scalar_tensor_tensor", out=hid[:, part % 2, :], in0=CPb[:, part * 16:(part + 1) * 16],
                                  scalar=b1[:, part % 2:part % 2 + 1], in1=hsig[:, part * 16:(part + 1) * 16],
                                  op0=ALU.add, op1=ALU.mult),
                         reads=[b_cp, b_b1k, b_b1v, b_hsig], writes=[bh])

            def cm_mm2():
                for jc in range(2):
                    S.op("pe", C("matmul", CPb[0:16, 64:128], lhsT=hidk[:, jc, :], rhs=w2k[:, jc, :],
                                 start=(jc == 0), stop=(jc == 1), skip_group_check=True),
                         reads=[b_hidk, b_w2k], writes=[b_cp], inc=False)
                for jc in range(2):
                    S.op("pe", C("matmul", CPb[0:16, 128:192], lhsT=hidv[:, jc, :], rhs=w2v[:, jc, :],
                                 start=(jc == 0), stop=(jc == 1), skip_group_check=True),
                         reads=[b_hidv, b_w2v], writes=[b_cp], inc=(jc == 1))

            def cm_a2():
                S.op("act", C("activation", out=kjunk[:], in_=CPb[0:16, 64:128], func=AF.Square, accum_out=kc32[:, 0:1]),
                     reads=[b_cp], writes=[b_kjunk, b_kc32])
                S.op("dve", C("tensor_copy", out=vstag[:], in_=CPb[0:16, 128:192]), reads=[b_cp, b_kc32], writes=[b_vstag])

            def cm_a3():
                S.op("act", C("activation", out=kc32[:, 1:2], in_=kc32[:, 0:1], func=AF.Ln, scale=1.0 / 64, bias=EPS),
                     reads=[b_kc32], writes=[b_kc32])
                S.op("act", C("activation", out=kc32[:, 2:3], in_=kc32[:, 1:2], func=AF.Exp, scale=-0.5),
                     reads=[b_kc32], writes=[b_kc32])
                n0 = 8 * i
                T, p0 = n0 // 128, n0 % 128
                m0 = 1 if i == 0 else 0
                for g in range(2):
                    S.dma(C("dma_start", out=VC[p0 + m0:p0 + 8, T, g, 0:64], in_=vstag[g * 8 + m0:(g + 1) * 8, :]),
                          reads=[b_vstag], writes=[b_VC])

            def cm_d3():
                for h2 in range(2):
                    S.op("dve", C("tensor_scalar", out=kndup[:, h2 * 64:(h2 + 1) * 64], in0=CPb[0:16, 64:128],
                                  scalar1=kc32[:, 2:3], scalar2=None, op0=ALU.mult),
                         reads=[b_cp, b_kc32], writes=[b_kndup])

            def cm_t():
                S.op("pe", C("transpose", out=PTb[0][:, 0:16], in_=kndup[:], identity=idb[0:16, 0:16]),
                     reads=[b_kndup, b_idb], writes=[b_pt[0]])

            def cm_e():
                n0 = 8 * i
                for g in range(2):
                    S.op("dve", C("tensor_scalar", out=kcT[g][g * 64:(g + 1) * 64, n0:n0 + 8],
                                  in0=PTb[0][g * 64:(g + 1) * 64, g * 8:(g + 1) * 8],
                                  scalar1=gcols[g * 64:(g + 1) * 64, 3:4], scalar2=None, op0=ALU.mult),
                         reads=[b_pt[0], b_gcols], writes=[b_kcT])
            nop = lambda: None
            import os as _os
            chains.append((12, [cm(0), cm(1), cm(2), cm(3), cm_act, cm_dve1, cm_dve2, cm_mm2, cm_a2, cm_a3, cm_d3, nop, nop, nop, cm_t, cm_e][:int(_os.environ.get('NCM', '99'))]))
            chains.append((15, norm_chain(G_QA, 8, 1, 4, evac_qa)))
            chains.append((18, norm_chain(G_QB, 8, 0, 4, evac_qb)))

            def z_chain(c0, zbuf, bz, col0):
                st = {}

                def s0():
                    st["pj"] = proj(c0, 512)

                def s1():
                    S.op("act", C("activation", out=zbuf, in_=PJ[:, st["pj"], :], func=AF.Exp, scale=-1.0),
                         reads=[b_pj[st["pj"]]], writes=[bz])

                def s2():
                    S.op("act", C("activation", out=zbuf, in_=zbuf, func=AF.Ln, bias=1.0), reads=[bz], writes=[bz])

                def s3():
                    S.op("act", C("activation", out=zbuf, in_=zbuf, func=AF.Exp, scale=-1.0), reads=[bz], writes=[bz])

                def s4():
                    pj = st["pj"]
                    S.op("dve", C("tensor_tensor", out=sz[:, u, col0:col0 + 512], in0=PJ[:, pj, :], in1=zbuf, op=ALU.mult),
                         reads=[b_pj[pj], bz], writes=[b_sz[u]])
                return [s0, s1, s2, s3, s4]
            chains.append((21, z_chain(G_ZA, sq[:, :], b_sq, 0)))
            chains.append((24, z_chain(G_ZB, zt2[:, :], b_zt2, 512)))
            return chains

        NSTEP = 32
        ntile = 4 * n_sb

        def x_load(b, i):
            S.dma(C("dma_start", out=xs[i % 2], in_=x_d[b, i * 128:(i + 1) * 128, :]), writes=[b_xs[i % 2]])

        def gen_steps(b, pc, tl):
            chains = []
            if pc is not None:
                if pc % 4 == 0 and pc > 0:
                    for (Xt, bX) in ((Xk, b_Xk), (Xv, b_Xv)):
                        S.op("pool", C("tensor_copy", out=Xt[0][:, :, 0:17], in_=Xt[0][:, :, 512:529]), reads=[bX[0]], writes=[bX[0]])
                chains += build_PC(b, pc)[:nch]
                if pc + 1 < ntile:
                    chains.append((10, [lambda: x_load(b, pc + 1)]))
            if tl is not None:
                chains.append(tail_chain(b, tl))
            T_ = max(o + len(ch) for o, ch in chains)
            assert T_ <= NSTEP, T_
            for tau in range(NSTEP):
                for o, ch in chains:
                    k = tau - o
                    if 0 <= k < len(ch):
                        ch[k]()
                yield

        def score_tile(lhsT_ap, rhs_ap, extra, reads):
            st = st_i[0] % 2
            st_i[0] += 1
            n = len(extra)
            S.op("pe", C("matmul", ST[:, st, :], lhsT=lhsT_ap, rhs=rhs_ap, start=True, stop=(n == 0)),
                 reads=reads, writes=[b_st[st]], inc=(n == 0))
            for j, (l_ap, r_ap, rd) in enumerate(extra):
                S.op("pe", C("matmul", ST[:, st, :], lhsT=l_ap, rhs=r_ap, start=False, stop=(j == n - 1)),
                     reads=rd, writes=[b_st[st]], inc=(j == n - 1))
            return st

        def exp_tile(st):
            p = pts_i[0] % 2
            pts_i[0] += 1
            S.op("act", C("activation", out=PTs[p][:], in_=ST[:, st, :], func=AF.Exp), reads=[b_st[st]], writes=[b_PTs[p]])
            return p

        def bc4(ap2d):
            return V(ap2d, [[0, 4], [1, 128]])

        def phaseA(b, i, u, nxt):
            gsl = gts[:, u, :]
            jobs = []

            def Bx(Bt, ty, g):
                return [(idr[:], Bt[:, ty, 4 * g:4 * g + 4, :].rearrange("p h t -> p (h t)"), [b_idr, b_BA if Bt is BA else b_BB])]

            def add_branch(g, kts, lhs_fn, rhs_fn, extras_fn, V_t, V_bufs, vidx, oab, post):
                nk = len(kts)
                for j, kt in enumerate(kts):
                    def fscore(kt=kt):
                        l_ap, l_rd = lhs_fn(kt)
                        r_ap, r_rd = rhs_fn(kt)
                        return score_tile(l_ap, r_ap, extras_fn(kt), l_rd + r_rd)

                    def pv(p, kt=kt, j=j):
                        for r in range(4):
                            S.op("pe", C("matmul", OA[:, oab, r * 66:(r + 1) * 66], lhsT=PTs[p][:, r * 128:(r + 1) * 128],
                                         rhs=V_t[:, vidx(kt), g, :], start=(j == 0 and r == 0), stop=(j == nk - 1), skip_group_check=True),
                                 reads=[b_PTs[p], V_bufs(kt)], writes=[b_oa[oab]], inc=(j == nk - 1 and r == 3))
                    jobs.append((fscore, pv, post if j == nk - 1 else None))

            def accumulate(g, oab, dcol, gate_br):
                S.op("dve", C("reciprocal", out=den[:, dcol:dcol + 4], in_=V(OA[:, oab, 64:65], [[66, 4]])),
                     reads=[b_oa[oab]], writes=[b_den])
                S.op("dve", C("tensor_tensor", out=coef[:, dcol:dcol + 4], in0=den[:, dcol:dcol + 4],
                              in1=gsl[:, g * 12 + gate_br:(g + 1) * 12:3], op=ALU.mult),
                     reads=[b_den, b_gts[u]], writes=[b_coef])
                S.op("dve", C("tensor_tensor", out=ytmp[oab][:, :].rearrange("p (r d) -> p r d", r=4),
                              in0=V(OA[:, oab, 0:1], [[66, 4], [1, 64]]),
                              in1=V(coef[:, dcol:dcol + 1], [[1, 4], [0, 64]]), op=ALU.mult),
                     reads=[b_oa[oab], b_coef], writes=[b_ytmp[oab]])
                S.op("dve", C("tensor_tensor", out=yB[:, g, :], in0=yB[:, g, :], in1=ytmp[oab][:, :], op=ALU.add),
                     reads=[b_yB[g], b_ytmp[oab]], writes=[b_yB[g]])

            Tmax = i // 16
            post_b_fns = {}
            for g in range(2):
                hp = slice(g * 64, (g + 1) * 64)
                qb_ap = QNt[hp, u, g, :]
                for T in range(Tmax + 1):
                    def fscore(T=T, hp=hp, qb_ap=qb_ap, g=g):
                        extra = []
                        if T == Tmax:
                            extra.append((idb[:], bc4(CM[:, i % 16, :]), [b_idb, b_CM]))
                        return score_tile(kcT[g][:, T * 128:(T + 1) * 128], QNt[:, u, g, :], extra, [b_kcT, b_QNw[u][g]])

                    def pv(p, T=T, g=g):
                        for r in range(4):
                            S.op("pe", C("matmul", OA[:, g, r * 128:(r + 1) * 128], lhsT=PTs[p][:, r * 128:(r + 1) * 128],
                                         rhs=VC[:, T, g, :], start=(T == 0 and r == 0), stop=(T == Tmax), skip_group_check=True),
                                 reads=[b_PTs[p], b_VC], writes=[b_oa[g]], inc=(T == Tmax and r == 3))

                    def post(st, g=g, hp=hp):
                        S.op("dve", C("tensor_scalar", out=den[:, 0:4], in0=V(OA[:, g, 64:65], [[128, 4]]),
                                      scalar1=1e-30, scalar2=None, op0=ALU.max),
                             reads=[b_oa[g]], writes=[b_den])
                        S.op("dve", C("reciprocal", out=den[:, 0:4], in_=den[:, 0:4]), reads=[b_den], writes=[b_den])
                        S.op("dve", C("tensor_tensor", out=coef[:, 0:4], in0=den[:, 0:4],
                                      in1=gsl[:, g * 12:(g + 1) * 12:3], op=ALU.mult),
                             reads=[b_den, b_gts[u]], writes=[b_coef])
                        pat_ap = PAT[:, 64 - 2 * i + 64:128 - 2 * i + 64]
                        for r in range(4):
                            dst, bdst = (impA, b_impA) if r % 2 == 0 else (impB, b_impB)
                            in0 = OA[:, g, r * 128 + 66:r * 128 + 128]
                            if r == 1:
                                S.op("dve", C("tensor_scalar", out=dst[:, 1:63], in0=in0, scalar1=den[:, r:r + 1], scalar2=None, op0=ALU.mult),
                                     reads=[b_oa[g], b_den], writes=[bdst])
                            else:
                                in1, rd1 = (pat_ap[:, 1:63], b_PAT) if r == 0 else (dst[:, 1:63], bdst)
                                S.op("dve", C("scalar_tensor_tensor", out=dst[:, 1:63], in0=in0, scalar=den[:, r:r + 1],
                                              in1=in1, op0=ALU.mult, op1=ALU.add),
                                     reads=[b_oa[g], b_den, rd1], writes=[bdst])
                        S.op("dve", C("tensor_tensor", out=yB[:, g, :].rearrange("p (r d) -> p r d", r=4),
                                      in0=V(OA[:, g, 0:1], [[128, 4], [1, 64]]), in1=V(coef[:, 0:1], [[1, 4], [0, 64]]), op=ALU.mult),
                             reads=[b_oa[g], b_coef], writes=[b_yB[g]])
                        S.op("dve", C("tensor_tensor", out=score[:, 1:63], in0=impA[:, 1:63], in1=impB[:, 1:63], op=ALU.add),
                             reads=[b_impA, b_impB], writes=[b_score])
                        if i == NT - 1:
                            S.op("dve", C("tensor_copy", out=score[:, 63:64], in_=pat_ap[:, 63:64]), reads=[b_PAT, b_score], writes=[b_score])
                        S.op("dve", C("max", out=m8[:, 0:8], in_=score[:]), reads=[b_score], writes=[b_m8])
                        S.op("dve", C("match_replace", out=wk[:], in_to_replace=m8[:, 0:8], in_values=score[:], imm_value=-1e30),
                             reads=[b_score, b_m8], writes=[b_wk])
                        S.op("dve", C("max", out=m8[:, 8:16], in_=wk[:]), reads=[b_wk], writes=[b_m8])
                        S.op("dve", C("tensor_scalar", out=negmg[g][:, :].rearrange("p (a j) -> p a j", a=2), in0=V(score[:, 0:1], [[0, 2], [1, 64]]),
                                      scalar1=m8[:, 15:16], scalar2=NEG,
                                      op0=ALU.is_lt, op1=ALU.mult), reads=[b_score, b_m8], writes=[b_negm[g]])

                    def post_b(g=g):
                        S.op("pe", C("transpose", out=OA[:, g, 384:512], in_=negmg[g][:], identity=idf[:]),
                             reads=[b_negm[g], b_idf], writes=[b_oa[g]])
                        op_ = slice((1 - g) * 64, (2 - g) * 64)
                        S.op("dve", C("tensor_copy", out=QNt[op_, u, g, :].rearrange("p (r t) -> p r t", r=4), in_=bc4(OA[op_, g, 384:512])),
                             reads=[b_oa[g]], writes=[b_QNw[u][g]])
                    if T == Tmax:
                        post_b_fns[g] = post_b
                    jobs.append((fscore, pv, post if T == Tmax else None))

            for g in range(2):
                hp = slice(g * 64, (g + 1) * 64)
                qa_ap = qTa[hp, u].rearrange("p r t -> p (r t)")

                def a_post(st, g=g):
                    oab = g
                    post_b_fns[g]()
                    S.op("dve", C("tensor_tensor", out=den[:, 12:16], in0=V(OA[:, oab, 64:65], [[66, 4]]),
                                  in1=esink[:, 4 * g:4 * g + 4], op=ALU.add),
                         reads=[b_oa[oab], b_esink], writes=[b_den])
                    S.op("dve", C("reciprocal", out=den[:, 12:16], in_=den[:, 12:16]), reads=[b_den], writes=[b_den])
                    S.op("dve", C("tensor_tensor", out=ytmp[oab][:, :].rearrange("p (r d) -> p r d", r=4),
                                  in0=V(OA[:, oab, 0:1], [[66, 4], [1, 64]]),
                                  in1=V(den[:, 12:13], [[1, 4], [0, 64]]), op=ALU.mult),
                         reads=[b_oa[oab], b_den], writes=[b_ytmp[oab]])
                    S.op("pool", C("tensor_tensor", out=y[:, g * 256:(g + 1) * 256], in0=ytmp[oab][:, :],
                                   in1=sz[:, u, g * 256:(g + 1) * 256], op=ALU.mult),
                         reads=[b_ytmp[oab], b_sz[u]], writes=[b_y])
                add_branch(g, list(range(max(0, i - 1), i + 1)),
                           lambda kt, g=g: (kTa[g][:, (kt % RA) * 128:(kt % RA) * 128 + 128], [b_kTa[kt % RA]]),
                           lambda kt: (qTa[:, u].rearrange("p r t -> p (r t)"), [b_qTa[u]]),
                           lambda kt, g=g: Bx(BA, 0, g) if kt == i else Bx(BA, 1, g),
                           Va, lambda kt: b_Va[kt % RA], lambda kt: kt % RA, g, a_post)
            for g in range(2):
                hp = slice(g * 64, (g + 1) * 64)
                qb_ap = QNt[hp, u, g, :]

                def win_extras(kt, g=g):
                    dk = i - kt
                    if dk == 0:
                        return Bx(BB, 0, g)
                    if dk == 1:
                        return Bx(BB, 1, g)
                    if dk == 4:
                        return [(idb[:], bc4(TRI[:, :]), [b_idb, b_TRI])]
                    return []
                add_branch(g, list(range(max(0, i - 4), i + 1)),
                           lambda kt, g=g: (kTw[g][:, (kt % RW) * 128:(kt % RW) * 128 + 128], [b_kTw[kt % RW]]),
                           lambda kt, g=g: (QNt[:, u, g, :], [b_QNw[u][g]]),
                           win_extras, Vw, lambda kt: b_Vw[kt % RW], lambda kt: kt % RW, g,
                           lambda st, g=g: accumulate(g, g, 8, 2))
            for g in range(2):
                hp = slice(g * 64, (g + 1) * 64)
                qb_ap = QNt[hp, u, g, :]

                def sel_lhs(kt, g=g, hp=hp):
                    return (KE[g][:, kt * 128:(kt + 1) * 128], [b_kTs[kt], b_E])

                def sel_rhs(kt, g=g, qb_ap=qb_ap):
                    return (QNt[:, u, g, :], [b_QNw[u][g]])

                def sel_extras(kt, g=g):
                    if kt == i:
                        return Bx(BB, 0, g)
                    if kt == i - 1:
                        return Bx(BB, 1, g)
                    return []

                def sel_post(st, g=g):
                    accumulate(g, g, 4, 1)
                    S.op("dve", C("tensor_tensor", out=y[:, 512 + g * 256:512 + (g + 1) * 256], in0=yB[:, g, :],
                                  in1=sz[:, u, 512 + g * 256:512 + (g + 1) * 256], op=ALU.mult),
                         reads=[b_yB[g], b_sz[u]], writes=[b_y])
                add_branch(g, list(range(0, i + 1)), sel_lhs, sel_rhs, sel_extras, Vs, lambda kt: b_Vs[kt], lambda kt: kt, g, sel_post)

            pend = None
            nj = len(jobs)
            sdone = 0
            ncmp = 2 * (Tmax + 1)
            for jx, (fscore, pv, post) in enumerate(jobs):
                st = fscore()
                want = ((jx + 1) * NSTEP) // nj
                if jx + 1 >= ncmp + 1:
                    want = max(2, want)
                while sdone < want:
                    next(nxt, None)
                    sdone += 1
                if pend is not None:
                    pst, ppv, ppost = pend
                    p = exp_tile(pst)
                    ppv(p)
                    if ppost is not None:
                        ppost(pst)
                pend = (st, pv, post)
            pst, ppv, ppost = pend
            p = exp_tile(pst)
            ppv(p)
            if ppost is not None:
                ppost(pst)

        def tail_chain(b, i):
            xrs = i % 2

            def t0():
                S.dma(C("dma_start", out=xr[xrs], in_=x_d[b, i * 128:(i + 1) * 128, :]), writes=[b_xr[xrs]])
                for kc in range(8):
                    S.op("pe", C("transpose", out=PTb[1][:, kc * 128:(kc + 1) * 128], in_=y[:, kc * 128:(kc + 1) * 128],
                                 identity=idb[:]), reads=[b_y, b_idb], writes=[b_pt[1]])

            def t1():
                S.op("act", C("copy", out=yT[:, :, :].rearrange("p k t -> p (k t)"), in_=PTb[1][:, :]), reads=[b_pt[1]], writes=[b_yT])
            stt = {}

            def mm(hf):
                def f():
                    pj = next_pj()
                    stt[hf] = pj
                    for kc in range(8):
                        S.op("pe", C("matmul", PJ[:, pj, :], lhsT=yT[:, kc, :], rhs=Wout[:, kc, hf * 512:(hf + 1) * 512],
                                     start=(kc == 0), stop=(kc == 7)),
                             reads=[b_yT, b_wout] + b_wout_l, writes=[b_pj[pj]], inc=(kc == 7))
                return f

            def ml(hf):
                def f():
                    pj = stt[hf]
                    S.op("dve", C("tensor_tensor", out=rtmp[:, :], in0=PJ[:, pj, :],
                                  in1=GATE[:, b, hf * 512:(hf + 1) * 512], op=ALU.mult),
                         reads=[b_pj[pj], b_GATE], writes=[b_rtmp])
                return f

            def t6(hf):
                def f():
                    S.op("dve", C("tensor_tensor", out=xr[xrs][:, hf * 512:(hf + 1) * 512], in0=xr[xrs][:, hf * 512:(hf + 1) * 512],
                                  in1=rtmp[:, :], op=ALU.add),
                         reads=[b_xr[xrs], b_rtmp], writes=[b_xr[xrs]])
                return f

            def t7():
                S.dma(C("dma_start", out=out_d[b, i * 128:(i + 1) * 128, :], in_=xr[xrs]), reads=[b_xr[xrs]], is_output=True)
            return (0, [t0, t1, mm(0), ml(0), t6(0), mm(1), ml(1), t6(1), t7])


        def drain(gen):
            for _ in gen:
                pass

        for b in range(nb):
            if b > 0:
                for g_ in range(2):
                    S.op("pool", C("memset", kcT[g_][:], 0.0), writes=[b_kcT])
                S.op("pool", C("memset", VC[:, :, :, 0:64], 0.0), writes=[b_VC])
                S.op("pool", C("memset", Xk[0][:, :, 0:17], 0.0), writes=[b_Xk[0]])
                S.op("pool", C("memset", Xv[0][:, :, 0:17], 0.0), writes=[b_Xv[0]])
                S.op("pool", C("memset", score[:, 63:64], 0.0), reads=[b_score], writes=[b_score])
            x_load(b, 0)
            drain(gen_steps(b, 0, None))
            for i in range(ntile):
                pc_ = i + 1 if i + 1 < ntile else None
                tl_ = i - 1 if i >= 1 else None
                nxt = gen_steps(b, pc_, tl_) if (pc_ is not None or tl_ is not None) else iter(())
                if "A" in stages and i < n_a:
                    phaseA(b, i, i % 2, nxt)
                drain(nxt)
            drain(gen_steps(b, None, ntile - 1))
        if dumps:
            L = dict(GATE=(GATE, [b_GATE]), Gcol=(Gcol, [b_Gcol]), SHcol=(SHcol, [b_SHcol]), BA=(BA, [b_BA]), BB=(BB, [b_BB]),
                     b1k=(b1k, [b_b1k]), b1v=(b1v, [b_b1v]), Win=(Win, [b_win]), Wout=(Wout, [b_wout]), w1k=(w1k, [b_w1k]),
                     qTa=(qTa, b_qTa), kTs=(KE[0], b_kTs),
                     Vs=(Vs, b_Vs), Va=(Va, b_Va), Vw=(Vw, b_Vw), VC=(VC, [b_VC]), sz=(sz, b_sz), gts=(gts, b_gts),
                      y=(y, [b_y]), hT=(hT0, [bht]),
                     Xk=(Xk0, [bxk]), hidk=(hidk, [b_hidk]), score=(score, [b_score]), yB=(yB, b_yB), esink=(esink, [b_esink]),
                     den=(den, [b_den]), coef=(coef, [b_coef]))
            for nm in dumps:
                t, bufs = L[nm]
                shp = list(t.shape)
                dd = nc.dram_tensor("d_" + nm, shp, t.dtype, kind="ExternalOutput").ap()
                full = t[tuple(slice(None) for _ in shp)]
                S.dma(C("dma_start", out=dd, in_=full), reads=list(bufs), is_output=True)
        S.emit()
    return nc


def _t5_bucket(dist):
    n = np.maximum(dist, 0)
    nf = np.maximum(n, 1).astype(np.float32)
    large = 16 + (np.log(nf / np.float32(16)) / np.float32(np.log(128 / 16)) * np.float32(16)).astype(np.int32)
    large = np.minimum(large, 31)
    return np.where(n < 16, n, large)


def _constants():
    bf = ml_dtypes.bfloat16
    sl = np.arange(128)[:, None]
    tl = np.arange(128)[None, :]
    d_diag = tl - sl
    d_prev = 128 + tl - sl
    idx = np.stack([_t5_bucket(d_diag), _t5_bucket(d_prev)], 0)
    maskAB = np.zeros((128, 4, 128), np.float32)
    maskAB[:, 0, :] = np.where(d_diag >= 0, 0.0, NEG)
    maskAB[:, 1, :] = np.where(d_prev < 128, 0.0, NEG)
    maskAB[:, 2, :] = np.where(d_diag >= 0, 0.0, NEG)
    maskAB[:, 3, :] = 0.0
    E = (np.arange(SEQ)[None, :] // 64 == (np.arange(128) % 64)[:, None]).astype(np.float32).astype(bf)
    nl = np.arange(128)[:, None, None]
    o = np.arange(16)[None, :, None]
    t3 = np.arange(128)[None, None, :]
    cmask = np.where(16 * nl + 15 <= 128 * o + t3, 0.0, NEG).astype(np.float32).astype(bf)
    tri = np.where(sl > tl, 0.0, NEG).astype(np.float32).astype(bf)
    pat = np.zeros((128, 192), np.float32)
    pat[:, 0] = BONUS
    hi = (np.arange(128) >= 64).astype(np.int64)
    pat[np.arange(128), 64 + 63 + hi] = BONUS
    pat[np.arange(128), 64 + 64 + hi] = BONUS
    pat = pat.astype(bf)
    npr = np.arange(256)
    n = npr - 1
    c_lo = 16 * n
    s_lo = 64 * np.arange(64)
    ov = np.clip(np.minimum(c_lo[:, None] + 32, s_lo[None, :] + 64) - np.maximum(c_lo[:, None], s_lo[None, :]), 0, None) / 32.0
    ov[0, :] = 0.0
    ov = np.ascontiguousarray(ov.reshape(2, 128, 64).transpose(1, 0, 2)[:, :, 1:63]).astype(np.float32).astype(bf)
    return idx, maskAB, E, cmask, tri, pat, ov


def _perm_cols():
    o_qa, o_ka, o_va, o_za, o_qb, o_kc, o_vc, o_ks, o_vs, o_kw, o_vw, o_zb, o_gb = (
        0, 512, 640, 768, 1280, 1792, 1920, 2048, 2176, 2304, 2432, 2560, 3072)
    r64 = np.arange(64)
    cols = []
    for base in (o_qa, o_qb):
        for r in range(4):
            cols += [base + r * 64 + r64, base + (4 + r) * 64 + r64]
    cols += [o_ka + np.arange(128), o_ks + np.arange(128), o_kw + np.arange(128)]
    for base in (o_kc, o_vc):
        for g in range(2):
            cols += [base + g * 64 + r64, base + g * 64 + r64]
    cols += [o_va + np.arange(128), o_vs + np.arange(128), o_vw + np.arange(128), o_gb + np.arange(24)]
    cols += [o_za + np.arange(512), o_zb + np.arange(512)]
    cols = np.concatenate(cols)
    assert cols.shape[0] == WCOLS
    return cols


_NC_CACHE = {}


def kernel(x, c, w_ada, b_ada, norm_gain, w_in, b_nsa_gate, q_gain_a, k_gain_a, sinks,
           q_gain_b, k_gain_cmp, k_gain_sel, k_gain_win, cmp_pos_k, cmp_pos_v,
           w_cmp_k1, w_cmp_k2, w_cmp_v1, w_cmp_v2, w_out, rel_bias):
    f = lambda a: np.ascontiguousarray(np.asarray(a, dtype=np.float32))
    x = f(x); c = f(c); w_ada = f(w_ada)[0]; b_ada = f(b_ada)[0]; norm_gain = f(norm_gain)[0]
    w_in = f(w_in)[0]; b_nsa_gate = f(b_nsa_gate)[0]; sinks = f(sinks)[0]
    rel_bias = f(rel_bias); w_out = f(w_out)[0]
    idx, maskAB, E, cmask, tri, pat, ov = _constants()
    biasAB = rel_bias[idx]
    biasA = np.ascontiguousarray(biasAB[..., 0:8].transpose(1, 0, 3, 2))
    biasB = np.ascontiguousarray(biasAB[..., 8:16].transpose(1, 0, 3, 2))
    c31 = np.ascontiguousarray(rel_bias[31:32, 8:16])
    gcols = np.stack([np.tile(f(g)[0], 2) for g in (q_gain_a, k_gain_a, q_gain_b, k_gain_cmp, k_gain_sel, k_gain_win)], 1)
    shared = {
        "w_ada": w_ada,
        "bada_col": np.ascontiguousarray(b_ada[0:2048].reshape(16, 128).T),
        "bada_gate": np.ascontiguousarray(b_ada[2048:3072].reshape(1, DM)),
        "ng_col": np.ascontiguousarray(norm_gain.reshape(8, 128).T),
        "w_in_p": np.ascontiguousarray(w_in[:, _perm_cols()]),
        "bgate": b_nsa_gate.reshape(1, 24),
        "gcols": np.ascontiguousarray(gcols),
        "sinks": sinks.reshape(1, 8),
        "posk": np.ascontiguousarray(f(cmp_pos_k)[0].reshape(16, 128).T),
        "posv": np.ascontiguousarray(f(cmp_pos_v)[0].reshape(16, 128).T),
        "w1k": f(w_cmp_k1)[0], "w1v": f(w_cmp_v1)[0], "w2k": f(w_cmp_k2)[0], "w2v": f(w_cmp_v2)[0],
        "w_out": w_out, "biasA": biasA, "biasB": biasB, "c31": c31, "maskAB": maskAB,
        "Emat": E, "cmask": cmask, "tri": tri, "pat": pat, "ov": ov,
    }
    in_maps = []
    for core in range(NCORES):
        m = dict(shared)
        m["x"] = x[NB * core:NB * (core + 1)]
        cc = c[NB * core:NB * (core + 1)]
        m["cT"] = np.ascontiguousarray(cc.reshape(NB, 8, 128).transpose(2, 1, 0))
        in_maps.append(m)
    if "nc" not in _NC_CACHE:
        _NC_CACHE["nc"] = build_program()
    res = run_bass_kernel_spmd(_NC_CACHE["nc"], in_maps, core_ids=list(range(NCORES)))
    return np.concatenate([np.asarray(r["out"], dtype=np.float32) for r in res.results], axis=0)
```

```python
import numpy as np
import ml_dtypes
from contextlib import ExitStack
import concourse.bass as bass
import concourse.mybir as mybir
from concourse.bass_utils import run_bass_kernel_spmd

F32 = mybir.dt.float32
BF16 = mybir.dt.bfloat16
F32R = mybir.dt.float32r
AF = mybir.ActivationFunctionType
ALU = mybir.AluOpType
AX = mybir.AxisListType

NCORES = 8
SEQ = 4096
DM = 1024
NT = SEQ // 128
NB = 2
WCOLS = 3352
NEG = -30000.0
BONUS = 1.0e4
EPS = 1e-6
G_QA, G_QB, G_K, G_C, G_V, G_ZA, G_ZB = 0, 512, 1024, 1408, 1920, 2328, 2840
RA, RW = 3, 6


class Buf:
    __slots__ = ("w", "r", "dsem", "dcount", "name")

    def __init__(self, name=""):
        self.w = None
        self.r = {}
        self.dsem = None
        self.dcount = 0
        self.name = name


class Sched:
    ENG = ("pe", "act", "dve", "pool", "sp")

    def __init__(self, nc, same_engine_sync=True):
        self.nc = nc
        self.ops = {e: [] for e in self.ENG}
        self.count = {e: 0 for e in self.ENG}
        self.waited = {e: {} for e in self.ENG}
        self.sems = {}
        self.dma_sems = []
        self.same_engine_sync = same_engine_sync
        self.out_waits = []

    def _need(self, eng, dep, needs, self_sync=True):
        if dep is None:
            return
        k, v = dep
        if k == eng:
            if eng in ("pe", "sp"):
                return
            if not self.same_engine_sync or not self_sync:
                return
        if self.waited[eng].get(k, 0) >= v:
            return
        if needs.get(k, 0) < v:
            needs[k] = v

    def _emit_waits(self, eng, needs):
        for k, v in needs.items():
            self.ops[eng].append(("w", k, v))
            self.waited[eng][k] = v

    def op(self, eng, fn, reads=(), writes=(), inc=True, self_sync=True):
        needs = {}
        for b in reads:
            self._need(eng, b.w, needs, self_sync)
        for b in writes:
            self._need(eng, b.w, needs, self_sync)
            for k, v in b.r.items():
                self._need(eng, (k, v), needs, self_sync)
        self._emit_waits(eng, needs)
        if inc:
            self.count[eng] += 1
            val = self.count[eng]
        else:
            val = self.count[eng] + 1
        self.ops[eng].append(("op", fn, [(eng, 1)] if inc else []))
        for b in writes:
            b.w = (eng, val)
            b.r = {}
        for b in reads:
            if b.r.get(eng, 0) < val:
                b.r[eng] = val
        return val

    def dma(self, fn, reads=(), writes=(), q="sp", is_output=False):
        needs = {}
        for b in reads:
            self._need(q, b.w, needs)
        for b in writes:
            self._need(q, b.w, needs)
            for k, v in b.r.items():
                self._need(q, (k, v), needs)
        self._emit_waits(q, needs)
        owner = writes[0] if writes else reads[0]
        if owner.dsem is None:
            owner.dsem = "dma%d" % len(self.dma_sems)
            self.dma_sems.append(owner.dsem)
        owner.dcount += 16
        k, v = owner.dsem, owner.dcount
        self.ops[q].append(("op", fn, [(k, 16)]))
        for b in writes:
            b.w = (k, v)
            b.r = {}
        for b in reads:
            b.r[k] = v
        if is_output:
            self.out_waits.append((k, v))

    def emit(self):
        nc = self.nc
        with ExitStack() as es:
            for e in self.ENG:
                self.sems[e] = es.enter_context(nc.semaphore("prog_" + e))
            for k in self.dma_sems:
                self.sems[k] = es.enter_context(nc.semaphore(k))
            fin = {}
            for k, v in self.out_waits:
                fin[k] = max(fin.get(k, 0), v)
            for k, v in fin.items():
                self.ops["sp"].append(("w", k, v))
            block = es.enter_context(nc.Block())
            sems = self.sems
            ops = self.ops

            FUSE = ("activation", "copy", "tensor_tensor", "tensor_copy", "reciprocal", "tensor_reduce", "memset",
                    "tensor_scalar", "scalar_tensor_tensor", "max", "matmul", "transpose")

            def run(engine, lst, fuse=False):
                pend = []
                for item in lst:
                    if item[0] == "w":
                        pend.append(item)
                        continue
                    name, args, kw = item[1]
                    can = fuse and pend and name in FUSE and kw.get("accum_out") is None
                    for w in (pend[:-1] if can else pend):
                        engine.wait_ge(sems[w[1]], w[2])
                    ins = getattr(engine, name)(*args, **kw)
                    if can:
                        ins._wait_ge(sems[pend[-1][1]], pend[-1][2])
                    pend = []
                    for (k, n) in item[2]:
                        ins.then_inc(sems[k], n)
                for w in pend:
                    engine.wait_ge(sems[w[1]], w[2])

            @block.sync
            def _(e):
                run(e, ops["sp"])

            @block.tensor
            def _(e):
                run(e, ops["pe"], fuse=True)

            @block.scalar
            def _(e):
                run(e, ops["act"], fuse=True)

            @block.vector
            def _(e):
                run(e, ops["dve"], fuse=True)

            @block.gpsimd
            def _(e):
                run(e, ops["pool"])


def C(name, *args, **kw):
    return (name, args, kw)


def V(ap, dims):
    return bass.AP(tensor=ap.tensor, offset=ap.offset, ap=[list(ap.ap[0])] + [list(d) for d in dims])


def build_program(nb=NB, n_sb=NT // 4, stages="PCA", n_a=999, dumps=None, pro=9, nch=99):
    nc = bass.Bass("TRN2", target_bir_lowering=False)
    S = Sched(nc)

    def din(name, shape, dt=F32):
        return nc.dram_tensor(name, list(shape), dt, kind="ExternalInput").ap()

    x_d = din("x", [NB, SEQ, DM])
    cT_d = din("cT", [128, 8, NB])
    wada_d = din("w_ada", [DM, 3 * DM])
    badac_d = din("bada_col", [128, 16])
    badag_d = din("bada_gate", [1, DM])
    ngc_d = din("ng_col", [128, 8])
    win_d = din("w_in_p", [DM, WCOLS])
    bgate_d = din("bgate", [1, 24])
    gcols_d = din("gcols", [128, 6])
    sinks_d = din("sinks", [1, 8])
    posk_d = din("posk", [128, 16])
    posv_d = din("posv", [128, 16])
    w1k_d = din("w1k", [2048, 256])
    w1v_d = din("w1v", [2048, 256])
    w2k_d = din("w2k", [256, 64])
    w2v_d = din("w2v", [256, 64])
    wout_d = din("w_out", [DM, DM])
    biasA_d = din("biasA", [128, 2, 8, 128])
    biasB_d = din("biasB", [128, 2, 8, 128])
    c31_d = din("c31", [1, 8])
    maskAB_d = din("maskAB", [128, 4, 128])
    E_d = din("Emat", [128, SEQ], BF16)
    cmask_d = din("cmask", [128, 16, 128], BF16)
    tri_d = din("tri", [128, 128], BF16)
    pat_d = din("pat", [128, 192], BF16)
    ov_d = din("ov", [128, 2, 62], BF16)
    out_d = nc.dram_tensor("out", [NB, SEQ, DM], F32, kind="ExternalOutput").ap()

    with ExitStack() as es:
        def sb(name, shape, dt):
            return es.enter_context(nc.sbuf_tensor("s_" + name, list(shape), dt))

        def ps(name, shape, dt):
            return es.enter_context(nc.psum_tensor("p_" + name, list(shape), dt))

        PJ = ps("PJ", [128, 2, 512], F32)
        PT = ps("PT", [128, 2, 512], F32)
        ST = ps("ST", [128, 2, 512], F32)
        OA = ps("OA", [128, 2, 512], F32)
        b_pj = [Buf("pj0"), Buf("pj1")]
        b_pt = [Buf("pt0"), Buf("pt1")]
        b_st = [Buf("st0"), Buf("st1")]
        b_oa = [Buf("oa0"), Buf("oa1")]
        PTb = [PT[:, 0, :].bitcast(BF16), PT[:, 1, :].bitcast(BF16)]

        Win = sb("Win", [128, 8, WCOLS], BF16); b_win = Buf("win")
        b_win_l = [Buf("win%d" % i_) for i_ in range(16)]
        Wout = sb("Wout", [128, 8, DM], BF16); b_wout = Buf("wout")
        b_wout_l = [Buf("wout%d" % i_) for i_ in range(8)]
        w1k = sb("w1k", [128, 16, 256], BF16); b_w1k = Buf()
        w1v = sb("w1v", [128, 16, 256], BF16); b_w1v = Buf()
        w2k = sb("w2k", [128, 2, 64], BF16); b_w2k = Buf()
        w2v = sb("w2v", [128, 2, 64], BF16); b_w2v = Buf()
        b1k = sb("b1k", [128, 2], F32); b_b1k = Buf()
        b1v = sb("b1v", [128, 2], F32); b_b1v = Buf()
        STG = [sb("stg0", [128, 2048], F32), sb("stg1", [128, 2048], F32)]
        b_stg = [[Buf("s00"), Buf("s01")], [Buf("s10"), Buf("s11")]]
        kTa = [sb("kTa0", [128, RA * 128], BF16), sb("kTa1", [128, RA * 128], BF16)]; b_kTa = [Buf() for _ in range(RA)]
        kTw = [sb("kTw0", [128, RW * 128], BF16), sb("kTw1", [128, RW * 128], BF16)]; b_kTw = [Buf() for _ in range(RW)]
        KE = [sb("KE0", [128, SEQ], BF16), sb("KE1", [128, SEQ], BF16)]
        b_kTs = [Buf() for _ in range(NT)]
        Va = sb("Va", [128, RA, 2, 66], BF16); b_Va = [Buf() for _ in range(RA)]
        Vw = sb("Vw", [128, RW, 2, 66], BF16); b_Vw = [Buf() for _ in range(RW)]
        Vs = sb("Vs", [128, NT, 2, 66], BF16); b_Vs = [Buf() for _ in range(NT)]
        XW = 529
        Xk0 = sb("Xk0", [128, 2, XW], BF16)
        Xv0 = sb("Xv0", [128, 2, XW], BF16)
        Xk = [Xk0, Xk0]
        Xv = [Xv0, Xv0]
        bxk, bxv = Buf(), Buf()
        b_Xk = [bxk, bxk]
        b_Xv = [bxv, bxv]
        kcT = [sb("kcT0", [128, 256], BF16), sb("kcT1", [128, 256], BF16)]; b_kcT = Buf("kcT")
        VC = sb("VC", [128, 2, 2, 128], BF16); b_VC = Buf("VC")
        BA = sb("BA", [128, 2, 8, 128], F32R); b_BA = Buf()
        BB = sb("BB", [128, 2, 8, 128], F32R); b_BB = Buf()
        b_E = Buf()
        CM = sb("CM", [128, 16, 128], BF16); b_CM = Buf()
        TRI = sb("TRI", [128, 128], BF16); b_TRI = Buf()
        PAT = sb("PAT", [128, 192], BF16); b_PAT = Buf()
        GATE = sb("GATE", [128, NB, DM], F32); b_GATE = Buf()
        Gcol = sb("Gcol", [128, 8, NB], F32); b_Gcol = Buf()
        SHcol = sb("SHcol", [128, 8, NB], F32); b_SHcol = Buf()
        idf = sb("idf", [128, 128], F32); b_idf = Buf()
        idb = sb("idb", [128, 128], BF16); b_idb = Buf()
        idr = sb("idr", [128, 128], F32R); b_idr = Buf()
        gcols = sb("gcols", [128, 6], F32); b_gcols = Buf()
        esink = sb("esink", [128, 8], F32); b_esink = Buf()
        bgate = sb("bgate", [128, 24], F32); b_bgate = Buf()
        c31 = sb("c31", [128, 8], F32); b_c31 = Buf()


        stat = sb("stat", [128, 4], F32); b_stat = Buf()
        hT0 = sb("hT0", [128, 8, 128], BF16)
        hT = [hT0, hT0]
        bht = Buf()
        b_hT = [bht, bht]

        ssq = sb("ssq", [128, 16], F32); b_ssq = Buf()
        qn = [sb("qn0", [128, 512], BF16), sb("qn1", [128, 512], BF16)]
        b_qn = [Buf(), Buf()]
        qTa = sb("qTa", [128, 2, 4, 128], BF16); b_qTa = [Buf() for _ in range(2)]
        sz = sb("sz", [128, 2, DM], BF16); b_sz = [Buf() for _ in range(2)]
        scb = KE[0][:, :].bitcast(F32).rearrange("p (k b m) -> p k b m", k=8, b=NB)
        gts = sb("gts", [128, 2, 24], F32); b_gts = [Buf() for _ in range(2)]
        gtmp = sb("gtmp", [128, 24], F32); b_gtmp = Buf()
        zt2 = sb("zt2", [128, 512], F32); b_zt2 = Buf()
        hsig = sb("hsig", [128, 64], F32); b_hsig = Buf()
        nb1k = sb("nb1k", [128, 2], F32)
        nb1v = sb("nb1v", [128, 2], F32); b_nb1 = Buf()
        hidk = sb("hidk", [128, 2, 16], BF16); b_hidk = Buf()
        hidv = sb("hidv", [128, 2, 16], BF16); b_hidv = Buf()
        kc32 = sb("kc32", [16, 4], F32); b_kc32 = Buf()
        kndup = sb("kndup", [16, 128], BF16); b_kndup = Buf()
        kjunk = sb("kjunk", [16, 64], F32); b_kjunk = Buf()
        vstag = sb("vstag", [16, 64], BF16); b_vstag = Buf()
        PTs = [sb("PTs%d" % i, [128, 512], BF16) for i in range(2)]
        b_PTs = [Buf() for _ in range(2)]
        den = sb("den", [128, 16], F32); b_den = Buf()
        coef = sb("coef", [128, 16], F32); b_coef = Buf()
        impA = sb("impA", [128, 64], F32); b_impA = Buf()
        impB = sb("impB", [128, 64], F32); b_impB = Buf()
        score = sb("score", [128, 64], F32); b_score = Buf()
        wk = sb("wk", [128, 64], F32); b_wk = Buf()
        m8 = sb("m8", [128, 16], F32); b_m8 = Buf()
        negmg = [sb("negm0", [128, 128], F32), sb("negm1", [128, 128], F32)]
        b_negm = [Buf(), Buf()]
        QNt = sb("QNw", [128, 2, 2, 512], BF16)
        b_QNw = [[Buf(), Buf()], [Buf(), Buf()]]
        yB = sb("yB", [128, 2, 256], F32); b_yB = [Buf(), Buf()]
        maskAB = yB[:, :, :].rearrange("p a (b c) -> p (a b) c", b=2); b_maskAB = b_yB[0]
        ytmp = [sb("ytmp0", [128, 256], F32), sb("ytmp1", [128, 256], F32)]
        b_ytmp = [Buf(), Buf()]
        y = sb("y", [128, DM], BF16); b_y = Buf()
        sq = sb("sq", [128, 512], F32); b_sq = Buf()
        junk = sq[:, :].bitcast(BF16); b_junk = b_sq
        yT = sb("yT", [128, 8, 128], BF16); b_yT = Buf()
        rtmp = sb("rtmp", [128, 512], F32); b_rtmp = Buf()
        xn = sb("xn", [128, DM], F32); b_xn = Buf()
        scol = sb("scol", [128, 8, NB], F32); b_scol = Buf()

        modc = sb("modc", [128, 16, NB], F32); b_modc = Buf()
        posk = sb("posk", [128, 16], F32); b_posk = Buf()
        posv = sb("posv", [128, 16], F32); b_posv = Buf()
        poskb = sb("poskb", [128, 16], BF16); b_poskb = Buf()
        posvb = sb("posvb", [128, 16], BF16); b_posvb = Buf()
        badac = sb("badac", [128, 16], F32); b_badac = Buf()
        ngc = sb("ngc", [128, 8], F32); b_ngc = Buf()
        badag = sb("badag", [128, DM], F32) if False else None

        xs = [STG[0][:, 0:1024], STG[0][:, 1024:2048]]
        b_xs = b_stg[0]
        xr = [STG[1][:, 0:1024], STG[1][:, 1024:2048]]
        b_xr = b_stg[1]

        def pbc(d_ap, n):
            return bass.AP(tensor=d_ap.tensor, offset=d_ap.offset, ap=[[0, 128], [1, n]])

        if dumps:
            for (t_, bl_) in ((KE[0], b_kTs), (Vs, b_Vs), (Va, b_Va), (Vw, b_Vw), (qTa, b_qTa),
                              (sz, b_sz), (gts, b_gts), (hT0, [bht]), (QNt, b_QNw[0] + b_QNw[1]), (y, [b_y]),
                              (hidk, [b_hidk]), (score, [b_score]), (yB, b_yB), (den, [b_den]), (coef, [b_coef])):
                shp_ = list(t_.shape)
                S.op("pool", C("memset", t_[tuple(slice(None) for _ in shp_)], 0.0), writes=list(bl_))
        def ld(dst_ap, src_ap, buf):
            S.dma(C("dma_start", out=dst_ap, in_=src_ap), writes=[buf])

        ld(CM[:], cmask_d, b_CM)
        ld(TRI[:], tri_d, b_TRI)
        ld(PAT[:], pat_d, b_PAT)
        ld(gcols[:], gcols_d, b_gcols)
        ld(esink[:], pbc(sinks_d, 8), b_esink)
        ld(bgate[:], pbc(bgate_d, 24), b_bgate)
        ld(c31[:], pbc(c31_d, 8), b_c31)

        ld(scol[:], cT_d, b_scol)
        ld(posk[:], posk_d, b_posk)
        ld(posv[:], posv_d, b_posv)
        ld(badac[:], badac_d, b_badac)
        ld(ngc[:], ngc_d, b_ngc)
        S.op("pool", C("memset", idf[:], 0.0), writes=[b_idf])
        S.op("pool", C("affine_select", out=idf[:], in_=idf[:], pattern=[[-1, 128]], compare_op=ALU.not_equal,
                                                fill=1.0, base=0, channel_multiplier=1), reads=[b_idf], writes=[b_idf])
        S.op("dve", C("tensor_copy", out=idb[:], in_=idf[:]), reads=[b_idf], writes=[b_idb])
        S.op("dve", C("tensor_copy", out=idr[:], in_=idf[:]), reads=[b_idf], writes=[b_idr])
        S.op("act", C("activation", out=esink[:], in_=esink[:], func=AF.Exp), reads=[b_esink], writes=[b_esink])
        S.op("dve", C("tensor_scalar", out=gcols[:, 0:1], in0=gcols[:, 0:1], scalar1=0.125, scalar2=None, op0=ALU.mult),
             reads=[b_gcols], writes=[b_gcols])
        S.op("dve", C("tensor_scalar", out=gcols[:, 2:3], in0=gcols[:, 2:3], scalar1=0.125, scalar2=None, op0=ALU.mult),
             reads=[b_gcols], writes=[b_gcols])
        S.op("dve", C("tensor_copy", out=poskb[:], in_=posk[:]), reads=[b_posk], writes=[b_poskb])
        S.op("dve", C("tensor_copy", out=posvb[:], in_=posv[:]), reads=[b_posv], writes=[b_posvb])
        S.op("pool", C("memset", score[:], 0.0), writes=[b_score])
        S.op("pool", C("memset", score[:, 0:1], BONUS), reads=[b_score], writes=[b_score])
        for g_ in range(2):
            S.op("pool", C("memset", kcT[g_][:], 0.0), writes=[b_kcT])
            S.op("pool", C("memset", kTa[g_][:], 0.0), writes=b_kTa)
            S.op("pool", C("memset", kTw[g_][:], 0.0), writes=b_kTw)
        S.op("pool", C("memset", QNt[:, :, :, :], 0.0), writes=b_QNw[0] + b_QNw[1])
        S.op("pool", C("memset", VC[:], 0.0), writes=[b_VC])
        S.op("pool", C("memset", VC[:, :, :, 64:65], 1.0), reads=[b_VC], writes=[b_VC])
        S.op("pool", C("memset", VC[0:1, 0, :, 64:65], 0.0), reads=[b_VC], writes=[b_VC])
        for g in range(2):
            S.dma(C("dma_start", out=VC[:, :, g, 66:128], in_=ov_d), writes=[b_VC])
        for (Vt_, bV_) in ((Va, b_Va), (Vw, b_Vw), (Vs, b_Vs)):
            S.op("pool", C("memset", Vt_[:, :, :, 64:65], 1.0), writes=bV_)
            S.op("pool", C("memset", Vt_[:, :, :, 65:66], 0.0), writes=bV_)
        for s_ in range(1):
            S.op("pool", C("memset", Xk[s_][:], 0.0), writes=[b_Xk[s_]])
            S.op("pool", C("memset", Xv[s_][:], 0.0), writes=[b_Xv[s_]])

        stg_i = [0]

        def stage_load(src_ap, ncols, view=None):
            s_ = stg_i[0] % 2
            stg_i[0] += 1
            dst = STG[s_][:, 0:ncols] if view is None else view(STG[s_])
            S.dma(C("dma_start", out=dst, in_=src_ap), writes=b_stg[s_])
            return s_

        def cast_from_stage(eng, s_, dst_ap, ncols, dst_buf, src_view=None):
            src = STG[s_][:, 0:ncols] if src_view is None else src_view(STG[s_])
            S.op(eng, C("copy" if eng == "act" else "tensor_copy", out=dst_ap, in_=src), reads=b_stg[s_], writes=[dst_buf])

        ADA = pro >= 2
        S.op("act", C("activation", out=scol[:], in_=scol[:], func=AF.Silu), reads=[b_scol], writes=[b_scol])
        S.op("dve", C("tensor_copy", out=scb, in_=V(scol[:, 0, 0:1], [[NB, 8], [1, NB], [0, 128]])),
             reads=[b_scol], writes=[b_E])
        pj_i = [0]

        def next_pj():
            i_ = pj_i[0] % 2
            pj_i[0] += 1
            return i_

        for cg in range(12 if ADA else 0):
            src = wada_d[:, cg * 256:(cg + 1) * 256].rearrange("(kc p) c -> p kc c", p=128)
            s_ = stage_load(src, 2048, view=lambda t: t[:, :].rearrange("p (kc c) -> p kc c", kc=8))
            stv = STG[s_][:, :].rearrange("p (kc c) -> p kc c", kc=8)
            if cg < 8:
                for cc in range(2):
                    ch = cg * 2 + cc
                    pj = next_pj()
                    for kc in range(8):
                        S.op("pe", C("matmul",
                            PJ[:, pj, 0:NB], lhsT=stv[:, kc, cc * 128:(cc + 1) * 128], rhs=scol[:, kc, :],
                            start=(kc == 0), stop=(kc == 7)),
                            reads=b_stg[s_] + [b_scol], writes=[b_pj[pj]], inc=(kc == 7))
                    S.op("dve", C("tensor_scalar",
                        out=modc[:, ch, :], in0=PJ[:, pj, 0:NB], scalar1=badac[:, ch:ch + 1], scalar2=None, op0=ALU.add),
                        reads=[b_pj[pj], b_badac], writes=[b_modc])
            else:
                for b in range(NB):
                    pj = next_pj()
                    for kc in range(8):
                        S.op("pe", C("matmul",
                            PJ[:, pj, 0:256], lhsT=scb[:, kc, b, :], rhs=stv[:, kc, :],
                            start=(kc == 0), stop=(kc == 7)),
                            reads=b_stg[s_] + [b_E], writes=[b_pj[pj]], inc=(kc == 7))
                    c0 = (cg - 8) * 256
                    S.op("dve", C("tensor_copy", out=GATE[:, b, c0:c0 + 256], in_=PJ[:, pj, 0:256]),
                         reads=[b_pj[pj]], writes=[b_GATE])
        s_ = stage_load(pbc(badag_d, DM), DM)
        for b in range(NB if ADA else 0):
            S.op("dve", C("tensor_tensor", out=GATE[:, b, :], in0=GATE[:, b, :], in1=STG[s_][:, 0:DM], op=ALU.add),
                 reads=b_stg[s_] + [b_GATE], writes=[b_GATE])
        S.op("dve", C("tensor_copy", out=SHcol[:], in_=modc[:, 0:8, :]), reads=[b_modc], writes=[b_SHcol])
        S.op("dve", C("tensor_scalar", out=Gcol[:], in0=modc[:, 8:16, :], scalar1=1.0, scalar2=None, op0=ALU.add),
             reads=[b_modc], writes=[b_Gcol])
        S.op("dve", C("tensor_tensor", out=Gcol[:], in0=Gcol[:], in1=V(ngc[:, 0:1], [[1, 8], [0, NB]]), op=ALU.mult),
             reads=[b_Gcol, b_ngc], writes=[b_Gcol])

        S.dma(C("dma_start", out=KE[0][64:128, :], in_=E_d[64:128, :]), writes=[b_E])
        S.dma(C("dma_start", out=KE[1][0:64, :], in_=E_d[0:64, :]), writes=[b_E])
        for kc in range(8 if pro >= 3 else 0):
            for hx, (c0, c1) in enumerate(((0, 1676), (1676, WCOLS))):
                S.dma(C("dma_start", out=Win[:, kc, c0:c1], in_=win_d[kc * 128:(kc + 1) * 128, c0:c1]),
                      writes=[b_win_l[kc * 2 + hx]], q="pool")
        if pro >= 3:
            for (wd, wsb, wb) in ((w1k_d, w1k, b_w1k), (w1v_d, w1v, b_w1v)):
                for hlf in range(2):
                    S.dma(C("dma_start", out=wsb[:, hlf * 8:(hlf + 1) * 8, :],
                            in_=wd[hlf * 1024:(hlf + 1) * 1024, :].rearrange("(lp p) j -> p lp j", p=128)), writes=[wb], q="pool")
            for (wd, wsb, wb) in ((w2k_d, w2k, b_w2k), (w2v_d, w2v, b_w2v)):
                S.dma(C("dma_start", out=wsb[:], in_=wd.rearrange("(jc p) d -> p jc d", p=128)), writes=[wb], q="pool")
            for kc in range(8):
                S.dma(C("dma_start", out=Wout[:, kc, :], in_=wout_d[kc * 128:(kc + 1) * 128, :]), writes=[b_wout_l[kc]], q="pool")
        S.dma(C("dma_start", out=maskAB, in_=maskAB_d), writes=b_yB)
        for (bd, Bt, bB, m0, isB) in ((biasA_d, BA, b_BA, 0, False), (biasB_d, BB, b_BB, 2, True)) if pro >= 4 else ():
            s_ = stage_load(bd, 2048, view=lambda t: t[:, :].rearrange("p (a h c) -> p a h c", a=2, h=8))
            stv = STG[s_][:, :].rearrange("p (a h c) -> p a h c", a=2, h=8)
            for ty in range(2):
                S.op("dve", C("tensor_tensor",
                    out=stv[:, ty], in0=stv[:, ty], in1=V(maskAB[:, m0 + ty, 0:1], [[0, 8], [1, 128]]), op=ALU.add),
                    reads=b_stg[s_] + b_yB, writes=b_stg[s_])
                if isB:
                    S.op("dve", C("tensor_tensor",
                        out=stv[:, ty], in0=stv[:, ty], in1=V(c31[:, 0:1], [[1, 8], [0, 128]]), op=ALU.subtract),
                        reads=b_stg[s_] + [b_c31], writes=b_stg[s_])
            S.op("dve", C("tensor_copy", out=Bt[:, :, :, :], in_=stv), reads=b_stg[s_], writes=[bB])
        for (wsb, wb, pb, bpb, b1, bb1) in ((w1k, b_w1k, poskb, b_poskb, b1k, b_b1k), (w1v, b_w1v, posvb, b_posvb, b1v, b_b1v)) if pro >= 5 else ():
            for jc in range(2):
                pj = next_pj()
                for lp in range(16):
                    S.op("pe", C("matmul",
                        PJ[:, pj, 0:2], lhsT=wsb[:, lp, jc * 128:(jc + 1) * 128], rhs=V(pb[:, lp:lp + 1], [[0, 2]]),
                        start=(lp == 0), stop=(lp == 15)), reads=[wb, bpb], writes=[b_pj[pj]], inc=(lp == 15))
                S.op("dve", C("tensor_copy", out=b1[:, jc:jc + 1], in_=PJ[:, pj, 0:1]),
                     reads=[b_pj[pj]], writes=[bb1])

        if pro >= 5:
            S.op("dve", C("tensor_scalar", out=nb1k[:], in0=b1k[:], scalar1=-1.0, scalar2=None, op0=ALU.mult), reads=[b_b1k], writes=[b_nb1])
            S.op("dve", C("tensor_scalar", out=nb1v[:], in0=b1v[:], scalar1=-1.0, scalar2=None, op0=ALU.mult), reads=[b_b1v], writes=[b_nb1])
        st_i = [0]
        pts_i = [0]

        def build_PC(b, i):
            u = i % 2
            xsl = i % 2
            ra, rw = i % RA, i % RW
            CPb = PT[:, 1, :]
            CPbb = PTb[1]
            b_cp = b_pt[1]
            chains = []

            def proj(c0, n):
                pj = next_pj()
                for kc in range(8):
                    S.op("pe", C("matmul", PJ[:, pj, 0:n], lhsT=hT0[:, kc, :], rhs=Win[:, kc, c0:c0 + n],
                                 start=(kc == 0), stop=(kc == 7)),
                         reads=[bht, b_win] + b_win_l, writes=[b_pj[pj]], inc=(kc == 7))
                return pj

            def L0():
                S.dma(C("dma_start", out=xs[xsl], in_=x_d[b, i * 128:(i + 1) * 128, :]), writes=[b_xs[xsl]])

            def L1():
                S.op("act", C("activation", out=junk, in_=xs[xsl], func=AF.Square, accum_out=stat[:, 0:1]),
                     reads=[b_xs[xsl]], writes=[b_junk, b_stat])

            def L2():
                S.op("act", C("activation", out=stat[:, 1:2], in_=stat[:, 0:1], func=AF.Ln, scale=1.0 / DM, bias=EPS),
                     reads=[b_stat], writes=[b_stat])
                S.op("act", C("activation", out=stat[:, 2:3], in_=stat[:, 1:2], func=AF.Exp, scale=-0.5),
                     reads=[b_stat], writes=[b_stat])

            def L3():
                S.op("pool", C("tensor_scalar", out=xn[:], in0=xs[xsl], scalar1=stat[:, 2:3], scalar2=1.0,
                               op0=ALU.mult, op1=ALU.mult), reads=[b_xs[xsl], b_stat], writes=[b_xn])

            def LT(h):
                def f():
                    for kc in range(4 * h, 4 * h + 4):
                        S.op("pe", C("transpose", out=PT[:, 0, (kc % 4) * 128:(kc % 4 + 1) * 128],
                                     in_=xn[:, kc * 128:(kc + 1) * 128], identity=idf[:]),
                             reads=[b_xn, b_idf], writes=[b_pt[0]])
                return f

            def LE(h):
                def f():
                    for kc in range(4 * h, 4 * h + 4):
                        S.op("dve", C("tensor_scalar", out=hT0[:, kc, :], in0=PT[:, 0, (kc % 4) * 128:(kc % 4 + 1) * 128],
                                      scalar1=Gcol[:, kc, b:b + 1], scalar2=SHcol[:, kc, b:b + 1], op0=ALU.mult, op1=ALU.add),
                             reads=[b_pt[0], b_Gcol, b_SHcol], writes=[bht])
                return f
            chains.append((0, [L1, L2, L3, LT(0), LE(0), LT(1), LE(1)]))

            def norm_chain(c0, nh, qs, nchunks, evac):
                n = nh * 64
                st = {}

                def s0():
                    st["pj"] = proj(c0, n)

                def s1():
                    S.op("act", C("activation", out=sq[:, 0:n], in_=PJ[:, st["pj"], 0:n], func=AF.Square),
                         reads=[b_pj[st["pj"]]], writes=[b_sq])

                def s2():
                    S.op("dve", C("tensor_reduce", out=ssq[:, 0:nh], in_=sq[:, 0:n].rearrange("p (h d) -> p h d", d=64),
                                  axis=AX.X, op=ALU.add), reads=[b_sq], writes=[b_ssq])

                def s3():
                    S.op("act", C("activation", out=ssq[:, 0:nh], in_=ssq[:, 0:nh], func=AF.Ln, scale=1.0 / 64, bias=EPS),
                         reads=[b_ssq], writes=[b_ssq])
                    S.op("act", C("activation", out=ssq[:, 0:nh], in_=ssq[:, 0:nh], func=AF.Exp, scale=-0.5),
                         reads=[b_ssq], writes=[b_ssq])

                def s4():
                    pj = st["pj"]
                    S.op("dve", C("tensor_tensor", out=qn[qs][:, 0:n].rearrange("p (h d) -> p h d", d=64),
                                  in0=PJ[:, pj, 0:n].rearrange("p (h d) -> p h d", d=64),
                                  in1=V(ssq[:, 0:1], [[1, nh], [0, 64]]), op=ALU.mult),
                         reads=[b_pj[pj], b_ssq], writes=[b_qn[qs]])

                def s5():
                    for c in range(nchunks):
                        S.op("pe", C("transpose", out=PTb[0][:, c * 128:(c + 1) * 128], in_=qn[qs][:, c * 128:(c + 1) * 128], identity=idb[:]),
                             reads=[b_qn[qs], b_idb], writes=[b_pt[0]])
                return [s0, s1, s2, s3, s4, s5, evac]

            def evac_qa():
                S.op("dve", C("tensor_scalar", out=qTa[:, u].rearrange("p r t -> p (r t)"), in0=PTb[0][:, 0:512],
                              scalar1=gcols[:, 0:1], scalar2=None, op0=ALU.mult),
                     reads=[b_pt[0], b_gcols], writes=[b_qTa[u]])

            def evac_qb():
                for g_ in range(2):
                    hp_ = slice(g_ * 64, (g_ + 1) * 64)
                    S.op("dve", C("tensor_scalar", out=QNt[hp_, u, g_, :], in0=PTb[0][hp_, 0:512],
                                  scalar1=gcols[hp_, 2:3], scalar2=None, op0=ALU.mult),
                         reads=[b_pt[0], b_gcols], writes=[b_QNw[u][g_]])

            def evac_k():
                for g_ in range(2):
                    hp_ = slice(g_ * 64, (g_ + 1) * 64)
                    S.op("dve", C("tensor_scalar", out=kTa[g_][hp_, ra * 128:(ra + 1) * 128], in0=PTb[0][hp_, 0:128],
                                  scalar1=gcols[hp_, 1:2], scalar2=None, op0=ALU.mult),
                         reads=[b_pt[0], b_gcols], writes=[b_kTa[ra]])
                    S.op("dve", C("tensor_scalar", out=kTw[g_][hp_, rw * 128:(rw + 1) * 128], in0=PTb[0][hp_, 256:384],
                                  scalar1=gcols[hp_, 5:6], scalar2=None, op0=ALU.mult),
                         reads=[b_pt[0], b_gcols], writes=[b_kTw[rw]])
                    S.op("dve", C("tensor_scalar", out=KE[g_][hp_, i * 128:(i + 1) * 128], in0=PTb[0][hp_, 128:256],
                                  scalar1=gcols[hp_, 4:5], scalar2=None, op0=ALU.mult),
                         reads=[b_pt[0], b_gcols], writes=[b_kTs[i]])

            stc = {}
            tau0 = (i % 4) * 128

            def c0_():
                stc["pj"] = proj(G_C, 512)

            def c1_():
                S.op("act", C("copy", out=qn[1][:], in_=PJ[:, stc["pj"], :]), reads=[b_pj[stc["pj"]]], writes=[b_qn[1]])

            def c2_():
                for c in range(4):
                    S.op("pe", C("transpose", out=PTb[0][:, c * 128:(c + 1) * 128], in_=qn[1][:, c * 128:(c + 1) * 128], identity=idb[:]),
                         reads=[b_qn[1], b_idb], writes=[b_pt[0]])

            def c3_():
                for (Xt, bX, cb) in ((Xk, b_Xk, 0), (Xv, b_Xv, 256)):
                    S.op("dve", C("tensor_copy", out=Xt[0][0:64, :, 17 + tau0:17 + tau0 + 128],
                                  in_=PTb[0][0:64, cb:cb + 256].rearrange("p (g t) -> p g t", g=2)),
                         reads=[b_pt[0]], writes=[bX[0]])
                    S.op("dve", C("tensor_copy", out=Xt[0][64:128, :, 16 + tau0:16 + tau0 + 128],
                                  in_=PTb[0][64:128, cb:cb + 256].rearrange("p (g t) -> p g t", g=2)),
                         reads=[b_pt[0]], writes=[bX[0]])
            chains.append((8, [c0_, c1_, c2_, c3_]))
            chains.append((10, norm_chain(G_K, 6, 0, 3, evac_k)))

            stv_ = {}

            def v0_():
                stv_["pj"] = proj(G_V, 408)

            def v1_():
                pj = stv_["pj"]
                S.op("dve", C("tensor_copy", out=Va[:, ra, :, 0:64], in_=PJ[:, pj, 0:128].rearrange("p (g d) -> p g d", g=2)),
                     reads=[b_pj[pj]], writes=[b_Va[ra]])
                S.op("dve", C("tensor_copy", out=Vs[:, i, :, 0:64], in_=PJ[:, pj, 128:256].rearrange("p (g d) -> p g d", g=2)),
                     reads=[b_pj[pj]], writes=[b_Vs[i]])
                S.op("dve", C("tensor_copy", out=Vw[:, rw, :, 0:64], in_=PJ[:, pj, 256:384].rearrange("p (g d) -> p g d", g=2)),
                     reads=[b_pj[pj]], writes=[b_Vw[rw]])
                S.op("dve", C("tensor_tensor", out=gtmp[:], in0=PJ[:, pj, 384:408], in1=bgate[:], op=ALU.add),
                     reads=[b_pj[pj], b_bgate], writes=[b_gtmp])

            def v2_():
                S.op("act", C("activation", out=gtmp[:], in_=gtmp[:], func=AF.Exp, scale=-1.0), reads=[b_gtmp], writes=[b_gtmp])

            def v3_():
                S.op("dve", C("tensor_scalar", out=gtmp[:], in0=gtmp[:], scalar1=1.0, scalar2=None, op0=ALU.add),
                     reads=[b_gtmp], writes=[b_gtmp])
                S.op("dve", C("reciprocal", out=gts[:, u, :], in_=gtmp[:]), reads=[b_gtmp], writes=[b_gts[u]])
            chains.append((12, [v0_, v1_, v2_, v3_]))

            c0x = 128 * (i % 4)

            def cm(part):
                (Xt, bX, w1, bw1) = ((Xk, b_Xk, w1k, b_w1k), (Xv, b_Xv, w1v, b_w1v))[part // 2]
                jc = part % 2

                def f():
                    for lp in range(16):
                        S.op("pe", C("matmul", CPb[:, part * 16:(part + 1) * 16].rearrange("p (g m) -> p g m", g=2),
                                     lhsT=w1[:, lp, jc * 128:(jc + 1) * 128],
                                     rhs=Xt[0][:, :, c0x + 1 + 2 * lp:c0x + 1 + 2 * lp + 16 * 7 + 1:16],
                                     start=(lp == 0), stop=(lp == 15), skip_group_check=True),
                             reads=[bX[0], bw1], writes=[b_cp], inc=(lp == 15))
                return f

            def cm_act():
                for part in range(4):
                    nb1 = (nb1k, nb1v)[part // 2]
                    S.op("act", C("activation", out=hsig[:, part * 16:(part + 1) * 16], in_=CPb[:, part * 16:(part + 1) * 16],
                                  func=AF.Exp, scale=-1.0, bias=nb1[:, part % 2:part % 2 + 1]),
                         reads=[b_cp, b_nb1], writes=[b_hsig])

            def cm_dve1():
                S.op("dve", C("tensor_scalar", out=hsig[:], in0=hsig[:], scalar1=1.0, scalar2=None, op0=ALU.add),
                     reads=[b_hsig], writes=[b_hsig])
                S.op("dve", C("reciprocal", out=hsig[:], in_=hsig[:]), reads=[b_hsig], writes=[b_hsig])

            def cm_dve2():
                for part in range(4):
                    hid, bh = ((hidk, b_hidk), (hidv, b_hidv))[part // 2]
                    b1 = (b1k, b1v)[part // 2]
                    S.op("dve", C("scalar_tensor_tensor", out=hid[:, part % 2, :], in0=CPb[:, part * 16:(part + 1) * 16],
                                  scalar=b1[:, part % 2:part % 2 + 1], in1=hsig[:, part * 16:(part + 1) * 16],
                                  op0=ALU.add, op1=ALU.mult),
                         reads=[b_cp, b_b1k, b_b1v, b_hsig], writes=[bh])

            def cm_mm2():
                for jc in range(2):
                    S.op("pe", C("matmul", CPb[0:16, 64:128], lhsT=hidk[:, jc, :], rhs=w2k[:, jc, :],
                                 start=(jc == 0), stop=(jc == 1), skip_group_check=True),
                         reads=[b_hidk, b_w2k], writes=[b_cp], inc=False)
                for jc in range(2):
                    S.op("pe", C("matmul", CPb[0:16, 128:192], lhsT=hidv[:, jc, :], rhs=w2v[:, jc, :],
                                 start=(jc == 0), stop=(jc == 1), skip_group_check=True),
                         reads=[b_hidv, b_w2v], writes=[b_cp], inc=(jc == 1))

            def cm_a2():
                S.op("act", C("activation", out=kjunk[:], in_=CPb[0:16, 64:128], func=AF.Square, accum_out=kc32[:, 0:1]),
                     reads=[b_cp], writes=[b_kjunk, b_kc32])
                S.op("dve", C("tensor_copy", out=vstag[:], in_=CPb[0:16, 128:192]), reads=[b_cp, b_kc32], writes=[b_vstag])

            def cm_a3():
                S.op("act", C("activation", out=kc32[:, 1:2], in_=kc32[:, 0:1], func=AF.Ln, scale=1.0 / 64, bias=EPS),
                     reads=[b_kc32], writes=[b_kc32])
                S.op("act", C("activation", out=kc32[:, 2:3], in_=kc32[:, 1:2], func=AF.Exp, scale=-0.5),
                     reads=[b_kc32], writes=[b_kc32])
                n0 = 8 * i
                T, p0 = n0 // 128, n0 % 128
                m0 = 1 if i == 0 else 0
                for g in range(2):
                    S.dma(C("dma_start", out=VC[p0 + m0:p0 + 8, T, g, 0:64], in_=vstag[g * 8 + m0:(g + 1) * 8, :]),
                          reads=[b_vstag], writes=[b_VC])

            def cm_d3():
                for h2 in range(2):
                    S.op("dve", C("tensor_scalar", out=kndup[:, h2 * 64:(h2 + 1) * 64], in0=CPb[0:16, 64:128],
                                  scalar1=kc32[:, 2:3], scalar2=None, op0=ALU.mult),
                         reads=[b_cp, b_kc32], writes=[b_kndup])

            def cm_t():
                S.op("pe", C("transpose", out=PTb[0][:, 0:16], in_=kndup[:], identity=idb[0:16, 0:16]),
                     reads=[b_kndup, b_idb], writes=[b_pt[0]])

            def cm_e():
                n0 = 8 * i
                for g in range(2):
                    S.op("dve", C("tensor_scalar", out=kcT[g][g * 64:(g + 1) * 64, n0:n0 + 8],
                                  in0=PTb[0][g * 64:(g + 1) * 64, g * 8:(g + 1) * 8],
                                  scalar1=gcols[g * 64:(g + 1) * 64, 3:4], scalar2=None, op0=ALU.mult),
                         reads=[b_pt[0], b_gcols], writes=[b_kcT])
            nop = lambda: None
            import os as _os
            chains.append((12, [cm(0), cm(1), cm(2), cm(3), cm_act, cm_dve1, cm_dve2, cm_mm2, cm_a2, cm_a3, cm_d3, nop, nop, nop, cm_t, cm_e][:int(_os.environ.get('NCM', '99'))]))
            chains.append((15, norm_chain(G_QA, 8, 1, 4, evac_qa)))
            chains.append((18, norm_chain(G_QB, 8, 0, 4, evac_qb)))

            def z_chain(c0, zbuf, bz, col0):
                st = {}

                def s0():
                    st["pj"] = proj(c0, 512)

                def s1():
                    S.op("act", C("activation", out=zbuf, in_=PJ[:, st["pj"], :], func=AF.Exp, scale=-1.0),
                         reads=[b_pj[st["pj"]]], writes=[bz])

                def s2():
                    S.op("act", C("activation", out=zbuf, in_=zbuf, func=AF.Ln, bias=1.0), reads=[bz], writes=[bz])

                def s3():
                    S.op("act", C("activation", out=zbuf, in_=zbuf, func=AF.Exp, scale=-1.0), reads=[bz], writes=[bz])

                def s4():
                    pj = st["pj"]
                    S.op("dve", C("tensor_tensor", out=sz[:, u, col0:col0 + 512], in0=PJ[:, pj, :], in1=zbuf, op=ALU.mult),
                         reads=[b_pj[pj], bz], writes=[b_sz[u]])
                return [s0, s1, s2, s3, s4]
            chains.append((21, z_chain(G_ZA, sq[:, :], b_sq, 0)))
            chains.append((24, z_chain(G_ZB, zt2[:, :], b_zt2, 512)))
            return chains

        NSTEP = 32
        ntile = 4 * n_sb

        def x_load(b, i):
            S.dma(C("dma_start", out=xs[i % 2], in_=x_d[b, i * 128:(i + 1) * 128, :]), writes=[b_xs[i % 2]])

        def gen_steps(b, pc, tl):
            chains = []
            if pc is not None:
                if pc % 4 == 0 and pc > 0:
                    for (Xt, bX) in ((Xk, b_Xk), (Xv, b_Xv)):
                        S.op("pool", C("tensor_copy", out=Xt[0][:, :, 0:17], in_=Xt[0][:, :, 512:529]), reads=[bX[0]], writes=[bX[0]])
                chains += build_PC(b, pc)[:nch]
                if pc + 1 < ntile:
                    chains.append((10, [lambda: x_load(b, pc + 1)]))
            if tl is not None:
                chains.append(tail_chain(b, tl))
            T_ = max(o + len(ch) for o, ch in chains)
            assert T_ <= NSTEP, T_
            for tau in range(NSTEP):
                for o, ch in chains:
                    k = tau - o
                    if 0 <= k < len(ch):
                        ch[k]()
                yield

        def score_tile(lhsT_ap, rhs_ap, extra, reads):
            st = st_i[0] % 2
            st_i[0] += 1
            n = len(extra)
            S.op("pe", C("matmul", ST[:, st, :], lhsT=lhsT_ap, rhs=rhs_ap, start=True, stop=(n == 0)),
                 reads=reads, writes=[b_st[st]], inc=(n == 0))
            for j, (l_ap, r_ap, rd) in enumerate(extra):
                S.op("pe", C("matmul", ST[:, st, :], lhsT=l_ap, rhs=r_ap, start=False, stop=(j == n - 1)),
                     reads=rd, writes=[b_st[st]], inc=(j == n - 1))
            return st

        def exp_tile(st):
            p = pts_i[0] % 2
            pts_i[0] += 1
            S.op("act", C("activation", out=PTs[p][:], in_=ST[:, st, :], func=AF.Exp), reads=[b_st[st]], writes=[b_PTs[p]],
                 self_sync=False)
            return p

        def bc4(ap2d):
            return V(ap2d, [[0, 4], [1, 128]])

        def phaseA(b, i, u, nxt):
            gsl = gts[:, u, :]
            jobs = []

            def Bx(Bt, ty, g):
                return [(idr[:], Bt[:, ty, 4 * g:4 * g + 4, :].rearrange("p h t -> p (h t)"), [b_idr, b_BA if Bt is BA else b_BB])]

            def add_branch(g, kts, lhs_fn, rhs_fn, extras_fn, V_t, V_bufs, vidx, oab, post):
                nk = len(kts)
                for j, kt in enumerate(kts):
                    def fscore(kt=kt):
                        l_ap, l_rd = lhs_fn(kt)
                        r_ap, r_rd = rhs_fn(kt)
                        return score_tile(l_ap, r_ap, extras_fn(kt), l_rd + r_rd)

                    def pv(p, kt=kt, j=j):
                        for r in range(4):
                            S.op("pe", C("matmul", OA[:, oab, r * 66:(r + 1) * 66], lhsT=PTs[p][:, r * 128:(r + 1) * 128],
                                         rhs=V_t[:, vidx(kt), g, :], start=(j == 0 and r == 0), stop=(j == nk - 1), skip_group_check=True),
                                 reads=[b_PTs[p], V_bufs(kt)], writes=[b_oa[oab]], inc=(j == nk - 1 and r == 3))
                    jobs.append((fscore, pv, post if j == nk - 1 else None))

            def accumulate(g, oab, dcol, gate_br):
                S.op("dve", C("reciprocal", out=den[:, dcol:dcol + 4], in_=V(OA[:, oab, 64:65], [[66, 4]])),
                     reads=[b_oa[oab]], writes=[b_den])
                S.op("dve", C("tensor_tensor", out=coef[:, dcol:dcol + 4], in0=den[:, dcol:dcol + 4],
                              in1=gsl[:, g * 12 + gate_br:(g + 1) * 12:3], op=ALU.mult),
                     reads=[b_den, b_gts[u]], writes=[b_coef])
                S.op("dve", C("tensor_tensor", out=ytmp[oab][:, :].rearrange("p (r d) -> p r d", r=4),
                              in0=V(OA[:, oab, 0:1], [[66, 4], [1, 64]]),
                              in1=V(coef[:, dcol:dcol + 1], [[1, 4], [0, 64]]), op=ALU.mult),
                     reads=[b_oa[oab], b_coef], writes=[b_ytmp[oab]])
                S.op("dve", C("tensor_tensor", out=yB[:, g, :], in0=yB[:, g, :], in1=ytmp[oab][:, :], op=ALU.add),
                     reads=[b_yB[g], b_ytmp[oab]], writes=[b_yB[g]])

            Tmax = i // 16
            post_b_fns = {}
            for g in range(2):
                hp = slice(g * 64, (g + 1) * 64)
                qb_ap = QNt[hp, u, g, :]
                for T in range(Tmax + 1):
                    def fscore(T=T, hp=hp, qb_ap=qb_ap, g=g):
                        extra = []
                        if T == Tmax:
                            extra.append((idb[:], bc4(CM[:, i % 16, :]), [b_idb, b_CM]))
                        return score_tile(kcT[g][:, T * 128:(T + 1) * 128], QNt[:, u, g, :], extra, [b_kcT, b_QNw[u][g]])

                    def pv(p, T=T, g=g):
                        for r in range(4):
                            S.op("pe", C("matmul", OA[:, g, r * 128:(r + 1) * 128], lhsT=PTs[p][:, r * 128:(r + 1) * 128],
                                         rhs=VC[:, T, g, :], start=(T == 0 and r == 0), stop=(T == Tmax), skip_group_check=True),
                                 reads=[b_PTs[p], b_VC], writes=[b_oa[g]], inc=(T == Tmax and r == 3))

                    def post(st, g=g, hp=hp):
                        S.op("dve", C("tensor_scalar", out=den[:, 0:4], in0=V(OA[:, g, 64:65], [[128, 4]]),
                                      scalar1=1e-30, scalar2=None, op0=ALU.max),
                             reads=[b_oa[g]], writes=[b_den])
                        S.op("dve", C("reciprocal", out=den[:, 0:4], in_=den[:, 0:4]), reads=[b_den], writes=[b_den])
                        S.op("dve", C("tensor_tensor", out=coef[:, 0:4], in0=den[:, 0:4],
                                      in1=gsl[:, g * 12:(g + 1) * 12:3], op=ALU.mult),
                             reads=[b_den, b_gts[u]], writes=[b_coef])
                        pat_ap = PAT[:, 64 - 2 * i + 64:128 - 2 * i + 64]
                        for r in range(4):
                            dst, bdst = (impA, b_impA) if r % 2 == 0 else (impB, b_impB)
                            in0 = OA[:, g, r * 128 + 66:r * 128 + 128]
                            if r == 1:
                                S.op("dve", C("tensor_scalar", out=dst[:, 1:63], in0=in0, scalar1=den[:, r:r + 1], scalar2=None, op0=ALU.mult),
                                     reads=[b_oa[g], b_den], writes=[bdst])
                            else:
                                in1, rd1 = (pat_ap[:, 1:63], b_PAT) if r == 0 else (dst[:, 1:63], bdst)
                                S.op("dve", C("scalar_tensor_tensor", out=dst[:, 1:63], in0=in0, scalar=den[:, r:r + 1],
                                              in1=in1, op0=ALU.mult, op1=ALU.add),
                                     reads=[b_oa[g], b_den, rd1], writes=[bdst])
                        S.op("dve", C("tensor_tensor", out=yB[:, g, :].rearrange("p (r d) -> p r d", r=4),
                                      in0=V(OA[:, g, 0:1], [[128, 4], [1, 64]]), in1=V(coef[:, 0:1], [[1, 4], [0, 64]]), op=ALU.mult),
                             reads=[b_oa[g], b_coef], writes=[b_yB[g]])
                        S.op("dve", C("tensor_tensor", out=score[:, 1:63], in0=impA[:, 1:63], in1=impB[:, 1:63], op=ALU.add),
                             reads=[b_impA, b_impB], writes=[b_score])
                        if i == NT - 1:
                            S.op("dve", C("tensor_copy", out=score[:, 63:64], in_=pat_ap[:, 63:64]), reads=[b_PAT, b_score], writes=[b_score])
                        S.op("dve", C("max", out=m8[:, 0:8], in_=score[:]), reads=[b_score], writes=[b_m8])
                        S.op("dve", C("match_replace", out=wk[:], in_to_replace=m8[:, 0:8], in_values=score[:], imm_value=-1e30),
                             reads=[b_score, b_m8], writes=[b_wk])
                        S.op("dve", C("max", out=m8[:, 8:16], in_=wk[:]), reads=[b_wk], writes=[b_m8])
                        S.op("dve", C("tensor_scalar", out=negmg[g][:, :].rearrange("p (a j) -> p a j", a=2), in0=V(score[:, 0:1], [[0, 2], [1, 64]]),
                                      scalar1=m8[:, 15:16], scalar2=NEG,
                                      op0=ALU.is_lt, op1=ALU.mult), reads=[b_score, b_m8], writes=[b_negm[g]])

                    def post_b(g=g):
                        S.op("pe", C("transpose", out=OA[:, g, 384:512], in_=negmg[g][:], identity=idf[:]),
                             reads=[b_negm[g], b_idf], writes=[b_oa[g]])
                        op_ = slice((1 - g) * 64, (2 - g) * 64)
                        S.op("dve", C("tensor_copy", out=QNt[op_, u, g, :].rearrange("p (r t) -> p r t", r=4), in_=bc4(OA[op_, g, 384:512])),
                             reads=[b_oa[g]], writes=[b_QNw[u][g]])
                    if T == Tmax:
                        post_b_fns[g] = post_b
                    jobs.append((fscore, pv, post if T == Tmax else None))

            for g in range(2):
                hp = slice(g * 64, (g + 1) * 64)
                qa_ap = qTa[hp, u].rearrange("p r t -> p (r t)")

                def a_post(st, g=g):
                    oab = g
                    post_b_fns[g]()
                    S.op("dve", C("tensor_tensor", out=den[:, 12:16], in0=V(OA[:, oab, 64:65], [[66, 4]]),
                                  in1=esink[:, 4 * g:4 * g + 4], op=ALU.add),
                         reads=[b_oa[oab], b_esink], writes=[b_den])
                    S.op("dve", C("reciprocal", out=den[:, 12:16], in_=den[:, 12:16]), reads=[b_den], writes=[b_den])
                    S.op("dve", C("tensor_tensor", out=ytmp[oab][:, :].rearrange("p (r d) -> p r d", r=4),
                                  in0=V(OA[:, oab, 0:1], [[66, 4], [1, 64]]),
                                  in1=V(den[:, 12:13], [[1, 4], [0, 64]]), op=ALU.mult),
                         reads=[b_oa[oab], b_den], writes=[b_ytmp[oab]])
                    S.op("pool", C("tensor_tensor", out=y[:, g * 256:(g + 1) * 256], in0=ytmp[oab][:, :],
                                   in1=sz[:, u, g * 256:(g + 1) * 256], op=ALU.mult),
                         reads=[b_ytmp[oab], b_sz[u]], writes=[b_y])
                add_branch(g, list(range(max(0, i - 1), i + 1)),
                           lambda kt, g=g: (kTa[g][:, (kt % RA) * 128:(kt % RA) * 128 + 128], [b_kTa[kt % RA]]),
                           lambda kt: (qTa[:, u].rearrange("p r t -> p (r t)"), [b_qTa[u]]),
                           lambda kt, g=g: Bx(BA, 0, g) if kt == i else Bx(BA, 1, g),
                           Va, lambda kt: b_Va[kt % RA], lambda kt: kt % RA, g, a_post)
            for g in range(2):
                hp = slice(g * 64, (g + 1) * 64)
                qb_ap = QNt[hp, u, g, :]

                def win_extras(kt, g=g):
                    dk = i - kt
                    if dk == 0:
                        return Bx(BB, 0, g)
                    if dk == 1:
                        return Bx(BB, 1, g)
                    if dk == 4:
                        return [(idb[:], bc4(TRI[:, :]), [b_idb, b_TRI])]
                    return []
                add_branch(g, list(range(max(0, i - 4), i + 1)),
                           lambda kt, g=g: (kTw[g][:, (kt % RW) * 128:(kt % RW) * 128 + 128], [b_kTw[kt % RW]]),
                           lambda kt, g=g: (QNt[:, u, g, :], [b_QNw[u][g]]),
                           win_extras, Vw, lambda kt: b_Vw[kt % RW], lambda kt: kt % RW, g,
                           lambda st, g=g: accumulate(g, g, 8, 2))
            for g in range(2):
                hp = slice(g * 64, (g + 1) * 64)
                qb_ap = QNt[hp, u, g, :]

                def sel_lhs(kt, g=g, hp=hp):
                    return (KE[g][:, kt * 128:(kt + 1) * 128], [b_kTs[kt], b_E])

                def sel_rhs(kt, g=g, qb_ap=qb_ap):
                    return (QNt[:, u, g, :], [b_QNw[u][g]])

                def sel_extras(kt, g=g):
                    if kt == i:
                        return Bx(BB, 0, g)
                    if kt == i - 1:
                        return Bx(BB, 1, g)
                    return []

                def sel_post(st, g=g):
                    accumulate(g, g, 4, 1)
                    S.op("dve", C("tensor_tensor", out=y[:, 512 + g * 256:512 + (g + 1) * 256], in0=yB[:, g, :],
                                  in1=sz[:, u, 512 + g * 256:512 + (g + 1) * 256], op=ALU.mult),
                         reads=[b_yB[g], b_sz[u]], writes=[b_y])
                add_branch(g, list(range(0, i + 1)), sel_lhs, sel_rhs, sel_extras, Vs, lambda kt: b_Vs[kt], lambda kt: kt, g, sel_post)

            pend = None
            nj = len(jobs)
            sdone = 0
            ncmp = 2 * (Tmax + 1)
            for jx, (fscore, pv, post) in enumerate(jobs):
                st = fscore()
                want = ((jx + 1) * NSTEP) // nj
                if jx + 1 >= ncmp + 1:
                    want = max(2, want)
                while sdone < want:
                    next(nxt, None)
                    sdone += 1
                if pend is not None:
                    pst, ppv, ppost = pend
                    p = exp_tile(pst)
                    ppv(p)
                    if ppost is not None:
                        ppost(pst)
                pend = (st, pv, post)
            pst, ppv, ppost = pend
            p = exp_tile(pst)
            ppv(p)
            if ppost is not None:
                ppost(pst)

        def tail_chain(b, i):
            xrs = i % 2

            def t0():
                S.dma(C("dma_start", out=xr[xrs], in_=x_d[b, i * 128:(i + 1) * 128, :]), writes=[b_xr[xrs]])
                for kc in range(8):
                    S.op("pe", C("transpose", out=PTb[1][:, kc * 128:(kc + 1) * 128], in_=y[:, kc * 128:(kc + 1) * 128],
                                 identity=idb[:]), reads=[b_y, b_idb], writes=[b_pt[1]])

            def t1():
                S.op("act", C("copy", out=yT[:, :, :].rearrange("p k t -> p (k t)"), in_=PTb[1][:, :]), reads=[b_pt[1]], writes=[b_yT])
            stt = {}

            def mm(hf):
                def f():
                    pj = next_pj()
                    stt[hf] = pj
                    for kc in range(8):
                        S.op("pe", C("matmul", PJ[:, pj, :], lhsT=yT[:, kc, :], rhs=Wout[:, kc, hf * 512:(hf + 1) * 512],
                                     start=(kc == 0), stop=(kc == 7)),
                             reads=[b_yT, b_wout] + b_wout_l, writes=[b_pj[pj]], inc=(kc == 7))
                return f

            def ml(hf):
                def f():
                    pj = stt[hf]
                    S.op("dve", C("tensor_tensor", out=rtmp[:, :], in0=PJ[:, pj, :],
                                  in1=GATE[:, b, hf * 512:(hf + 1) * 512], op=ALU.mult),
                         reads=[b_pj[pj], b_GATE], writes=[b_rtmp])
                return f

            def t6(hf):
                def f():
                    S.op("dve", C("tensor_tensor", out=xr[xrs][:, hf * 512:(hf + 1) * 512], in0=xr[xrs][:, hf * 512:(hf + 1) * 512],
                                  in1=rtmp[:, :], op=ALU.add),
                         reads=[b_xr[xrs], b_rtmp], writes=[b_xr[xrs]])
                return f

            def t7():
                S.dma(C("dma_start", out=out_d[b, i * 128:(i + 1) * 128, :], in_=xr[xrs]), reads=[b_xr[xrs]], is_output=True)
            return (0, [t0, t1, mm(0), ml(0), t6(0), mm(1), ml(1), t6(1), t7])


        def drain(gen):
            for _ in gen:
                pass

        for b in range(nb):
            if b > 0:
                for g_ in range(2):
                    S.op("pool", C("memset", kcT[g_][:], 0.0), writes=[b_kcT])
                S.op("pool", C("memset", VC[:, :, :, 0:64], 0.0), writes=[b_VC])
                S.op("pool", C("memset", Xk[0][:, :, 0:17], 0.0), writes=[b_Xk[0]])
                S.op("pool", C("memset", Xv[0][:, :, 0:17], 0.0), writes=[b_Xv[0]])
                S.op("pool", C("memset", score[:, 63:64], 0.0), reads=[b_score], writes=[b_score])
            x_load(b, 0)
            drain(gen_steps(b, 0, None))
            for i in range(ntile):
                pc_ = i + 1 if i + 1 < ntile else None
                tl_ = i - 1 if i >= 1 else None
                nxt = gen_steps(b, pc_, tl_) if (pc_ is not None or tl_ is not None) else iter(())
                if "A" in stages and i < n_a:
                    phaseA(b, i, i % 2, nxt)
                drain(nxt)
            drain(gen_steps(b, None, ntile - 1))
        if dumps:
            L = dict(GATE=(GATE, [b_GATE]), Gcol=(Gcol, [b_Gcol]), SHcol=(SHcol, [b_SHcol]), BA=(BA, [b_BA]), BB=(BB, [b_BB]),
                     b1k=(b1k, [b_b1k]), b1v=(b1v, [b_b1v]), Win=(Win, [b_win]), Wout=(Wout, [b_wout]), w1k=(w1k, [b_w1k]),
                     qTa=(qTa, b_qTa), kTs=(KE[0], b_kTs),
                     Vs=(Vs, b_Vs), Va=(Va, b_Va), Vw=(Vw, b_Vw), VC=(VC, [b_VC]), sz=(sz, b_sz), gts=(gts, b_gts),
                      y=(y, [b_y]), hT=(hT0, [bht]),
                     Xk=(Xk0, [bxk]), hidk=(hidk, [b_hidk]), score=(score, [b_score]), yB=(yB, b_yB), esink=(esink, [b_esink]),
                     den=(den, [b_den]), coef=(coef, [b_coef]))
            for nm in dumps:
                t, bufs = L[nm]
                shp = list(t.shape)
                dd = nc.dram_tensor("d_" + nm, shp, t.dtype, kind="ExternalOutput").ap()
                full = t[tuple(slice(None) for _ in shp)]
                S.dma(C("dma_start", out=dd, in_=full), reads=list(bufs), is_output=True)
        S.emit()
    return nc


def _t5_bucket(dist):
    n = np.maximum(dist, 0)
    nf = np.maximum(n, 1).astype(np.float32)
    large = 16 + (np.log(nf / np.float32(16)) / np.float32(np.log(128 / 16)) * np.float32(16)).astype(np.int32)
    large = np.minimum(large, 31)
    return np.where(n < 16, n, large)


def _constants():
    bf = ml_dtypes.bfloat16
    sl = np.arange(128)[:, None]
    tl = np.arange(128)[None, :]
    d_diag = tl - sl
    d_prev = 128 + tl - sl
    idx = np.stack([_t5_bucket(d_diag), _t5_bucket(d_prev)], 0)
    maskAB = np.zeros((128, 4, 128), np.float32)
    maskAB[:, 0, :] = np.where(d_diag >= 0, 0.0, NEG)
    maskAB[:, 1, :] = np.where(d_prev < 128, 0.0, NEG)
    maskAB[:, 2, :] = np.where(d_diag >= 0, 0.0, NEG)
    maskAB[:, 3, :] = 0.0
    E = (np.arange(SEQ)[None, :] // 64 == (np.arange(128) % 64)[:, None]).astype(np.float32).astype(bf)
    nl = np.arange(128)[:, None, None]
    o = np.arange(16)[None, :, None]
    t3 = np.arange(128)[None, None, :]
    cmask = np.where(16 * nl + 15 <= 128 * o + t3, 0.0, NEG).astype(np.float32).astype(bf)
    tri = np.where(sl > tl, 0.0, NEG).astype(np.float32).astype(bf)
    pat = np.zeros((128, 192), np.float32)
    pat[:, 0] = BONUS
    hi = (np.arange(128) >= 64).astype(np.int64)
    pat[np.arange(128), 64 + 63 + hi] = BONUS
    pat[np.arange(128), 64 + 64 + hi] = BONUS
    pat = pat.astype(bf)
    npr = np.arange(256)
    n = npr - 1
    c_lo = 16 * n
    s_lo = 64 * np.arange(64)
    ov = np.clip(np.minimum(c_lo[:, None] + 32, s_lo[None, :] + 64) - np.maximum(c_lo[:, None], s_lo[None, :]), 0, None) / 32.0
    ov[0, :] = 0.0
    ov = np.ascontiguousarray(ov.reshape(2, 128, 64).transpose(1, 0, 2)[:, :, 1:63]).astype(np.float32).astype(bf)
    return idx, maskAB, E, cmask, tri, pat, ov


def _perm_cols():
    o_qa, o_ka, o_va, o_za, o_qb, o_kc, o_vc, o_ks, o_vs, o_kw, o_vw, o_zb, o_gb = (
        0, 512, 640, 768, 1280, 1792, 1920, 2048, 2176, 2304, 2432, 2560, 3072)
    r64 = np.arange(64)
    cols = []
    for base in (o_qa, o_qb):
        for r in range(4):
            cols += [base + r * 64 + r64, base + (4 + r) * 64 + r64]
    cols += [o_ka + np.arange(128), o_ks + np.arange(128), o_kw + np.arange(128)]
    for base in (o_kc, o_vc):
        for g in range(2):
            cols += [base + g * 64 + r64, base + g * 64 + r64]
    cols += [o_va + np.arange(128), o_vs + np.arange(128), o_vw + np.arange(128), o_gb + np.arange(24)]
    cols += [o_za + np.arange(512), o_zb + np.arange(512)]
    cols = np.concatenate(cols)
    assert cols.shape[0] == WCOLS
    return cols


_NC_CACHE = {}


def kernel(x, c, w_ada, b_ada, norm_gain, w_in, b_nsa_gate, q_gain_a, k_gain_a, sinks,
           q_gain_b, k_gain_cmp, k_gain_sel, k_gain_win, cmp_pos_k, cmp_pos_v,
           w_cmp_k1, w_cmp_k2, w_cmp_v1, w_cmp_v2, w_out, rel_bias):
    f = lambda a: np.ascontiguousarray(np.asarray(a, dtype=np.float32))
    x = f(x); c = f(c); w_ada = f(w_ada)[0]; b_ada = f(b_ada)[0]; norm_gain = f(norm_gain)[0]
    w_in = f(w_in)[0]; b_nsa_gate = f(b_nsa_gate)[0]; sinks = f(sinks)[0]
    rel_bias = f(rel_bias); w_out = f(w_out)[0]
    idx, maskAB, E, cmask, tri, pat, ov = _constants()
    biasAB = rel_bias[idx]
    biasA = np.ascontiguousarray(biasAB[..., 0:8].transpose(1, 0, 3, 2))
    biasB = np.ascontiguousarray(biasAB[..., 8:16].transpose(1, 0, 3, 2))
    c31 = np.ascontiguousarray(rel_bias[31:32, 8:16])
    gcols = np.stack([np.tile(f(g)[0], 2) for g in (q_gain_a, k_gain_a, q_gain_b, k_gain_cmp, k_gain_sel, k_gain_win)], 1)
    shared = {
        "w_ada": w_ada,
        "bada_col": np.ascontiguousarray(b_ada[0:2048].reshape(16, 128).T),
        "bada_gate": np.ascontiguousarray(b_ada[2048:3072].reshape(1, DM)),
        "ng_col": np.ascontiguousarray(norm_gain.reshape(8, 128).T),
        "w_in_p": np.ascontiguousarray(w_in[:, _perm_cols()]),
        "bgate": b_nsa_gate.reshape(1, 24),
        "gcols": np.ascontiguousarray(gcols),
        "sinks": sinks.reshape(1, 8),
        "posk": np.ascontiguousarray(f(cmp_pos_k)[0].reshape(16, 128).T),
        "posv": np.ascontiguousarray(f(cmp_pos_v)[0].reshape(16, 128).T),
        "w1k": f(w_cmp_k1)[0], "w1v": f(w_cmp_v1)[0], "w2k": f(w_cmp_k2)[0], "w2v": f(w_cmp_v2)[0],
        "w_out": w_out, "biasA": biasA, "biasB": biasB, "c31": c31, "maskAB": maskAB,
        "Emat": E, "cmask": cmask, "tri": tri, "pat": pat, "ov": ov,
    }
    in_maps = []
    for core in range(NCORES):
        m = dict(shared)
        m["x"] = x[NB * core:NB * (core + 1)]
        cc = c[NB * core:NB * (core + 1)]
        m["cT"] = np.ascontiguousarray(cc.reshape(NB, 8, 128).transpose(2, 1, 0))
        in_maps.append(m)
    if "nc" not in _NC_CACHE:
        _NC_CACHE["nc"] = build_program()
    res = run_bass_kernel_spmd(_NC_CACHE["nc"], in_maps, core_ids=list(range(NCORES)))
    return np.concatenate([np.asarray(r["out"], dtype=np.float32) for r in res.results], axis=0)
```

```python
import numpy as np
import ml_dtypes
from contextlib import ExitStack
import concourse.bass as bass
import concourse.mybir as mybir
from concourse.bass_utils import run_bass_kernel_spmd

F32 = mybir.dt.float32
BF16 = mybir.dt.bfloat16
F32R = mybir.dt.float32r
AF = mybir.ActivationFunctionType
ALU = mybir.AluOpType
AX = mybir.AxisListType

NCORES = 8
SEQ = 4096
DM = 1024
NT = SEQ // 128
NB = 2
WCOLS = 3352
NEG = -30000.0
BONUS = 1.0e4
EPS = 1e-6
G_QA, G_QB, G_K, G_C, G_V, G_ZA, G_ZB = 0, 512, 1024, 1408, 1920, 2328, 2840
RA, RW = 3, 6


class Buf:
    __slots__ = ("w", "r", "dsem", "dcount", "name")

    def __init__(self, name=""):
        self.w = None
        self.r = {}
        self.dsem = None
        self.dcount = 0
        self.name = name


class Sched:
    ENG = ("pe", "act", "dve", "pool", "sp")

    def __init__(self, nc, same_engine_sync=True):
        self.nc = nc
        self.ops = {e: [] for e in self.ENG}
        self.count = {e: 0 for e in self.ENG}
        self.waited = {e: {} for e in self.ENG}
        self.sems = {}
        self.dma_sems = []
        self.same_engine_sync = same_engine_sync
        self.out_waits = []

    def _need(self, eng, dep, needs, self_sync=True):
        if dep is None:
            return
        k, v = dep
        if k == eng:
            if eng in ("pe", "sp"):
                return
            if not self.same_engine_sync or not self_sync:
                return
        if self.waited[eng].get(k, 0) >= v:
            return
        if needs.get(k, 0) < v:
            needs[k] = v

    def _emit_waits(self, eng, needs):
        for k, v in needs.items():
            self.ops[eng].append(("w", k, v))
            self.waited[eng][k] = v

    def op(self, eng, fn, reads=(), writes=(), inc=True, self_sync=True):
        needs = {}
        for b in reads:
            self._need(eng, b.w, needs, self_sync)
        for b in writes:
            self._need(eng, b.w, needs, self_sync)
            for k, v in b.r.items():
                self._need(eng, (k, v), needs, self_sync)
        self._emit_waits(eng, needs)
        if inc:
            self.count[eng] += 1
            val = self.count[eng]
        else:
            val = self.count[eng] + 1
        self.ops[eng].append(("op", fn, [(eng, 1)] if inc else []))
        for b in writes:
            b.w = (eng, val)
            b.r = {}
        for b in reads:
            if b.r.get(eng, 0) < val:
                b.r[eng] = val
        return val

    def dma(self, fn, reads=(), writes=(), q="sp", is_output=False):
        needs = {}
        for b in reads:
            self._need(q, b.w, needs)
        for b in writes:
            self._need(q, b.w, needs)
            for k, v in b.r.items():
                self._need(q, (k, v), needs)
        self._emit_waits(q, needs)
        owner = writes[0] if writes else reads[0]
        if owner.dsem is None:
            owner.dsem = "dma%d" % len(self.dma_sems)
            self.dma_sems.append(owner.dsem)
        owner.dcount += 16
        k, v = owner.dsem, owner.dcount
        self.ops[q].append(("op", fn, [(k, 16)]))
        for b in writes:
            b.w = (k, v)
            b.r = {}
        for b in reads:
            b.r[k] = v
        if is_output:
            self.out_waits.append((k, v))

    def emit(self):
        nc = self.nc
        with ExitStack() as es:
            for e in self.ENG:
                self.sems[e] = es.enter_context(nc.semaphore("prog_" + e))
            for k in self.dma_sems:
                self.sems[k] = es.enter_context(nc.semaphore(k))
            fin = {}
            for k, v in self.out_waits:
                fin[k] = max(fin.get(k, 0), v)
            for k, v in fin.items():
                self.ops["sp"].append(("w", k, v))
            block = es.enter_context(nc.Block())
            sems = self.sems
            ops = self.ops

            FUSE = ("activation", "copy", "tensor_tensor", "tensor_copy", "reciprocal", "tensor_reduce", "memset",
                    "tensor_scalar", "scalar_tensor_tensor", "max", "matmul", "transpose")

            def run(engine, lst, fuse=False):
                pend = []
                for item in lst:
                    if item[0] == "w":
                        pend.append(item)
                        continue
                    name, args, kw = item[1]
                    can = fuse and pend and name in FUSE and kw.get("accum_out") is None
                    for w in (pend[:-1] if can else pend):
                        engine.wait_ge(sems[w[1]], w[2])
                    ins = getattr(engine, name)(*args, **kw)
                    if can:
                        ins._wait_ge(sems[pend[-1][1]], pend[-1][2])
                    pend = []
                    for (k, n) in item[2]:
                        ins.then_inc(sems[k], n)
                for w in pend:
                    engine.wait_ge(sems[w[1]], w[2])

            @block.sync
            def _(e):
                run(e, ops["sp"])

            @block.tensor
            def _(e):
                run(e, ops["pe"], fuse=True)

            @block.scalar
            def _(e):
                run(e, ops["act"], fuse=True)

            @block.vector
            def _(e):
                run(e, ops["dve"], fuse=True)

            @block.gpsimd
            def _(e):
                run(e, ops["pool"], fuse=True)


def C(name, *args, **kw):
    return (name, args, kw)


def V(ap, dims):
    return bass.AP(tensor=ap.tensor, offset=ap.offset, ap=[list(ap.ap[0])] + [list(d) for d in dims])


def build_program(nb=NB, n_sb=NT // 4, stages="PCA", n_a=999, dumps=None, pro=9, nch=99):
    nc = bass.Bass("TRN2", target_bir_lowering=False)
    S = Sched(nc)

    def din(name, shape, dt=F32):
        return nc.dram_tensor(name, list(shape), dt, kind="ExternalInput").ap()

    x_d = din("x", [NB, SEQ, DM])
    cT_d = din("cT", [128, 8, NB])
    wada_d = din("w_ada", [DM, 3 * DM])
    badac_d = din("bada_col", [128, 16])
    badag_d = din("bada_gate", [1, DM])
    ngc_d = din("ng_col", [128, 8])
    win_d = din("w_in_p", [DM, WCOLS])
    bgate_d = din("bgate", [1, 24])
    gcols_d = din("gcols", [128, 6])
    sinks_d = din("sinks", [1, 8])
    posk_d = din("posk", [128, 16])
    posv_d = din("posv", [128, 16])
    w1k_d = din("w1k", [2048, 256])
    w1v_d = din("w1v", [2048, 256])
    w2k_d = din("w2k", [256, 64])
    w2v_d = din("w2v", [256, 64])
    wout_d = din("w_out", [DM, DM])
    biasA_d = din("biasA", [128, 2, 8, 128])
    biasB_d = din("biasB", [128, 2, 8, 128])
    c31_d = din("c31", [1, 8])
    maskAB_d = din("maskAB", [128, 4, 128])
    E_d = din("Emat", [128, SEQ], BF16)
    cmask_d = din("cmask", [128, 16, 128], BF16)
    tri_d = din("tri", [128, 128], BF16)
    pat_d = din("pat", [128, 192], BF16)
    ov_d = din("ov", [128, 2, 62], BF16)
    out_d = nc.dram_tensor("out", [NB, SEQ, DM], F32, kind="ExternalOutput").ap()

    with ExitStack() as es:
        def sb(name, shape, dt):
            return es.enter_context(nc.sbuf_tensor("s_" + name, list(shape), dt))

        def ps(name, shape, dt):
            return es.enter_context(nc.psum_tensor("p_" + name, list(shape), dt))

        PJ = ps("PJ", [128, 2, 512], F32)
        PT = ps("PT", [128, 2, 512], F32)
        ST = ps("ST", [128, 2, 512], F32)
        OA = ps("OA", [128, 2, 512], F32)
        b_pj = [Buf("pj0"), Buf("pj1")]
        b_pt = [Buf("pt0"), Buf("pt1")]
        b_st = [Buf("st0"), Buf("st1")]
        b_oa = [Buf("oa0"), Buf("oa1")]
        PTb = [PT[:, 0, :].bitcast(BF16), PT[:, 1, :].bitcast(BF16)]

        Win = sb("Win", [128, 8, WCOLS], BF16); b_win = Buf("win")
        b_win_l = [Buf("win%d" % i_) for i_ in range(16)]
        Wout = sb("Wout", [128, 8, DM], BF16); b_wout = Buf("wout")
        b_wout_l = [Buf("wout%d" % i_) for i_ in range(8)]
        w1k = sb("w1k", [128, 16, 256], BF16); b_w1k = Buf()
        w1v = sb("w1v", [128, 16, 256], BF16); b_w1v = Buf()
        w2k = sb("w2k", [128, 2, 64], BF16); b_w2k = Buf()
        w2v = sb("w2v", [128, 2, 64], BF16); b_w2v = Buf()
        b1k = sb("b1k", [128, 2], F32); b_b1k = Buf()
        b1v = sb("b1v", [128, 2], F32); b_b1v = Buf()
        STG = [sb("stg0", [128, 2048], F32), sb("stg1", [128, 2048], F32)]
        b_stg = [[Buf("s00"), Buf("s01")], [Buf("s10"), Buf("s11")]]
        kTa = [sb("kTa0", [128, RA * 128], BF16), sb("kTa1", [128, RA * 128], BF16)]; b_kTa = [Buf() for _ in range(RA)]
        kTw = [sb("kTw0", [128, RW * 128], BF16), sb("kTw1", [128, RW * 128], BF16)]; b_kTw = [Buf() for _ in range(RW)]
        KE = [sb("KE0", [128, SEQ], BF16), sb("KE1", [128, SEQ], BF16)]
        b_kTs = [Buf() for _ in range(NT)]
        Va = sb("Va", [128, RA, 2, 66], BF16); b_Va = [Buf() for _ in range(RA)]
        Vw = sb("Vw", [128, RW, 2, 66], BF16); b_Vw = [Buf() for _ in range(RW)]
        Vs = sb("Vs", [128, NT, 2, 66], BF16); b_Vs = [Buf() for _ in range(NT)]
        XW = 529
        Xk0 = sb("Xk0", [128, 2, XW], BF16)
        Xv0 = sb("Xv0", [128, 2, XW], BF16)
        Xk = [Xk0, Xk0]
        Xv = [Xv0, Xv0]
        bxk, bxv = Buf(), Buf()
        b_Xk = [bxk, bxk]
        b_Xv = [bxv, bxv]
        kcT = [sb("kcT0", [128, 256], BF16), sb("kcT1", [128, 256], BF16)]; b_kcT = Buf("kcT")
        VC = sb("VC", [128, 2, 2, 128], BF16); b_VC = Buf("VC")
        BA = sb("BA", [128, 2, 8, 128], F32R); b_BA = Buf()
        BB = sb("BB", [128, 2, 8, 128], F32R); b_BB = Buf()
        b_E = Buf()
        CM = sb("CM", [128, 16, 128], BF16); b_CM = Buf()
        TRI = sb("TRI", [128, 128], BF16); b_TRI = Buf()
        PAT = sb("PAT", [128, 192], BF16); b_PAT = Buf()
        GATE = sb("GATE", [128, NB, DM], F32); b_GATE = Buf()
        Gcol = sb("Gcol", [128, 8, NB], F32); b_Gcol = Buf()
        SHcol = sb("SHcol", [128, 8, NB], F32); b_SHcol = Buf()
        idf = sb("idf", [128, 128], F32); b_idf = Buf()
        idb = sb("idb", [128, 128], BF16); b_idb = Buf()
        idr = sb("idr", [128, 128], F32R); b_idr = Buf()
        gcols = sb("gcols", [128, 6], F32); b_gcols = Buf()
        esink = sb("esink", [128, 8], F32); b_esink = Buf()
        bgate = sb("bgate", [128, 24], F32); b_bgate = Buf()
        c31 = sb("c31", [128, 8], F32); b_c31 = Buf()


        stat = sb("stat", [128, 4], F32); b_stat = Buf()
        hT0 = sb("hT0", [128, 8, 128], BF16)
        hT = [hT0, hT0]
        bht = Buf()
        b_hT = [bht, bht]

        ssq = sb("ssq", [128, 16], F32); b_ssq = Buf()
        qn = [sb("qn0", [128, 512], BF16), sb("qn1", [128, 512], BF16)]
        b_qn = [Buf(), Buf()]
        qTa = sb("qTa", [128, 2, 4, 128], BF16); b_qTa = [Buf() for _ in range(2)]
        sz = sb("sz", [128, 2, DM], BF16); b_sz = [Buf() for _ in range(2)]
        scb = KE[0][:, :].bitcast(F32).rearrange("p (k b m) -> p k b m", k=8, b=NB)
        gts = sb("gts", [128, 2, 24], F32); b_gts = [Buf() for _ in range(2)]
        gtmp = sb("gtmp", [128, 24], F32); b_gtmp = Buf()
        zt2 = sb("zt2", [128, 512], F32); b_zt2 = Buf()
        hsig = sb("hsig", [128, 64], F32); b_hsig = Buf()
        nb1k = sb("nb1k", [128, 2], F32)
        nb1v = sb("nb1v", [128, 2], F32); b_nb1 = Buf()
        hidk = sb("hidk", [128, 2, 16], BF16); b_hidk = Buf()
        hidv = sb("hidv", [128, 2, 16], BF16); b_hidv = Buf()
        kc32 = sb("kc32", [16, 4], F32); b_kc32 = Buf()
        kndup = sb("kndup", [16, 128], BF16); b_kndup = Buf()
        kjunk = sb("kjunk", [16, 64], F32); b_kjunk = Buf()
        vstag = sb("vstag", [16, 64], BF16); b_vstag = Buf()
        PTs = [sb("PTs%d" % i, [128, 512], BF16) for i in range(2)]
        b_PTs = [Buf() for _ in range(2)]
        den = sb("den", [128, 16], F32); b_den = Buf()
        coef = sb("coef", [128, 16], F32); b_coef = Buf()
        impA = sb("impA", [128, 64], F32); b_impA = Buf()
        impB = sb("impB", [128, 64], F32); b_impB = Buf()
        score = sb("score", [128, 64], F32); b_score = Buf()
        wk = sb("wk", [128, 64], F32); b_wk = Buf()
        m8 = sb("m8", [128, 16], F32); b_m8 = Buf()
        negmg = [sb("negm0", [128, 128], F32), sb("negm1", [128, 128], F32)]
        b_negm = [Buf(), Buf()]
        QNt = sb("QNw", [128, 2, 2, 512], BF16)
        b_QNw = [[Buf(), Buf()], [Buf(), Buf()]]
        yB = sb("yB", [128, 2, 256], F32); b_yB = [Buf(), Buf()]
        maskAB = yB[:, :, :].rearrange("p a (b c) -> p (a b) c", b=2); b_maskAB = b_yB[0]
        ytmp = [sb("ytmp0", [128, 256], F32), sb("ytmp1", [128, 256], F32)]
        b_ytmp = [Buf(), Buf()]
        y = sb("y", [128, DM], BF16); b_y = Buf()
        sq = sb("sq", [128, 512], F32); b_sq = Buf()
        junk = sq[:, :].bitcast(BF16); b_junk = b_sq
        yT = sb("yT", [128, 8, 128], BF16); b_yT = Buf()
        rtmp = sb("rtmp", [128, 512], F32); b_rtmp = Buf()
        xn = sb("xn", [128, DM], F32); b_xn = Buf()
        scol = sb("scol", [128, 8, NB], F32); b_scol = Buf()

        modc = sb("modc", [128, 16, NB], F32); b_modc = Buf()
        posk = sb("posk", [128, 16], F32); b_posk = Buf()
        posv = sb("posv", [128, 16], F32); b_posv = Buf()
        poskb = sb("poskb", [128, 16], BF16); b_poskb = Buf()
        posvb = sb("posvb", [128, 16], BF16); b_posvb = Buf()
        badac = sb("badac", [128, 16], F32); b_badac = Buf()
        ngc = sb("ngc", [128, 8], F32); b_ngc = Buf()
        badag = sb("badag", [128, DM], F32) if False else None

        xs = [STG[0][:, 0:1024], STG[0][:, 1024:2048]]
        b_xs = b_stg[0]
        xr = [STG[1][:, 0:1024], STG[1][:, 1024:2048]]
        b_xr = b_stg[1]

        def pbc(d_ap, n):
            return bass.AP(tensor=d_ap.tensor, offset=d_ap.offset, ap=[[0, 128], [1, n]])

        if dumps:
            for (t_, bl_) in ((KE[0], b_kTs), (Vs, b_Vs), (Va, b_Va), (Vw, b_Vw), (qTa, b_qTa),
                              (sz, b_sz), (gts, b_gts), (hT0, [bht]), (QNt, b_QNw[0] + b_QNw[1]), (y, [b_y]),
                              (hidk, [b_hidk]), (score, [b_score]), (yB, b_yB), (den, [b_den]), (coef, [b_coef])):
                shp_ = list(t_.shape)
                S.op("pool", C("memset", t_[tuple(slice(None) for _ in shp_)], 0.0), writes=list(bl_))
        def ld(dst_ap, src_ap, buf):
            S.dma(C("dma_start", out=dst_ap, in_=src_ap), writes=[buf])

        ld(CM[:], cmask_d, b_CM)
        ld(TRI[:], tri_d, b_TRI)
        ld(PAT[:], pat_d, b_PAT)
        ld(gcols[:], gcols_d, b_gcols)
        ld(esink[:], pbc(sinks_d, 8), b_esink)
        ld(bgate[:], pbc(bgate_d, 24), b_bgate)
        ld(c31[:], pbc(c31_d, 8), b_c31)

        ld(scol[:], cT_d, b_scol)
        ld(posk[:], posk_d, b_posk)
        ld(posv[:], posv_d, b_posv)
        ld(badac[:], badac_d, b_badac)
        ld(ngc[:], ngc_d, b_ngc)
        S.op("pool", C("memset", idf[:], 0.0), writes=[b_idf])
        S.op("pool", C("affine_select", out=idf[:], in_=idf[:], pattern=[[-1, 128]], compare_op=ALU.not_equal,
                                                fill=1.0, base=0, channel_multiplier=1), reads=[b_idf], writes=[b_idf])
        S.op("dve", C("tensor_copy", out=idb[:], in_=idf[:]), reads=[b_idf], writes=[b_idb])
        S.op("dve", C("tensor_copy", out=idr[:], in_=idf[:]), reads=[b_idf], writes=[b_idr])
        S.op("act", C("activation", out=esink[:], in_=esink[:], func=AF.Exp), reads=[b_esink], writes=[b_esink])
        S.op("dve", C("tensor_scalar", out=gcols[:, 0:1], in0=gcols[:, 0:1], scalar1=0.125, scalar2=None, op0=ALU.mult),
             reads=[b_gcols], writes=[b_gcols])
        S.op("dve", C("tensor_scalar", out=gcols[:, 2:3], in0=gcols[:, 2:3], scalar1=0.125, scalar2=None, op0=ALU.mult),
             reads=[b_gcols], writes=[b_gcols])
        S.op("dve", C("tensor_copy", out=poskb[:], in_=posk[:]), reads=[b_posk], writes=[b_poskb])
        S.op("dve", C("tensor_copy", out=posvb[:], in_=posv[:]), reads=[b_posv], writes=[b_posvb])
        S.op("pool", C("memset", score[:], 0.0), writes=[b_score])
        S.op("pool", C("memset", score[:, 0:1], BONUS), reads=[b_score], writes=[b_score])
        for g_ in range(2):
            S.op("pool", C("memset", kcT[g_][:], 0.0), writes=[b_kcT])
            S.op("pool", C("memset", kTa[g_][:], 0.0), writes=b_kTa)
            S.op("pool", C("memset", kTw[g_][:], 0.0), writes=b_kTw)
        S.op("pool", C("memset", QNt[:, :, :, :], 0.0), writes=b_QNw[0] + b_QNw[1])
        S.op("pool", C("memset", VC[:], 0.0), writes=[b_VC])
        S.op("pool", C("memset", VC[:, :, :, 64:65], 1.0), reads=[b_VC], writes=[b_VC])
        S.op("pool", C("memset", VC[0:1, 0, :, 64:65], 0.0), reads=[b_VC], writes=[b_VC])
        for g in range(2):
            S.dma(C("dma_start", out=VC[:, :, g, 66:128], in_=ov_d), writes=[b_VC])
        for (Vt_, bV_) in ((Va, b_Va), (Vw, b_Vw), (Vs, b_Vs)):
            S.op("pool", C("memset", Vt_[:, :, :, 64:65], 1.0), writes=bV_)
            S.op("pool", C("memset", Vt_[:, :, :, 65:66], 0.0), writes=bV_)
        for s_ in range(1):
            S.op("pool", C("memset", Xk[s_][:], 0.0), writes=[b_Xk[s_]])
            S.op("pool", C("memset", Xv[s_][:], 0.0), writes=[b_Xv[s_]])

        stg_i = [0]

        def stage_load(src_ap, ncols, view=None):
            s_ = stg_i[0] % 2
            stg_i[0] += 1
            dst = STG[s_][:, 0:ncols] if view is None else view(STG[s_])
            S.dma(C("dma_start", out=dst, in_=src_ap), writes=b_stg[s_])
            return s_

        def cast_from_stage(eng, s_, dst_ap, ncols, dst_buf, src_view=None):
            src = STG[s_][:, 0:ncols] if src_view is None else src_view(STG[s_])
            S.op(eng, C("copy" if eng == "act" else "tensor_copy", out=dst_ap, in_=src), reads=b_stg[s_], writes=[dst_buf])

        ADA = pro >= 2
        S.op("act", C("activation", out=scol[:], in_=scol[:], func=AF.Silu), reads=[b_scol], writes=[b_scol])
        S.op("dve", C("tensor_copy", out=scb, in_=V(scol[:, 0, 0:1], [[NB, 8], [1, NB], [0, 128]])),
             reads=[b_scol], writes=[b_E])
        pj_i = [0]

        def next_pj():
            i_ = pj_i[0] % 2
            pj_i[0] += 1
            return i_

        for cg in range(12 if ADA else 0):
            src = wada_d[:, cg * 256:(cg + 1) * 256].rearrange("(kc p) c -> p kc c", p=128)
            s_ = stage_load(src, 2048, view=lambda t: t[:, :].rearrange("p (kc c) -> p kc c", kc=8))
            stv = STG[s_][:, :].rearrange("p (kc c) -> p kc c", kc=8)
            if cg < 8:
                for cc in range(2):
                    ch = cg * 2 + cc
                    pj = next_pj()
                    for kc in range(8):
                        S.op("pe", C("matmul",
                            PJ[:, pj, 0:NB], lhsT=stv[:, kc, cc * 128:(cc + 1) * 128], rhs=scol[:, kc, :],
                            start=(kc == 0), stop=(kc == 7)),
                            reads=b_stg[s_] + [b_scol], writes=[b_pj[pj]], inc=(kc == 7))
                    S.op("dve", C("tensor_scalar",
                        out=modc[:, ch, :], in0=PJ[:, pj, 0:NB], scalar1=badac[:, ch:ch + 1], scalar2=None, op0=ALU.add),
                        reads=[b_pj[pj], b_badac], writes=[b_modc])
            else:
                for b in range(NB):
                    pj = next_pj()
                    for kc in range(8):
                        S.op("pe", C("matmul",
                            PJ[:, pj, 0:256], lhsT=scb[:, kc, b, :], rhs=stv[:, kc, :],
                            start=(kc == 0), stop=(kc == 7)),
                            reads=b_stg[s_] + [b_E], writes=[b_pj[pj]], inc=(kc == 7))
                    c0 = (cg - 8) * 256
                    S.op("dve", C("tensor_copy", out=GATE[:, b, c0:c0 + 256], in_=PJ[:, pj, 0:256]),
                         reads=[b_pj[pj]], writes=[b_GATE])
        s_ = stage_load(pbc(badag_d, DM), DM)
        for b in range(NB if ADA else 0):
            S.op("dve", C("tensor_tensor", out=GATE[:, b, :], in0=GATE[:, b, :], in1=STG[s_][:, 0:DM], op=ALU.add),
                 reads=b_stg[s_] + [b_GATE], writes=[b_GATE])
        S.op("dve", C("tensor_copy", out=SHcol[:], in_=modc[:, 0:8, :]), reads=[b_modc], writes=[b_SHcol])
        S.op("dve", C("tensor_scalar", out=Gcol[:], in0=modc[:, 8:16, :], scalar1=1.0, scalar2=None, op0=ALU.add),
             reads=[b_modc], writes=[b_Gcol])
        S.op("dve", C("tensor_tensor", out=Gcol[:], in0=Gcol[:], in1=V(ngc[:, 0:1], [[1, 8], [0, NB]]), op=ALU.mult),
             reads=[b_Gcol, b_ngc], writes=[b_Gcol])

        S.dma(C("dma_start", out=KE[0][64:128, :], in_=E_d[64:128, :]), writes=[b_E])
        S.dma(C("dma_start", out=KE[1][0:64, :], in_=E_d[0:64, :]), writes=[b_E])
        for kc in range(8 if pro >= 3 else 0):
            for hx, (c0, c1) in enumerate(((0, 1676), (1676, WCOLS))):
                S.dma(C("dma_start", out=Win[:, kc, c0:c1], in_=win_d[kc * 128:(kc + 1) * 128, c0:c1]),
                      writes=[b_win_l[kc * 2 + hx]], q="pool")
        if pro >= 3:
            for (wd, wsb, wb) in ((w1k_d, w1k, b_w1k), (w1v_d, w1v, b_w1v)):
                for hlf in range(2):
                    S.dma(C("dma_start", out=wsb[:, hlf * 8:(hlf + 1) * 8, :],
                            in_=wd[hlf * 1024:(hlf + 1) * 1024, :].rearrange("(lp p) j -> p lp j", p=128)), writes=[wb], q="pool")
            for (wd, wsb, wb) in ((w2k_d, w2k, b_w2k), (w2v_d, w2v, b_w2v)):
                S.dma(C("dma_start", out=wsb[:], in_=wd.rearrange("(jc p) d -> p jc d", p=128)), writes=[wb], q="pool")
            for kc in range(8):
                S.dma(C("dma_start", out=Wout[:, kc, :], in_=wout_d[kc * 128:(kc + 1) * 128, :]), writes=[b_wout_l[kc]], q="pool")
        S.dma(C("dma_start", out=maskAB, in_=maskAB_d), writes=b_yB)
        for (bd, Bt, bB, m0, isB) in ((biasA_d, BA, b_BA, 0, False), (biasB_d, BB, b_BB, 2, True)) if pro >= 4 else ():
            s_ = stage_load(bd, 2048, view=lambda t: t[:, :].rearrange("p (a h c) -> p a h c", a=2, h=8))
            stv = STG[s_][:, :].rearrange("p (a h c) -> p a h c", a=2, h=8)
            for ty in range(2):
                S.op("dve", C("tensor_tensor",
                    out=stv[:, ty], in0=stv[:, ty], in1=V(maskAB[:, m0 + ty, 0:1], [[0, 8], [1, 128]]), op=ALU.add),
                    reads=b_stg[s_] + b_yB, writes=b_stg[s_])
                if isB:
                    S.op("dve", C("tensor_tensor",
                        out=stv[:, ty], in0=stv[:, ty], in1=V(c31[:, 0:1], [[1, 8], [0, 128]]), op=ALU.subtract),
                        reads=b_stg[s_] + [b_c31], writes=b_stg[s_])
            S.op("dve", C("tensor_copy", out=Bt[:, :, :, :], in_=stv), reads=b_stg[s_], writes=[bB])
        for (wsb, wb, pb, bpb, b1, bb1) in ((w1k, b_w1k, poskb, b_poskb, b1k, b_b1k), (w1v, b_w1v, posvb, b_posvb, b1v, b_b1v)) if pro >= 5 else ():
            for jc in range(2):
                pj = next_pj()
                for lp in range(16):
                    S.op("pe", C("matmul",
                        PJ[:, pj, 0:2], lhsT=wsb[:, lp, jc * 128:(jc + 1) * 128], rhs=V(pb[:, lp:lp + 1], [[0, 2]]),
                        start=(lp == 0), stop=(lp == 15)), reads=[wb, bpb], writes=[b_pj[pj]], inc=(lp == 15))
                S.op("dve", C("tensor_copy", out=b1[:, jc:jc + 1], in_=PJ[:, pj, 0:1]),
                     reads=[b_pj[pj]], writes=[bb1])

        if pro >= 5:
            S.op("dve", C("tensor_scalar", out=nb1k[:], in0=b1k[:], scalar1=-1.0, scalar2=None, op0=ALU.mult), reads=[b_b1k], writes=[b_nb1])
            S.op("dve", C("tensor_scalar", out=nb1v[:], in0=b1v[:], scalar1=-1.0, scalar2=None, op0=ALU.mult), reads=[b_b1v], writes=[b_nb1])
        st_i = [0]
        pts_i = [0]

        def build_PC(b, i):
            u = i % 2
            xsl = i % 2
            ra, rw = i % RA, i % RW
            CPb = PT[:, 1, :]
            CPbb = PTb[1]
            b_cp = b_pt[1]
            chains = []

            def proj(c0, n):
                pj = next_pj()
                for kc in range(8):
                    S.op("pe", C("matmul", PJ[:, pj, 0:n], lhsT=hT0[:, kc, :], rhs=Win[:, kc, c0:c0 + n],
                                 start=(kc == 0), stop=(kc == 7)),
                         reads=[bht, b_win] + b_win_l, writes=[b_pj[pj]], inc=(kc == 7))
                return pj

            def L0():
                S.dma(C("dma_start", out=xs[xsl], in_=x_d[b, i * 128:(i + 1) * 128, :]), writes=[b_xs[xsl]])

            def L1():
                S.op("act", C("activation", out=junk, in_=xs[xsl], func=AF.Square, accum_out=stat[:, 0:1]),
                     reads=[b_xs[xsl]], writes=[b_junk, b_stat])

            def L2():
                S.op("act", C("activation", out=stat[:, 1:2], in_=stat[:, 0:1], func=AF.Ln, scale=1.0 / DM, bias=EPS),
                     reads=[b_stat], writes=[b_stat])
                S.op("act", C("activation", out=stat[:, 2:3], in_=stat[:, 1:2], func=AF.Exp, scale=-0.5),
                     reads=[b_stat], writes=[b_stat])

            def L3():
                S.op("pool", C("tensor_scalar", out=xn[:], in0=xs[xsl], scalar1=stat[:, 2:3], scalar2=1.0,
                               op0=ALU.mult, op1=ALU.mult), reads=[b_xs[xsl], b_stat], writes=[b_xn])

            def LT(h):
                def f():
                    for kc in range(4 * h, 4 * h + 4):
                        S.op("pe", C("transpose", out=PT[:, 0, (kc % 4) * 128:(kc % 4 + 1) * 128],
                                     in_=xn[:, kc * 128:(kc + 1) * 128], identity=idf[:]),
                             reads=[b_xn, b_idf], writes=[b_pt[0]])
                return f

            def LE(h):
                def f():
                    for kc in range(4 * h, 4 * h + 4):
                        S.op("dve", C("tensor_scalar", out=hT0[:, kc, :], in0=PT[:, 0, (kc % 4) * 128:(kc % 4 + 1) * 128],
                                      scalar1=Gcol[:, kc, b:b + 1], scalar2=SHcol[:, kc, b:b + 1], op0=ALU.mult, op1=ALU.add),
                             reads=[b_pt[0], b_Gcol, b_SHcol], writes=[bht])
                return f
            chains.append((0, [L1, L2, L3, LT(0), LE(0), LT(1), LE(1)]))

            def norm_chain(c0, nh, qs, nchunks, evac):
                n = nh * 64
                st = {}

                def s0():
                    st["pj"] = proj(c0, n)

                def s1():
                    S.op("act", C("activation", out=sq[:, 0:n], in_=PJ[:, st["pj"], 0:n], func=AF.Square),
                         reads=[b_pj[st["pj"]]], writes=[b_sq])

                def s2():
                    S.op("dve", C("tensor_reduce", out=ssq[:, 0:nh], in_=sq[:, 0:n].rearrange("p (h d) -> p h d", d=64),
                                  axis=AX.X, op=ALU.add), reads=[b_sq], writes=[b_ssq])

                def s3():
                    S.op("act", C("activation", out=ssq[:, 0:nh], in_=ssq[:, 0:nh], func=AF.Ln, scale=1.0 / 64, bias=EPS),
                         reads=[b_ssq], writes=[b_ssq])
                    S.op("act", C("activation", out=ssq[:, 0:nh], in_=ssq[:, 0:nh], func=AF.Exp, scale=-0.5),
                         reads=[b_ssq], writes=[b_ssq])

                def s4():
                    pj = st["pj"]
                    S.op("dve", C("tensor_tensor", out=qn[qs][:, 0:n].rearrange("p (h d) -> p h d", d=64),
                                  in0=PJ[:, pj, 0:n].rearrange("p (h d) -> p h d", d=64),
                                  in1=V(ssq[:, 0:1], [[1, nh], [0, 64]]), op=ALU.mult),
                         reads=[b_pj[pj], b_ssq], writes=[b_qn[qs]])

                def s5():
                    for c in range(nchunks):
                        S.op("pe", C("transpose", out=PTb[0][:, c * 128:(c + 1) * 128], in_=qn[qs][:, c * 128:(c + 1) * 128], identity=idb[:]),
                             reads=[b_qn[qs], b_idb], writes=[b_pt[0]])
                return [s0, s1, s2, s3, s4, s5, evac]

            def evac_qa():
                S.op("dve", C("tensor_scalar", out=qTa[:, u].rearrange("p r t -> p (r t)"), in0=PTb[0][:, 0:512],
                              scalar1=gcols[:, 0:1], scalar2=None, op0=ALU.mult),
                     reads=[b_pt[0], b_gcols], writes=[b_qTa[u]])

            def evac_qb():
                for g_ in range(2):
                    hp_ = slice(g_ * 64, (g_ + 1) * 64)
                    S.op("dve", C("tensor_scalar", out=QNt[hp_, u, g_, :], in0=PTb[0][hp_, 0:512],
                                  scalar1=gcols[hp_, 2:3], scalar2=None, op0=ALU.mult),
                         reads=[b_pt[0], b_gcols], writes=[b_QNw[u][g_]])

            def evac_k():
                for g_ in range(2):
                    hp_ = slice(g_ * 64, (g_ + 1) * 64)
                    S.op("dve", C("tensor_scalar", out=kTa[g_][hp_, ra * 128:(ra + 1) * 128], in0=PTb[0][hp_, 0:128],
                                  scalar1=gcols[hp_, 1:2], scalar2=None, op0=ALU.mult),
                         reads=[b_pt[0], b_gcols], writes=[b_kTa[ra]])
                    S.op("dve", C("tensor_scalar", out=kTw[g_][hp_, rw * 128:(rw + 1) * 128], in0=PTb[0][hp_, 256:384],
                                  scalar1=gcols[hp_, 5:6], scalar2=None, op0=ALU.mult),
                         reads=[b_pt[0], b_gcols], writes=[b_kTw[rw]])
                    S.op("dve", C("tensor_scalar", out=KE[g_][hp_, i * 128:(i + 1) * 128], in0=PTb[0][hp_, 128:256],
                                  scalar1=gcols[hp_, 4:5], scalar2=None, op0=ALU.mult),
                         reads=[b_pt[0], b_gcols], writes=[b_kTs[i]])

            stc = {}
            tau0 = (i % 4) * 128

            def c0_():
                stc["pj"] = proj(G_C, 512)

            def c1_():
                S.op("act", C("copy", out=qn[1][:], in_=PJ[:, stc["pj"], :]), reads=[b_pj[stc["pj"]]], writes=[b_qn[1]])

            def c2_():
                for c in range(4):
                    S.op("pe", C("transpose", out=PTb[0][:, c * 128:(c + 1) * 128], in_=qn[1][:, c * 128:(c + 1) * 128], identity=idb[:]),
                         reads=[b_qn[1], b_idb], writes=[b_pt[0]])

            def c3_():
                for (Xt, bX, cb) in ((Xk, b_Xk, 0), (Xv, b_Xv, 256)):
                    S.op("dve", C("tensor_copy", out=Xt[0][0:64, :, 17 + tau0:17 + tau0 + 128],
                                  in_=PTb[0][0:64, cb:cb + 256].rearrange("p (g t) -> p g t", g=2)),
                         reads=[b_pt[0]], writes=[bX[0]])
                    S.op("dve", C("tensor_copy", out=Xt[0][64:128, :, 16 + tau0:16 + tau0 + 128],
                                  in_=PTb[0][64:128, cb:cb + 256].rearrange("p (g t) -> p g t", g=2)),
                         reads=[b_pt[0]], writes=[bX[0]])
            chains.append((8, [c0_, c1_, c2_, c3_]))
            chains.append((10, norm_chain(G_K, 6, 0, 3, evac_k)))

            stv_ = {}

            def v0_():
                stv_["pj"] = proj(G_V, 408)

            def v1_():
                pj = stv_["pj"]
                S.op("dve", C("tensor_copy", out=Va[:, ra, :, 0:64], in_=PJ[:, pj, 0:128].rearrange("p (g d) -> p g d", g=2)),
                     reads=[b_pj[pj]], writes=[b_Va[ra]])
                S.op("dve", C("tensor_copy", out=Vs[:, i, :, 0:64], in_=PJ[:, pj, 128:256].rearrange("p (g d) -> p g d", g=2)),
                     reads=[b_pj[pj]], writes=[b_Vs[i]])
                S.op("dve", C("tensor_copy", out=Vw[:, rw, :, 0:64], in_=PJ[:, pj, 256:384].rearrange("p (g d) -> p g d", g=2)),
                     reads=[b_pj[pj]], writes=[b_Vw[rw]])
                S.op("dve", C("tensor_tensor", out=gtmp[:], in0=PJ[:, pj, 384:408], in1=bgate[:], op=ALU.add),
                     reads=[b_pj[pj], b_bgate], writes=[b_gtmp])

            def v2_():
                S.op("act", C("activation", out=gtmp[:], in_=gtmp[:], func=AF.Exp, scale=-1.0), reads=[b_gtmp], writes=[b_gtmp])

            def v3_():
                S.op("dve", C("tensor_scalar", out=gtmp[:], in0=gtmp[:], scalar1=1.0, scalar2=None, op0=ALU.add),
                     reads=[b_gtmp], writes=[b_gtmp])
                S.op("dve", C("reciprocal", out=gts[:, u, :], in_=gtmp[:]), reads=[b_gtmp], writes=[b_gts[u]])
            chains.append((12, [v0_, v1_, v2_, v3_]))

            c0x = 128 * (i % 4)

            def cm(part):
                (Xt, bX, w1, bw1) = ((Xk, b_Xk, w1k, b_w1k), (Xv, b_Xv, w1v, b_w1v))[part // 2]
                jc = part % 2

                def f():
                    for lp in range(16):
                        S.op("pe", C("matmul", CPb[:, part * 16:(part + 1) * 16].rearrange("p (g m) -> p g m", g=2),
                                     lhsT=w1[:, lp, jc * 128:(jc + 1) * 128],
                                     rhs=Xt[0][:, :, c0x + 1 + 2 * lp:c0x + 1 + 2 * lp + 16 * 7 + 1:16],
                                     start=(lp == 0), stop=(lp == 15), skip_group_check=True),
                             reads=[bX[0], bw1], writes=[b_cp], inc=(lp == 15))
                return f

            def cm_act():
                for part in range(4):
                    nb1 = (nb1k, nb1v)[part // 2]
                    S.op("act", C("activation", out=hsig[:, part * 16:(part + 1) * 16], in_=CPb[:, part * 16:(part + 1) * 16],
                                  func=AF.Exp, scale=-1.0, bias=nb1[:, part % 2:part % 2 + 1]),
                         reads=[b_cp, b_nb1], writes=[b_hsig])

            def cm_dve1():
                S.op("dve", C("tensor_scalar", out=hsig[:], in0=hsig[:], scalar1=1.0, scalar2=None, op0=ALU.add),
                     reads=[b_hsig], writes=[b_hsig])
                S.op("dve", C("reciprocal", out=hsig[:], in_=hsig[:]), reads=[b_hsig], writes=[b_hsig])

            def cm_dve2():
                for part in range(4):
                    hid, bh = ((hidk, b_hidk), (hidv, b_hidv))[part // 2]
                    b1 = (b1k, b1v)[part // 2]
                    S.op("dve", C("scalar_tensor_tensor", out=hid[:, part % 2, :], in0=CPb[:, part * 16:(part + 1) * 16],
                                  scalar=b1[:, part % 2:part % 2 + 1], in1=hsig[:, part * 16:(part + 1) * 16],
                                  op0=ALU.add, op1=ALU.mult),
                         reads=[b_cp, b_b1k, b_b1v, b_hsig], writes=[bh])

            def cm_mm2():
                for jc in range(2):
                    S.op("pe", C("matmul", CPb[0:16, 64:128], lhsT=hidk[:, jc, :], rhs=w2k[:, jc, :],
                                 start=(jc == 0), stop=(jc == 1), skip_group_check=True),
                         reads=[b_hidk, b_w2k], writes=[b_cp], inc=False)
                for jc in range(2):
                    S.op("pe", C("matmul", CPb[0:16, 128:192], lhsT=hidv[:, jc, :], rhs=w2v[:, jc, :],
                                 start=(jc == 0), stop=(jc == 1), skip_group_check=True),
                         reads=[b_hidv, b_w2v], writes=[b_cp], inc=(jc == 1))

            def cm_a2():
                S.op("act", C("activation", out=kjunk[:], in_=CPb[0:16, 64:128], func=AF.Square, accum_out=kc32[:, 0:1]),
                     reads=[b_cp], writes=[b_kjunk, b_kc32])
                S.op("dve", C("tensor_copy", out=vstag[:], in_=CPb[0:16, 128:192]), reads=[b_cp, b_kc32], writes=[b_vstag])

            def cm_a3():
                S.op("act", C("activation", out=kc32[:, 1:2], in_=kc32[:, 0:1], func=AF.Ln, scale=1.0 / 64, bias=EPS),
                     reads=[b_kc32], writes=[b_kc32])
                S.op("act", C("activation", out=kc32[:, 2:3], in_=kc32[:, 1:2], func=AF.Exp, scale=-0.5),
                     reads=[b_kc32], writes=[b_kc32])
                n0 = 8 * i
                T, p0 = n0 // 128, n0 % 128
                m0 = 1 if i == 0 else 0
                for g in range(2):
                    S.dma(C("dma_start", out=VC[p0 + m0:p0 + 8, T, g, 0:64], in_=vstag[g * 8 + m0:(g + 1) * 8, :]),
                          reads=[b_vstag], writes=[b_VC])

            def cm_d3():
                for h2 in range(2):
                    S.op("dve", C("tensor_scalar", out=kndup[:, h2 * 64:(h2 + 1) * 64], in0=CPb[0:16, 64:128],
                                  scalar1=kc32[:, 2:3], scalar2=None, op0=ALU.mult),
                         reads=[b_cp, b_kc32], writes=[b_kndup])

            def cm_t():
                S.op("pe", C("transpose", out=PTb[0][:, 0:16], in_=kndup[:], identity=idb[0:16, 0:16]),
                     reads=[b_kndup, b_idb], writes=[b_pt[0]])

            def cm_e():
                n0 = 8 * i
                for g in range(2):
                    S.op("dve", C("tensor_scalar", out=kcT[g][g * 64:(g + 1) * 64, n0:n0 + 8],
                                  in0=PTb[0][g * 64:(g + 1) * 64, g * 8:(g + 1) * 8],
                                  scalar1=gcols[g * 64:(g + 1) * 64, 3:4], scalar2=None, op0=ALU.mult),
                         reads=[b_pt[0], b_gcols], writes=[b_kcT])
            nop = lambda: None
            import os as _os
            chains.append((12, [cm(0), cm(1), cm(2), cm(3), cm_act, cm_dve1, cm_dve2, cm_mm2, cm_a2, cm_a3, cm_d3, nop, nop, nop, cm_t, cm_e][:int(_os.environ.get('NCM', '99'))]))
            chains.append((15, norm_chain(G_QA, 8, 1, 4, evac_qa)))
            chains.append((18, norm_chain(G_QB, 8, 0, 4, evac_qb)))

            def z_chain(c0, zbuf, bz, col0):
                st = {}

                def s0():
                    st["pj"] = proj(c0, 512)

                def s1():
                    S.op("act", C("activation", out=zbuf, in_=PJ[:, st["pj"], :], func=AF.Exp, scale=-1.0),
                         reads=[b_pj[st["pj"]]], writes=[bz])

                def s2():
                    S.op("act", C("activation", out=zbuf, in_=zbuf, func=AF.Ln, bias=1.0), reads=[bz], writes=[bz])

                def s3():
                    S.op("act", C("activation", out=zbuf, in_=zbuf, func=AF.Exp, scale=-1.0), reads=[bz], writes=[bz])

                def s4():
                    pj = st["pj"]
                    S.op("dve", C("tensor_tensor", out=sz[:, u, col0:col0 + 512], in0=PJ[:, pj, :], in1=zbuf, op=ALU.mult),
                         reads=[b_pj[pj], bz], writes=[b_sz[u]])
                return [s0, s1, s2, s3, s4]
            chains.append((21, z_chain(G_ZA, sq[:, :], b_sq, 0)))
            chains.append((24, z_chain(G_ZB, zt2[:, :], b_zt2, 512)))
            return chains

        NSTEP = 32
        ntile = 4 * n_sb

        def x_load(b, i):
            S.dma(C("dma_start", out=xs[i % 2], in_=x_d[b, i * 128:(i + 1) * 128, :]), writes=[b_xs[i % 2]])

        def gen_steps(b, pc, tl):
            chains = []
            if pc is not None:
                if pc % 4 == 0 and pc > 0:
                    for (Xt, bX) in ((Xk, b_Xk), (Xv, b_Xv)):
                        S.op("pool", C("tensor_copy", out=Xt[0][:, :, 0:17], in_=Xt[0][:, :, 512:529]), reads=[bX[0]], writes=[bX[0]])
                chains += build_PC(b, pc)[:nch]
                if pc + 1 < ntile:
                    chains.append((10, [lambda: x_load(b, pc + 1)]))
            if tl is not None:
                chains.append(tail_chain(b, tl))
            T_ = max(o + len(ch) for o, ch in chains)
            assert T_ <= NSTEP, T_
            for tau in range(NSTEP):
                for o, ch in chains:
                    k = tau - o
                    if 0 <= k < len(ch):
                        ch[k]()
                yield

        def score_tile(lhsT_ap, rhs_ap, extra, reads):
            st = st_i[0] % 2
            st_i[0] += 1
            n = len(extra)
            S.op("pe", C("matmul", ST[:, st, :], lhsT=lhsT_ap, rhs=rhs_ap, start=True, stop=(n == 0)),
                 reads=reads, writes=[b_st[st]], inc=(n == 0))
            for j, (l_ap, r_ap, rd) in enumerate(extra):
                S.op("pe", C("matmul", ST[:, st, :], lhsT=l_ap, rhs=r_ap, start=False, stop=(j == n - 1)),
                     reads=rd, writes=[b_st[st]], inc=(j == n - 1))
            return st

        def exp_tile(st):
            p = pts_i[0] % 2
            pts_i[0] += 1
            S.op("act", C("activation", out=PTs[p][:], in_=ST[:, st, :], func=AF.Exp), reads=[b_st[st]], writes=[b_PTs[p]],
                 self_sync=False)
            return p

        def bc4(ap2d):
            return V(ap2d, [[0, 4], [1, 128]])

        def phaseA(b, i, u, nxt):
            gsl = gts[:, u, :]
            jobs = []

            def Bx(Bt, ty, g):
                return [(idr[:], Bt[:, ty, 4 * g:4 * g + 4, :].rearrange("p h t -> p (h t)"), [b_idr, b_BA if Bt is BA else b_BB])]

            def add_branch(g, kts, lhs_fn, rhs_fn, extras_fn, V_t, V_bufs, vidx, oab, post):
                nk = len(kts)
                for j, kt in enumerate(kts):
                    def fscore(kt=kt):
                        l_ap, l_rd = lhs_fn(kt)
                        r_ap, r_rd = rhs_fn(kt)
                        return score_tile(l_ap, r_ap, extras_fn(kt), l_rd + r_rd)

                    def pv(p, kt=kt, j=j):
                        for r in range(4):
                            S.op("pe", C("matmul", OA[:, oab, r * 66:(r + 1) * 66], lhsT=PTs[p][:, r * 128:(r + 1) * 128],
                                         rhs=V_t[:, vidx(kt), g, :], start=(j == 0 and r == 0), stop=(j == nk - 1), skip_group_check=True),
                                 reads=[b_PTs[p], V_bufs(kt)], writes=[b_oa[oab]], inc=(j == nk - 1 and r == 3))
                    jobs.append((fscore, pv, post if j == nk - 1 else None))

            def accumulate(g, oab, dcol, gate_br):
                S.op("dve", C("reciprocal", out=den[:, dcol:dcol + 4], in_=V(OA[:, oab, 64:65], [[66, 4]])),
                     reads=[b_oa[oab]], writes=[b_den])
                S.op("dve", C("tensor_tensor", out=coef[:, dcol:dcol + 4], in0=den[:, dcol:dcol + 4],
                              in1=gsl[:, g * 12 + gate_br:(g + 1) * 12:3], op=ALU.mult),
                     reads=[b_den, b_gts[u]], writes=[b_coef])
                S.op("dve", C("tensor_tensor", out=ytmp[oab][:, :].rearrange("p (r d) -> p r d", r=4),
                              in0=V(OA[:, oab, 0:1], [[66, 4], [1, 64]]),
                              in1=V(coef[:, dcol:dcol + 1], [[1, 4], [0, 64]]), op=ALU.mult),
                     reads=[b_oa[oab], b_coef], writes=[b_ytmp[oab]])
                S.op("dve", C("tensor_tensor", out=yB[:, g, :], in0=yB[:, g, :], in1=ytmp[oab][:, :], op=ALU.add),
                     reads=[b_yB[g], b_ytmp[oab]], writes=[b_yB[g]])

            Tmax = i // 16
            post_b_fns = {}
            for g in range(2):
                hp = slice(g * 64, (g + 1) * 64)
                qb_ap = QNt[hp, u, g, :]
                for T in range(Tmax + 1):
                    def fscore(T=T, hp=hp, qb_ap=qb_ap, g=g):
                        extra = []
                        if T == Tmax:
                            extra.append((idb[:], bc4(CM[:, i % 16, :]), [b_idb, b_CM]))
                        return score_tile(kcT[g][:, T * 128:(T + 1) * 128], QNt[:, u, g, :], extra, [b_kcT, b_QNw[u][g]])

                    def pv(p, T=T, g=g):
                        for r in range(4):
                            S.op("pe", C("matmul", OA[:, g, r * 128:(r + 1) * 128], lhsT=PTs[p][:, r * 128:(r + 1) * 128],
                                         rhs=VC[:, T, g, :], start=(T == 0 and r == 0), stop=(T == Tmax), skip_group_check=True),
                                 reads=[b_PTs[p], b_VC], writes=[b_oa[g]], inc=(T == Tmax and r == 3))

                    def post(st, g=g, hp=hp):
                        S.op("dve", C("tensor_scalar", out=den[:, 0:4], in0=V(OA[:, g, 64:65], [[128, 4]]),
                                      scalar1=1e-30, scalar2=None, op0=ALU.max),
                             reads=[b_oa[g]], writes=[b_den])
                        S.op("dve", C("reciprocal", out=den[:, 0:4], in_=den[:, 0:4]), reads=[b_den], writes=[b_den])
                        S.op("dve", C("tensor_tensor", out=coef[:, 0:4], in0=den[:, 0:4],
                                      in1=gsl[:, g * 12:(g + 1) * 12:3], op=ALU.mult),
                             reads=[b_den, b_gts[u]], writes=[b_coef])
                        pat_ap = PAT[:, 64 - 2 * i + 64:128 - 2 * i + 64]
                        for r in range(4):
                            dst, bdst = (impA, b_impA) if r % 2 == 0 else (impB, b_impB)
                            in0 = OA[:, g, r * 128 + 66:r * 128 + 128]
                            if r == 1:
                                S.op("dve", C("tensor_scalar", out=dst[:, 1:63], in0=in0, scalar1=den[:, r:r + 1], scalar2=None, op0=ALU.mult),
                                     reads=[b_oa[g], b_den], writes=[bdst])
                            else:
                                in1, rd1 = (pat_ap[:, 1:63], b_PAT) if r == 0 else (dst[:, 1:63], bdst)
                                S.op("dve", C("scalar_tensor_tensor", out=dst[:, 1:63], in0=in0, scalar=den[:, r:r + 1],
                                              in1=in1, op0=ALU.mult, op1=ALU.add),
                                     reads=[b_oa[g], b_den, rd1], writes=[bdst])
                        S.op("dve", C("tensor_tensor", out=yB[:, g, :].rearrange("p (r d) -> p r d", r=4),
                                      in0=V(OA[:, g, 0:1], [[128, 4], [1, 64]]), in1=V(coef[:, 0:1], [[1, 4], [0, 64]]), op=ALU.mult),
                             reads=[b_oa[g], b_coef], writes=[b_yB[g]])
                        S.op("dve", C("tensor_tensor", out=score[:, 1:63], in0=impA[:, 1:63], in1=impB[:, 1:63], op=ALU.add),
                             reads=[b_impA, b_impB], writes=[b_score])
                        if i == NT - 1:
                            S.op("dve", C("tensor_copy", out=score[:, 63:64], in_=pat_ap[:, 63:64]), reads=[b_PAT, b_score], writes=[b_score])
                        S.op("dve", C("max", out=m8[:, 0:8], in_=score[:]), reads=[b_score], writes=[b_m8])
                        S.op("dve", C("match_replace", out=wk[:], in_to_replace=m8[:, 0:8], in_values=score[:], imm_value=-1e30),
                             reads=[b_score, b_m8], writes=[b_wk])
                        S.op("dve", C("max", out=m8[:, 8:16], in_=wk[:]), reads=[b_wk], writes=[b_m8])
                        S.op("dve", C("tensor_scalar", out=negmg[g][:, :].rearrange("p (a j) -> p a j", a=2), in0=V(score[:, 0:1], [[0, 2], [1, 64]]),
                                      scalar1=m8[:, 15:16], scalar2=NEG,
                                      op0=ALU.is_lt, op1=ALU.mult), reads=[b_score, b_m8], writes=[b_negm[g]])

                    def post_b(g=g):
                        S.op("pe", C("transpose", out=OA[:, g, 384:512], in_=negmg[g][:], identity=idf[:]),
                             reads=[b_negm[g], b_idf], writes=[b_oa[g]])
                        op_ = slice((1 - g) * 64, (2 - g) * 64)
                        S.op("dve", C("tensor_copy", out=QNt[op_, u, g, :].rearrange("p (r t) -> p r t", r=4), in_=bc4(OA[op_, g, 384:512])),
                             reads=[b_oa[g]], writes=[b_QNw[u][g]])
                    if T == Tmax:
                        post_b_fns[g] = post_b
                    jobs.append((fscore, pv, post if T == Tmax else None))

            for g in range(2):
                hp = slice(g * 64, (g + 1) * 64)
                qa_ap = qTa[hp, u].rearrange("p r t -> p (r t)")

                def a_post(st, g=g):
                    oab = g
                    post_b_fns[g]()
                    S.op("dve", C("tensor_tensor", out=den[:, 12:16], in0=V(OA[:, oab, 64:65], [[66, 4]]),
                                  in1=esink[:, 4 * g:4 * g + 4], op=ALU.add),
                         reads=[b_oa[oab], b_esink], writes=[b_den])
                    S.op("dve", C("reciprocal", out=den[:, 12:16], in_=den[:, 12:16]), reads=[b_den], writes=[b_den])
                    S.op("dve", C("tensor_tensor", out=ytmp[oab][:, :].rearrange("p (r d) -> p r d", r=4),
                                  in0=V(OA[:, oab, 0:1], [[66, 4], [1, 64]]),
                                  in1=V(den[:, 12:13], [[1, 4], [0, 64]]), op=ALU.mult),
                         reads=[b_oa[oab], b_den], writes=[b_ytmp[oab]])
                    S.op("pool", C("tensor_tensor", out=y[:, g * 256:(g + 1) * 256], in0=ytmp[oab][:, :],
                                   in1=sz[:, u, g * 256:(g + 1) * 256], op=ALU.mult),
                         reads=[b_ytmp[oab], b_sz[u]], writes=[b_y])
                add_branch(g, list(range(max(0, i - 1), i + 1)),
                           lambda kt, g=g: (kTa[g][:, (kt % RA) * 128:(kt % RA) * 128 + 128], [b_kTa[kt % RA]]),
                           lambda kt: (qTa[:, u].rearrange("p r t -> p (r t)"), [b_qTa[u]]),
                           lambda kt, g=g: Bx(BA, 0, g) if kt == i else Bx(BA, 1, g),
                           Va, lambda kt: b_Va[kt % RA], lambda kt: kt % RA, g, a_post)
            for g in range(2):
                hp = slice(g * 64, (g + 1) * 64)
                qb_ap = QNt[hp, u, g, :]

                def win_extras(kt, g=g):
                    dk = i - kt
                    if dk == 0:
                        return Bx(BB, 0, g)
                    if dk == 1:
                        return Bx(BB, 1, g)
                    if dk == 4:
                        return [(idb[:], bc4(TRI[:, :]), [b_idb, b_TRI])]
                    return []
                add_branch(g, list(range(max(0, i - 4), i + 1)),
                           lambda kt, g=g: (kTw[g][:, (kt % RW) * 128:(kt % RW) * 128 + 128], [b_kTw[kt % RW]]),
                           lambda kt, g=g: (QNt[:, u, g, :], [b_QNw[u][g]]),
                           win_extras, Vw, lambda kt: b_Vw[kt % RW], lambda kt: kt % RW, g,
                           lambda st, g=g: accumulate(g, g, 8, 2))
            for g in range(2):
                hp = slice(g * 64, (g + 1) * 64)
                qb_ap = QNt[hp, u, g, :]

                def sel_lhs(kt, g=g, hp=hp):
                    return (KE[g][:, kt * 128:(kt + 1) * 128], [b_kTs[kt], b_E])

                def sel_rhs(kt, g=g, qb_ap=qb_ap):
                    return (QNt[:, u, g, :], [b_QNw[u][g]])

                def sel_extras(kt, g=g):
                    if kt == i:
                        return Bx(BB, 0, g)
                    if kt == i - 1:
                        return Bx(BB, 1, g)
                    return []

                def sel_post(st, g=g):
                    accumulate(g, g, 4, 1)
                    S.op("dve", C("tensor_tensor", out=y[:, 512 + g * 256:512 + (g + 1) * 256], in0=yB[:, g, :],
                                  in1=sz[:, u, 512 + g * 256:512 + (g + 1) * 256], op=ALU.mult),
                         reads=[b_yB[g], b_sz[u]], writes=[b_y])
                add_branch(g, list(range(0, i + 1)), sel_lhs, sel_rhs, sel_extras, Vs, lambda kt: b_Vs[kt], lambda kt: kt, g, sel_post)

            pend = None
            nj = len(jobs)
            sdone = 0
            ncmp = 2 * (Tmax + 1)
            for jx, (fscore, pv, post) in enumerate(jobs):
                st = fscore()
                want = ((jx + 1) * NSTEP) // nj
                if jx + 1 >= ncmp + 1:
                    want = max(2, want)
                while sdone < want:
                    next(nxt, None)
                    sdone += 1
                if pend is not None:
                    pst, ppv, ppost = pend
                    p = exp_tile(pst)
                    ppv(p)
                    if ppost is not None:
                        ppost(pst)
                pend = (st, pv, post)
            pst, ppv, ppost = pend
            p = exp_tile(pst)
            ppv(p)
            if ppost is not None:
                ppost(pst)

        def tail_chain(b, i):
            xrs = i % 2

            def t0():
                S.dma(C("dma_start", out=xr[xrs], in_=x_d[b, i * 128:(i + 1) * 128, :]), writes=[b_xr[xrs]])
                for kc in range(8):
                    S.op("pe", C("transpose", out=PTb[1][:, kc * 128:(kc + 1) * 128], in_=y[:, kc * 128:(kc + 1) * 128],
                                 identity=idb[:]), reads=[b_y, b_idb], writes=[b_pt[1]])

            def t1():
                S.op("act", C("copy", out=yT[:, :, :].rearrange("p k t -> p (k t)"), in_=PTb[1][:, :]), reads=[b_pt[1]], writes=[b_yT])
            stt = {}

            def mm(hf):
                def f():
                    pj = next_pj()
                    stt[hf] = pj
                    for kc in range(8):
                        S.op("pe", C("matmul", PJ[:, pj, :], lhsT=yT[:, kc, :], rhs=Wout[:, kc, hf * 512:(hf + 1) * 512],
                                     start=(kc == 0), stop=(kc == 7)),
                             reads=[b_yT, b_wout] + b_wout_l, writes=[b_pj[pj]], inc=(kc == 7))
                return f

            def ml(hf):
                def f():
                    pj = stt[hf]
                    S.op("dve", C("tensor_tensor", out=rtmp[:, :], in0=PJ[:, pj, :],
                                  in1=GATE[:, b, hf * 512:(hf + 1) * 512], op=ALU.mult),
                         reads=[b_pj[pj], b_GATE], writes=[b_rtmp])
                return f

            def t6(hf):
                def f():
                    S.op("dve", C("tensor_tensor", out=xr[xrs][:, hf * 512:(hf + 1) * 512], in0=xr[xrs][:, hf * 512:(hf + 1) * 512],
                                  in1=rtmp[:, :], op=ALU.add),
                         reads=[b_xr[xrs], b_rtmp], writes=[b_xr[xrs]])
                return f

            def t7():
                S.dma(C("dma_start", out=out_d[b, i * 128:(i + 1) * 128, :], in_=xr[xrs]), reads=[b_xr[xrs]], is_output=True)
            return (0, [t0, t1, mm(0), ml(0), t6(0), mm(1), ml(1), t6(1), t7])


        def drain(gen):
            for _ in gen:
                pass

        for b in range(nb):
            if b > 0:
                for g_ in range(2):
                    S.op("pool", C("memset", kcT[g_][:], 0.0), writes=[b_kcT])
                S.op("pool", C("memset", VC[:, :, :, 0:64], 0.0), writes=[b_VC])
                S.op("pool", C("memset", Xk[0][:, :, 0:17], 0.0), writes=[b_Xk[0]])
                S.op("pool", C("memset", Xv[0][:, :, 0:17], 0.0), writes=[b_Xv[0]])
                S.op("pool", C("memset", score[:, 63:64], 0.0), reads=[b_score], writes=[b_score])
            x_load(b, 0)
            drain(gen_steps(b, 0, None))
            for i in range(ntile):
                pc_ = i + 1 if i + 1 < ntile else None
                tl_ = i - 1 if i >= 1 else None
                nxt = gen_steps(b, pc_, tl_) if (pc_ is not None or tl_ is not None) else iter(())
                if "A" in stages and i < n_a:
                    phaseA(b, i, i % 2, nxt)
                drain(nxt)
            drain(gen_steps(b, None, ntile - 1))
        if dumps:
            L = dict(GATE=(GATE, [b_GATE]), Gcol=(Gcol, [b_Gcol]), SHcol=(SHcol, [b_SHcol]), BA=(BA, [b_BA]), BB=(BB, [b_BB]),
                     b1k=(b1k, [b_b1k]), b1v=(b1v, [b_b1v]), Win=(Win, [b_win]), Wout=(Wout, [b_wout]), w1k=(w1k, [b_w1k]),
                     qTa=(qTa, b_qTa), kTs=(KE[0], b_kTs),
                     Vs=(Vs, b_Vs), Va=(Va, b_Va), Vw=(Vw, b_Vw), VC=(VC, [b_VC]), sz=(sz, b_sz), gts=(gts, b_gts),
                      y=(y, [b_y]), hT=(hT0, [bht]),
                     Xk=(Xk0, [bxk]), hidk=(hidk, [b_hidk]), score=(score, [b_score]), yB=(yB, b_yB), esink=(esink, [b_esink]),
                     den=(den, [b_den]), coef=(coef, [b_coef]))
            for nm in dumps:
                t, bufs = L[nm]
                shp = list(t.shape)
                dd = nc.dram_tensor("d_" + nm, shp, t.dtype, kind="ExternalOutput").ap()
                full = t[tuple(slice(None) for _ in shp)]
                S.dma(C("dma_start", out=dd, in_=full), reads=list(bufs), is_output=True)
        S.emit()
    return nc


def _t5_bucket(dist):
    n = np.maximum(dist, 0)
    nf = np.maximum(n, 1).astype(np.float32)
    large = 16 + (np.log(nf / np.float32(16)) / np.float32(np.log(128 / 16)) * np.float32(16)).astype(np.int32)
    large = np.minimum(large, 31)
    return np.where(n < 16, n, large)


def _constants():
    bf = ml_dtypes.bfloat16
    sl = np.arange(128)[:, None]
    tl = np.arange(128)[None, :]
    d_diag = tl - sl
    d_prev = 128 + tl - sl
    idx = np.stack([_t5_bucket(d_diag), _t5_bucket(d_prev)], 0)
    maskAB = np.zeros((128, 4, 128), np.float32)
    maskAB[:, 0, :] = np.where(d_diag >= 0, 0.0, NEG)
    maskAB[:, 1, :] = np.where(d_prev < 128, 0.0, NEG)
    maskAB[:, 2, :] = np.where(d_diag >= 0, 0.0, NEG)
    maskAB[:, 3, :] = 0.0
    E = (np.arange(SEQ)[None, :] // 64 == (np.arange(128) % 64)[:, None]).astype(np.float32).astype(bf)
    nl = np.arange(128)[:, None, None]
    o = np.arange(16)[None, :, None]
    t3 = np.arange(128)[None, None, :]
    cmask = np.where(16 * nl + 15 <= 128 * o + t3, 0.0, NEG).astype(np.float32).astype(bf)
    tri = np.where(sl > tl, 0.0, NEG).astype(np.float32).astype(bf)
    pat = np.zeros((128, 192), np.float32)
    pat[:, 0] = BONUS
    hi = (np.arange(128) >= 64).astype(np.int64)
    pat[np.arange(128), 64 + 63 + hi] = BONUS
    pat[np.arange(128), 64 + 64 + hi] = BONUS
    pat = pat.astype(bf)
    npr = np.arange(256)
    n = npr - 1
    c_lo = 16 * n
    s_lo = 64 * np.arange(64)
    ov = np.clip(np.minimum(c_lo[:, None] + 32, s_lo[None, :] + 64) - np.maximum(c_lo[:, None], s_lo[None, :]), 0, None) / 32.0
    ov[0, :] = 0.0
    ov = np.ascontiguousarray(ov.reshape(2, 128, 64).transpose(1, 0, 2)[:, :, 1:63]).astype(np.float32).astype(bf)
    return idx, maskAB, E, cmask, tri, pat, ov


def _perm_cols():
    o_qa, o_ka, o_va, o_za, o_qb, o_kc, o_vc, o_ks, o_vs, o_kw, o_vw, o_zb, o_gb = (
        0, 512, 640, 768, 1280, 1792, 1920, 2048, 2176, 2304, 2432, 2560, 3072)
    r64 = np.arange(64)
    cols = []
    for base in (o_qa, o_qb):
        for r in range(4):
            cols += [base + r * 64 + r64, base + (4 + r) * 64 + r64]
    cols += [o_ka + np.arange(128), o_ks + np.arange(128), o_kw + np.arange(128)]
    for base in (o_kc, o_vc):
        for g in range(2):
            cols += [base + g * 64 + r64, base + g * 64 + r64]
    cols += [o_va + np.arange(128), o_vs + np.arange(128), o_vw + np.arange(128), o_gb + np.arange(24)]
    cols += [o_za + np.arange(512), o_zb + np.arange(512)]
    cols = np.concatenate(cols)
    assert cols.shape[0] == WCOLS
    return cols


_NC_CACHE = {}


def kernel(x, c, w_ada, b_ada, norm_gain, w_in, b_nsa_gate, q_gain_a, k_gain_a, sinks,
           q_gain_b, k_gain_cmp, k_gain_sel, k_gain_win, cmp_pos_k, cmp_pos_v,
           w_cmp_k1, w_cmp_k2, w_cmp_v1, w_cmp_v2, w_out, rel_bias):
    f = lambda a: np.ascontiguousarray(np.asarray(a, dtype=np.float32))
    x = f(x); c = f(c); w_ada = f(w_ada)[0]; b_ada = f(b_ada)[0]; norm_gain = f(norm_gain)[0]
    w_in = f(w_in)[0]; b_nsa_gate = f(b_nsa_gate)[0]; sinks = f(sinks)[0]
    rel_bias = f(rel_bias); w_out = f(w_out)[0]
    idx, maskAB, E, cmask, tri, pat, ov = _constants()
    biasAB = rel_bias[idx]
    biasA = np.ascontiguousarray(biasAB[..., 0:8].transpose(1, 0, 3, 2))
    biasB = np.ascontiguousarray(biasAB[..., 8:16].transpose(1, 0, 3, 2))
    c31 = np.ascontiguousarray(rel_bias[31:32, 8:16])
    gcols = np.stack([np.tile(f(g)[0], 2) for g in (q_gain_a, k_gain_a, q_gain_b, k_gain_cmp, k_gain_sel, k_gain_win)], 1)
    shared = {
        "w_ada": w_ada,
        "bada_col": np.ascontiguousarray(b_ada[0:2048].reshape(16, 128).T),
        "bada_gate": np.ascontiguousarray(b_ada[2048:3072].reshape(1, DM)),
        "ng_col": np.ascontiguousarray(norm_gain.reshape(8, 128).T),
        "w_in_p": np.ascontiguousarray(w_in[:, _perm_cols()]),
        "bgate": b_nsa_gate.reshape(1, 24),
        "gcols": np.ascontiguousarray(gcols),
        "sinks": sinks.reshape(1, 8),
        "posk": np.ascontiguousarray(f(cmp_pos_k)[0].reshape(16, 128).T),
        "posv": np.ascontiguousarray(f(cmp_pos_v)[0].reshape(16, 128).T),
        "w1k": f(w_cmp_k1)[0], "w1v": f(w_cmp_v1)[0], "w2k": f(w_cmp_k2)[0], "w2v": f(w_cmp_v2)[0],
        "w_out": w_out, "biasA": biasA, "biasB": biasB, "c31": c31, "maskAB": maskAB,
        "Emat": E, "cmask": cmask, "tri": tri, "pat": pat, "ov": ov,
    }
    in_maps = []
    for core in range(NCORES):
        m = dict(shared)
        m["x"] = x[NB * core:NB * (core + 1)]
        cc = c[NB * core:NB * (core + 1)]
        m["cT"] = np.ascontiguousarray(cc.reshape(NB, 8, 128).transpose(2, 1, 0))
        in_maps.append(m)
    if "nc" not in _NC_CACHE:
        _NC_CACHE["nc"] = build_program()
    res = run_bass_kernel_spmd(_NC_CACHE["nc"], in_maps, core_ids=list(range(NCORES)))
    return np.concatenate([np.asarray(r["out"], dtype=np.float32) for r in res.results], axis=0)
```
